# Optimizing a Trainium2 kernel written in Bass

```python
import math
import jax
import jax.numpy as jnp
from jax import lax
import numpy as np

D_MODEL = 1024
BATCH = 8
SEQ = 2048
DEPTH = 1
DEC_BATCH = 128
DEC_SEQ = 8
PAST_LEN = 16384
PAGE_SIZE = 128

RWKV_WIDTH = D_MODEL // 2
RWKV_HEAD = 64
RWKV_HEADS = RWKV_WIDTH // RWKV_HEAD
RWKV_DECAY_LORA = 64
RWKV_A_LORA = 64
RWKV_GATE_LORA = 128
RWKV_LN_EPS = 1e-5 * RWKV_HEAD
GLA_HEADS = 4
GLA_KEY_WIDTH = D_MODEL // 4
GLA_VALUE_WIDTH = D_MODEL // 2
GLA_DK = GLA_KEY_WIDTH // GLA_HEADS
GLA_DV = GLA_VALUE_WIDTH // GLA_HEADS
GLA_GATE_LORA = 16
GLA_GATE_TEMP = 16.0
GLA_CHUNK = 32
FFN_HIDDEN = ((8 * D_MODEL // 3 + 127) // 128) * 128
CONV_WIDTH = 3
NORM_EPS = 1e-6
SHIFT_COLS = 3 * RWKV_WIDTH + RWKV_DECAY_LORA + RWKV_A_LORA + RWKV_GATE_LORA
GLA_COLS = 2 * GLA_KEY_WIDTH + 2 * GLA_VALUE_WIDTH + GLA_GATE_LORA
GATE_COLS = 2 * D_MODEL
IN_COLS = SHIFT_COLS + GLA_COLS + GATE_COLS

kernel_name = "hybrid_rwkv7_gla_convffn_step"


def _split(t, sizes):
    return jnp.split(t, [int(s) for s in np.cumsum(sizes)[:-1]], axis=-1)


def _rmsnorm(x, g):
    xf = x.astype(jnp.float32)
    y = xf * lax.rsqrt(jnp.mean(xf * xf, axis=-1, keepdims=True) + NORM_EPS)
    return (y * g.astype(jnp.float32)).astype(x.dtype)


def _rwkv7_recurrence(r, decay, k, v, a, b, S0):
    def step(S, inp):
        r_t, w_t, k_t, v_t, a_t, b_t = inp
        sa = jnp.einsum("bhvk,bhk->bhv", S, a_t)
        S = S * w_t[:, :, None, :] + sa[..., None] * b_t[:, :, None, :] + v_t[..., None] * k_t[:, :, None, :]
        return S, jnp.einsum("bhvk,bhk->bhv", S, r_t)
    xs = tuple(jnp.moveaxis(t.astype(jnp.float32), 1, 0) for t in (r, decay, k, v, a, b))
    S, o = lax.scan(step, S0.astype(jnp.float32), xs)
    return jnp.moveaxis(o, 0, 1), S


def _gla_chunked(q, k, v, log_a, S0):
    B, T, H, _ = q.shape
    dv = v.shape[-1]
    C = math.gcd(T, GLA_CHUNK)
    N = T // C

    def chunks(t):
        return jnp.moveaxis(t.astype(jnp.float32).reshape(B, N, C, H, t.shape[-1]), 1, 0)

    causal = jnp.tril(jnp.ones((C, C), dtype=bool))

    def step(S, inp):
        qc, kc, vc, gc = inp
        cum = jnp.cumsum(gc, axis=1)
        last = cum[:, -1]
        q_dec = qc * jnp.exp(cum)
        k_inv = kc * jnp.exp(-cum)
        scores = jnp.where(causal, jnp.einsum("bchd,bshd->bhcs", q_dec, k_inv), 0.0)
        o = jnp.einsum("bhcs,bshv->bchv", scores, vc) + jnp.einsum("bchd,bhdv->bchv", q_dec, S)
        k_end = kc * jnp.exp(last[:, None] - cum)
        S = jnp.exp(last)[..., None] * S + jnp.einsum("bshd,bshv->bhdv", k_end, vc)
        return S, o

    S, o = lax.scan(step, S0.astype(jnp.float32), (chunks(q), chunks(k), chunks(v), chunks(log_a)))
    return jnp.moveaxis(o, 0, 1).reshape(B, T, H, dv), S


def _mixer(h, shift_prev, wkv0, gla0, p, l):
    B, T, _ = h.shape
    f32 = jnp.float32
    proj = h @ p["w_in"][l]
    p_rw, p_gla, p_gate = _split(proj, [SHIFT_COLS, GLA_COLS, GATE_COLS])

    prev = jnp.concatenate([shift_prev[:, None].astype(p_rw.dtype), p_rw[:, :-1]], axis=1)
    xs = p_rw + (prev - p_rw) * p["mu_shift"][l]
    new_shift = p_rw[:, -1]
    r, k, v, lw, la, lg = _split(xs, [RWKV_WIDTH] * 3 + [RWKV_DECAY_LORA, RWKV_A_LORA, RWKV_GATE_LORA])
    w = -jax.nn.softplus(-(p["rwkv_w0"][l] + jnp.tanh(lw) @ p["rwkv_w2"][l]).astype(f32)) - 0.5
    decay = jnp.exp(-jnp.exp(w))
    a = jax.nn.sigmoid((p["rwkv_a0"][l] + la @ p["rwkv_a2"][l]).astype(f32))
    g = jax.nn.sigmoid(lg) @ p["rwkv_g2"][l]

    def heads(t):
        return t.reshape(B, T, RWKV_HEADS, RWKV_HEAD)

    rh = heads(r.astype(f32))
    vh = heads(v.astype(f32))
    ah = heads(a)
    kk = heads((k * p["rwkv_k_k"][l]).astype(f32))
    kk = kk / jnp.maximum(jnp.sqrt(jnp.sum(kk * kk, axis=-1, keepdims=True)), 1e-12)
    kh = heads(k.astype(f32) * (1.0 + (a - 1.0) * p["rwkv_k_a"][l].astype(f32)))
    o, wkv = _rwkv7_recurrence(rh, heads(decay), kh, vh, -kk, kk * ah, wkv0)
    mu = jnp.mean(o, axis=-1, keepdims=True)
    var = jnp.mean(jnp.square(o - mu), axis=-1, keepdims=True)
    o = ((o - mu) * lax.rsqrt(var + RWKV_LN_EPS)).reshape(B, T, RWKV_WIDTH)
    o = o * p["rwkv_ln_w"][l] + p["rwkv_ln_b"][l]
    bonus = jnp.sum(rh * kh * p["rwkv_r_k"][l].astype(f32), axis=-1, keepdims=True) * vh
    o = o + bonus.reshape(B, T, RWKV_WIDTH)
    y_a = (o.astype(h.dtype) * g) @ p["w_out_a"][l]

    q, kg, vg, lga, og = _split(p_gla, [GLA_KEY_WIDTH, GLA_KEY_WIDTH, GLA_VALUE_WIDTH, GLA_GATE_LORA, GLA_VALUE_WIDTH])
    log_a = jax.nn.log_sigmoid((lga @ p["gla_wg2"][l] + p["gla_bg"][l]).astype(f32)) / GLA_GATE_TEMP
    ob, gla_state = _gla_chunked(
        (q * GLA_DK ** -0.5).reshape(B, T, GLA_HEADS, GLA_DK),
        kg.reshape(B, T, GLA_HEADS, GLA_DK),
        vg.reshape(B, T, GLA_HEADS, GLA_DV),
        log_a.reshape(B, T, GLA_HEADS, GLA_DK),
        gla0)
    ob = ob * lax.rsqrt(jnp.mean(ob * ob, axis=-1, keepdims=True) + NORM_EPS) * p["gla_norm_w"][l]
    ob = ob.reshape(B, T, GLA_VALUE_WIDTH) * jax.nn.silu(og.astype(f32))
    y_b = ob.astype(h.dtype) @ p["w_out_b"][l]

    gate_a, gate_b = _split(p_gate, [D_MODEL, D_MODEL])
    merged = jax.nn.sigmoid(gate_a) * y_a + jax.nn.sigmoid(gate_b) * y_b
    return merged @ p["w_o"][l], new_shift, wkv.astype(wkv0.dtype), gla_state.astype(gla0.dtype)


def _conv_ffn(h, conv_prev, p, l):
    T = h.shape[1]
    u = h @ p["ffn_w_up"][l]
    ext = jnp.concatenate([conv_prev.astype(u.dtype), u], axis=1)
    w = p["ffn_conv_w"][l]
    c = p["ffn_conv_b"][l] + w[0] * ext[:, 0:T]
    for j in range(1, CONV_WIDTH):
        c = c + w[j] * ext[:, j:j + T]
    val, gate = _split(c, [FFN_HIDDEN, FFN_HIDDEN])
    y = (jax.nn.gelu(gate) * val) @ p["ffn_w_down"][l]
    return y, ext[:, -(CONV_WIDTH - 1):]


def _trunk(x, shift0, wkv0, gla0, conv0, p):
    s_shift, s_wkv, s_gla, s_conv = [], [], [], []
    for l in range(DEPTH):
        y, n_shift, n_wkv, n_gla = _mixer(_rmsnorm(x, p["norm_mix"][l]), shift0[l], wkv0[l], gla0[l], p, l)
        x = x + y
        y, n_conv = _conv_ffn(_rmsnorm(x, p["norm_ffn"][l]), conv0[l], p, l)
        x = x + y
        s_shift.append(n_shift)
        s_wkv.append(n_wkv)
        s_gla.append(n_gla)
        s_conv.append(n_conv)
    return (_rmsnorm(x, p["norm_final"]), jnp.stack(s_shift), jnp.stack(s_wkv),
            jnp.stack(s_gla), jnp.stack(s_conv))


def setup_inputs(seed: int = 0) -> dict:
    key = jax.random.key(seed)
    ks = iter(jax.random.split(key, 40))

    def nrm(shape, scale):
        return scale * jax.random.normal(next(ks), shape, jnp.float32)

    def unif(shape, lo, hi):
        return jax.random.uniform(next(ks), shape, jnp.float32, lo, hi)

    L = DEPTH
    F2 = 2 * FFN_HIDDEN
    return {
        "x_prompt": nrm((BATCH, SEQ, D_MODEL), 1.0),
        "x_sample": nrm((DEC_BATCH, DEC_SEQ, D_MODEL), 1.0),
        "state_rwkv_shift": nrm((L, DEC_BATCH, SHIFT_COLS), 1.0),
        "state_rwkv_wkv": nrm((L, DEC_BATCH, RWKV_HEADS, RWKV_HEAD, RWKV_HEAD), 0.3),
        "state_gla": nrm((L, DEC_BATCH, GLA_HEADS, GLA_DK, GLA_DV), 0.3),
        "state_ffn_conv": nrm((L, DEC_BATCH, CONV_WIDTH - 1, F2), 1.0),
        "norm_mix": 1.0 + nrm((L, D_MODEL), 0.05),
        "w_in": nrm((L, D_MODEL, IN_COLS), D_MODEL ** -0.5),
        "mu_shift": unif((L, SHIFT_COLS), 0.0, 1.0),
        "rwkv_w0": unif((L, RWKV_WIDTH), -4.0, 0.0),
        "rwkv_w2": nrm((L, RWKV_DECAY_LORA, RWKV_WIDTH), 0.5 * RWKV_DECAY_LORA ** -0.5),
        "rwkv_a0": nrm((L, RWKV_WIDTH), 0.1),
        "rwkv_a2": nrm((L, RWKV_A_LORA, RWKV_WIDTH), RWKV_A_LORA ** -0.5),
        "rwkv_g2": nrm((L, RWKV_GATE_LORA, RWKV_WIDTH), RWKV_GATE_LORA ** -0.5),
        "rwkv_k_k": 0.85 + nrm((L, RWKV_WIDTH), 0.05),
        "rwkv_k_a": 1.0 + nrm((L, RWKV_WIDTH), 0.05),
        "rwkv_r_k": nrm((L, RWKV_HEADS, RWKV_HEAD), 0.1),
        "rwkv_ln_w": 1.0 + nrm((L, RWKV_WIDTH), 0.05),
        "rwkv_ln_b": nrm((L, RWKV_WIDTH), 0.02),
        "gla_wg2": nrm((L, GLA_GATE_LORA, GLA_KEY_WIDTH), GLA_GATE_LORA ** -0.5),
        "gla_bg": 1.0 + nrm((L, GLA_KEY_WIDTH), 0.5),
        "gla_norm_w": 1.0 + nrm((L, GLA_DV), 0.05),
        "w_out_a": nrm((L, RWKV_WIDTH, D_MODEL), RWKV_WIDTH ** -0.5),
        "w_out_b": nrm((L, GLA_VALUE_WIDTH, D_MODEL), GLA_VALUE_WIDTH ** -0.5),
        "w_o": nrm((L, D_MODEL, D_MODEL), D_MODEL ** -0.5),
        "norm_ffn": 1.0 + nrm((L, D_MODEL), 0.05),
        "ffn_w_up": nrm((L, D_MODEL, F2), D_MODEL ** -0.5),
        "ffn_conv_w": nrm((L, CONV_WIDTH, F2), CONV_WIDTH ** -0.5),
        "ffn_conv_b": nrm((L, F2), 0.02),
        "ffn_w_down": nrm((L, FFN_HIDDEN, D_MODEL), FFN_HIDDEN ** -0.5),
        "norm_final": 1.0 + nrm((D_MODEL,), 0.05),
    }


def reference(x_prompt, x_sample, state_rwkv_shift, state_rwkv_wkv, state_gla, state_ffn_conv,
              norm_mix, w_in, mu_shift, rwkv_w0, rwkv_w2, rwkv_a0, rwkv_a2, rwkv_g2,
              rwkv_k_k, rwkv_k_a, rwkv_r_k, rwkv_ln_w, rwkv_ln_b, gla_wg2, gla_bg, gla_norm_w,
              w_out_a, w_out_b, w_o, norm_ffn, ffn_w_up, ffn_conv_w, ffn_conv_b, ffn_w_down,
              norm_final):
    params = dict(norm_mix=norm_mix, w_in=w_in, mu_shift=mu_shift, rwkv_w0=rwkv_w0,
                  rwkv_w2=rwkv_w2, rwkv_a0=rwkv_a0, rwkv_a2=rwkv_a2, rwkv_g2=rwkv_g2,
                  rwkv_k_k=rwkv_k_k, rwkv_k_a=rwkv_k_a, rwkv_r_k=rwkv_r_k, rwkv_ln_w=rwkv_ln_w,
                  rwkv_ln_b=rwkv_ln_b, gla_wg2=gla_wg2, gla_bg=gla_bg, gla_norm_w=gla_norm_w,
                  w_out_a=w_out_a, w_out_b=w_out_b, w_o=w_o, norm_ffn=norm_ffn,
                  ffn_w_up=ffn_w_up, ffn_conv_w=ffn_conv_w, ffn_conv_b=ffn_conv_b,
                  ffn_w_down=ffn_w_down, norm_final=norm_final)
    bp = x_prompt.shape[0]
    dt = x_prompt.dtype
    zero_shift = jnp.zeros((DEPTH, bp, SHIFT_COLS), dt)
    zero_wkv = jnp.zeros((DEPTH, bp, RWKV_HEADS, RWKV_HEAD, RWKV_HEAD), dt)
    zero_gla = jnp.zeros((DEPTH, bp, GLA_HEADS, GLA_DK, GLA_DV), dt)
    zero_conv = jnp.zeros((DEPTH, bp, CONV_WIDTH - 1, 2 * FFN_HIDDEN), dt)
    y_p, shift_p, wkv_p, gla_p, conv_p = _trunk(x_prompt, zero_shift, zero_wkv, zero_gla, zero_conv, params)
    y_s, shift_s, wkv_s, gla_s, conv_s = _trunk(x_sample, state_rwkv_shift, state_rwkv_wkv, state_gla,
                                                state_ffn_conv, params)
    return (y_p, y_s, shift_p, wkv_p, gla_p, conv_p, shift_s, wkv_s, gla_s, conv_s)
```

```python
import contextlib
import numpy as np
import concourse.bass as bass
import concourse.mybir as mybir
from concourse.bass_utils import run_bass_kernel_spmd

F32 = mybir.dt.float32
BF16 = mybir.dt.bfloat16
AF = mybir.ActivationFunctionType
ALU = mybir.AluOpType
AX = mybir.AxisListType

NCORES = 8
D = 1024
NPB = 16
NSB = 4
NBLK = NPB + NSB
SHIFT = 1792
INC = 5392
FH = 2816
F2 = 5632
GLA0 = 1792
GATE0 = 3344
C0 = -0.6065306597126334
GELU_S = 1.5957691216057308

ENGS = ("pe", "act", "dve", "pool", "sp")
MAXOPS = None
LINES = []


class Op:
    __slots__ = ("eng", "fn", "reads", "writes", "dma", "deps", "sig", "idx",
                 "dsem", "dval", "dprev", "isout", "mm")

    def __init__(self, eng, fn, reads, writes, dma, isout, mm):
        self.eng = eng
        self.fn = fn
        self.reads = reads
        self.writes = writes
        self.dma = dma
        self.deps = []
        self.sig = None
        self.dsem = None
        self.dval = None
        self.dprev = None
        self.isout = isout
        self.mm = mm


def _norm(k):
    return k if isinstance(k, tuple) else (k, None)


class Prog:
    def __init__(self, nc, semstack, tag, n_dma_sems=6):
        self.nc = nc
        self.ops = []
        self.n_dma_sems = n_dma_sems
        self.st = {}
        self.semstack = semstack
        self.tag = tag

    @staticmethod
    def _conf(a, b):
        return a is None or b is None or a == b

    def add(self, eng, fn, reads=(), writes=(), dma=False, isout=False, mm=False):
        op = Op(eng, fn, [_norm(k) for k in reads], [_norm(k) for k in writes], dma, isout, mm)
        op.idx = len(self.ops)
        if MAXOPS is not None:
            import sys as _s
            f = _s._getframe(1)
            while f is not None and f.f_code.co_name != "build":
                f = f.f_back
            LINES.append(f.f_lineno if f is not None else -1)
        deps = {}
        for (name, sub) in op.reads:
            s = self.st.setdefault(name, {"w": {}, "r": {}})
            for ws, wop in s["w"].items():
                if self._conf(ws, sub):
                    deps[wop.idx] = wop
        for (name, sub) in op.writes:
            s = self.st.setdefault(name, {"w": {}, "r": {}})
            for ws, wop in s["w"].items():
                if self._conf(ws, sub):
                    if not (op.mm and wop.mm):
                        deps[wop.idx] = wop
            for rs, rops in s["r"].items():
                if self._conf(rs, sub):
                    for rop in rops:
                        deps[rop.idx] = rop
        for (name, sub) in op.reads:
            self.st[name]["r"].setdefault(sub, []).append(op)
        for (name, sub) in op.writes:
            s = self.st[name]
            if sub is None:
                s["w"] = {None: op}
                s["r"] = {}
            else:
                s["w"][sub] = op
                s["r"][sub] = []
        deps.pop(op.idx, None)
        op.deps = list(deps.values())
        self.ops.append(op)
        return op

    def emit(self):
        nc = self.nc
        if MAXOPS is not None:
            self.ops = self.ops[:MAXOPS]
        ops = self.ops
        needed = set()
        for op in ops:
            for d in op.deps:
                needed.add(d.idx)
        cnt = {e: 0 for e in ENGS}
        for op in ops:
            if not op.dma and op.idx in needed:
                cnt[op.eng] += 1
                op.sig = cnt[op.eng]
        dcount = {e: 0 for e in ENGS}
        last_on_slot = {}
        for op in ops:
            if not op.dma:
                continue
            j = dcount[op.eng]
            dcount[op.eng] += 1
            slot = j % self.n_dma_sems
            op.dsem = (op.eng, slot)
            op.dval = 16 * (j // self.n_dma_sems + 1)
            op.dprev = last_on_slot.get(op.dsem)
            last_on_slot[op.dsem] = op
        out_ops = [op for op in ops if op.dma and op.isout]
        per_eng = {e: [op for op in ops if op.eng == e] for e in ENGS}
        es = self.semstack
        csem = {e: es.enter_context(nc.semaphore("cs%s_%s" % (self.tag, e)))
                for e in ENGS if e != "sp"}
        dsem = {}
        for e in ENGS:
            for s in range(min(self.n_dma_sems, dcount[e])):
                dsem[(e, s)] = es.enter_context(nc.semaphore("ds%s_%s%d" % (self.tag, e, s)))

        def run_engine(e, eng):
            known = {}

            def wait(key, sem, val):
                if known.get(key, 0) >= val:
                    return
                known[key] = val
                eng.wait_ge(sem, val)

            for op in per_eng[e]:
                for d in op.deps:
                    if d.dma:
                        wait(d.dsem, dsem[d.dsem], d.dval)
                    else:
                        wait(d.eng, csem[d.eng], d.sig)
                if op.dma and op.dprev is not None:
                    wait(op.dsem, dsem[op.dsem], op.dprev.dval)
                ins = op.fn(eng)
                if op.dma:
                    ins.then_inc(dsem[op.dsem], 16)
                elif op.sig is not None:
                    ins.then_inc(csem[e], 1)
            if e == "sp":
                for op in out_ops:
                    wait(op.dsem, dsem[op.dsem], op.dval)
                for key, op in last_on_slot.items():
                    wait(op.dsem, dsem[op.dsem], op.dval)

        with nc.Block() as block:
            @block.sync
            def _(eng):
                run_engine("sp", eng)

            @block.tensor
            def _(eng):
                run_engine("pe", eng)

            @block.scalar
            def _(eng):
                run_engine("act", eng)

            @block.vector
            def _(eng):
                run_engine("dve", eng)

            @block.gpsimd
            def _(eng):
                run_engine("pool", eng)


class H:
    def __init__(self, P):
        self.P = P

    def act(self, out, in_, func, r, w, **kw):
        self.P.add("act", lambda e: e.activation(out=out, in_=in_, func=func, **kw), r, w)

    def tt(self, eng, out, in0, in1, op, r, w):
        self.P.add(eng, lambda e: e.tensor_tensor(out=out, in0=in0, in1=in1, op=op), r, w)

    def ts(self, eng, out, in0, s1, s2, op0, op1, r, w):
        if op1 is None:
            self.P.add(eng, lambda e: e.tensor_scalar(out=out, in0=in0, scalar1=s1, scalar2=None, op0=op0), r, w)
        else:
            self.P.add(eng, lambda e: e.tensor_scalar(out=out, in0=in0, scalar1=s1, scalar2=s2, op0=op0, op1=op1), r, w)

    def stt(self, eng, out, in0, scalar, in1, op0, op1, r, w):
        self.P.add(eng, lambda e: e.scalar_tensor_tensor(out=out, in0=in0, scalar=scalar, in1=in1, op0=op0, op1=op1), r, w)

    def cp(self, eng, out, in_, r, w):
        if eng == "act":
            self.P.add("act", lambda e: e.activation(out=out, in_=in_, func=AF.Copy), r, w)
        else:
            self.P.add(eng, lambda e: e.tensor_copy(out=out, in_=in_), r, w)

    def memset(self, eng, ap, val, w):
        self.P.add(eng, lambda e: e.memset(ap, val), [], w)

    def recip(self, out, in_, r, w):
        self.P.add("dve", lambda e: e.reciprocal(out=out, in_=in_), r, w)

    def scan(self, out, d0, d1, r, w):
        self.P.add("dve", lambda e: e.tensor_tensor_scan(out=out, data0=d0, data1=d1, initial=0.0,
                                                         op0=ALU.mult, op1=ALU.add), r, w)

    def rsum(self, out, in_, r, w):
        self.P.add("dve", lambda e: e.tensor_reduce(out=out, in_=in_, axis=AX.X, op=ALU.add), r, w)

    def mm(self, out, lhsT, rhs, start, stop, r, w, tp=None):
        if tp is None:
            self.P.add("pe", lambda e: e.matmul(out, lhsT=lhsT, rhs=rhs, start=start, stop=stop), r, w, mm=True)
        else:
            self.P.add("pe", lambda e: e.matmul(out, lhsT=lhsT, rhs=rhs, start=start, stop=stop,
                                                tile_position=tp), r, w, mm=True)

    def tr(self, out, in_, ident, r, w):
        self.P.add("pe", lambda e: e.transpose(out=out, in_=in_, identity=ident), r, w, mm=True)

    def dma(self, q, out, in_, r, w, isout=False, slow=False):
        if slow:
            self.P.add(q, lambda e: e.dma_start(out=out, in_=in_, allow_slow_non_contiguous=True), r, w,
                       dma=True, isout=isout)
        else:
            self.P.add(q, lambda e: e.dma_start(out=out, in_=in_), r, w, dma=True, isout=isout)


def bc(ap, shape):
    a = ap
    while len(a.shape) < len(shape):
        a = a.unsqueeze(len(a.shape))
    return a.to_broadcast(list(shape))


C_ID = 0
C_BO = 128
C_RST = 256
C_VS = 384
C_ONE = 512
C_M4 = 640
C_SL = 1152
C_IU = 1280
NCST = 1408


def make_consts():
    c = np.zeros((128, NCST), np.float32)
    i = np.arange(128)
    same = (i[:, None] // 32) == (i[None, :] // 32)
    su = (same & (i[:, None] < i[None, :])).astype(np.float32)
    iu = (same & (i[:, None] <= i[None, :])).astype(np.float32)
    sl = (same & (i[:, None] > i[None, :])).astype(np.float32)
    c[:, C_ID:C_ID + 128] = np.eye(128, dtype=np.float32)
    c[:, C_BO:C_BO + 128] = ((i[:, None] // 64) == (i[None, :] // 64)).astype(np.float32)
    c[:, C_RST:C_RST + 128] = (i[None, :] % 32 != 0).astype(np.float32)
    c[:, C_VS:C_VS + 128] = (i[None, :] % 32 < 8).astype(np.float32)
    c[:, C_ONE:C_ONE + 128] = 1.0
    c[:, C_M4:C_M4 + 512] = np.concatenate([su, iu, su, iu], axis=1)
    c[:, C_SL:C_SL + 128] = sl
    c[:, C_IU:C_IU + 128] = iu
    return c


PR_G = 0
PR_MU = 8
PR_W0 = 22
PR_A0 = 26
PR_KK = 30
PR_KA = 34
PR_RK = 38
PR_LW = 42
PR_LB = 46
PR_BG = 50
PR_NW = 52
PR_OMKA = 53
NPRM = 64


def build(nc, dbg=None, phases="abc", blocks=None):
    BLKS = list(range(NBLK)) if blocks is None else list(blocks)
    gs = contextlib.ExitStack()

    def din(name, shape, dt=F32):
        return nc.dram_tensor(name, list(shape), dt, kind="ExternalInput").ap()

    def dout(name, shape):
        return nc.dram_tensor(name, list(shape), F32, kind="ExternalOutput").ap()

    def dscr(name, shape, dt):
        return nc.dram_tensor(name, list(shape), dt, kind="Internal").ap()

    xp = din("xp", [2048, D])
    xsm = din("xs", [16, 8, D])
    st_shift = din("st_shift", [16, SHIFT])
    st_wkv = din("st_wkv", [16, 8, 64, 64])
    st_gla = din("st_gla", [16, 4, 64, 128])
    st_conv = din("st_conv", [16, 2, F2])
    cst_d = din("cst", [128, NCST])
    norm_mix = din("norm_mix", [D])
    w_in = din("w_in", [D, INC])
    mu_shift = din("mu_shift", [SHIFT])
    rwkv_w0 = din("rwkv_w0", [512])
    rwkv_w2 = din("rwkv_w2", [64, 512])
    rwkv_a0 = din("rwkv_a0", [512])
    rwkv_a2 = din("rwkv_a2", [64, 512])
    rwkv_g2 = din("rwkv_g2", [128, 512])
    rwkv_k_k = din("rwkv_k_k", [512])
    rwkv_k_a = din("rwkv_k_a", [512])
    rwkv_r_k = din("rwkv_r_k", [512])
    rwkv_ln_w = din("rwkv_ln_w", [512])
    rwkv_ln_b = din("rwkv_ln_b", [512])
    gla_wg2 = din("gla_wg2", [16, 256])
    gla_bg = din("gla_bg", [256])
    gla_norm_w = din("gla_norm_w", [128])
    w_out_a = din("w_out_a", [512, D])
    w_out_b = din("w_out_b", [512, D])
    w_o = din("w_o", [D, D])
    norm_ffn = din("norm_ffn", [D])
    ffn_w_up = din("ffn_w_up", [D, F2])
    ffn_conv_w = din("ffn_conv_w", [3, F2])
    ffn_conv_b = din("ffn_conv_b", [F2])
    ffn_w_down = din("ffn_w_down", [FH, D])
    norm_final = din("norm_final", [D])

    yp = dout("yp", [2048, D])
    ysm = dout("ys", [16, 8, D])
    o_shift_p = dout("o_shift_p", [1, SHIFT])
    o_wkv_p = dout("o_wkv_p", [1, 8, 64, 64])
    o_gla_p = dout("o_gla_p", [1, 4, 64, 128])
    o_conv_p = dout("o_conv_p", [1, 2, F2])
    o_shift_s = dout("o_shift_s", [16, SHIFT])
    o_wkv_s = dout("o_wkv_s", [16, 8, 64, 64])
    o_gla_s = dout("o_gla_s", [16, 4, 64, 128])
    o_conv_s = dout("o_conv_s", [16, 2, F2])

    og_s = dscr("og_s", [NBLK, 128, 512], BF16)
    ob_s = dscr("ob_s", [NBLK, 128, 512], BF16)
    x1_s = dscr("x1_s", [NBLK * 128, D], F32)

    dbg_out = None
    if dbg is not None:
        dbg_out = dout("dbg", dbg["shape"])

    def geom(b):
        return (b >= NPB, 4, 32) if b >= NPB else (False, 1, 128)

    def load_consts(h, sb, q="sp"):
        cst = sb("cst", [128, NCST], F32)
        h.dma(q, cst[:], cst_d, [], ["cst"])
        idb = sb("idb", [128, 128], BF16)
        h.cp("dve", idb[:], cst[:, C_ID:C_ID + 128], ["cst"], ["idb"])
        return cst, idb

    def load_x_block(h, b, xt, xtn):
        samp, nseq, L = geom(b)
        if not samp:
            h.dma("sp", xt[:], xp[b * 128:(b + 1) * 128, :], [], [xtn])
        else:
            j = b - NPB
            h.memset("pool", xt[:], 0.0, [xtn])
            for q in range(4):
                h.dma("sp", xt[32 * q:32 * q + 8, :], xsm[4 * j + q], [], [(xtn, q)])

    def rms_to_fm(h, T, xt, xtn, gcol, sfx=""):
        xn, ss, rstd, hT, psT, idb, prm = T["xn"], T["ss"], T["rstd"], T["hT"], T["psT"], T["idb"], T["prm"]
        h.act(xn[:], xt[:], AF.Square, [xtn], ["xn", "ss"], accum_out=ss[:])
        h.ts("dve", rstd[:], ss[:], 1.0 / D, 1e-6, ALU.mult, ALU.add, ["ss"], ["rstd"])
        h.act(rstd[:], rstd[:], AF.Sqrt, ["rstd"], ["rstd"])
        h.recip(rstd[:], rstd[:], ["rstd"], ["rstd"])
        h.act(xn[:], xt[:], AF.Copy, [xtn, "rstd"], ["xn"], scale=rstd[:, 0:1])
        for c in range(8):
            h.tr(psT[:, c, :], xn[:, c * 128:(c + 1) * 128], idb[:], ["xn", "idb"], [("psT", c)])
        h.tt("dve", hT[:], psT[:], bc(prm[:, gcol:gcol + 8], [128, 8, 128]), ALU.mult,
             ["psT", "prm"], ["hT"])

    def load_param_cols(h, prm, col, src, n):
        h.dma("sp", prm[:, col:col + n], src.rearrange("(c p) -> p c", p=128), [], [("prm", col)], slow=True)

    with contextlib.ExitStack() as ph:
      if "a" in phases:
        P = Prog(nc, gs, "a")
        h = H(P)

        def sb(name, shape, dt):
            return ph.enter_context(nc.sbuf_tensor("a_" + name, list(shape), dt))

        def psb(name, shape, dt):
            return ph.enter_context(nc.psum_tensor("a_" + name, list(shape), dt))

        T = {}
        cst, idb = load_consts(h, sb)
        T["idb"] = idb
        prm = sb("prm", [128, NPRM], F32)
        T["prm"] = prm
        load_param_cols(h, prm, PR_G, norm_mix, 8)
        load_param_cols(h, prm, PR_MU, mu_shift, 14)
        load_param_cols(h, prm, PR_W0, rwkv_w0, 4)
        load_param_cols(h, prm, PR_A0, rwkv_a0, 4)
        load_param_cols(h, prm, PR_KK, rwkv_k_k, 4)
        load_param_cols(h, prm, PR_KA, rwkv_k_a, 4)
        load_param_cols(h, prm, PR_RK, rwkv_r_k, 4)
        load_param_cols(h, prm, PR_LW, rwkv_ln_w, 4)
        load_param_cols(h, prm, PR_LB, rwkv_ln_b, 4)
        load_param_cols(h, prm, PR_BG, gla_bg, 2)
        load_param_cols(h, prm, PR_NW, gla_norm_w, 1)
        h.ts("dve", prm[:, PR_BG:PR_BG + 2], prm[:, PR_BG:PR_BG + 2], -1.0, None, ALU.mult, None,
             [("prm", PR_BG)], [("prm", PR_BG)])
        h.ts("dve", prm[:, PR_OMKA:PR_OMKA + 4], prm[:, PR_KA:PR_KA + 4], -1.0, 1.0, ALU.mult, ALU.add,
             [("prm", PR_KA)], [("prm", PR_OMKA)])

        NA1 = GATE0
        win = sb("win", [128, 8, NA1], BF16)
        w_in_v = w_in.rearrange("(c p) n -> p c n", p=128)
        for kc in range(8):
            h.dma("pool", win[:, kc, :], w_in_v[:, kc, 0:NA1], [], [("win", kc)])
        w2a2 = sb("w2a2", [128, 512], BF16)
        h.dma("pool", w2a2[0:64, :], rwkv_w2, [], [("w2a2", 0)])
        h.dma("pool", w2a2[64:128, :], rwkv_a2, [], [("w2a2", 1)])
        g2 = sb("g2", [128, 512], BF16)
        h.dma("pool", g2[:], rwkv_g2, [], ["g2"])
        wg2 = sb("wg2", [16, 256], BF16)
        h.dma("pool", wg2[:], gla_wg2, [], ["wg2"])

        xt = sb("xt", [128, D], F32)
        T["xn"] = sb("xn", [128, D], BF16)
        T["ss"] = sb("ss", [128, 1], F32)
        T["rstd"] = sb("rstd", [128, 1], F32)
        T["hT"] = sb("hT", [128, 8, 128], BF16)
        hT = T["hT"]
        prw = sb("prw", [128, 14, 132], F32)
        lastc = sb("lastc", [128, 14, 1], F32)
        xs = sb("xs", [128, 14, 128], F32)
        Fm = [sb("F%d" % i, [128, 4, 128], F32) for i in range(12)]
        Hm = [sb("Hb%d" % i, [128, 4, 128], BF16) for i in range(8)]
        AR = sb("AR", [128, 4, 2, 128], BF16)
        tok = [sb("tok%d" % i, [128, 512], BF16) for i in range(4)]
        SCH = sb("SCH", [128, 8, 4, 128], BF16)
        INV = [sb("INV%d" % i, [128, 8, 128], BF16) for i in range(8)]
        lora_in = sb("lora_in", [128, 128], BF16)
        slg = sb("slg", [128, 128], BF16)
        Zbf = sb("Zbf", [128, 512], BF16)
        Yf = sb("Yf", [128, 512], F32)
        WTbf = sb("WTbf", [128, 4, 128], BF16)
        Ubf = sb("Ubf", [128, 512], BF16)
        Pst = sb("Pst", [128, 4, 64], F32)
        Ptmp = sb("Ptmp", [128, 4, 64], F32)
        Pbf = sb("Pbf", [128, 4, 64], BF16)
        o1 = sb("o1", [128, 512], F32)
        oo = sb("oo", [128, 512], F32)
        osq = sb("osq", [128, 512], F32)
        stat = sb("stat", [128, 64], F32)
        ogbf = sb("ogbf", [128, 4, 128], BF16)
        obbf = sb("obbf", [128, 4, 128], BF16)
        Sin = sb("Sin", [64, 8, 64], F32)
        Sout = sb("Sout", [64, 8, 64], F32)
        shs = sb("shs", [4, SHIFT], F32)
        shf = sb("shf", [128, 14, 4], F32)
        sho = sb("sho", [4, SHIFT], F32)
        lga = sb("lga", [16, 128], BF16)
        Sg = sb("Sg", [128, 2, 128], F32)
        Sgt = sb("Sgt", [128, 2, 128], F32)
        Sgbf = sb("Sgbf", [128, 4, 2, 128], BF16)
        SgIn = sb("SgIn", [128, 4, 2, 128], F32)
        SgOut = sb("SgOut", [128, 4, 2, 128], F32)

        T["psT"] = psb("psT", [128, 8, 128], BF16)
        psT = T["psT"]
        PS = [psb("ps%d" % i, [128, 512], F32) for i in range(7)]

        idf = cst[:, C_ID:C_ID + 128]
        bones = cst[:, C_BO:C_BO + 128]
        rstm = cst[:, C_RST:C_RST + 128]
        m4 = cst[:, C_M4:C_M4 + 512]
        msl = cst[:, C_SL:C_SL + 128]
        miu = cst[:, C_IU:C_IU + 128]

        h.memset("pool", Pst[:], 0.0, ["Pst"])
        h.memset("pool", Pbf[:], 0.0, ["Pbf"])
        h.memset("pool", Sg[:], 0.0, ["Sg"])
        h.memset("pool", lastc[:], 0.0, ["lastc"])

        def v4(ap, nseq, L):
            return ap.rearrange("p m (s l) -> p m s l", s=nseq)

        for b in BLKS:
            samp, nseq, L = geom(b)
            j = b - NPB
            valid = cst[:, (C_VS if samp else C_ONE):(C_VS if samp else C_ONE) + 128]
            load_x_block(h, b, xt, "xt")
            rms_to_fm(h, T, xt, "xt", PR_G)

            W = nseq * (L + 1)
            pv = prw[:, :, 0:W].rearrange("p m (s l) -> p m s l", s=nseq)
            for gi in range(4):
                ms = list(range(4 * gi, min(4 * gi + 4, 14)))
                pt = PS[gi % 2]
                pn = "ps%d" % (gi % 2)
                for mi, m in enumerate(ms):
                    for kc in range(8):
                        h.mm(pt[:, mi * 128:(mi + 1) * 128], win[:, kc, m * 128:(m + 1) * 128], hT[:, kc, :],
                             kc == 0, kc == 7, [("win", kc), "hT"], [pn])
                nm = len(ms)
                h.cp("act", pv[:, ms[0]:ms[0] + nm, :, 1:L + 1],
                     pt[:, 0:nm * 128].rearrange("p (m s l) -> p m s l", m=nm, s=nseq),
                     [pn], ["prw"])
            if not samp:
                h.cp("pool", pv[:, :, 0, 0:1], lastc[:], ["lastc"], ["prw"])
            else:
                h.dma("sp", shs[:], st_shift[4 * j:4 * j + 4, :], [], ["shs"])
                for g4 in range(4):
                    ms = list(range(4 * g4, min(4 * g4 + 4, 14)))
                    for mi, m in enumerate(ms):
                        h.tr(PS[2][:, mi * 4:(mi + 1) * 4], shs[0:4, m * 128:(m + 1) * 128], idf[0:4, 0:4],
                             ["shs", "cst"], ["ps2"])
                    nm = len(ms)
                    h.cp("act", pv[:, ms[0]:ms[0] + nm, :, 0],
                         PS[2][:, 0:nm * 4].rearrange("p (m s) -> p m s", m=nm), ["ps2"], ["prw"])
            last_prompt = (b == NPB - 1)
            if samp or last_prompt:
                ncol = 4 if samp else 1
                if samp:
                    h.cp("pool", shf[:, :, 0:4], pv[:, :, :, 8], ["prw"], ["shf"])
                else:
                    h.cp("pool", shf[:, :, 0:1], pv[:, :, 0, 128:129], ["prw"], ["shf"])
                for g4 in range(4):
                    ms = list(range(4 * g4, min(4 * g4 + 4, 14)))
                    for mi, m in enumerate(ms):
                        h.tr(PS[2][0:ncol, mi * 128:(mi + 1) * 128], shf[:, m, 0:ncol], idf,
                             ["shf", "cst"], ["ps2"])
                    nm = len(ms)
                    h.cp("act", sho[0:ncol, ms[0] * 128:(ms[0] + nm) * 128], PS[2][0:ncol, 0:nm * 128],
                         ["ps2"], ["sho"])
                if samp:
                    h.dma("sp", o_shift_s[4 * j:4 * j + 4, :], sho[0:4, :], ["sho"], [], isout=True)
                else:
                    h.dma("sp", o_shift_p[0:1, :], sho[0:1, :], ["sho"], [], isout=True)
            if not samp:
                h.cp("pool", lastc[:], pv[:, :, 0, 128:129], ["prw"], ["lastc"])
            xs4 = xs[:].rearrange("p m (s l) -> p m s l", s=nseq)
            cur = pv[:, :, :, 1:L + 1]
            prv = pv[:, :, :, 0:L]
            h.tt("pool", xs4, prv, cur, ALU.subtract, ["prw"], ["xs"])
            h.tt("pool", xs4, xs4, bc(prm[:, PR_MU:PR_MU + 14], [128, 14, nseq, L]), ALU.mult,
                 ["xs", "prm"], ["xs"])
            h.tt("pool", xs4, xs4, cur, ALU.add, ["xs", "prw"], ["xs"])
            rT = xs[:, 0:4, :]
            kT = xs[:, 4:8, :]
            vT = xs[:, 8:12, :]

            sw, aa, gT, cum, E, Einv, Eprev, Eend, kk, kh, tmp, bon = Fm
            n = lambda i: "F%d" % i
            N_SW, N_AA, N_GT, N_CUM, N_E, N_EINV, N_EPREV, N_EEND, N_KK, N_KH, N_TMP, N_BON = [n(i) for i in range(12)]
            bT, kTb, BpT, KpT, vbf = Hm[0], Hm[1], Hm[2], Hm[3], Hm[4]
            h.act(lora_in[0:64, :], xs[0:64, 12, :], AF.Tanh, ["xs"], [("lora_in", 0)])
            h.cp("act", lora_in[64:128, :], xs[64:128, 12, :], ["xs"], [("lora_in", 1)])
            h.act(slg[:], xs[:, 13, :], AF.Sigmoid, ["xs"], ["slg"])
            for hg in range(4):
                h.mm(PS[2][:, hg * 128:(hg + 1) * 128], w2a2[0:64, hg * 128:(hg + 1) * 128], lora_in[0:64, :],
                     True, True, [("w2a2", 0), ("lora_in", 0)], ["ps2"])
            for hg in range(4):
                h.mm(PS[3][:, hg * 128:(hg + 1) * 128], w2a2[64:128, hg * 128:(hg + 1) * 128], lora_in[64:128, :],
                     True, True, [("w2a2", 1), ("lora_in", 1)], ["ps3"])
            for hg in range(4):
                h.mm(PS[4][:, hg * 128:(hg + 1) * 128], g2[:, hg * 128:(hg + 1) * 128], slg[:],
                     True, True, ["g2", "slg"], ["ps4"])
            for hg in range(4):
                h.act(sw[:, hg, :], PS[2][:, hg * 128:(hg + 1) * 128], AF.Sigmoid, ["ps2", "prm"], [(N_SW, hg)],
                      bias=prm[:, PR_W0 + hg:PR_W0 + hg + 1])
                h.act(aa[:, hg, :], PS[3][:, hg * 128:(hg + 1) * 128], AF.Sigmoid, ["ps3", "prm"], [(N_AA, hg)],
                      bias=prm[:, PR_A0 + hg:PR_A0 + hg + 1])
            h.cp("act", gT[:], PS[4][:].rearrange("p (c l) -> p c l", c=4), ["ps4"], [N_GT])
            h.stt("dve", sw[:], sw[:], C0, valid.unsqueeze(1).to_broadcast([128, 4, 128]),
                  ALU.mult, ALU.mult, [N_SW, "cst"], [N_SW])
            for hg in range(4):
                h.scan(cum[:, hg, :], rstm, sw[:, hg, :], [N_SW, "cst"], [(N_CUM, hg)])
            h.act(E[:], cum[:], AF.Exp, [N_CUM], [N_E])
            h.act(Einv[:], cum[:], AF.Exp, [N_CUM], [N_EINV], scale=-1.0)
            h.tt("pool", tmp[:], cum[:], sw[:], ALU.subtract, [N_CUM, N_SW], [N_TMP])
            h.act(Eprev[:], tmp[:], AF.Exp, [N_TMP], [N_EPREV])
            cum4 = cum[:].rearrange("p g (c l) -> p g c l", c=4)
            h.tt("pool", tmp[:].rearrange("p g (c l) -> p g c l", c=4),
                 cum4[:, :, :, 31:32].to_broadcast([128, 4, 4, 32]), cum4, ALU.subtract, [N_CUM], [N_TMP])
            h.act(Eend[:], tmp[:], AF.Exp, [N_TMP], [N_EEND])
            if samp:
                h.tt("pool", Eend[:], Eend[:], valid.unsqueeze(1).to_broadcast([128, 4, 128]), ALU.mult,
                     [N_EEND, "cst"], [N_EEND])
            h.tt("dve", kk[:], kT, bc(prm[:, PR_KK:PR_KK + 4], [128, 4, 128]), ALU.mult, ["xs", "prm"], [N_KK])
            h.act(tmp[:], kk[:], AF.Square, [N_KK], [N_TMP])
            for hg in range(4):
                h.mm(PS[2][:, hg * 128:(hg + 1) * 128], bones, tmp[:, hg, :], True, True, ["cst", N_TMP], ["ps2"])
            h.act(tmp[:], PS[2][:].rearrange("p (c l) -> p c l", c=4), AF.Sqrt, ["ps2"], [N_TMP])
            h.ts("dve", tmp[:], tmp[:], 1e-12, None, ALU.max, None, [N_TMP], [N_TMP])
            h.recip(tmp[:], tmp[:], [N_TMP], [N_TMP])
            h.tt("dve", kk[:], kk[:], tmp[:], ALU.mult, [N_KK, N_TMP], [N_KK])
            h.tt("pool", kh[:], aa[:], bc(prm[:, PR_KA:PR_KA + 4], [128, 4, 128]), ALU.mult, [N_AA, "prm"], [N_KH])
            h.tt("pool", kh[:], kh[:], bc(prm[:, PR_OMKA:PR_OMKA + 4], [128, 4, 128]), ALU.add, [N_KH, "prm"], [N_KH])
            h.tt("pool", kh[:], kh[:], kT, ALU.mult, [N_KH, "xs"], [N_KH])
            h.tt("dve", tmp[:], rT, bc(prm[:, PR_RK:PR_RK + 4], [128, 4, 128]), ALU.mult, ["xs", "prm"], [N_TMP])
            h.tt("dve", tmp[:], tmp[:], kh[:], ALU.mult, [N_TMP, N_KH], [N_TMP])
            for hg in range(4):
                h.mm(PS[3][:, hg * 128:(hg + 1) * 128], bones, tmp[:, hg, :], True, True, ["cst", N_TMP], ["ps3"])
            h.tt("dve", bon[:], PS[3][:].rearrange("p (c l) -> p c l", c=4), vT, ALU.mult, ["ps3", "xs"], [N_BON])
            h.tt("pool", aa[:], aa[:], kk[:], ALU.mult, [N_AA, N_KK], [N_AA])
            h.tt("dve", AR[:, :, 1, :], rT, E[:], ALU.mult, ["xs", N_E], [("AR", 1)])
            h.stt("dve", AR[:, :, 0, :], kk[:], -1.0, Eprev[:], ALU.mult, ALU.mult, [N_KK, N_EPREV], [("AR", 0)])
            h.tt("dve", bT[:], aa[:], Einv[:], ALU.mult, [N_AA, N_EINV], ["Hb0"])
            h.tt("dve", kTb[:], kh[:], Einv[:], ALU.mult, [N_KH, N_EINV], ["Hb1"])
            h.tt("pool", BpT[:], aa[:], Eend[:], ALU.mult, [N_AA, N_EEND], ["Hb2"])
            h.tt("pool", KpT[:], kh[:], Eend[:], ALU.mult, [N_KH, N_EEND], ["Hb3"])
            h.cp("pool", vbf[:], vT, ["xs"], ["Hb4"])
            h.cp("pool", stat[:, 0:16].rearrange("p (g c) -> p g c", g=4),
                 E[:].rearrange("p g (c l) -> p g c l", c=4)[:, :, :, 31], [N_E], [("stat", 0)])
            gam = stat[:, 0:16].rearrange("p (g c) -> p g c", g=4)

            for hd in range(8):
                hg, pb = hd // 2, 64 * (hd % 2)
                px = PS[hd % 2]
                pxn = "ps%d" % (hd % 2)
                h.mm(px[:, 0:256], bT[pb:pb + 64, hg, :], AR[pb:pb + 64, hg, :, :].rearrange("p a l -> p (a l)"),
                     True, True, ["Hb0", "AR"], [pxn])
                h.mm(px[:, 256:512], kTb[pb:pb + 64, hg, :], AR[pb:pb + 64, hg, :, :].rearrange("p a l -> p (a l)"),
                     True, True, ["Hb1", "AR"], [pxn])
                h.tt("dve", SCH[:, hd, :, :].rearrange("p a l -> p (a l)"), px[:], m4, ALU.mult,
                     [pxn, "cst"], [("SCH", hd)])
            Ac, Nn, An, Xc, Xtc, Xn, Xtn, Nc2 = INV
            NI = ["INV%d" % i for i in range(8)]
            Ac4 = Ac[:].rearrange("p (g a) l -> p g a l", a=2)
            for h2 in range(2):
                pt = PS[2 + h2]
                ptn = "ps%d" % (2 + h2)
                pb = 64 * h2
                for hg in range(4):
                    h.mm(pt[:, hg * 128:(hg + 1) * 128], AR[pb:pb + 64, hg, 0, :], bT[pb:pb + 64, hg, :],
                         True, True, ["AR", "Hb0"], [ptn])
                h.tt("dve", Ac4[:, :, h2, :], pt[:].rearrange("p (q l) -> p q l", q=4),
                     msl.unsqueeze(1).to_broadcast([128, 4, 128]), ALU.mult, [ptn, "cst"], [NI[0]])
            h.tt("pool", Xc[:], SCH[:, :, 0, :], idb[:].unsqueeze(1).to_broadcast([128, 8, 128]), ALU.add,
                 ["SCH", "idb"], [NI[3]])
            h.tt("pool", Xtc[:], Ac[:], idb[:].unsqueeze(1).to_broadcast([128, 8, 128]), ALU.add,
                 [NI[0], "idb"], [NI[4]])

            Ncur_ap = lambda hd: SCH[:, hd, 0, :]
            Ncur_key = "SCH"
            Acur, Acur_key = Ac, NI[0]
            Xcur, Xcur_key, Xtcur, Xtcur_key = Xc, NI[3], Xtc, NI[4]
            Nnext = [(Nn, NI[1]), (Nc2, NI[7])]
            Anext = [(An, NI[2]), (Ac, NI[0])]
            Xnext = [(Xn, NI[5]), (Xc, NI[3])]
            Xtnext = [(Xtn, NI[6]), (Xtc, NI[4])]
            for lvl in range(4):
                last = (lvl == 3)
                Nx, Nxk = Nnext[lvl % 2]
                Ax, Axk = Anext[lvl % 2]
                Xx, Xxk = Xnext[lvl % 2]
                Xtx, Xtxk = Xtnext[lvl % 2]
                for g2i in range(2):
                    pa, pan = PS[2 * g2i], "ps%d" % (2 * g2i)
                    pbk, pbn = PS[2 * g2i + 1], "ps%d" % (2 * g2i + 1)
                    for q in range(4):
                        hd = 4 * g2i + q
                        h.mm(pa[:, q * 128:(q + 1) * 128], Acur[:, hd, :], Ncur_ap(hd), True, True,
                             [Acur_key, Ncur_key], [pan])
                    h.cp("act", Nx[:, 4 * g2i:4 * g2i + 4, :], pa[:].rearrange("p (q l) -> p q l", q=4),
                         [pan], [(Nxk, g2i)])
                    if not last:
                        for q in range(4):
                            hd = 4 * g2i + q
                            h.mm(pbk[:, q * 128:(q + 1) * 128], Ncur_ap(hd), Acur[:, hd, :], True, True,
                                 [Acur_key, Ncur_key], [pbn])
                        h.cp("act", Ax[:, 4 * g2i:4 * g2i + 4, :], pbk[:].rearrange("p (q l) -> p q l", q=4),
                             [pbn], [(Axk, g2i)])
                for g2i in range(2):
                    pa, pan = PS[4 + g2i], "ps%d" % (4 + g2i)
                    for q in range(4):
                        hd = 4 * g2i + q
                        h.mm(pa[:, q * 128:(q + 1) * 128], Xtcur[:, hd, :], Nx[:, hd, :], True, True,
                             [Xtcur_key, (Nxk, g2i)], [pan])
                    h.tt("dve", Xx[:, 4 * g2i:4 * g2i + 4, :], pa[:].rearrange("p (q l) -> p q l", q=4),
                         Xcur[:, 4 * g2i:4 * g2i + 4, :], ALU.add, [pan, Xcur_key], [(Xxk, g2i)])
                if not last:
                    for g2i in range(2):
                        pa, pan = PS[2 * g2i], "ps%d" % (2 * g2i)
                        for q in range(4):
                            hd = 4 * g2i + q
                            h.mm(pa[:, q * 128:(q + 1) * 128], Nx[:, hd, :], Xtcur[:, hd, :], True, True,
                                 [Xtcur_key, (Nxk, g2i)], [pan])
                        h.tt("dve", Xtx[:, 4 * g2i:4 * g2i + 4, :], pa[:].rearrange("p (q l) -> p q l", q=4),
                             Xtcur[:, 4 * g2i:4 * g2i + 4, :], ALU.add, [pan, Xtcur_key], [(Xtxk, g2i)])
                Ncur_ap = (lambda t: (lambda hd: t[:, hd, :]))(Nx)
                Ncur_key = Nxk
                Acur, Acur_key = Ax, Axk
                Xcur, Xcur_key = Xx, Xxk
                Xtcur, Xtcur_key = Xtx, Xtxk
            X4, X4k = Xcur, Xcur_key

            Atok, Bptok, Kptok, Vtok = tok
            srcs = [(AR[:, :, 0, :], "AR", Atok, "tok0"), (BpT[:], "Hb2", Bptok, "tok1"),
                    (KpT[:], "Hb3", Kptok, "tok2"), (vbf[:], "Hb4", Vtok, "tok3")]
            for si in range(0, 4, 2):
                for u in range(2):
                    src, srck, dst, dstk = srcs[si + u]
                    for hg in range(4):
                        h.tr(psT[:, u * 4 + hg, :], src[:, hg, :], idb[:], [srck, "idb"], [("psT", u * 4 + hg)])
                for u in range(2):
                    src, srck, dst, dstk = srcs[si + u]
                    h.cp("act", dst[:], psT[:, u * 4:(u + 1) * 4, :].rearrange("p c l -> p (c l)"),
                         ["psT"], [dstk])

            for hd in range(8):
                h.mm(PS[0][:, hd * 64:(hd + 1) * 64], SCH[:, hd, 2, :], Vtok[:, hd * 64:(hd + 1) * 64],
                     True, True, ["SCH", "tok3"], ["ps0"])
            h.cp("act", Zbf[:], PS[0][:], ["ps0"], ["Zbf"])
            for hd in range(8):
                h.mm(PS[1][:, hd * 64:(hd + 1) * 64], X4[:, hd, :], Zbf[:, hd * 64:(hd + 1) * 64],
                     True, True, [X4k, "Zbf"], ["ps1"])
            h.cp("act", Yf[:], PS[1][:], ["ps1"], ["Yf"])
            for hd in range(8):
                hg, pb = hd // 2, 64 * (hd % 2)
                h.mm(PS[2][pb:pb + 64, hg * 128:(hg + 1) * 128], Atok[:, hd * 64:(hd + 1) * 64], X4[:, hd, :],
                     True, True, ["tok0", X4k], ["ps2"])
            h.cp("act", WTbf[:], PS[2][:].rearrange("p (c l) -> p c l", c=4), ["ps2"], ["WTbf"])

            psUs, psUn = (PS[0], PS[1]), ("ps0", "ps1")
            psOs, psOn = (PS[3], PS[4]), ("ps3", "ps4")
            psPn = PS[5]
            Ubf4 = Ubf[:].rearrange("p (g a v) -> p g a v", g=4, a=2)
            Yf4 = Yf[:].rearrange("p (g a v) -> p g a v", g=4, a=2)
            for c in range(4):
                s0 = 32 * c
                if samp:
                    seq = 4 * j + c
                    h.dma("sp", Sin[:], st_wkv[seq].rearrange("h v k -> v h k"), [], ["Sin"])
                    for hg in range(4):
                        h.tr(PS[6][:, hg * 64:(hg + 1) * 64],
                             Sin[:, 2 * hg:2 * hg + 2, :].rearrange("p a k -> p (a k)"), idf[0:64, 0:64],
                             ["Sin", "cst"], ["ps6"])
                    h.cp("act", Pst[:], PS[6][:, 0:256].rearrange("p (c v) -> p c v", c=4), ["ps6"], ["Pst"])
                    h.cp("dve", Pbf[:], Pst[:], ["Pst"], ["Pbf"])
                for hd in range(8):
                    hg, h2 = hd // 2, hd % 2
                    pb = 64 * h2
                    h.mm(psUs[h2][s0:s0 + 32, hg * 64:(hg + 1) * 64], WTbf[pb:pb + 64, hg, s0:s0 + 32],
                         Pbf[pb:pb + 64, hg, :], True, True, ["WTbf", "Pbf"], [psUn[h2]], tp=(pb, s0))
                for hd in range(8):
                    hg, h2 = hd // 2, hd % 2
                    pb = 64 * h2
                    h.mm(psOs[h2][s0:s0 + 32, hg * 64:(hg + 1) * 64], AR[pb:pb + 64, hg, 1, s0:s0 + 32],
                         Pbf[pb:pb + 64, hg, :], True, True, ["AR", "Pbf"], [psOn[h2]], tp=(pb, s0))
                for h2 in range(2):
                    h.tt("dve", Ubf4[s0:s0 + 32, :, h2, :],
                         psUs[h2][s0:s0 + 32, 0:256].rearrange("p (g v) -> p g v", g=4),
                         Yf4[s0:s0 + 32, :, h2, :], ALU.add, [psUn[h2], "Yf"], [("Ubf", c)])
                for hd in range(8):
                    hg, pb = hd // 2, 64 * (hd % 2)
                    h.mm(psPn[pb:pb + 64, hg * 64:(hg + 1) * 64], Bptok[s0:s0 + 32, hd * 64:(hd + 1) * 64],
                         Ubf[s0:s0 + 32, hd * 64:(hd + 1) * 64], True, False, ["tok1", ("Ubf", c)], ["ps5"],
                         tp=(s0, pb))
                    h.mm(psPn[pb:pb + 64, hg * 64:(hg + 1) * 64], Kptok[s0:s0 + 32, hd * 64:(hd + 1) * 64],
                         Vtok[s0:s0 + 32, hd * 64:(hd + 1) * 64], False, True, ["tok2", "tok3"], ["ps5"],
                         tp=(s0, pb))
                h.tt("dve", Ptmp[:], Pst[:], gam[:, :, c:c + 1].to_broadcast([128, 4, 64]), ALU.mult,
                     ["Pst", ("stat", 0)], ["Ptmp"])
                h.tt("dve", Pst[:], Ptmp[:], psPn[:, 0:256].rearrange("p (c v) -> p c v", c=4), ALU.add,
                     ["Ptmp", "ps5"], ["Pst"])
                if not samp and not (b == NPB - 1 and c == 3):
                    h.cp("act", Pbf[:], Pst[:], ["Pst"], ["Pbf"])
                if samp or (b == NPB - 1 and c == 3):
                    for hg in range(4):
                        h.tr(PS[6][0:64, hg * 128:(hg + 1) * 128], Pst[:, hg, :], idf, ["Pst", "cst"], ["ps6"])
                    h.cp("act", Sout[:].rearrange("p h k -> p (h k)"), PS[6][0:64, :], ["ps6"], ["Sout"])
                    dst = o_wkv_s[4 * j + c] if samp else o_wkv_p[0]
                    h.dma("sp", dst.rearrange("h v k -> v h k"), Sout[:], ["Sout"], [], isout=True)
            for hd in range(8):
                h.mm(PS[6][:, hd * 64:(hd + 1) * 64], SCH[:, hd, 1, :], Ubf[:, hd * 64:(hd + 1) * 64],
                     True, False, ["SCH", "Ubf"], ["ps6"])
                h.mm(PS[6][:, hd * 64:(hd + 1) * 64], SCH[:, hd, 3, :], Vtok[:, hd * 64:(hd + 1) * 64],
                     False, True, ["SCH", "tok3"], ["ps6"])
            o14 = o1[:].rearrange("p (g a v) -> p g a v", g=4, a=2)
            for h2 in range(2):
                h.cp("act", o14[:, :, h2, :], psOs[h2][:, 0:256].rearrange("p (g v) -> p g v", g=4),
                     [psOn[h2]], ["o1"])
            h.tt("dve", oo[:], o1[:], PS[6][:], ALU.add, ["o1", "ps6"], ["oo"])

            oo3 = oo[:].rearrange("p (h v) -> p h v", h=8)
            h.act(osq[:], oo[:], AF.Square, ["oo"], ["osq"])
            s1, s2, mean, msq, rs, nb = (stat[:, 16:24], stat[:, 24:32], stat[:, 32:40], stat[:, 40:48],
                                         stat[:, 48:56], stat[:, 56:64])
            h.rsum(s1, oo3, ["oo"], [("stat", 1)])
            h.rsum(s2, osq[:].rearrange("p (h v) -> p h v", h=8), ["osq"], [("stat", 2)])
            h.ts("dve", mean, s1, 1.0 / 64, None, ALU.mult, None, [("stat", 1)], [("stat", 3)])
            h.tt("dve", msq, mean, mean, ALU.mult, [("stat", 3)], [("stat", 4)])
            h.stt("dve", rs, s2, 1.0 / 64, msq, ALU.mult, ALU.subtract, [("stat", 2), ("stat", 4)], [("stat", 5)])
            h.ts("dve", rs, rs, 64e-5, None, ALU.add, None, [("stat", 5)], [("stat", 5)])
            h.act(rs, rs, AF.Sqrt, [("stat", 5)], [("stat", 5)])
            h.recip(rs, rs, [("stat", 5)], [("stat", 5)])
            h.tt("dve", oo3, oo3, mean.unsqueeze(2).to_broadcast([128, 8, 64]), ALU.subtract,
                 ["oo", ("stat", 3)], ["oo"])
            h.tt("dve", oo3, oo3, rs.unsqueeze(2).to_broadcast([128, 8, 64]), ALU.mult,
                 ["oo", ("stat", 5)], ["oo"])
            for hg in range(4):
                h.tr(PS[0][:, hg * 128:(hg + 1) * 128], oo[:, hg * 128:(hg + 1) * 128], idf, ["oo", "cst"], ["ps0"])
            ps0v = PS[0][:].rearrange("p (c l) -> p c l", c=4)
            h.tt("dve", tmp[:], ps0v, bc(prm[:, PR_LW:PR_LW + 4], [128, 4, 128]), ALU.mult, ["ps0", "prm"], [N_TMP])
            h.tt("pool", tmp[:], tmp[:], bc(prm[:, PR_LB:PR_LB + 4], [128, 4, 128]), ALU.add, [N_TMP, "prm"], [N_TMP])
            h.tt("pool", tmp[:], tmp[:], bon[:], ALU.add, [N_TMP, N_BON], [N_TMP])
            h.tt("dve", ogbf[:], tmp[:], gT[:], ALU.mult, [N_TMP, N_GT], ["ogbf"])
            h.dma("sp", og_s[b], ogbf[:].rearrange("p c l -> p (c l)"), ["ogbf"], [])

            gcols = [GLA0 + 128 * i for i in range(8)] + [GLA0 + 1040 + 128 * i for i in range(4)]
            for gi in range(3):
                pt, pn = PS[gi % 2], "ps%d" % (gi % 2)
                for mi in range(4):
                    c0 = gcols[4 * gi + mi]
                    for kc in range(8):
                        h.mm(pt[:, mi * 128:(mi + 1) * 128], win[:, kc, c0:c0 + 128], hT[:, kc, :],
                             kc == 0, kc == 7, [("win", kc), "hT"], [pn])
                h.cp("act", xs[:, 4 * gi:4 * gi + 4, :], pt[:].rearrange("p (c l) -> p c l", c=4), [pn], ["xs"])
            for kc in range(8):
                h.mm(PS[2][0:16, 0:128], win[:, kc, GLA0 + 1024:GLA0 + 1040], hT[:, kc, :], kc == 0, kc == 7,
                     [("win", kc), "hT"], ["ps2"])
            h.cp("act", lga[:], PS[2][0:16, 0:128], ["ps2"], ["lga"])
            qT, kgT, vgT, ogT = xs[:, 0:2, :], xs[:, 2:4, :], xs[:, 4:8, :], xs[:, 8:12, :]
            for c2 in range(2):
                h.mm(PS[3][:, c2 * 128:(c2 + 1) * 128], wg2[0:16, c2 * 128:(c2 + 1) * 128], lga[:], True, True,
                     ["wg2", "lga"], ["ps3"])
            la_, cg, Eg, Eginv, Egend = Fm[0], Fm[3], Fm[4], Fm[5], Fm[7]
            for c2 in range(2):
                h.act(la_[:, c2, :], PS[3][:, c2 * 128:(c2 + 1) * 128], AF.Exp, ["ps3", "prm"], [(N_SW, c2)],
                      scale=-1.0, bias=prm[:, PR_BG + c2:PR_BG + c2 + 1])
            h.ts("dve", la_[:, 0:2, :], la_[:, 0:2, :], 1.0, None, ALU.add, None, [N_SW], [N_SW])
            h.act(la_[:, 0:2, :], la_[:, 0:2, :], AF.Ln, [N_SW], [N_SW])
            h.stt("dve", la_[:, 0:2, :], la_[:, 0:2, :], -1.0 / 16.0,
                  valid.unsqueeze(1).to_broadcast([128, 2, 128]), ALU.mult, ALU.mult, [N_SW, "cst"], [N_SW])
            for c2 in range(2):
                h.scan(cg[:, c2, :], rstm, la_[:, c2, :], [N_SW, "cst"], [(N_CUM, c2)])
            h.act(Eg[:, 0:2, :], cg[:, 0:2, :], AF.Exp, [N_CUM], [N_E])
            h.act(Eginv[:, 0:2, :], cg[:, 0:2, :], AF.Exp, [N_CUM], [N_EINV], scale=-1.0)
            cg4 = cg[:, 0:2, :].rearrange("p g (c l) -> p g c l", c=4)
            h.tt("pool", tmp[:, 0:2, :].rearrange("p g (c l) -> p g c l", c=4),
                 cg4[:, :, :, 31:32].to_broadcast([128, 2, 4, 32]), cg4, ALU.subtract, [N_CUM], [N_TMP])
            h.act(Egend[:, 0:2, :], tmp[:, 0:2, :], AF.Exp, [N_TMP], [N_EEND])
            if samp:
                h.tt("pool", Egend[:, 0:2, :], Egend[:, 0:2, :], valid.unsqueeze(1).to_broadcast([128, 2, 128]),
                     ALU.mult, [N_EEND, "cst"], [N_EEND])
            h.cp("pool", stat[:, 0:8].rearrange("p (g c) -> p g c", g=2),
                 Eg[:, 0:2, :].rearrange("p g (c l) -> p g c l", c=4)[:, :, :, 31], [N_E], [("stat", 0)])
            gamg = stat[:, 0:8].rearrange("p (g c) -> p g c", g=2)
            qdT, kiT, keT, vgbf = Hm[5], Hm[6], Hm[7], Hm[4]
            h.stt("dve", qdT[:, 0:2, :], qT, 0.125, Eg[:, 0:2, :], ALU.mult, ALU.mult, ["xs", N_E], ["Hb5"])
            h.tt("dve", kiT[:, 0:2, :], kgT, Eginv[:, 0:2, :], ALU.mult, ["xs", N_EINV], ["Hb6"])
            h.tt("pool", keT[:, 0:2, :], kgT, Egend[:, 0:2, :], ALU.mult, ["xs", N_EEND], ["Hb7"])
            h.cp("pool", vgbf[:], vgT, ["xs"], ["Hb4"])
            silu = Fm[2]
            h.act(silu[:], ogT, AF.Silu, ["xs"], [N_GT])
            GS = Hm[0]
            GS4 = GS[:].rearrange("p (g a) l -> p g a l", a=2)
            for h2 in range(2):
                pb = 64 * h2
                pt, ptn = PS[3 * h2], "ps%d" % (3 * h2)
                for c2 in range(2):
                    h.mm(pt[:, c2 * 128:(c2 + 1) * 128], kiT[pb:pb + 64, c2, :], qdT[pb:pb + 64, c2, :], True, True,
                         ["Hb6", "Hb5"], [ptn])
                h.tt("dve", GS4[:, :, h2, :], pt[:, 0:256].rearrange("p (q l) -> p q l", q=2),
                     miu.unsqueeze(1).to_broadcast([128, 2, 128]), ALU.mult, [ptn, "cst"], ["Hb0"])
            Vgtok, Ketok = tok[0], tok[1]
            for hg in range(4):
                h.tr(psT[:, hg, :], vgbf[:, hg, :], idb[:], ["Hb4", "idb"], [("psT", hg)])
            for c2 in range(2):
                h.tr(psT[:, 4 + c2, :], keT[:, c2, :], idb[:], ["Hb7", "idb"], [("psT", 4 + c2)])
            h.cp("act", Vgtok[:], psT[:, 0:4, :].rearrange("p c l -> p (c l)"), ["psT"], ["tok0"])
            h.cp("act", Ketok[:, 0:256], psT[:, 4:6, :].rearrange("p c l -> p (c l)"), ["psT"], ["tok1"])
            DB = [1, 2, 5, 6]
            for c in range(4):
                s0 = 32 * c
                pt, pn = PS[DB[c]], "ps%d" % DB[c]
                for hd in range(4):
                    c2, pb = hd // 2, 64 * (hd % 2)
                    off = c2 * 128
                    h.mm(pt[pb:pb + 64, off:off + 128], Ketok[s0:s0 + 32, hd * 64:(hd + 1) * 64],
                         Vgtok[s0:s0 + 32, hd * 128:(hd + 1) * 128], True, True, ["tok1", "tok0"], [pn],
                         tp=(s0, pb))
            if samp:
                h.dma("sp", SgIn[:], st_gla[4 * j:4 * j + 4].rearrange("q (c2 h2) k v -> (h2 k) q c2 v", h2=2),
                      [], ["SgIn"])
                h.cp("act", Sgbf[:], SgIn[:], ["SgIn"], ["Sgbf"])
            else:
                h.cp("act", Sgbf[:, 0, :, :], Sg[:], ["Sg"], [("Sgbf", 0)])
            for c in range(4):
                pt, pn = PS[DB[c]], "ps%d" % DB[c]
                dv = pt[:, 0:256].rearrange("p (g v) -> p g v", g=2)
                gb = gamg[:, :, c:c + 1].to_broadcast([128, 2, 128])
                if samp:
                    h.tt("dve", Sgt[:], SgIn[:, c, :, :], gb, ALU.mult, ["SgIn", ("stat", 0)], ["Sgt"])
                    h.tt("dve", SgOut[:, c, :, :], Sgt[:], dv, ALU.add, ["Sgt", pn], [("SgOut", c)])
                else:
                    h.tt("dve", Sgt[:], Sg[:], gb, ALU.mult, ["Sg", ("stat", 0)], ["Sgt"])
                    h.tt("dve", Sg[:], Sgt[:], dv, ALU.add, ["Sgt", pn], ["Sg"])
                    if c < 3:
                        h.cp("act", Sgbf[:, c + 1, :, :], Sg[:], ["Sg"], [("Sgbf", c + 1)])
            if samp:
                h.dma("sp", o_gla_s[4 * j:4 * j + 4].rearrange("q (c2 h2) k v -> (h2 k) q c2 v", h2=2),
                      SgOut[:], ["SgOut"], [], isout=True)
            elif b == NPB - 1:
                h.dma("sp", o_gla_p[0].rearrange("(c2 h2) k v -> (h2 k) c2 v", h2=2), Sg[:], ["Sg"], [],
                      isout=True)
            for c in range(4):
                s0 = 32 * c
                for hd in range(4):
                    c2, h2 = hd // 2, hd % 2
                    pb = 64 * h2
                    h.mm(PS[3 * h2][s0:s0 + 32, c2 * 128:(c2 + 1) * 128], qdT[pb:pb + 64, c2, s0:s0 + 32],
                         Sgbf[pb:pb + 64, c, c2, :], True, True, ["Hb5", "Sgbf"], ["ps%d" % (3 * h2)], tp=(pb, s0))
            for hd in range(4):
                h.mm(PS[4][:, hd * 128:(hd + 1) * 128], GS[:, hd, :], Vgtok[:, hd * 128:(hd + 1) * 128], True, True,
                     ["Hb0", "tok0"], ["ps4"])
            o1g = o1[:].rearrange("p (g a v) -> p g a v", g=2, a=2)
            for h2 in range(2):
                h.cp("act", o1g[:, :, h2, :], PS[3 * h2][:, 0:256].rearrange("p (g v) -> p g v", g=2),
                     ["ps%d" % (3 * h2)], ["o1"])
            h.tt("dve", oo[:], o1[:], PS[4][:], ALU.add, ["o1", "ps4"], ["oo"])
            oo4 = oo[:].rearrange("p (h v) -> p h v", h=4)
            h.act(osq[:], oo[:], AF.Square, ["oo"], ["osq"])
            gs2, grs = stat[:, 16:20], stat[:, 24:28]
            h.rsum(gs2, osq[:].rearrange("p (h v) -> p h v", h=4), ["osq"], [("stat", 1)])
            h.ts("dve", grs, gs2, 1.0 / 128, 1e-6, ALU.mult, ALU.add, [("stat", 1)], [("stat", 2)])
            h.act(grs, grs, AF.Sqrt, [("stat", 2)], [("stat", 2)])
            h.recip(grs, grs, [("stat", 2)], [("stat", 2)])
            h.tt("dve", oo4, oo4, grs.unsqueeze(2).to_broadcast([128, 4, 128]), ALU.mult, ["oo", ("stat", 2)], ["oo"])
            for hd in range(4):
                h.tr(PS[0][:, hd * 128:(hd + 1) * 128], oo[:, hd * 128:(hd + 1) * 128], idf, ["oo", "cst"], ["ps0"])
            h.stt("dve", obbf[:], PS[0][:].rearrange("p (c l) -> p c l", c=4), prm[:, PR_NW:PR_NW + 1], silu[:],
                  ALU.mult, ALU.mult, ["ps0", "prm", N_GT], ["obbf"])
            h.dma("sp", ob_s[b], obbf[:].rearrange("p c l -> p (c l)"), ["obbf"], [])

            if dbg is not None and dbg.get("blk") == b and dbg.get("phase") == "a1":
                src, keys = dbg["fn"](dict(locals()))
                h.dma("sp", dbg_out, src, keys, [], isout=True)
        P.emit()

    with contextlib.ExitStack() as ph:
      if "b" in phases:
        P = Prog(nc, gs, "b")
        h = H(P)

        def sb(name, shape, dt):
            return ph.enter_context(nc.sbuf_tensor("b_" + name, list(shape), dt))

        def psb(name, shape, dt):
            return ph.enter_context(nc.psum_tensor("b_" + name, list(shape), dt))

        T = {}
        cst, idb = load_consts(h, sb)
        T["idb"] = idb
        prm = sb("prm", [128, NPRM], F32)
        T["prm"] = prm
        load_param_cols(h, prm, PR_G, norm_mix, 8)
        wing = sb("wing", [128, 8, 2048], BF16)
        w_in_v = w_in.rearrange("(c p) n -> p c n", p=128)
        for kc in range(8):
            h.dma("pool", wing[:, kc, :], w_in_v[:, kc, GATE0:INC], [], [("wing", kc)])
        woa = sb("woa", [128, 4, D], BF16)
        wob = sb("wob", [128, 4, D], BF16)
        wo = sb("wo", [128, 8, D], BF16)
        h.dma("pool", woa[:], w_out_a.rearrange("(c p) n -> p c n", p=128), [], ["woa"])
        h.dma("pool", wob[:], w_out_b.rearrange("(c p) n -> p c n", p=128), [], ["wob"])
        for kc in range(8):
            h.dma("pool", wo[:, kc, :], w_o.rearrange("(c p) n -> p c n", p=128)[:, kc, :], [], [("wo", kc)])
        xts = [sb("xt0", [128, D], F32), sb("xt1", [128, D], F32)]
        T["xn"] = sb("xn", [128, D], BF16)
        T["ss"] = sb("ss", [128, 1], F32)
        T["rstd"] = sb("rstd", [128, 1], F32)
        T["hT"] = sb("hT", [128, 8, 128], BF16)
        hT = T["hT"]
        ogb = sb("ogb", [128, 4, 128], BF16)
        obb = sb("obb", [128, 4, 128], BF16)
        sga = sb("sga", [128, 8, 128], F32)
        sgb = sb("sgb", [128, 8, 128], F32)
        ta = sb("ta", [128, 8, 128], F32)
        mg = sb("mg", [128, 8, 128], BF16)
        x1t = sb("x1t", [128, D], F32)
        T["psT"] = psb("psT", [128, 8, 128], BF16)
        PS = [psb("ps%d" % i, [128, 512], F32) for i in range(7)]
        for b in BLKS:
            xt, xtn = xts[b % 2], "xt%d" % (b % 2)
            load_x_block(h, b, xt, xtn)
            h.dma("sp", ogb[:].rearrange("p c l -> p (c l)"), og_s[b], [], ["ogb"])
            h.dma("sp", obb[:].rearrange("p c l -> p (c l)"), ob_s[b], [], ["obb"])
            rms_to_fm(h, T, xt, xtn, PR_G)
            for half, dst, dk in ((0, sga, "sga"), (1, sgb, "sgb")):
                for gi in range(2):
                    pt, pn = PS[gi], "ps%d" % gi
                    for mi in range(4):
                        c0 = half * 1024 + (4 * gi + mi) * 128
                        for kc in range(8):
                            h.mm(pt[:, mi * 128:(mi + 1) * 128], wing[:, kc, c0:c0 + 128], hT[:, kc, :],
                                 kc == 0, kc == 7, [("wing", kc), "hT"], [pn])
                    h.act(dst[:, 4 * gi:4 * gi + 4, :], pt[:].rearrange("p (c l) -> p c l", c=4), AF.Sigmoid,
                          [pn], [(dk, gi)])
            for gi in range(2):
                pt, pn = PS[2 + gi], "ps%d" % (2 + gi)
                for mi in range(4):
                    m = 4 * gi + mi
                    for kc in range(4):
                        h.mm(pt[:, mi * 128:(mi + 1) * 128], woa[:, kc, m * 128:(m + 1) * 128], ogb[:, kc, :],
                             kc == 0, kc == 3, ["woa", "ogb"], [pn])
                h.tt("dve", ta[:, 4 * gi:4 * gi + 4, :], pt[:].rearrange("p (c l) -> p c l", c=4),
                     sga[:, 4 * gi:4 * gi + 4, :], ALU.mult, [pn, ("sga", gi)], [("ta", gi)])
            for gi in range(2):
                pt, pn = PS[4 + gi], "ps%d" % (4 + gi)
                for mi in range(4):
                    m = 4 * gi + mi
                    for kc in range(4):
                        h.mm(pt[:, mi * 128:(mi + 1) * 128], wob[:, kc, m * 128:(m + 1) * 128], obb[:, kc, :],
                             kc == 0, kc == 3, ["wob", "obb"], [pn])
                h.tt("dve", sgb[:, 4 * gi:4 * gi + 4, :], pt[:].rearrange("p (c l) -> p c l", c=4),
                     sgb[:, 4 * gi:4 * gi + 4, :], ALU.mult, [pn, ("sgb", gi)], [("sgb", gi)])
                h.tt("pool", mg[:, 4 * gi:4 * gi + 4, :], ta[:, 4 * gi:4 * gi + 4, :],
                     sgb[:, 4 * gi:4 * gi + 4, :], ALU.add, [("ta", gi), ("sgb", gi)], [("mg", gi)])
            for nh in range(2):
                pt, pn = PS[nh], "ps%d" % nh
                for kc in range(8):
                    h.mm(pt[:], mg[:, kc, :], wo[:, kc, nh * 512:(nh + 1) * 512], kc == 0, kc == 7,
                         ["mg", ("wo", kc)], [pn])
                h.tt("dve", x1t[:, nh * 512:(nh + 1) * 512], pt[:], xt[:, nh * 512:(nh + 1) * 512], ALU.add,
                     [pn, xtn], [("x1t", nh)])
            h.dma("sp", x1_s[b * 128:(b + 1) * 128, :], x1t[:], ["x1t"], [])
        P.emit()

    with contextlib.ExitStack() as ph:
      if "c" in phases:
        P = Prog(nc, gs, "c")
        h = H(P)

        def sb(name, shape, dt):
            return ph.enter_context(nc.sbuf_tensor("c_" + name, list(shape), dt))

        def psb(name, shape, dt):
            return ph.enter_context(nc.psum_tensor("c_" + name, list(shape), dt))

        T = {}
        cst, idb = load_consts(h, sb)
        T["idb"] = idb
        idf = cst[:, C_ID:C_ID + 128]
        prm = sb("prm", [128, 8], F32)
        T["prm"] = prm
        load_param_cols(h, prm, 0, norm_ffn, 8)
        cw = sb("cw", [128, 4, 44], F32)
        for jx in range(3):
            h.dma("sp", cw[:, jx, :], ffn_conv_w[jx].rearrange("(c p) -> p c", p=128), [], [("cw", jx)], slow=True)
        h.dma("sp", cw[:, 3, :], ffn_conv_b.rearrange("(c p) -> p c", p=128), [], [("cw", 3)], slow=True)
        nfb = sb("nfb", [128, D], F32)
        h.dma("sp", nfb[:], norm_final.partition_broadcast(128), [], ["nfb"])
        wup = sb("wup", [128, 8, F2], BF16)
        w_up_v = ffn_w_up.rearrange("(c p) n -> p c n", p=128)
        for kc in range(8):
            h.dma("pool", wup[:, kc, :], w_up_v[:, kc, :], [], [("wup", kc)])
        wdn = sb("wdn", [128, 22, D], BF16)
        w_dn_v = ffn_w_down.rearrange("(c p) n -> p c n", p=128)
        for kc in range(22):
            h.dma("pool", wdn[:, kc, :], w_dn_v[:, kc, :], [], [("wdn", kc)])
        xts = [sb("xt0", [128, D], F32), sb("xt1", [128, D], F32)]
        T["xn"] = sb("xn", [128, D], BF16)
        T["ss"] = sb("ss", [128, 1], F32)
        T["rstd"] = sb("rstd", [128, 1], F32)
        T["hT"] = sb("hT", [128, 8, 128], BF16)
        hT = T["hT"]
        ub = [sb("ub0", [128, 4, 136], F32), sb("ub1", [128, 4, 136], F32)]
        ucar = sb("ucar", [128, 44, 2], F32)
        cin = sb("cin", [128, 44, 4, 2], F32)
        cout = sb("cout", [128, 44, 8], F32)
        cstg = sb("cstg", [8, F2], F32)
        cc = [sb("cc0", [128, 4, 128], F32), sb("cc1", [128, 4, 128], F32)]
        g1 = [sb("g10", [128, 2, 128], F32), sb("g11", [128, 2, 128], F32)]
        g2t = [sb("g20", [128, 2, 128], F32), sb("g21", [128, 2, 128], F32)]
        actT = sb("actT", [128, 22, 128], BF16)
        x2 = sb("x2", [128, D], F32)
        yt = sb("yt", [128, D], F32)
        T["psT"] = psb("psT", [128, 8, 128], BF16)
        PS = [psb("ps%d" % i, [128, 512], F32) for i in range(7)]
        h.memset("pool", ucar[:], 0.0, ["ucar"])
        for b in BLKS:
            samp, nseq, L = geom(b)
            j = b - NPB
            xt, xtn = xts[b % 2], "xt%d" % (b % 2)
            h.dma("sp", xt[:], x1_s[b * 128:(b + 1) * 128, :], [], [xtn])
            rms_to_fm(h, T, xt, xtn, 0)
            W = nseq * (L + 2)
            if samp:
                h.dma("sp", cstg[:], st_conv[4 * j:4 * j + 4].rearrange("q t f -> (q t) f"), [], ["cstg"])
                for g4 in range(11):
                    for mi in range(4):
                        m = 4 * g4 + mi
                        h.tr(PS[4][:, mi * 8:(mi + 1) * 8], cstg[0:8, m * 128:(m + 1) * 128], idf[0:8, 0:8],
                             ["cstg", "cst"], ["ps4"])
                    h.cp("act", cin[:, 4 * g4:4 * g4 + 4, :, :].rearrange("p m q t -> p m (q t)"),
                         PS[4][:, 0:32].rearrange("p (m x) -> p m x", m=4), ["ps4"], [("cin", g4)])
            last_prompt = (b == NPB - 1)
            for gi in range(11):
                u, un = ub[gi % 2], "ub%d" % (gi % 2)
                uv = u[:, :, 0:W].rearrange("p m (s l) -> p m s l", s=nseq)
                pt, pn = PS[gi % 2], "ps%d" % (gi % 2)
                chunks = [2 * gi, 2 * gi + 1, 22 + 2 * gi, 23 + 2 * gi]
                for mi, m in enumerate(chunks):
                    for kc in range(8):
                        h.mm(pt[:, mi * 128:(mi + 1) * 128], wup[:, kc, m * 128:(m + 1) * 128], hT[:, kc, :],
                             kc == 0, kc == 7, [("wup", kc), "hT"], [pn])
                h.cp("act", uv[:, :, :, 2:L + 2], pt[:].rearrange("p (m s l) -> p m s l", m=4, s=nseq),
                     [pn], [un])
                for half in range(2):
                    m0 = chunks[2 * half]
                    if samp:
                        h.cp("pool", uv[:, 2 * half:2 * half + 2, :, 0:2], cin[:, m0:m0 + 2, :, :],
                             [("cin", m0 // 4)], [un])
                    else:
                        h.cp("pool", uv[:, 2 * half:2 * half + 2, 0, 0:2], ucar[:, m0:m0 + 2, :],
                             [("ucar", gi)], [un])
                for half in range(2):
                    m0 = chunks[2 * half]
                    if samp:
                        h.cp("pool", cout[:, m0:m0 + 2, :].rearrange("p m (q t) -> p m q t", q=4),
                             uv[:, 2 * half:2 * half + 2, :, 8:10], [un], [("cout", gi)])
                    else:
                        h.cp("pool", ucar[:, m0:m0 + 2, :], uv[:, 2 * half:2 * half + 2, 0, 128:130],
                             [un], [("ucar", gi)])
                        if last_prompt:
                            h.cp("pool", cout[:, m0:m0 + 2, 0:2], uv[:, 2 * half:2 * half + 2, 0, 128:130],
                                 [un], [("cout", gi)])
                c_, cn = cc[gi % 2], "cc%d" % (gi % 2)
                for mi, m in enumerate(chunks):
                    eng = "dve"
                    c4 = c_[:, mi, :].rearrange("p (s l) -> p s l", s=nseq)
                    h.ts(eng, c4, uv[:, mi, :, 2:L + 2], cw[:, 2, m:m + 1], cw[:, 3, m:m + 1], ALU.mult, ALU.add,
                         [un, "cw"], [(cn, mi)])
                    h.stt(eng, c4, uv[:, mi, :, 1:L + 1], cw[:, 1, m:m + 1], c4, ALU.mult, ALU.add,
                          [un, "cw", (cn, mi)], [(cn, mi)])
                    h.stt(eng, c4, uv[:, mi, :, 0:L], cw[:, 0, m:m + 1], c4, ALU.mult, ALU.add,
                          [un, "cw", (cn, mi)], [(cn, mi)])
                ga, gan = g1[gi % 2], "g1%d" % (gi % 2)
                gb_, gbn = g2t[gi % 2], "g2%d" % (gi % 2)
                gate = c_[:, 2:4, :]
                val = c_[:, 0:2, :]
                h.act(ga[:], gate, AF.Square, [(cn, 2), (cn, 3)], [gan])
                h.ts("dve", ga[:], ga[:], 0.044715, 1.0, ALU.mult, ALU.add, [gan], [gan])
                h.tt("pool", ga[:], ga[:], gate, ALU.mult, [gan, (cn, 2), (cn, 3)], [gan])
                h.act(gb_[:], ga[:], AF.Sigmoid, [gan], [gbn], scale=GELU_S)
                h.tt("pool", gb_[:], gb_[:], gate, ALU.mult, [gbn, (cn, 2), (cn, 3)], [gbn])
                h.tt("dve", actT[:, 2 * gi:2 * gi + 2, :], gb_[:], val, ALU.mult, [gbn, (cn, 0), (cn, 1)],
                     [("actT", gi)])
            if samp or last_prompt:
                ncol = 8 if samp else 2
                for g4 in range(11):
                    for mi in range(4):
                        m = 4 * g4 + mi
                        h.tr(PS[4][0:ncol, mi * 128:(mi + 1) * 128], cout[:, m, 0:ncol], idf, ["cout", "cst"],
                             ["ps4"])
                    h.cp("act", cstg[0:ncol, g4 * 512:(g4 + 1) * 512], PS[4][0:ncol, :], ["ps4"], ["cstg"])
                if samp:
                    h.dma("sp", o_conv_s[4 * j:4 * j + 4].rearrange("q t f -> (q t) f"), cstg[:], ["cstg"], [],
                          isout=True)
                else:
                    h.dma("sp", o_conv_p[0], cstg[0:2, :], ["cstg"], [], isout=True)
            for nh in range(2):
                pt, pn = PS[2 + nh], "ps%d" % (2 + nh)
                for kc in range(22):
                    h.mm(pt[:], actT[:, kc, :], wdn[:, kc, nh * 512:(nh + 1) * 512], kc == 0, kc == 21,
                         ["actT", ("wdn", kc)], [pn])
                h.tt("dve", x2[:, nh * 512:(nh + 1) * 512], pt[:], xt[:, nh * 512:(nh + 1) * 512], ALU.add,
                     [pn, xtn], [("x2", nh)])
            ss2, rs2 = T["ss"], T["rstd"]
            h.act(yt[:], x2[:], AF.Square, ["x2"], ["yt", "ss"], accum_out=ss2[:])
            h.ts("dve", rs2[:], ss2[:], 1.0 / D, 1e-6, ALU.mult, ALU.add, ["ss"], ["rstd"])
            h.act(rs2[:], rs2[:], AF.Sqrt, ["rstd"], ["rstd"])
            h.recip(rs2[:], rs2[:], ["rstd"], ["rstd"])
            h.stt("dve", yt[:], x2[:], rs2[:, 0:1], nfb[:], ALU.mult, ALU.mult, ["x2", "rstd", "nfb"], ["yt"])
            if samp:
                for q in range(4):
                    h.dma("sp", ysm[4 * j + q], yt[32 * q:32 * q + 8, :], ["yt"], [], isout=True)
            else:
                h.dma("sp", yp[b * 128:(b + 1) * 128, :], yt[:], ["yt"], [], isout=True)
        P.emit()
    gs.close()
    return nc


_CACHE = {}


def kernel(**inputs):
    f32 = lambda a: np.ascontiguousarray(np.asarray(a), dtype=np.float32)
    if "nc" not in _CACHE:
        nc = bass.Bass("TRN2", target_bir_lowering=False)
        build(nc)
        _CACHE["nc"] = nc
    nc = _CACHE["nc"]
    cst = make_consts()
    shared = {
        "cst": cst,
        "norm_mix": f32(inputs["norm_mix"][0]), "w_in": f32(inputs["w_in"][0]),
        "mu_shift": f32(inputs["mu_shift"][0]), "rwkv_w0": f32(inputs["rwkv_w0"][0]),
        "rwkv_w2": f32(inputs["rwkv_w2"][0]), "rwkv_a0": f32(inputs["rwkv_a0"][0]),
        "rwkv_a2": f32(inputs["rwkv_a2"][0]), "rwkv_g2": f32(inputs["rwkv_g2"][0]),
        "rwkv_k_k": f32(inputs["rwkv_k_k"][0]), "rwkv_k_a": f32(inputs["rwkv_k_a"][0]),
        "rwkv_r_k": f32(inputs["rwkv_r_k"][0]).reshape(512), "rwkv_ln_w": f32(inputs["rwkv_ln_w"][0]),
        "rwkv_ln_b": f32(inputs["rwkv_ln_b"][0]), "gla_wg2": f32(inputs["gla_wg2"][0]),
        "gla_bg": f32(inputs["gla_bg"][0]), "gla_norm_w": f32(inputs["gla_norm_w"][0]),
        "w_out_a": f32(inputs["w_out_a"][0]), "w_out_b": f32(inputs["w_out_b"][0]),
        "w_o": f32(inputs["w_o"][0]), "norm_ffn": f32(inputs["norm_ffn"][0]),
        "ffn_w_up": f32(inputs["ffn_w_up"][0]), "ffn_conv_w": f32(inputs["ffn_conv_w"][0]),
        "ffn_conv_b": f32(inputs["ffn_conv_b"][0]), "ffn_w_down": f32(inputs["ffn_w_down"][0]),
        "norm_final": f32(inputs["norm_final"]),
    }
    in_maps = []
    for c in range(NCORES):
        m = dict(shared)
        sl = slice(16 * c, 16 * c + 16)
        m["xp"] = f32(inputs["x_prompt"][c])
        m["xs"] = f32(inputs["x_sample"][sl])
        m["st_shift"] = f32(inputs["state_rwkv_shift"][0, sl])
        m["st_wkv"] = f32(inputs["state_rwkv_wkv"][0, sl])
        m["st_gla"] = f32(inputs["state_gla"][0, sl])
        m["st_conv"] = f32(inputs["state_ffn_conv"][0, sl])
        in_maps.append(m)
    res = run_bass_kernel_spmd(nc, in_maps, core_ids=list(range(NCORES)))
    R = res.results
    cat = lambda k: np.concatenate([np.asarray(r[k]) for r in R], axis=0)
    y_p = np.stack([np.asarray(r["yp"]) for r in R], axis=0)
    y_s = cat("ys")
    outs = (
        y_p, y_s,
        cat("o_shift_p")[None], cat("o_wkv_p")[None], cat("o_gla_p")[None], cat("o_conv_p")[None],
        cat("o_shift_s")[None], cat("o_wkv_s")[None], cat("o_gla_s")[None], cat("o_conv_s")[None],
    )
    return tuple(np.ascontiguousarray(o, dtype=np.float32) for o in outs)
```

```python
import contextlib
import numpy as np
import concourse.bass as bass
import concourse.mybir as mybir
from concourse.bass_utils import run_bass_kernel_spmd

F32 = mybir.dt.float32
BF16 = mybir.dt.bfloat16
AF = mybir.ActivationFunctionType
ALU = mybir.AluOpType
AX = mybir.AxisListType

NCORES = 8
D = 1024
NPB = 16
NSB = 4
NBLK = NPB + NSB
SHIFT = 1792
INC = 5392
FH = 2816
F2 = 5632
GLA0 = 1792
GATE0 = 3344
C0 = -0.6065306597126334
GELU_S = 1.5957691216057308

ENGS = ("pe", "act", "dve", "pool", "sp")
MAXOPS = None
SCHED = True
VERBOSE = False
PROGS = []
WINDOW = 48
LINES = []


class Op:
    __slots__ = ("eng", "fn", "reads", "writes", "dma", "deps", "sig", "idx",
                 "dsem", "dval", "dprev", "isout", "mm", "odeps", "cost", "start", "fin")

    def __init__(self, eng, fn, reads, writes, dma, isout, mm, cost=300.0):
        self.odeps = []
        self.cost = cost
        self.eng = eng
        self.fn = fn
        self.reads = reads
        self.writes = writes
        self.dma = dma
        self.deps = []
        self.sig = None
        self.dsem = None
        self.dval = None
        self.dprev = None
        self.isout = isout
        self.mm = mm


def _norm(k):
    return k if isinstance(k, tuple) else (k, None)


class Prog:
    def __init__(self, nc, semstack, tag, n_dma_sems=6):
        self.nc = nc
        self.ops = []
        self.n_dma_sems = n_dma_sems
        self.st = {}
        self.semstack = semstack
        self.tag = tag

    @staticmethod
    def _conf(a, b):
        return a is None or b is None or a == b

    def add(self, eng, fn, reads=(), writes=(), dma=False, isout=False, mm=False, cost=300.0):
        op = Op(eng, fn, [_norm(k) for k in reads], [_norm(k) for k in writes], dma, isout, mm, cost)
        odeps = {}
        op.idx = len(self.ops)
        if MAXOPS is not None:
            import sys as _s
            f = _s._getframe(1)
            while f is not None and f.f_code.co_name != "build":
                f = f.f_back
            LINES.append(f.f_lineno if f is not None else -1)
        deps = {}
        for (name, sub) in op.reads:
            s = self.st.setdefault(name, {"w": {}, "r": {}})
            for ws, wop in s["w"].items():
                if self._conf(ws, sub):
                    deps[wop.idx] = wop
        for (name, sub) in op.writes:
            s = self.st.setdefault(name, {"w": {}, "r": {}})
            for ws, wop in s["w"].items():
                if self._conf(ws, sub):
                    if not (op.mm and wop.mm):
                        deps[wop.idx] = wop
                    else:
                        odeps[wop.idx] = wop
            for rs, rops in s["r"].items():
                if self._conf(rs, sub):
                    for rop in rops:
                        deps[rop.idx] = rop
        for (name, sub) in op.reads:
            self.st[name]["r"].setdefault(sub, []).append(op)
        for (name, sub) in op.writes:
            s = self.st[name]
            if sub is None:
                s["w"] = {None: op}
                s["r"] = {}
            else:
                s["w"][sub] = op
                s["r"][sub] = []
        deps.pop(op.idx, None)
        op.deps = list(deps.values())
        op.odeps = [o for k, o in odeps.items() if k not in deps]
        self.ops.append(op)
        return op

    def schedule(self, window=None):
        window = window or WINDOW
        ops = self.ops
        n = len(ops)
        ndep = [0] * n
        users = [[] for _ in range(n)]
        for op in ops:
            ds = {d.idx for d in op.deps} | {d.idx for d in op.odeps}
            ndep[op.idx] = len(ds)
            for d in ds:
                users[d].append(op.idx)
        ready_t = [0.0] * n
        per = {e: [op.idx for op in ops if op.eng == e] for e in ENGS}
        head = {e: 0 for e in ENGS}
        done = [False] * n
        t_e = {e: 0.0 for e in ENGS}
        order = []
        remaining = n
        LAT = 120.0
        while remaining:
            best = None
            for e in ENGS:
                lst = per[e]
                hp = head[e]
                while hp < len(lst) and done[lst[hp]]:
                    hp += 1
                head[e] = hp
                if hp >= len(lst):
                    continue
                cnt = 0
                k = hp
                cand = None
                while k < len(lst) and cnt < window:
                    i = lst[k]
                    if not done[i]:
                        cnt += 1
                        if ndep[i] == 0:
                            st = max(t_e[e], ready_t[i])
                            key = (st + 15.0 * (cnt - 1), i)
                            if cand is None or key < cand[0]:
                                cand = (key, i, st)
                    k += 1
                if cand is not None:
                    if best is None or cand[0] < best[0]:
                        best = (cand[0], cand[1], cand[2], e)
            assert best is not None, "scheduler deadlock"
            _, i, st, e = best
            op = ops[i]
            op.start = st
            if op.dma:
                t_e[e] = st + 60.0
            else:
                t_e[e] = st + op.cost
            op.fin = st + op.cost
            done[i] = True
            remaining -= 1
            order.append(op)
            for u in users[i]:
                ndep[u] -= 1
                if ready_t[u] < op.fin + LAT:
                    ready_t[u] = op.fin + LAT
        self.ops = order
        self.est = max(op.fin for op in order) if order else 0.0
        if VERBOSE:
            PROGS.append(self)
            busy = {e: sum(o.cost for o in order if o.eng == e and not o.dma) for e in ENGS}
            print("[sched %s] n=%d est=%.1f us busy(us): %s" % (
                self.tag, n, self.est / 1e3, " ".join("%s=%.0f" % (e, busy[e] / 1e3) for e in ENGS)), flush=True)

    def emit(self):
        nc = self.nc
        if MAXOPS is not None:
            self.ops = self.ops[:MAXOPS]
        if SCHED:
            self.schedule()
        ops = self.ops
        needed = set()
        for op in ops:
            for d in op.deps:
                needed.add(d.idx)
        cnt = {e: 0 for e in ENGS}
        for op in ops:
            if not op.dma and op.idx in needed:
                cnt[op.eng] += 1
                op.sig = cnt[op.eng]
        dcount = {e: 0 for e in ENGS}
        last_on_slot = {}
        for op in ops:
            if not op.dma:
                continue
            j = dcount[op.eng]
            dcount[op.eng] += 1
            slot = j % self.n_dma_sems
            op.dsem = (op.eng, slot)
            op.dval = 16 * (j // self.n_dma_sems + 1)
            op.dprev = last_on_slot.get(op.dsem)
            last_on_slot[op.dsem] = op
        out_ops = [op for op in ops if op.dma and op.isout]
        per_eng = {e: [op for op in ops if op.eng == e] for e in ENGS}
        es = self.semstack
        csem = {e: es.enter_context(nc.semaphore("cs%s_%s" % (self.tag, e)))
                for e in ENGS if e != "sp"}
        dsem = {}
        for e in ENGS:
            for s in range(min(self.n_dma_sems, dcount[e])):
                dsem[(e, s)] = es.enter_context(nc.semaphore("ds%s_%s%d" % (self.tag, e, s)))

        def run_engine(e, eng):
            known = {}

            def wait(key, sem, val):
                if known.get(key, 0) >= val:
                    return
                known[key] = val
                eng.wait_ge(sem, val)

            for op in per_eng[e]:
                for d in op.deps:
                    if d.dma:
                        wait(d.dsem, dsem[d.dsem], d.dval)
                    else:
                        wait(d.eng, csem[d.eng], d.sig)
                if op.dma and op.dprev is not None:
                    wait(op.dsem, dsem[op.dsem], op.dprev.dval)
                ins = op.fn(eng)
                if op.dma:
                    ins.then_inc(dsem[op.dsem], 16)
                elif op.sig is not None:
                    ins.then_inc(csem[e], 1)
            if e == "sp":
                for op in out_ops:
                    wait(op.dsem, dsem[op.dsem], op.dval)
                for key, op in last_on_slot.items():
                    wait(op.dsem, dsem[op.dsem], op.dval)

        with nc.Block() as block:
            @block.sync
            def _(eng):
                run_engine("sp", eng)

            @block.tensor
            def _(eng):
                run_engine("pe", eng)

            @block.scalar
            def _(eng):
                run_engine("act", eng)

            @block.vector
            def _(eng):
                run_engine("dve", eng)

            @block.gpsimd
            def _(eng):
                run_engine("pool", eng)


def _fsz(ap):
    n = 1
    for d in ap.shape[1:]:
        n *= int(d)
    return n


class H:
    def __init__(self, P):
        self.P = P

    def act(self, out, in_, func, r, w, **kw):
        self.P.add("act", lambda e: e.activation(out=out, in_=in_, func=func, **kw), r, w,
                   cost=220.0 + 1.05 * _fsz(out))

    def tt(self, eng, out, in0, in1, op, r, w):
        self.P.add(eng, lambda e: e.tensor_tensor(out=out, in0=in0, in1=in1, op=op), r, w,
                   cost=(100.0 + 1.05 * _fsz(out)) if eng == "dve" else (160.0 + 2.1 * _fsz(out)))

    def ts(self, eng, out, in0, s1, s2, op0, op1, r, w):
        if op1 is None:
            self.P.add(eng, lambda e: e.tensor_scalar(out=out, in0=in0, scalar1=s1, scalar2=None, op0=op0), r, w,
                       cost=100.0 + 1.05 * _fsz(out))
        else:
            self.P.add(eng, lambda e: e.tensor_scalar(out=out, in0=in0, scalar1=s1, scalar2=s2, op0=op0, op1=op1), r, w,
                       cost=100.0 + 1.05 * _fsz(out))

    def stt(self, eng, out, in0, scalar, in1, op0, op1, r, w):
        self.P.add(eng, lambda e: e.scalar_tensor_tensor(out=out, in0=in0, scalar=scalar, in1=in1, op0=op0, op1=op1), r, w,
                   cost=100.0 + 1.05 * _fsz(out))

    def cp(self, eng, out, in_, r, w):
        if eng == "act":
            self.P.add("act", lambda e: e.activation(out=out, in_=in_, func=AF.Copy), r, w,
                       cost=220.0 + 1.05 * _fsz(out))
        else:
            self.P.add(eng, lambda e: e.tensor_copy(out=out, in_=in_), r, w,
                       cost=(100.0 + 1.05 * _fsz(out)) if eng == "dve" else (160.0 + 2.1 * _fsz(out)))

    def memset(self, eng, ap, val, w):
        self.P.add(eng, lambda e: e.memset(ap, val), [], w, cost=160.0 + 1.0 * _fsz(ap))

    def recip(self, out, in_, r, w):
        self.P.add("dve", lambda e: e.reciprocal(out=out, in_=in_), r, w, cost=100.0 + 1.05 * _fsz(out))

    def scan(self, out, d0, d1, r, w):
        self.P.add("dve", lambda e: e.tensor_tensor_scan(out=out, data0=d0, data1=d1, initial=0.0,
                                                         op0=ALU.mult, op1=ALU.add), r, w,
                   cost=100.0 + 2.1 * _fsz(out))

    def rsum(self, out, in_, r, w):
        self.P.add("dve", lambda e: e.tensor_reduce(out=out, in_=in_, axis=AX.X, op=ALU.add), r, w,
                   cost=100.0 + 1.05 * _fsz(in_))

    def mm(self, out, lhsT, rhs, start, stop, r, w, tp=None):
        c = max(64.0, float(_fsz(rhs))) / 2.0 + 16.0
        if lhsT.dtype == F32:
            c *= 4.0
        if tp is None:
            self.P.add("pe", lambda e: e.matmul(out, lhsT=lhsT, rhs=rhs, start=start, stop=stop), r, w, mm=True,
                       cost=c)
        else:
            self.P.add("pe", lambda e: e.matmul(out, lhsT=lhsT, rhs=rhs, start=start, stop=stop,
                                                tile_position=tp), r, w, mm=True, cost=c)

    def tr(self, out, in_, ident, r, w):
        self.P.add("pe", lambda e: e.transpose(out=out, in_=in_, identity=ident), r, w, mm=True, cost=110.0)

    def dma(self, q, out, in_, r, w, isout=False, slow=False):
        nbytes = 1
        for d in out.shape:
            nbytes *= int(d)
        c = 2500.0 + 4.0 * nbytes / 150.0
        if slow:
            self.P.add(q, lambda e: e.dma_start(out=out, in_=in_, allow_slow_non_contiguous=True), r, w,
                       dma=True, isout=isout, cost=c)
        else:
            self.P.add(q, lambda e: e.dma_start(out=out, in_=in_), r, w, dma=True, isout=isout, cost=c)


def bc(ap, shape):
    a = ap
    while len(a.shape) < len(shape):
        a = a.unsqueeze(len(a.shape))
    return a.to_broadcast(list(shape))


C_ID = 0
C_BO = 128
C_RST = 256
C_VS = 384
C_ONE = 512
C_M4 = 640
C_SL = 1152
C_IU = 1280
NCST = 1408


def make_consts():
    c = np.zeros((128, NCST), np.float32)
    i = np.arange(128)
    same = (i[:, None] // 32) == (i[None, :] // 32)
    su = (same & (i[:, None] < i[None, :])).astype(np.float32)
    iu = (same & (i[:, None] <= i[None, :])).astype(np.float32)
    sl = (same & (i[:, None] > i[None, :])).astype(np.float32)
    c[:, C_ID:C_ID + 128] = np.eye(128, dtype=np.float32)
    c[:, C_BO:C_BO + 128] = ((i[:, None] // 64) == (i[None, :] // 64)).astype(np.float32)
    c[:, C_RST:C_RST + 128] = (i[None, :] % 32 != 0).astype(np.float32)
    c[:, C_VS:C_VS + 128] = (i[None, :] % 32 < 8).astype(np.float32)
    c[:, C_ONE:C_ONE + 128] = 1.0
    c[:, C_M4:C_M4 + 512] = np.concatenate([su, iu, su, iu], axis=1)
    c[:, C_SL:C_SL + 128] = sl
    c[:, C_IU:C_IU + 128] = iu
    return c


PR_G = 0
PR_MU = 8
PR_W0 = 22
PR_A0 = 26
PR_KK = 30
PR_KA = 34
PR_RK = 38
PR_LW = 42
PR_LB = 46
PR_BG = 50
PR_NW = 52
PR_OMKA = 53
NPRM = 64


def build(nc, dbg=None, phases="abc", blocks=None):
    BLKS = list(range(NBLK)) if blocks is None else list(blocks)
    gs = contextlib.ExitStack()

    def din(name, shape, dt=F32):
        return nc.dram_tensor(name, list(shape), dt, kind="ExternalInput").ap()

    def dout(name, shape):
        return nc.dram_tensor(name, list(shape), F32, kind="ExternalOutput").ap()

    def dscr(name, shape, dt):
        return nc.dram_tensor(name, list(shape), dt, kind="Internal").ap()

    xp = din("xp", [2048, D])
    xsm = din("xs", [16, 8, D])
    st_shift = din("st_shift", [16, SHIFT])
    st_wkv = din("st_wkv", [16, 8, 64, 64])
    st_gla = din("st_gla", [16, 4, 64, 128])
    st_conv = din("st_conv", [16, 2, F2])
    cst_d = din("cst", [128, NCST])
    norm_mix = din("norm_mix", [D])
    w_in = din("w_in", [D, INC])
    mu_shift = din("mu_shift", [SHIFT])
    rwkv_w0 = din("rwkv_w0", [512])
    rwkv_w2 = din("rwkv_w2", [64, 512])
    rwkv_a0 = din("rwkv_a0", [512])
    rwkv_a2 = din("rwkv_a2", [64, 512])
    rwkv_g2 = din("rwkv_g2", [128, 512])
    rwkv_k_k = din("rwkv_k_k", [512])
    rwkv_k_a = din("rwkv_k_a", [512])
    rwkv_r_k = din("rwkv_r_k", [512])
    rwkv_ln_w = din("rwkv_ln_w", [512])
    rwkv_ln_b = din("rwkv_ln_b", [512])
    gla_wg2 = din("gla_wg2", [16, 256])
    gla_bg = din("gla_bg", [256])
    gla_norm_w = din("gla_norm_w", [128])
    w_out_a = din("w_out_a", [512, D])
    w_out_b = din("w_out_b", [512, D])
    w_o = din("w_o", [D, D])
    norm_ffn = din("norm_ffn", [D])
    ffn_w_up = din("ffn_w_up", [D, F2])
    ffn_conv_w = din("ffn_conv_w", [3, F2])
    ffn_conv_b = din("ffn_conv_b", [F2])
    ffn_w_down = din("ffn_w_down", [FH, D])
    norm_final = din("norm_final", [D])

    yp = dout("yp", [2048, D])
    ysm = dout("ys", [16, 8, D])
    o_shift_p = dout("o_shift_p", [1, SHIFT])
    o_wkv_p = dout("o_wkv_p", [1, 8, 64, 64])
    o_gla_p = dout("o_gla_p", [1, 4, 64, 128])
    o_conv_p = dout("o_conv_p", [1, 2, F2])
    o_shift_s = dout("o_shift_s", [16, SHIFT])
    o_wkv_s = dout("o_wkv_s", [16, 8, 64, 64])
    o_gla_s = dout("o_gla_s", [16, 4, 64, 128])
    o_conv_s = dout("o_conv_s", [16, 2, F2])

    og_s = dscr("og_s", [NBLK, 128, 512], BF16)
    ob_s = dscr("ob_s", [NBLK, 128, 512], BF16)
    x1_s = dscr("x1_s", [NBLK * 128, D], F32)

    dbg_out = None
    if dbg is not None:
        dbg_out = dout("dbg", dbg["shape"])

    def geom(b):
        return (b >= NPB, 4, 32) if b >= NPB else (False, 1, 128)

    def load_consts(h, sb, q="sp"):
        cst = sb("cst", [128, NCST], F32)
        h.dma(q, cst[:], cst_d, [], ["cst"])
        idb = sb("idb", [128, 128], BF16)
        h.cp("dve", idb[:], cst[:, C_ID:C_ID + 128], ["cst"], ["idb"])
        return cst, idb

    def load_x_block(h, b, xt, xtn):
        samp, nseq, L = geom(b)
        if not samp:
            h.dma("sp", xt[:], xp[b * 128:(b + 1) * 128, :], [], [xtn])
        else:
            j = b - NPB
            h.memset("pool", xt[:], 0.0, [xtn])
            for q in range(4):
                h.dma("sp", xt[32 * q:32 * q + 8, :], xsm[4 * j + q], [], [(xtn, q)])

    def rms_to_fm(h, T, xt, xtn, gcol, par=0):
        xn, ss, rstd, hT = T["xn"][par], T["ss"][par], T["rstd"][par], T["hT"][par]
        xnk, ssk, rsk, hTk = "xn%d" % par, "ss%d" % par, "rstd%d" % par, "hT%d" % par
        psT, idb, prm = T["psT"], T["idb"], T["prm"]
        h.act(xn[:], xt[:], AF.Square, [xtn], [xnk, ssk], accum_out=ss[:])
        h.ts("dve", rstd[:], ss[:], 1.0 / D, 1e-6, ALU.mult, ALU.add, [ssk], [rsk])
        h.act(rstd[:], rstd[:], AF.Sqrt, [rsk], [rsk])
        h.recip(rstd[:], rstd[:], [rsk], [rsk])
        h.act(xn[:], xt[:], AF.Copy, [xtn, rsk], [xnk], scale=rstd[:, 0:1])
        for c in range(8):
            h.tr(psT[:, c, :], xn[:, c * 128:(c + 1) * 128], idb[:], [xnk, "idb"], [("psT", c)])
        h.tt("dve", hT[:], psT[:], bc(prm[:, gcol:gcol + 8], [128, 8, 128]), ALU.mult,
             ["psT", "prm"], [hTk])
        return hT, hTk

    def alloc_rms(T, sb):
        T["xn"] = [sb("xn%d" % i, [128, D], BF16) for i in range(2)]
        T["ss"] = [sb("ss%d" % i, [128, 1], F32) for i in range(2)]
        T["rstd"] = [sb("rstd%d" % i, [128, 1], F32) for i in range(2)]
        T["hT"] = [sb("hT%d" % i, [128, 8, 128], BF16) for i in range(2)]

    def load_param_cols(h, prm, col, src, n):
        h.dma("sp", prm[:, col:col + n], src.rearrange("(c p) -> p c", p=128), [], [("prm", col)], slow=True)

    with contextlib.ExitStack() as ph:
      if "a" in phases:
        P = Prog(nc, gs, "a")
        h = H(P)

        def sb(name, shape, dt):
            return ph.enter_context(nc.sbuf_tensor("a_" + name, list(shape), dt))

        def psb(name, shape, dt):
            return ph.enter_context(nc.psum_tensor("a_" + name, list(shape), dt))

        T = {}
        cst, idb = load_consts(h, sb)
        T["idb"] = idb
        prm = sb("prm", [128, NPRM], F32)
        T["prm"] = prm
        load_param_cols(h, prm, PR_G, norm_mix, 8)
        load_param_cols(h, prm, PR_MU, mu_shift, 14)
        load_param_cols(h, prm, PR_W0, rwkv_w0, 4)
        load_param_cols(h, prm, PR_A0, rwkv_a0, 4)
        load_param_cols(h, prm, PR_KK, rwkv_k_k, 4)
        load_param_cols(h, prm, PR_KA, rwkv_k_a, 4)
        load_param_cols(h, prm, PR_RK, rwkv_r_k, 4)
        load_param_cols(h, prm, PR_LW, rwkv_ln_w, 4)
        load_param_cols(h, prm, PR_LB, rwkv_ln_b, 4)
        load_param_cols(h, prm, PR_BG, gla_bg, 2)
        load_param_cols(h, prm, PR_NW, gla_norm_w, 1)
        h.ts("dve", prm[:, PR_BG:PR_BG + 2], prm[:, PR_BG:PR_BG + 2], -1.0, None, ALU.mult, None,
             [("prm", PR_BG)], [("prm", PR_BG)])
        h.ts("dve", prm[:, PR_OMKA:PR_OMKA + 4], prm[:, PR_KA:PR_KA + 4], -1.0, 1.0, ALU.mult, ALU.add,
             [("prm", PR_KA)], [("prm", PR_OMKA)])

        NA1 = GATE0
        win = sb("win", [128, 8, NA1], BF16)
        w_in_v = w_in.rearrange("(c p) n -> p c n", p=128)
        for kc in range(8):
            h.dma("pool", win[:, kc, :], w_in_v[:, kc, 0:NA1], [], [("win", kc)])
        w2a2 = sb("w2a2", [128, 512], BF16)
        h.dma("pool", w2a2[0:64, :], rwkv_w2, [], [("w2a2", 0)])
        h.dma("pool", w2a2[64:128, :], rwkv_a2, [], [("w2a2", 1)])
        g2 = sb("g2", [128, 512], BF16)
        h.dma("pool", g2[:], rwkv_g2, [], ["g2"])
        wg2 = sb("wg2", [16, 256], BF16)
        h.dma("pool", wg2[:], gla_wg2, [], ["wg2"])

        xt = sb("xt", [128, D], F32)
        alloc_rms(T, sb)
        prw = sb("prw", [128, 14, 132], F32)
        lastc = sb("lastc", [128, 14, 1], F32)
        xs = sb("xs", [128, 14, 128], F32)
        Fm = [sb("F%d" % i, [128, 4, 128], F32) for i in range(12)]
        Hm = [sb("Hb%d" % i, [128, 4, 128], BF16) for i in range(8)]
        AR = sb("AR", [128, 4, 2, 128], BF16)
        tok = [sb("tok%d" % i, [128, 512], BF16) for i in range(4)]
        SCH = sb("SCH", [128, 8, 4, 128], BF16)
        INV = [sb("INV%d" % i, [128, 8, 128], BF16) for i in range(8)]
        lora_in = sb("lora_in", [128, 128], BF16)
        slg = sb("slg", [128, 128], BF16)
        Zbf = sb("Zbf", [128, 512], BF16)
        Yf = sb("Yf", [128, 512], F32)
        WTbf = sb("WTbf", [128, 4, 128], BF16)
        Ubf = sb("Ubf", [128, 512], BF16)
        Pst = sb("Pst", [128, 4, 64], F32)
        Ptmp = sb("Ptmp", [128, 4, 64], F32)
        Pbf = sb("Pbf", [128, 4, 64], BF16)
        o1 = sb("o1", [128, 512], F32)
        oo = sb("oo", [128, 512], F32)
        osq = sb("osq", [128, 512], F32)
        stat = sb("stat", [128, 64], F32)
        ogbf = sb("ogbf", [128, 4, 128], BF16)
        obbf = sb("obbf", [128, 4, 128], BF16)
        Sin = sb("Sin", [64, 8, 64], F32)
        Sout = sb("Sout", [64, 8, 64], F32)
        shs = sb("shs", [4, SHIFT], F32)
        shf = sb("shf", [128, 14, 4], F32)
        sho = sb("sho", [4, SHIFT], F32)
        lga = sb("lga", [16, 128], BF16)
        Sg = sb("Sg", [128, 2, 128], F32)
        Sgt = sb("Sgt", [128, 2, 128], F32)
        Sgbf = sb("Sgbf", [128, 4, 2, 128], BF16)
        SgIn = sb("SgIn", [128, 4, 2, 128], F32)
        SgOut = sb("SgOut", [128, 4, 2, 128], F32)

        T["psT"] = psb("psT", [128, 8, 128], BF16)
        psT = T["psT"]
        PS = [psb("ps%d" % i, [128, 512], F32) for i in range(7)]

        idf = cst[:, C_ID:C_ID + 128]
        bones = cst[:, C_BO:C_BO + 128]
        rstm = cst[:, C_RST:C_RST + 128]
        m4 = cst[:, C_M4:C_M4 + 512]
        msl = cst[:, C_SL:C_SL + 128]
        miu = cst[:, C_IU:C_IU + 128]

        h.memset("pool", Pst[:], 0.0, ["Pst"])
        h.memset("pool", Pbf[:], 0.0, ["Pbf"])
        h.memset("pool", Sg[:], 0.0, ["Sg"])
        h.memset("pool", lastc[:], 0.0, ["lastc"])

        def v4(ap, nseq, L):
            return ap.rearrange("p m (s l) -> p m s l", s=nseq)

        for b in BLKS:
            samp, nseq, L = geom(b)
            j = b - NPB
            valid = cst[:, (C_VS if samp else C_ONE):(C_VS if samp else C_ONE) + 128]
            load_x_block(h, b, xt, "xt")
            hT, hTk = rms_to_fm(h, T, xt, "xt", PR_G, b % 2)

            W = nseq * (L + 1)
            pv = prw[:, :, 0:W].rearrange("p m (s l) -> p m s l", s=nseq)
            for gi in range(4):
                ms = list(range(4 * gi, min(4 * gi + 4, 14)))
                pt = PS[gi % 2]
                pn = "ps%d" % (gi % 2)
                for mi, m in enumerate(ms):
                    for kc in range(8):
                        h.mm(pt[:, mi * 128:(mi + 1) * 128], win[:, kc, m * 128:(m + 1) * 128], hT[:, kc, :],
                             kc == 0, kc == 7, [("win", kc), hTk], [pn])
                nm = len(ms)
                h.cp("act", pv[:, ms[0]:ms[0] + nm, :, 1:L + 1],
                     pt[:, 0:nm * 128].rearrange("p (m s l) -> p m s l", m=nm, s=nseq),
                     [pn], ["prw"])
            if not samp:
                h.cp("pool", pv[:, :, 0, 0:1], lastc[:], ["lastc"], ["prw"])
            else:
                h.dma("sp", shs[:], st_shift[4 * j:4 * j + 4, :], [], ["shs"])
                for g4 in range(4):
                    ms = list(range(4 * g4, min(4 * g4 + 4, 14)))
                    for mi, m in enumerate(ms):
                        h.tr(PS[2][:, mi * 4:(mi + 1) * 4], shs[0:4, m * 128:(m + 1) * 128], idf[0:4, 0:4],
                             ["shs", "cst"], ["ps2"])
                    nm = len(ms)
                    h.cp("act", pv[:, ms[0]:ms[0] + nm, :, 0],
                         PS[2][:, 0:nm * 4].rearrange("p (m s) -> p m s", m=nm), ["ps2"], ["prw"])
            last_prompt = (b == NPB - 1)
            if samp or last_prompt:
                ncol = 4 if samp else 1
                if samp:
                    h.cp("pool", shf[:, :, 0:4], pv[:, :, :, 8], ["prw"], ["shf"])
                else:
                    h.cp("pool", shf[:, :, 0:1], pv[:, :, 0, 128:129], ["prw"], ["shf"])
                for g4 in range(4):
                    ms = list(range(4 * g4, min(4 * g4 + 4, 14)))
                    for mi, m in enumerate(ms):
                        h.tr(PS[2][0:ncol, mi * 128:(mi + 1) * 128], shf[:, m, 0:ncol], idf,
                             ["shf", "cst"], ["ps2"])
                    nm = len(ms)
                    h.cp("act", sho[0:ncol, ms[0] * 128:(ms[0] + nm) * 128], PS[2][0:ncol, 0:nm * 128],
                         ["ps2"], ["sho"])
                if samp:
                    h.dma("sp", o_shift_s[4 * j:4 * j + 4, :], sho[0:4, :], ["sho"], [], isout=True)
                else:
                    h.dma("sp", o_shift_p[0:1, :], sho[0:1, :], ["sho"], [], isout=True)
            if not samp:
                h.cp("pool", lastc[:], pv[:, :, 0, 128:129], ["prw"], ["lastc"])
            xs4 = xs[:].rearrange("p m (s l) -> p m s l", s=nseq)
            cur = pv[:, :, :, 1:L + 1]
            prv = pv[:, :, :, 0:L]
            h.tt("pool", xs4, prv, cur, ALU.subtract, ["prw"], ["xs"])
            h.tt("pool", xs4, xs4, bc(prm[:, PR_MU:PR_MU + 14], [128, 14, nseq, L]), ALU.mult,
                 ["xs", "prm"], ["xs"])
            h.tt("pool", xs4, xs4, cur, ALU.add, ["xs", "prw"], ["xs"])
            rT = xs[:, 0:4, :]
            kT = xs[:, 4:8, :]
            vT = xs[:, 8:12, :]

            sw, aa, gT, cum, E, Einv, Eprev, Eend, kk, kh, tmp, bon = Fm
            n = lambda i: "F%d" % i
            N_SW, N_AA, N_GT, N_CUM, N_E, N_EINV, N_EPREV, N_EEND, N_KK, N_KH, N_TMP, N_BON = [n(i) for i in range(12)]
            bT, kTb, BpT, KpT, vbf = Hm[0], Hm[1], Hm[2], Hm[3], Hm[4]
            h.act(lora_in[0:64, :], xs[0:64, 12, :], AF.Tanh, ["xs"], [("lora_in", 0)])
            h.cp("act", lora_in[64:128, :], xs[64:128, 12, :], ["xs"], [("lora_in", 1)])
            h.act(slg[:], xs[:, 13, :], AF.Sigmoid, ["xs"], ["slg"])
            for hg in range(4):
                h.mm(PS[2][:, hg * 128:(hg + 1) * 128], w2a2[0:64, hg * 128:(hg + 1) * 128], lora_in[0:64, :],
                     True, True, [("w2a2", 0), ("lora_in", 0)], ["ps2"])
            for hg in range(4):
                h.mm(PS[3][:, hg * 128:(hg + 1) * 128], w2a2[64:128, hg * 128:(hg + 1) * 128], lora_in[64:128, :],
                     True, True, [("w2a2", 1), ("lora_in", 1)], ["ps3"])
            for hg in range(4):
                h.mm(PS[4][:, hg * 128:(hg + 1) * 128], g2[:, hg * 128:(hg + 1) * 128], slg[:],
                     True, True, ["g2", "slg"], ["ps4"])
            for hg in range(4):
                h.act(sw[:, hg, :], PS[2][:, hg * 128:(hg + 1) * 128], AF.Sigmoid, ["ps2", "prm"], [(N_SW, hg)],
                      bias=prm[:, PR_W0 + hg:PR_W0 + hg + 1])
                h.act(aa[:, hg, :], PS[3][:, hg * 128:(hg + 1) * 128], AF.Sigmoid, ["ps3", "prm"], [(N_AA, hg)],
                      bias=prm[:, PR_A0 + hg:PR_A0 + hg + 1])
            h.cp("act", gT[:], PS[4][:].rearrange("p (c l) -> p c l", c=4), ["ps4"], [N_GT])
            h.stt("dve", sw[:], sw[:], C0, valid.unsqueeze(1).to_broadcast([128, 4, 128]),
                  ALU.mult, ALU.mult, [N_SW, "cst"], [N_SW])
            for hg in range(4):
                h.scan(cum[:, hg, :], rstm, sw[:, hg, :], [N_SW, "cst"], [(N_CUM, hg)])
            h.act(E[:], cum[:], AF.Exp, [N_CUM], [N_E])
            h.act(Einv[:], cum[:], AF.Exp, [N_CUM], [N_EINV], scale=-1.0)
            h.tt("pool", tmp[:], cum[:], sw[:], ALU.subtract, [N_CUM, N_SW], [N_TMP])
            h.act(Eprev[:], tmp[:], AF.Exp, [N_TMP], [N_EPREV])
            cum4 = cum[:].rearrange("p g (c l) -> p g c l", c=4)
            h.tt("pool", tmp[:].rearrange("p g (c l) -> p g c l", c=4),
                 cum4[:, :, :, 31:32].to_broadcast([128, 4, 4, 32]), cum4, ALU.subtract, [N_CUM], [N_TMP])
            h.act(Eend[:], tmp[:], AF.Exp, [N_TMP], [N_EEND])
            if samp:
                h.tt("pool", Eend[:], Eend[:], valid.unsqueeze(1).to_broadcast([128, 4, 128]), ALU.mult,
                     [N_EEND, "cst"], [N_EEND])
            h.tt("dve", kk[:], kT, bc(prm[:, PR_KK:PR_KK + 4], [128, 4, 128]), ALU.mult, ["xs", "prm"], [N_KK])
            h.act(tmp[:], kk[:], AF.Square, [N_KK], [N_TMP])
            for hg in range(4):
                h.mm(PS[2][:, hg * 128:(hg + 1) * 128], bones, tmp[:, hg, :], True, True, ["cst", N_TMP], ["ps2"])
            h.act(tmp[:], PS[2][:].rearrange("p (c l) -> p c l", c=4), AF.Sqrt, ["ps2"], [N_TMP])
            h.ts("dve", tmp[:], tmp[:], 1e-12, None, ALU.max, None, [N_TMP], [N_TMP])
            h.recip(tmp[:], tmp[:], [N_TMP], [N_TMP])
            h.tt("dve", kk[:], kk[:], tmp[:], ALU.mult, [N_KK, N_TMP], [N_KK])
            h.tt("pool", kh[:], aa[:], bc(prm[:, PR_KA:PR_KA + 4], [128, 4, 128]), ALU.mult, [N_AA, "prm"], [N_KH])
            h.tt("pool", kh[:], kh[:], bc(prm[:, PR_OMKA:PR_OMKA + 4], [128, 4, 128]), ALU.add, [N_KH, "prm"], [N_KH])
            h.tt("pool", kh[:], kh[:], kT, ALU.mult, [N_KH, "xs"], [N_KH])
            h.tt("dve", tmp[:], rT, bc(prm[:, PR_RK:PR_RK + 4], [128, 4, 128]), ALU.mult, ["xs", "prm"], [N_TMP])
            h.tt("dve", tmp[:], tmp[:], kh[:], ALU.mult, [N_TMP, N_KH], [N_TMP])
            for hg in range(4):
                h.mm(PS[3][:, hg * 128:(hg + 1) * 128], bones, tmp[:, hg, :], True, True, ["cst", N_TMP], ["ps3"])
            h.tt("dve", bon[:], PS[3][:].rearrange("p (c l) -> p c l", c=4), vT, ALU.mult, ["ps3", "xs"], [N_BON])
            h.tt("pool", aa[:], aa[:], kk[:], ALU.mult, [N_AA, N_KK], [N_AA])
            h.tt("dve", AR[:, :, 1, :], rT, E[:], ALU.mult, ["xs", N_E], [("AR", 1)])
            h.stt("dve", AR[:, :, 0, :], kk[:], -1.0, Eprev[:], ALU.mult, ALU.mult, [N_KK, N_EPREV], [("AR", 0)])
            h.tt("dve", bT[:], aa[:], Einv[:], ALU.mult, [N_AA, N_EINV], ["Hb0"])
            h.tt("dve", kTb[:], kh[:], Einv[:], ALU.mult, [N_KH, N_EINV], ["Hb1"])
            h.tt("pool", BpT[:], aa[:], Eend[:], ALU.mult, [N_AA, N_EEND], ["Hb2"])
            h.tt("pool", KpT[:], kh[:], Eend[:], ALU.mult, [N_KH, N_EEND], ["Hb3"])
            h.cp("pool", vbf[:], vT, ["xs"], ["Hb4"])
            h.cp("pool", stat[:, 0:16].rearrange("p (g c) -> p g c", g=4),
                 E[:].rearrange("p g (c l) -> p g c l", c=4)[:, :, :, 31], [N_E], [("stat", 0)])
            gam = stat[:, 0:16].rearrange("p (g c) -> p g c", g=4)

            for hd in range(8):
                hg, pb = hd // 2, 64 * (hd % 2)
                px = PS[hd % 2]
                pxn = "ps%d" % (hd % 2)
                h.mm(px[:, 0:256], bT[pb:pb + 64, hg, :], AR[pb:pb + 64, hg, :, :].rearrange("p a l -> p (a l)"),
                     True, True, ["Hb0", "AR"], [pxn])
                h.mm(px[:, 256:512], kTb[pb:pb + 64, hg, :], AR[pb:pb + 64, hg, :, :].rearrange("p a l -> p (a l)"),
                     True, True, ["Hb1", "AR"], [pxn])
                h.tt("dve", SCH[:, hd, :, :].rearrange("p a l -> p (a l)"), px[:], m4, ALU.mult,
                     [pxn, "cst"], [("SCH", hd)])
            Ac, Nn, An, Xc, Xtc, Xn, Xtn, Nc2 = INV
            NI = ["INV%d" % i for i in range(8)]
            Ac4 = Ac[:].rearrange("p (g a) l -> p g a l", a=2)
            for h2 in range(2):
                pt = PS[2 + h2]
                ptn = "ps%d" % (2 + h2)
                pb = 64 * h2
                for hg in range(4):
                    h.mm(pt[:, hg * 128:(hg + 1) * 128], AR[pb:pb + 64, hg, 0, :], bT[pb:pb + 64, hg, :],
                         True, True, ["AR", "Hb0"], [ptn])
                h.tt("dve", Ac4[:, :, h2, :], pt[:].rearrange("p (q l) -> p q l", q=4),
                     msl.unsqueeze(1).to_broadcast([128, 4, 128]), ALU.mult, [ptn, "cst"], [NI[0]])
            h.tt("pool", Xc[:], SCH[:, :, 0, :], idb[:].unsqueeze(1).to_broadcast([128, 8, 128]), ALU.add,
                 ["SCH", "idb"], [NI[3]])
            h.tt("pool", Xtc[:], Ac[:], idb[:].unsqueeze(1).to_broadcast([128, 8, 128]), ALU.add,
                 [NI[0], "idb"], [NI[4]])

            Ncur_ap = lambda hd: SCH[:, hd, 0, :]
            Ncur_key = "SCH"
            Acur, Acur_key = Ac, NI[0]
            Xcur, Xcur_key, Xtcur, Xtcur_key = Xc, NI[3], Xtc, NI[4]
            Nnext = [(Nn, NI[1]), (Nc2, NI[7])]
            Anext = [(An, NI[2]), (Ac, NI[0])]
            Xnext = [(Xn, NI[5]), (Xc, NI[3])]
            Xtnext = [(Xtn, NI[6]), (Xtc, NI[4])]
            for lvl in range(4):
                last = (lvl == 3)
                Nx, Nxk = Nnext[lvl % 2]
                Ax, Axk = Anext[lvl % 2]
                Xx, Xxk = Xnext[lvl % 2]
                Xtx, Xtxk = Xtnext[lvl % 2]
                for g2i in range(2):
                    pa, pan = PS[2 * g2i], "ps%d" % (2 * g2i)
                    pbk, pbn = PS[2 * g2i + 1], "ps%d" % (2 * g2i + 1)
                    for q in range(4):
                        hd = 4 * g2i + q
                        h.mm(pa[:, q * 128:(q + 1) * 128], Acur[:, hd, :], Ncur_ap(hd), True, True,
                             [Acur_key, Ncur_key], [pan])
                    h.cp("act", Nx[:, 4 * g2i:4 * g2i + 4, :], pa[:].rearrange("p (q l) -> p q l", q=4),
                         [pan], [(Nxk, g2i)])
                    if not last:
                        for q in range(4):
                            hd = 4 * g2i + q
                            h.mm(pbk[:, q * 128:(q + 1) * 128], Ncur_ap(hd), Acur[:, hd, :], True, True,
                                 [Acur_key, Ncur_key], [pbn])
                        h.cp("act", Ax[:, 4 * g2i:4 * g2i + 4, :], pbk[:].rearrange("p (q l) -> p q l", q=4),
                             [pbn], [(Axk, g2i)])
                for g2i in range(2):
                    pa, pan = PS[4 + g2i], "ps%d" % (4 + g2i)
                    for q in range(4):
                        hd = 4 * g2i + q
                        h.mm(pa[:, q * 128:(q + 1) * 128], Xtcur[:, hd, :], Nx[:, hd, :], True, True,
                             [Xtcur_key, (Nxk, g2i)], [pan])
                    h.tt("dve", Xx[:, 4 * g2i:4 * g2i + 4, :], pa[:].rearrange("p (q l) -> p q l", q=4),
                         Xcur[:, 4 * g2i:4 * g2i + 4, :], ALU.add, [pan, Xcur_key], [(Xxk, g2i)])
                if not last:
                    for g2i in range(2):
                        pa, pan = PS[2 * g2i], "ps%d" % (2 * g2i)
                        for q in range(4):
                            hd = 4 * g2i + q
                            h.mm(pa[:, q * 128:(q + 1) * 128], Nx[:, hd, :], Xtcur[:, hd, :], True, True,
                                 [Xtcur_key, (Nxk, g2i)], [pan])
                        h.tt("dve", Xtx[:, 4 * g2i:4 * g2i + 4, :], pa[:].rearrange("p (q l) -> p q l", q=4),
                             Xtcur[:, 4 * g2i:4 * g2i + 4, :], ALU.add, [pan, Xtcur_key], [(Xtxk, g2i)])
                Ncur_ap = (lambda t: (lambda hd: t[:, hd, :]))(Nx)
                Ncur_key = Nxk
                Acur, Acur_key = Ax, Axk
                Xcur, Xcur_key = Xx, Xxk
                Xtcur, Xtcur_key = Xtx, Xtxk
            X4, X4k = Xcur, Xcur_key

            Atok, Bptok, Kptok, Vtok = tok
            srcs = [(AR[:, :, 0, :], "AR", Atok, "tok0"), (BpT[:], "Hb2", Bptok, "tok1"),
                    (KpT[:], "Hb3", Kptok, "tok2"), (vbf[:], "Hb4", Vtok, "tok3")]
            for si in range(0, 4, 2):
                for u in range(2):
                    src, srck, dst, dstk = srcs[si + u]
                    for hg in range(4):
                        h.tr(psT[:, u * 4 + hg, :], src[:, hg, :], idb[:], [srck, "idb"], [("psT", u * 4 + hg)])
                for u in range(2):
                    src, srck, dst, dstk = srcs[si + u]
                    h.cp("act", dst[:], psT[:, u * 4:(u + 1) * 4, :].rearrange("p c l -> p (c l)"),
                         ["psT"], [dstk])

            for hd in range(8):
                h.mm(PS[0][:, hd * 64:(hd + 1) * 64], SCH[:, hd, 2, :], Vtok[:, hd * 64:(hd + 1) * 64],
                     True, True, ["SCH", "tok3"], ["ps0"])
            h.cp("act", Zbf[:], PS[0][:], ["ps0"], ["Zbf"])
            for hd in range(8):
                h.mm(PS[1][:, hd * 64:(hd + 1) * 64], X4[:, hd, :], Zbf[:, hd * 64:(hd + 1) * 64],
                     True, True, [X4k, "Zbf"], ["ps1"])
            h.cp("act", Yf[:], PS[1][:], ["ps1"], ["Yf"])
            for hd in range(8):
                hg, pb = hd // 2, 64 * (hd % 2)
                h.mm(PS[2][pb:pb + 64, hg * 128:(hg + 1) * 128], Atok[:, hd * 64:(hd + 1) * 64], X4[:, hd, :],
                     True, True, ["tok0", X4k], ["ps2"])
            h.cp("act", WTbf[:], PS[2][:].rearrange("p (c l) -> p c l", c=4), ["ps2"], ["WTbf"])

            psUs, psUn = (PS[0], PS[1]), ("ps0", "ps1")
            psOs, psOn = (PS[3], PS[4]), ("ps3", "ps4")
            psPn = PS[5]
            Ubf4 = Ubf[:].rearrange("p (g a v) -> p g a v", g=4, a=2)
            Yf4 = Yf[:].rearrange("p (g a v) -> p g a v", g=4, a=2)
            for c in range(4):
                s0 = 32 * c
                if samp:
                    seq = 4 * j + c
                    h.dma("sp", Sin[:], st_wkv[seq].rearrange("h v k -> v h k"), [], ["Sin"])
                    for hg in range(4):
                        h.tr(PS[6][:, hg * 64:(hg + 1) * 64],
                             Sin[:, 2 * hg:2 * hg + 2, :].rearrange("p a k -> p (a k)"), idf[0:64, 0:64],
                             ["Sin", "cst"], ["ps6"])
                    h.cp("act", Pst[:], PS[6][:, 0:256].rearrange("p (c v) -> p c v", c=4), ["ps6"], ["Pst"])
                    h.cp("dve", Pbf[:], Pst[:], ["Pst"], ["Pbf"])
                for hd in range(8):
                    hg, h2 = hd // 2, hd % 2
                    pb = 64 * h2
                    h.mm(psUs[h2][s0:s0 + 32, hg * 64:(hg + 1) * 64], WTbf[pb:pb + 64, hg, s0:s0 + 32],
                         Pbf[pb:pb + 64, hg, :], True, True, ["WTbf", "Pbf"], [psUn[h2]], tp=(pb, s0))
                for hd in range(8):
                    hg, h2 = hd // 2, hd % 2
                    pb = 64 * h2
                    h.mm(psOs[h2][s0:s0 + 32, hg * 64:(hg + 1) * 64], AR[pb:pb + 64, hg, 1, s0:s0 + 32],
                         Pbf[pb:pb + 64, hg, :], True, True, ["AR", "Pbf"], [psOn[h2]], tp=(pb, s0))
                for h2 in range(2):
                    h.tt("dve", Ubf4[s0:s0 + 32, :, h2, :],
                         psUs[h2][s0:s0 + 32, 0:256].rearrange("p (g v) -> p g v", g=4),
                         Yf4[s0:s0 + 32, :, h2, :], ALU.add, [psUn[h2], "Yf"], [("Ubf", c)])
                for hd in range(8):
                    hg, pb = hd // 2, 64 * (hd % 2)
                    h.mm(psPn[pb:pb + 64, hg * 64:(hg + 1) * 64], Bptok[s0:s0 + 32, hd * 64:(hd + 1) * 64],
                         Ubf[s0:s0 + 32, hd * 64:(hd + 1) * 64], True, False, ["tok1", ("Ubf", c)], ["ps5"],
                         tp=(s0, pb))
                    h.mm(psPn[pb:pb + 64, hg * 64:(hg + 1) * 64], Kptok[s0:s0 + 32, hd * 64:(hd + 1) * 64],
                         Vtok[s0:s0 + 32, hd * 64:(hd + 1) * 64], False, True, ["tok2", "tok3"], ["ps5"],
                         tp=(s0, pb))
                h.tt("dve", Ptmp[:], Pst[:], gam[:, :, c:c + 1].to_broadcast([128, 4, 64]), ALU.mult,
                     ["Pst", ("stat", 0)], ["Ptmp"])
                h.tt("dve", Pst[:], Ptmp[:], psPn[:, 0:256].rearrange("p (c v) -> p c v", c=4), ALU.add,
                     ["Ptmp", "ps5"], ["Pst"])
                if not samp and not (b == NPB - 1 and c == 3):
                    h.cp("act", Pbf[:], Pst[:], ["Pst"], ["Pbf"])
                if samp or (b == NPB - 1 and c == 3):
                    for hg in range(4):
                        h.tr(PS[6][0:64, hg * 128:(hg + 1) * 128], Pst[:, hg, :], idf, ["Pst", "cst"], ["ps6"])
                    h.cp("act", Sout[:].rearrange("p h k -> p (h k)"), PS[6][0:64, :], ["ps6"], ["Sout"])
                    dst = o_wkv_s[4 * j + c] if samp else o_wkv_p[0]
                    h.dma("sp", dst.rearrange("h v k -> v h k"), Sout[:], ["Sout"], [], isout=True)
            for hd in range(8):
                h.mm(PS[6][:, hd * 64:(hd + 1) * 64], SCH[:, hd, 1, :], Ubf[:, hd * 64:(hd + 1) * 64],
                     True, False, ["SCH", "Ubf"], ["ps6"])
                h.mm(PS[6][:, hd * 64:(hd + 1) * 64], SCH[:, hd, 3, :], Vtok[:, hd * 64:(hd + 1) * 64],
                     False, True, ["SCH", "tok3"], ["ps6"])
            o14 = o1[:].rearrange("p (g a v) -> p g a v", g=4, a=2)
            for h2 in range(2):
                h.cp("act", o14[:, :, h2, :], psOs[h2][:, 0:256].rearrange("p (g v) -> p g v", g=4),
                     [psOn[h2]], ["o1"])
            h.tt("dve", oo[:], o1[:], PS[6][:], ALU.add, ["o1", "ps6"], ["oo"])

            oo3 = oo[:].rearrange("p (h v) -> p h v", h=8)
            h.act(osq[:], oo[:], AF.Square, ["oo"], ["osq"])
            s1, s2, mean, msq, rs, nb = (stat[:, 16:24], stat[:, 24:32], stat[:, 32:40], stat[:, 40:48],
                                         stat[:, 48:56], stat[:, 56:64])
            h.rsum(s1, oo3, ["oo"], [("stat", 1)])
            h.rsum(s2, osq[:].rearrange("p (h v) -> p h v", h=8), ["osq"], [("stat", 2)])
            h.ts("dve", mean, s1, 1.0 / 64, None, ALU.mult, None, [("stat", 1)], [("stat", 3)])
            h.tt("dve", msq, mean, mean, ALU.mult, [("stat", 3)], [("stat", 4)])
            h.stt("dve", rs, s2, 1.0 / 64, msq, ALU.mult, ALU.subtract, [("stat", 2), ("stat", 4)], [("stat", 5)])
            h.ts("dve", rs, rs, 64e-5, None, ALU.add, None, [("stat", 5)], [("stat", 5)])
            h.act(rs, rs, AF.Sqrt, [("stat", 5)], [("stat", 5)])
            h.recip(rs, rs, [("stat", 5)], [("stat", 5)])
            h.tt("dve", oo3, oo3, mean.unsqueeze(2).to_broadcast([128, 8, 64]), ALU.subtract,
                 ["oo", ("stat", 3)], ["oo"])
            h.tt("dve", oo3, oo3, rs.unsqueeze(2).to_broadcast([128, 8, 64]), ALU.mult,
                 ["oo", ("stat", 5)], ["oo"])
            for hg in range(4):
                h.tr(PS[0][:, hg * 128:(hg + 1) * 128], oo[:, hg * 128:(hg + 1) * 128], idf, ["oo", "cst"], ["ps0"])
            ps0v = PS[0][:].rearrange("p (c l) -> p c l", c=4)
            h.tt("dve", tmp[:], ps0v, bc(prm[:, PR_LW:PR_LW + 4], [128, 4, 128]), ALU.mult, ["ps0", "prm"], [N_TMP])
            h.tt("pool", tmp[:], tmp[:], bc(prm[:, PR_LB:PR_LB + 4], [128, 4, 128]), ALU.add, [N_TMP, "prm"], [N_TMP])
            h.tt("pool", tmp[:], tmp[:], bon[:], ALU.add, [N_TMP, N_BON], [N_TMP])
            h.tt("dve", ogbf[:], tmp[:], gT[:], ALU.mult, [N_TMP, N_GT], ["ogbf"])
            h.dma("sp", og_s[b], ogbf[:].rearrange("p c l -> p (c l)"), ["ogbf"], [])

            gcols = [GLA0 + 128 * i for i in range(8)] + [GLA0 + 1040 + 128 * i for i in range(4)]
            for gi in range(3):
                pt, pn = PS[gi % 2], "ps%d" % (gi % 2)
                for mi in range(4):
                    c0 = gcols[4 * gi + mi]
                    for kc in range(8):
                        h.mm(pt[:, mi * 128:(mi + 1) * 128], win[:, kc, c0:c0 + 128], hT[:, kc, :],
                             kc == 0, kc == 7, [("win", kc), hTk], [pn])
                h.cp("act", xs[:, 4 * gi:4 * gi + 4, :], pt[:].rearrange("p (c l) -> p c l", c=4), [pn], ["xs"])
            for kc in range(8):
                h.mm(PS[2][0:16, 0:128], win[:, kc, GLA0 + 1024:GLA0 + 1040], hT[:, kc, :], kc == 0, kc == 7,
                     [("win", kc), hTk], ["ps2"])
            h.cp("act", lga[:], PS[2][0:16, 0:128], ["ps2"], ["lga"])
            qT, kgT, vgT, ogT = xs[:, 0:2, :], xs[:, 2:4, :], xs[:, 4:8, :], xs[:, 8:12, :]
            for c2 in range(2):
                h.mm(PS[3][:, c2 * 128:(c2 + 1) * 128], wg2[0:16, c2 * 128:(c2 + 1) * 128], lga[:], True, True,
                     ["wg2", "lga"], ["ps3"])
            la_, cg, Eg, Eginv, Egend = Fm[0], Fm[3], Fm[4], Fm[5], Fm[7]
            for c2 in range(2):
                h.act(la_[:, c2, :], PS[3][:, c2 * 128:(c2 + 1) * 128], AF.Exp, ["ps3", "prm"], [(N_SW, c2)],
                      scale=-1.0, bias=prm[:, PR_BG + c2:PR_BG + c2 + 1])
            h.ts("dve", la_[:, 0:2, :], la_[:, 0:2, :], 1.0, None, ALU.add, None, [N_SW], [N_SW])
            h.act(la_[:, 0:2, :], la_[:, 0:2, :], AF.Ln, [N_SW], [N_SW])
            h.stt("dve", la_[:, 0:2, :], la_[:, 0:2, :], -1.0 / 16.0,
                  valid.unsqueeze(1).to_broadcast([128, 2, 128]), ALU.mult, ALU.mult, [N_SW, "cst"], [N_SW])
            for c2 in range(2):
                h.scan(cg[:, c2, :], rstm, la_[:, c2, :], [N_SW, "cst"], [(N_CUM, c2)])
            h.act(Eg[:, 0:2, :], cg[:, 0:2, :], AF.Exp, [N_CUM], [N_E])
            h.act(Eginv[:, 0:2, :], cg[:, 0:2, :], AF.Exp, [N_CUM], [N_EINV], scale=-1.0)
            cg4 = cg[:, 0:2, :].rearrange("p g (c l) -> p g c l", c=4)
            h.tt("pool", tmp[:, 0:2, :].rearrange("p g (c l) -> p g c l", c=4),
                 cg4[:, :, :, 31:32].to_broadcast([128, 2, 4, 32]), cg4, ALU.subtract, [N_CUM], [N_TMP])
            h.act(Egend[:, 0:2, :], tmp[:, 0:2, :], AF.Exp, [N_TMP], [N_EEND])
            if samp:
                h.tt("pool", Egend[:, 0:2, :], Egend[:, 0:2, :], valid.unsqueeze(1).to_broadcast([128, 2, 128]),
                     ALU.mult, [N_EEND, "cst"], [N_EEND])
            h.cp("pool", stat[:, 0:8].rearrange("p (g c) -> p g c", g=2),
                 Eg[:, 0:2, :].rearrange("p g (c l) -> p g c l", c=4)[:, :, :, 31], [N_E], [("stat", 0)])
            gamg = stat[:, 0:8].rearrange("p (g c) -> p g c", g=2)
            qdT, kiT, keT, vgbf = Hm[5], Hm[6], Hm[7], Hm[4]
            h.stt("dve", qdT[:, 0:2, :], qT, 0.125, Eg[:, 0:2, :], ALU.mult, ALU.mult, ["xs", N_E], ["Hb5"])
            h.tt("dve", kiT[:, 0:2, :], kgT, Eginv[:, 0:2, :], ALU.mult, ["xs", N_EINV], ["Hb6"])
            h.tt("pool", keT[:, 0:2, :], kgT, Egend[:, 0:2, :], ALU.mult, ["xs", N_EEND], ["Hb7"])
            h.cp("pool", vgbf[:], vgT, ["xs"], ["Hb4"])
            silu = Fm[2]
            h.act(silu[:], ogT, AF.Silu, ["xs"], [N_GT])
            GS = Hm[0]
            GS4 = GS[:].rearrange("p (g a) l -> p g a l", a=2)
            for h2 in range(2):
                pb = 64 * h2
                pt, ptn = PS[3 * h2], "ps%d" % (3 * h2)
                for c2 in range(2):
                    h.mm(pt[:, c2 * 128:(c2 + 1) * 128], kiT[pb:pb + 64, c2, :], qdT[pb:pb + 64, c2, :], True, True,
                         ["Hb6", "Hb5"], [ptn])
                h.tt("dve", GS4[:, :, h2, :], pt[:, 0:256].rearrange("p (q l) -> p q l", q=2),
                     miu.unsqueeze(1).to_broadcast([128, 2, 128]), ALU.mult, [ptn, "cst"], ["Hb0"])
            Vgtok, Ketok = tok[0], tok[1]
            for hg in range(4):
                h.tr(psT[:, hg, :], vgbf[:, hg, :], idb[:], ["Hb4", "idb"], [("psT", hg)])
            for c2 in range(2):
                h.tr(psT[:, 4 + c2, :], keT[:, c2, :], idb[:], ["Hb7", "idb"], [("psT", 4 + c2)])
            h.cp("act", Vgtok[:], psT[:, 0:4, :].rearrange("p c l -> p (c l)"), ["psT"], ["tok0"])
            h.cp("act", Ketok[:, 0:256], psT[:, 4:6, :].rearrange("p c l -> p (c l)"), ["psT"], ["tok1"])
            DB = [1, 2, 5, 6]
            for c in range(4):
                s0 = 32 * c
                pt, pn = PS[DB[c]], "ps%d" % DB[c]
                for hd in range(4):
                    c2, pb = hd // 2, 64 * (hd % 2)
                    off = c2 * 128
                    h.mm(pt[pb:pb + 64, off:off + 128], Ketok[s0:s0 + 32, hd * 64:(hd + 1) * 64],
                         Vgtok[s0:s0 + 32, hd * 128:(hd + 1) * 128], True, True, ["tok1", "tok0"], [pn],
                         tp=(s0, pb))
            if samp:
                h.dma("sp", SgIn[:], st_gla[4 * j:4 * j + 4].rearrange("q (c2 h2) k v -> (h2 k) q c2 v", h2=2),
                      [], ["SgIn"])
                h.cp("act", Sgbf[:], SgIn[:], ["SgIn"], ["Sgbf"])
            else:
                h.cp("act", Sgbf[:, 0, :, :], Sg[:], ["Sg"], [("Sgbf", 0)])
            for c in range(4):
                pt, pn = PS[DB[c]], "ps%d" % DB[c]
                dv = pt[:, 0:256].rearrange("p (g v) -> p g v", g=2)
                gb = gamg[:, :, c:c + 1].to_broadcast([128, 2, 128])
                if samp:
                    h.tt("dve", Sgt[:], SgIn[:, c, :, :], gb, ALU.mult, ["SgIn", ("stat", 0)], ["Sgt"])
                    h.tt("dve", SgOut[:, c, :, :], Sgt[:], dv, ALU.add, ["Sgt", pn], [("SgOut", c)])
                else:
                    h.tt("dve", Sgt[:], Sg[:], gb, ALU.mult, ["Sg", ("stat", 0)], ["Sgt"])
                    h.tt("dve", Sg[:], Sgt[:], dv, ALU.add, ["Sgt", pn], ["Sg"])
                    if c < 3:
                        h.cp("act", Sgbf[:, c + 1, :, :], Sg[:], ["Sg"], [("Sgbf", c + 1)])
            if samp:
                h.dma("sp", o_gla_s[4 * j:4 * j + 4].rearrange("q (c2 h2) k v -> (h2 k) q c2 v", h2=2),
                      SgOut[:], ["SgOut"], [], isout=True)
            elif b == NPB - 1:
                h.dma("sp", o_gla_p[0].rearrange("(c2 h2) k v -> (h2 k) c2 v", h2=2), Sg[:], ["Sg"], [],
                      isout=True)
            for c in range(4):
                s0 = 32 * c
                for hd in range(4):
                    c2, h2 = hd // 2, hd % 2
                    pb = 64 * h2
                    h.mm(PS[3 * h2][s0:s0 + 32, c2 * 128:(c2 + 1) * 128], qdT[pb:pb + 64, c2, s0:s0 + 32],
                         Sgbf[pb:pb + 64, c, c2, :], True, True, ["Hb5", "Sgbf"], ["ps%d" % (3 * h2)], tp=(pb, s0))
            for hd in range(4):
                h.mm(PS[4][:, hd * 128:(hd + 1) * 128], GS[:, hd, :], Vgtok[:, hd * 128:(hd + 1) * 128], True, True,
                     ["Hb0", "tok0"], ["ps4"])
            o1g = o1[:].rearrange("p (g a v) -> p g a v", g=2, a=2)
            for h2 in range(2):
                h.cp("act", o1g[:, :, h2, :], PS[3 * h2][:, 0:256].rearrange("p (g v) -> p g v", g=2),
                     ["ps%d" % (3 * h2)], ["o1"])
            h.tt("dve", oo[:], o1[:], PS[4][:], ALU.add, ["o1", "ps4"], ["oo"])
            oo4 = oo[:].rearrange("p (h v) -> p h v", h=4)
            h.act(osq[:], oo[:], AF.Square, ["oo"], ["osq"])
            gs2, grs = stat[:, 16:20], stat[:, 24:28]
            h.rsum(gs2, osq[:].rearrange("p (h v) -> p h v", h=4), ["osq"], [("stat", 1)])
            h.ts("dve", grs, gs2, 1.0 / 128, 1e-6, ALU.mult, ALU.add, [("stat", 1)], [("stat", 2)])
            h.act(grs, grs, AF.Sqrt, [("stat", 2)], [("stat", 2)])
            h.recip(grs, grs, [("stat", 2)], [("stat", 2)])
            h.tt("dve", oo4, oo4, grs.unsqueeze(2).to_broadcast([128, 4, 128]), ALU.mult, ["oo", ("stat", 2)], ["oo"])
            for hd in range(4):
                h.tr(PS[0][:, hd * 128:(hd + 1) * 128], oo[:, hd * 128:(hd + 1) * 128], idf, ["oo", "cst"], ["ps0"])
            h.stt("dve", obbf[:], PS[0][:].rearrange("p (c l) -> p c l", c=4), prm[:, PR_NW:PR_NW + 1], silu[:],
                  ALU.mult, ALU.mult, ["ps0", "prm", N_GT], ["obbf"])
            h.dma("sp", ob_s[b], obbf[:].rearrange("p c l -> p (c l)"), ["obbf"], [])

            if dbg is not None and dbg.get("blk") == b and dbg.get("phase") == "a1":
                src, keys = dbg["fn"](dict(locals()))
                h.dma("sp", dbg_out, src, keys, [], isout=True)
        P.emit()

    with contextlib.ExitStack() as ph:
      if "b" in phases:
        P = Prog(nc, gs, "b")
        h = H(P)

        def sb(name, shape, dt):
            return ph.enter_context(nc.sbuf_tensor("b_" + name, list(shape), dt))

        def psb(name, shape, dt):
            return ph.enter_context(nc.psum_tensor("b_" + name, list(shape), dt))

        T = {}
        cst, idb = load_consts(h, sb)
        T["idb"] = idb
        prm = sb("prm", [128, NPRM], F32)
        T["prm"] = prm
        load_param_cols(h, prm, PR_G, norm_mix, 8)
        wing = sb("wing", [128, 8, 2048], BF16)
        w_in_v = w_in.rearrange("(c p) n -> p c n", p=128)
        for kc in range(8):
            h.dma("pool", wing[:, kc, :], w_in_v[:, kc, GATE0:INC], [], [("wing", kc)])
        woa = sb("woa", [128, 4, D], BF16)
        wob = sb("wob", [128, 4, D], BF16)
        wo = sb("wo", [128, 8, D], BF16)
        h.dma("pool", woa[:], w_out_a.rearrange("(c p) n -> p c n", p=128), [], ["woa"])
        h.dma("pool", wob[:], w_out_b.rearrange("(c p) n -> p c n", p=128), [], ["wob"])
        for kc in range(8):
            h.dma("pool", wo[:, kc, :], w_o.rearrange("(c p) n -> p c n", p=128)[:, kc, :], [], [("wo", kc)])
        xts = [sb("xt%d" % i, [128, D], F32) for i in range(4)]
        alloc_rms(T, sb)
        ogbs = [sb("ogb%d" % i, [128, 4, 128], BF16) for i in range(2)]
        obbs = [sb("obb%d" % i, [128, 4, 128], BF16) for i in range(2)]
        sgas = [sb("sga%d" % i, [128, 8, 128], F32) for i in range(2)]
        sgbs = [sb("sgb%d" % i, [128, 8, 128], F32) for i in range(2)]
        tas = [sb("ta%d" % i, [128, 8, 128], F32) for i in range(2)]
        mgs = [sb("mg%d" % i, [128, 8, 128], BF16) for i in range(2)]
        x1ts = [sb("x1t%d" % i, [128, D], F32) for i in range(2)]
        T["psT"] = psb("psT", [128, 8, 128], BF16)
        PS = [psb("ps%d" % i, [128, 512], F32) for i in range(7)]
        for b in BLKS:
            xt, xtn = xts[b % 4], "xt%d" % (b % 4)
            load_x_block(h, b, xt, xtn)
            p2 = b % 2
            ogb, obb, sga, sgb, ta, mg, x1t = ogbs[p2], obbs[p2], sgas[p2], sgbs[p2], tas[p2], mgs[p2], x1ts[p2]
            K_ogb, K_obb, K_sga, K_sgb, K_ta, K_mg, K_x1t = ["%s%d" % (nm, p2) for nm in
                                                            ("ogb", "obb", "sga", "sgb", "ta", "mg", "x1t")]
            h.dma("sp", ogb[:].rearrange("p c l -> p (c l)"), og_s[b], [], [K_ogb])
            h.dma("sp", obb[:].rearrange("p c l -> p (c l)"), ob_s[b], [], [K_obb])
            hT, hTk = rms_to_fm(h, T, xt, xtn, PR_G, b % 2)
            for half, dst, dk in ((0, sga, K_sga), (1, sgb, K_sgb)):
                for gi in range(2):
                    pt, pn = PS[gi], "ps%d" % gi
                    for mi in range(4):
                        c0 = half * 1024 + (4 * gi + mi) * 128
                        for kc in range(8):
                            h.mm(pt[:, mi * 128:(mi + 1) * 128], wing[:, kc, c0:c0 + 128], hT[:, kc, :],
                                 kc == 0, kc == 7, [("wing", kc), hTk], [pn])
                    h.act(dst[:, 4 * gi:4 * gi + 4, :], pt[:].rearrange("p (c l) -> p c l", c=4), AF.Sigmoid,
                          [pn], [(dk, gi)])
            for gi in range(2):
                pt, pn = PS[2 + gi], "ps%d" % (2 + gi)
                for mi in range(4):
                    m = 4 * gi + mi
                    for kc in range(4):
                        h.mm(pt[:, mi * 128:(mi + 1) * 128], woa[:, kc, m * 128:(m + 1) * 128], ogb[:, kc, :],
                             kc == 0, kc == 3, ["woa", K_ogb], [pn])
                h.tt("dve", ta[:, 4 * gi:4 * gi + 4, :], pt[:].rearrange("p (c l) -> p c l", c=4),
                     sga[:, 4 * gi:4 * gi + 4, :], ALU.mult, [pn, (K_sga, gi)], [(K_ta, gi)])
            for gi in range(2):
                pt, pn = PS[4 + gi], "ps%d" % (4 + gi)
                for mi in range(4):
                    m = 4 * gi + mi
                    for kc in range(4):
                        h.mm(pt[:, mi * 128:(mi + 1) * 128], wob[:, kc, m * 128:(m + 1) * 128], obb[:, kc, :],
                             kc == 0, kc == 3, ["wob", K_obb], [pn])
                h.tt("dve", sgb[:, 4 * gi:4 * gi + 4, :], pt[:].rearrange("p (c l) -> p c l", c=4),
                     sgb[:, 4 * gi:4 * gi + 4, :], ALU.mult, [pn, (K_sgb, gi)], [(K_sgb, gi)])
                h.tt("pool", mg[:, 4 * gi:4 * gi + 4, :], ta[:, 4 * gi:4 * gi + 4, :],
                     sgb[:, 4 * gi:4 * gi + 4, :], ALU.add, [(K_ta, gi), (K_sgb, gi)], [(K_mg, gi)])
            for nh in range(2):
                pt, pn = PS[2 + nh], "ps%d" % (2 + nh)
                for kc in range(8):
                    h.mm(pt[:], mg[:, kc, :], wo[:, kc, nh * 512:(nh + 1) * 512], kc == 0, kc == 7,
                         [K_mg, ("wo", kc)], [pn])
                h.tt("dve", x1t[:, nh * 512:(nh + 1) * 512], pt[:], xt[:, nh * 512:(nh + 1) * 512], ALU.add,
                     [pn, xtn], [(K_x1t, nh)])
            h.dma("sp", x1_s[b * 128:(b + 1) * 128, :], x1t[:], [K_x1t], [])
        P.emit()

    with contextlib.ExitStack() as ph:
      if "c" in phases:
        P = Prog(nc, gs, "c")
        h = H(P)

        def sb(name, shape, dt):
            return ph.enter_context(nc.sbuf_tensor("c_" + name, list(shape), dt))

        def psb(name, shape, dt):
            return ph.enter_context(nc.psum_tensor("c_" + name, list(shape), dt))

        T = {}
        cst, idb = load_consts(h, sb)
        T["idb"] = idb
        idf = cst[:, C_ID:C_ID + 128]
        prm = sb("prm", [128, 8], F32)
        T["prm"] = prm
        load_param_cols(h, prm, 0, norm_ffn, 8)
        cw = sb("cw", [128, 4, 44], F32)
        for jx in range(3):
            h.dma("sp", cw[:, jx, :], ffn_conv_w[jx].rearrange("(c p) -> p c", p=128), [], [("cw", jx)], slow=True)
        h.dma("sp", cw[:, 3, :], ffn_conv_b.rearrange("(c p) -> p c", p=128), [], [("cw", 3)], slow=True)
        nfb = sb("nfb", [128, D], F32)
        h.dma("sp", nfb[:], norm_final.partition_broadcast(128), [], ["nfb"])
        wup = sb("wup", [128, 8, F2], BF16)
        w_up_v = ffn_w_up.rearrange("(c p) n -> p c n", p=128)
        for kc in range(8):
            h.dma("pool", wup[:, kc, :], w_up_v[:, kc, :], [], [("wup", kc)])
        wdn = sb("wdn", [128, 22, D], BF16)
        w_dn_v = ffn_w_down.rearrange("(c p) n -> p c n", p=128)
        for kc in range(22):
            h.dma("pool", wdn[:, kc, :], w_dn_v[:, kc, :], [], [("wdn", kc)])
        xts = [sb("xt0", [128, D], F32), sb("xt1", [128, D], F32)]
        alloc_rms(T, sb)
        ub = [sb("ub0", [128, 4, 136], F32), sb("ub1", [128, 4, 136], F32)]
        ucar = sb("ucar", [128, 44, 2], F32)
        cin = sb("cin", [128, 44, 4, 2], F32)
        cout = sb("cout", [128, 44, 8], F32)
        cstg = [sb("cstg%d" % i, [8, 512], F32) for i in range(2)]
        csto = [sb("csto%d" % i, [8, 512], F32) for i in range(2)]
        fss = [sb("fss%d" % i, [128, 1], F32) for i in range(2)]
        frs = [sb("frs%d" % i, [128, 1], F32) for i in range(2)]
        cc = [sb("cc0", [128, 4, 128], F32), sb("cc1", [128, 4, 128], F32)]
        g1 = [sb("g10", [128, 2, 128], F32), sb("g11", [128, 2, 128], F32)]
        g2t = [sb("g20", [128, 2, 128], F32), sb("g21", [128, 2, 128], F32)]
        actTs = [sb("actT%d" % i, [128, 22, 128], BF16) for i in range(2)]
        x2s = [sb("x2%d" % i, [128, D], F32) for i in range(2)]
        yts = [sb("yt0", [128, D], F32)] * 2
        T["psT"] = psb("psT", [128, 8, 128], BF16)
        PS = [psb("ps%d" % i, [128, 512], F32) for i in range(7)]
        h.memset("pool", ucar[:], 0.0, ["ucar"])
        for b in BLKS:
            samp, nseq, L = geom(b)
            j = b - NPB
            xt, xtn = xts[b % 2], "xt%d" % (b % 2)
            actT, actk = actTs[b % 2], "actT%d" % (b % 2)
            x2, x2k = x2s[b % 2], "x2%d" % (b % 2)
            yt, ytk = yts[0], "yt0"
            h.dma("sp", xt[:], x1_s[b * 128:(b + 1) * 128, :], [], [xtn])
            hT, hTk = rms_to_fm(h, T, xt, xtn, 0, b % 2)
            W = nseq * (L + 2)
            if samp:
                stc_v = st_conv[4 * j:4 * j + 4].rearrange("q t f -> (q t) f")
                for g4 in range(11):
                    cg_, cgk = cstg[g4 % 2], "cstg%d" % (g4 % 2)
                    h.dma("sp", cg_[:], stc_v[:, g4 * 512:(g4 + 1) * 512], [], [cgk])
                    for mi in range(4):
                        m = 4 * g4 + mi
                        h.tr(PS[4][:, mi * 8:(mi + 1) * 8], cg_[0:8, mi * 128:(mi + 1) * 128], idf[0:8, 0:8],
                             [cgk, "cst"], ["ps4"])
                    h.cp("act", cin[:, 4 * g4:4 * g4 + 4, :, :].rearrange("p m q t -> p m (q t)"),
                         PS[4][:, 0:32].rearrange("p (m x) -> p m x", m=4), ["ps4"], [("cin", g4)])
            last_prompt = (b == NPB - 1)
            for gi in range(11):
                u, un = ub[gi % 2], "ub%d" % (gi % 2)
                uv = u[:, :, 0:W].rearrange("p m (s l) -> p m s l", s=nseq)
                pt, pn = PS[gi % 2], "ps%d" % (gi % 2)
                chunks = [2 * gi, 2 * gi + 1, 22 + 2 * gi, 23 + 2 * gi]
                for mi, m in enumerate(chunks):
                    for kc in range(8):
                        h.mm(pt[:, mi * 128:(mi + 1) * 128], wup[:, kc, m * 128:(m + 1) * 128], hT[:, kc, :],
                             kc == 0, kc == 7, [("wup", kc), hTk], [pn])
                h.cp("act", uv[:, :, :, 2:L + 2], pt[:].rearrange("p (m s l) -> p m s l", m=4, s=nseq),
                     [pn], [un])
                for half in range(2):
                    m0 = chunks[2 * half]
                    if samp:
                        h.cp("pool", uv[:, 2 * half:2 * half + 2, :, 0:2], cin[:, m0:m0 + 2, :, :],
                             [("cin", m0 // 4)], [un])
                    else:
                        h.cp("pool", uv[:, 2 * half:2 * half + 2, 0, 0:2], ucar[:, m0:m0 + 2, :],
                             [("ucar", gi)], [un])
                for half in range(2):
                    m0 = chunks[2 * half]
                    if samp:
                        h.cp("pool", cout[:, m0:m0 + 2, :].rearrange("p m (q t) -> p m q t", q=4),
                             uv[:, 2 * half:2 * half + 2, :, 8:10], [un], [("cout", gi)])
                    else:
                        h.cp("pool", ucar[:, m0:m0 + 2, :], uv[:, 2 * half:2 * half + 2, 0, 128:130],
                             [un], [("ucar", gi)])
                        if last_prompt:
                            h.cp("pool", cout[:, m0:m0 + 2, 0:2], uv[:, 2 * half:2 * half + 2, 0, 128:130],
                                 [un], [("cout", gi)])
                c_, cn = cc[gi % 2], "cc%d" % (gi % 2)
                for mi, m in enumerate(chunks):
                    eng = "dve"
                    c4 = c_[:, mi, :].rearrange("p (s l) -> p s l", s=nseq)
                    h.ts(eng, c4, uv[:, mi, :, 2:L + 2], cw[:, 2, m:m + 1], cw[:, 3, m:m + 1], ALU.mult, ALU.add,
                         [un, "cw"], [(cn, mi)])
                    h.stt(eng, c4, uv[:, mi, :, 1:L + 1], cw[:, 1, m:m + 1], c4, ALU.mult, ALU.add,
                          [un, "cw", (cn, mi)], [(cn, mi)])
                    h.stt(eng, c4, uv[:, mi, :, 0:L], cw[:, 0, m:m + 1], c4, ALU.mult, ALU.add,
                          [un, "cw", (cn, mi)], [(cn, mi)])
                ga, gan = g1[gi % 2], "g1%d" % (gi % 2)
                gb_, gbn = g2t[gi % 2], "g2%d" % (gi % 2)
                gate = c_[:, 2:4, :]
                val = c_[:, 0:2, :]
                h.act(ga[:], gate, AF.Square, [(cn, 2), (cn, 3)], [gan])
                h.ts("dve", ga[:], ga[:], 0.044715, 1.0, ALU.mult, ALU.add, [gan], [gan])
                h.tt("pool", ga[:], ga[:], gate, ALU.mult, [gan, (cn, 2), (cn, 3)], [gan])
                h.act(gb_[:], ga[:], AF.Sigmoid, [gan], [gbn], scale=GELU_S)
                h.tt("pool", gb_[:], gb_[:], gate, ALU.mult, [gbn, (cn, 2), (cn, 3)], [gbn])
                h.tt("dve", actT[:, 2 * gi:2 * gi + 2, :], gb_[:], val, ALU.mult, [gbn, (cn, 0), (cn, 1)],
                     [(actk, gi)])
            if samp or last_prompt:
                ncol = 8 if samp else 2
                for g4 in range(11):
                    for mi in range(4):
                        m = 4 * g4 + mi
                        h.tr(PS[4][0:ncol, mi * 128:(mi + 1) * 128], cout[:, m, 0:ncol], idf, ["cout", "cst"],
                             ["ps4"])
                    co_, cok = csto[g4 % 2], "csto%d" % (g4 % 2)
                    h.cp("act", co_[0:ncol, :], PS[4][0:ncol, :], ["ps4"], [cok])
                    if samp:
                        h.dma("sp", o_conv_s[4 * j:4 * j + 4].rearrange("q t f -> (q t) f")[:, g4 * 512:(g4 + 1) * 512],
                              co_[:], [cok], [], isout=True)
                    else:
                        h.dma("sp", o_conv_p[0][:, g4 * 512:(g4 + 1) * 512], co_[0:2, :], [cok], [], isout=True)
            for nh in range(2):
                pt, pn = PS[2 + nh], "ps%d" % (2 + nh)
                for kc in range(22):
                    h.mm(pt[:], actT[:, kc, :], wdn[:, kc, nh * 512:(nh + 1) * 512], kc == 0, kc == 21,
                         [(actk, kc // 2), ("wdn", kc)], [pn])
                h.tt("dve", x2[:, nh * 512:(nh + 1) * 512], pt[:], xt[:, nh * 512:(nh + 1) * 512], ALU.add,
                     [pn, xtn], [(x2k, nh)])
            ss2, rs2 = fss[b % 2], frs[b % 2]
            ssk, rsk = "fss%d" % (b % 2), "frs%d" % (b % 2)
            h.act(yt[:], x2[:], AF.Square, [x2k], [ytk, ssk], accum_out=ss2[:])
            h.ts("dve", rs2[:], ss2[:], 1.0 / D, 1e-6, ALU.mult, ALU.add, [ssk], [rsk])
            h.act(rs2[:], rs2[:], AF.Sqrt, [rsk], [rsk])
            h.recip(rs2[:], rs2[:], [rsk], [rsk])
            h.stt("dve", yt[:], x2[:], rs2[:, 0:1], nfb[:], ALU.mult, ALU.mult, [x2k, rsk, "nfb"], [ytk])
            if samp:
                for q in range(4):
                    h.dma("sp", ysm[4 * j + q], yt[32 * q:32 * q + 8, :], [ytk], [], isout=True)
            else:
                h.dma("sp", yp[b * 128:(b + 1) * 128, :], yt[:], [ytk], [], isout=True)
        P.emit()
    gs.close()
    return nc


_CACHE = {}


def kernel(**inputs):
    f32 = lambda a: np.ascontiguousarray(np.asarray(a), dtype=np.float32)
    if "nc" not in _CACHE:
        nc = bass.Bass("TRN2", target_bir_lowering=False)
        build(nc)
        _CACHE["nc"] = nc
    nc = _CACHE["nc"]
    cst = make_consts()
    shared = {
        "cst": cst,
        "norm_mix": f32(inputs["norm_mix"][0]), "w_in": f32(inputs["w_in"][0]),
        "mu_shift": f32(inputs["mu_shift"][0]), "rwkv_w0": f32(inputs["rwkv_w0"][0]),
        "rwkv_w2": f32(inputs["rwkv_w2"][0]), "rwkv_a0": f32(inputs["rwkv_a0"][0]),
        "rwkv_a2": f32(inputs["rwkv_a2"][0]), "rwkv_g2": f32(inputs["rwkv_g2"][0]),
        "rwkv_k_k": f32(inputs["rwkv_k_k"][0]), "rwkv_k_a": f32(inputs["rwkv_k_a"][0]),
        "rwkv_r_k": f32(inputs["rwkv_r_k"][0]).reshape(512), "rwkv_ln_w": f32(inputs["rwkv_ln_w"][0]),
        "rwkv_ln_b": f32(inputs["rwkv_ln_b"][0]), "gla_wg2": f32(inputs["gla_wg2"][0]),
        "gla_bg": f32(inputs["gla_bg"][0]), "gla_norm_w": f32(inputs["gla_norm_w"][0]),
        "w_out_a": f32(inputs["w_out_a"][0]), "w_out_b": f32(inputs["w_out_b"][0]),
        "w_o": f32(inputs["w_o"][0]), "norm_ffn": f32(inputs["norm_ffn"][0]),
        "ffn_w_up": f32(inputs["ffn_w_up"][0]), "ffn_conv_w": f32(inputs["ffn_conv_w"][0]),
        "ffn_conv_b": f32(inputs["ffn_conv_b"][0]), "ffn_w_down": f32(inputs["ffn_w_down"][0]),
        "norm_final": f32(inputs["norm_final"]),
    }
    in_maps = []
    for c in range(NCORES):
        m = dict(shared)
        sl = slice(16 * c, 16 * c + 16)
        m["xp"] = f32(inputs["x_prompt"][c])
        m["xs"] = f32(inputs["x_sample"][sl])
        m["st_shift"] = f32(inputs["state_rwkv_shift"][0, sl])
        m["st_wkv"] = f32(inputs["state_rwkv_wkv"][0, sl])
        m["st_gla"] = f32(inputs["state_gla"][0, sl])
        m["st_conv"] = f32(inputs["state_ffn_conv"][0, sl])
        in_maps.append(m)
    res = run_bass_kernel_spmd(nc, in_maps, core_ids=list(range(NCORES)))
    R = res.results
    cat = lambda k: np.concatenate([np.asarray(r[k]) for r in R], axis=0)
    y_p = np.stack([np.asarray(r["yp"]) for r in R], axis=0)
    y_s = cat("ys")
    outs = (
        y_p, y_s,
        cat("o_shift_p")[None], cat("o_wkv_p")[None], cat("o_gla_p")[None], cat("o_conv_p")[None],
        cat("o_shift_s")[None], cat("o_wkv_s")[None], cat("o_gla_s")[None], cat("o_conv_s")[None],
    )
    return tuple(np.ascontiguousarray(o, dtype=np.float32) for o in outs)
```

```python
import contextlib
import numpy as np
import concourse.bass as bass
import concourse.mybir as mybir
from concourse.bass_utils import run_bass_kernel_spmd

F32 = mybir.dt.float32
BF16 = mybir.dt.bfloat16
AF = mybir.ActivationFunctionType
ALU = mybir.AluOpType
AX = mybir.AxisListType

NCORES = 8
D = 1024
NPB = 16
NSB = 4
NBLK = NPB + NSB
SHIFT = 1792
INC = 5392
FH = 2816
F2 = 5632
GLA0 = 1792
GATE0 = 3344
C0 = -0.6065306597126334
GELU_S = 1.5957691216057308

ENGS = ("pe", "act", "dve", "pool", "sp")
MAXOPS = None
SCHED = True
VERBOSE = False
PROGS = []
WINDOW = 300
LINES = []
TAGS = []


class Op:
    __slots__ = ("eng", "fn", "reads", "writes", "dma", "deps", "sig", "idx",
                 "dsem", "dval", "dprev", "isout", "mm", "odeps", "cost", "start", "fin")

    def __init__(self, eng, fn, reads, writes, dma, isout, mm, cost=300.0):
        self.odeps = []
        self.cost = cost
        self.eng = eng
        self.fn = fn
        self.reads = reads
        self.writes = writes
        self.dma = dma
        self.deps = []
        self.sig = None
        self.dsem = None
        self.dval = None
        self.dprev = None
        self.isout = isout
        self.mm = mm


def _norm(k):
    return k if isinstance(k, tuple) else (k, None)


class Prog:
    def __init__(self, nc, semstack, tag, n_dma_sems=6):
        self.nc = nc
        self.ops = []
        self.n_dma_sems = n_dma_sems
        self.st = {}
        self.semstack = semstack
        self.tag = tag

    @staticmethod
    def _conf(a, b):
        return a is None or b is None or a == b

    def add(self, eng, fn, reads=(), writes=(), dma=False, isout=False, mm=False, cost=300.0):
        op = Op(eng, fn, [_norm(k) for k in reads], [_norm(k) for k in writes], dma, isout, mm, cost)
        odeps = {}
        op.idx = len(self.ops)
        if MAXOPS is not None:
            import sys as _s
            f = _s._getframe(1)
            while f is not None and f.f_code.co_name != "build":
                f = f.f_back
            LINES.append(f.f_lineno if f is not None else -1)
            TAGS.append(f.f_locals.get("b", -1) if f is not None else -1)
        deps = {}
        for (name, sub) in op.reads:
            s = self.st.setdefault(name, {"w": {}, "r": {}})
            for ws, wop in s["w"].items():
                if self._conf(ws, sub):
                    deps[wop.idx] = wop
        for (name, sub) in op.writes:
            s = self.st.setdefault(name, {"w": {}, "r": {}})
            for ws, wop in s["w"].items():
                if self._conf(ws, sub):
                    if not (op.mm and wop.mm):
                        deps[wop.idx] = wop
                    else:
                        odeps[wop.idx] = wop
            for rs, rops in s["r"].items():
                if self._conf(rs, sub):
                    for rop in rops:
                        deps[rop.idx] = rop
        for (name, sub) in op.reads:
            self.st[name]["r"].setdefault(sub, []).append(op)
        for (name, sub) in op.writes:
            s = self.st[name]
            if sub is None:
                s["w"] = {None: op}
                s["r"] = {}
            else:
                s["w"][sub] = op
                s["r"][sub] = []
        deps.pop(op.idx, None)
        op.deps = list(deps.values())
        op.odeps = [o for k, o in odeps.items() if k not in deps]
        self.ops.append(op)
        return op

    def schedule(self, window=None):
        window = window or WINDOW
        ops = self.ops
        n = len(ops)
        ndep = [0] * n
        users = [[] for _ in range(n)]
        for op in ops:
            ds = {d.idx for d in op.deps} | {d.idx for d in op.odeps}
            ndep[op.idx] = len(ds)
            for d in ds:
                users[d].append(op.idx)
        ready_t = [0.0] * n
        per = {e: [op.idx for op in ops if op.eng == e] for e in ENGS}
        head = {e: 0 for e in ENGS}
        done = [False] * n
        t_e = {e: 0.0 for e in ENGS}
        order = []
        remaining = n
        LAT = 120.0
        while remaining:
            best = None
            for e in ENGS:
                lst = per[e]
                hp = head[e]
                while hp < len(lst) and done[lst[hp]]:
                    hp += 1
                head[e] = hp
                if hp >= len(lst):
                    continue
                cnt = 0
                k = hp
                cand = None
                rdy = []
                while k < len(lst) and cnt < window:
                    i = lst[k]
                    if not done[i]:
                        cnt += 1
                        if ndep[i] == 0:
                            st = max(t_e[e], ready_t[i])
                            key = (st + 2.0 * (cnt - 1), i)
                            rdy.append((st, i))
                            if cand is None or key < cand[0]:
                                cand = (key, i, st)
                    k += 1
                if cand is not None:
                    bst = cand[2]
                    for (st, i) in rdy:
                        if i != cand[1] and st + (60.0 if ops[i].dma else ops[i].cost) <= bst:
                            cand = ((st, i), i, st)
                            break
                    if best is None or cand[0] < best[0]:
                        best = (cand[0], cand[1], cand[2], e)
            assert best is not None, "scheduler deadlock"
            _, i, st, e = best
            op = ops[i]
            op.start = st
            if op.dma:
                t_e[e] = st + 60.0
            else:
                t_e[e] = st + op.cost
            op.fin = st + op.cost
            done[i] = True
            remaining -= 1
            order.append(op)
            for u in users[i]:
                ndep[u] -= 1
                if ready_t[u] < op.fin + LAT:
                    ready_t[u] = op.fin + LAT
        self.ops = order
        self.est = max(op.fin for op in order) if order else 0.0
        if VERBOSE:
            PROGS.append(self)
            busy = {e: sum(o.cost for o in order if o.eng == e and not o.dma) for e in ENGS}
            print("[sched %s] n=%d est=%.1f us busy(us): %s" % (
                self.tag, n, self.est / 1e3, " ".join("%s=%.0f" % (e, busy[e] / 1e3) for e in ENGS)), flush=True)

    def emit(self):
        nc = self.nc
        if MAXOPS is not None:
            self.ops = self.ops[:MAXOPS]
        if SCHED:
            self.schedule()
        ops = self.ops
        needed = set()
        for op in ops:
            for d in op.deps:
                needed.add(d.idx)
        cnt = {e: 0 for e in ENGS}
        for op in ops:
            if not op.dma and op.idx in needed:
                cnt[op.eng] += 1
                op.sig = cnt[op.eng]
        dcount = {e: 0 for e in ENGS}
        last_on_slot = {}
        for op in ops:
            if not op.dma:
                continue
            j = dcount[op.eng]
            dcount[op.eng] += 1
            slot = j % self.n_dma_sems
            op.dsem = (op.eng, slot)
            op.dval = 16 * (j // self.n_dma_sems + 1)
            op.dprev = last_on_slot.get(op.dsem)
            last_on_slot[op.dsem] = op
        out_ops = [op for op in ops if op.dma and op.isout]
        per_eng = {e: [op for op in ops if op.eng == e] for e in ENGS}
        es = self.semstack
        csem = {e: es.enter_context(nc.semaphore("cs%s_%s" % (self.tag, e)))
                for e in ENGS if e != "sp"}
        dsem = {}
        for e in ENGS:
            for s in range(min(self.n_dma_sems, dcount[e])):
                dsem[(e, s)] = es.enter_context(nc.semaphore("ds%s_%s%d" % (self.tag, e, s)))

        def run_engine(e, eng):
            known = {}

            def wait(key, sem, val):
                if known.get(key, 0) >= val:
                    return
                known[key] = val
                eng.wait_ge(sem, val)

            for op in per_eng[e]:
                for d in op.deps:
                    if d.dma:
                        wait(d.dsem, dsem[d.dsem], d.dval)
                    else:
                        wait(d.eng, csem[d.eng], d.sig)
                if op.dma and op.dprev is not None:
                    wait(op.dsem, dsem[op.dsem], op.dprev.dval)
                ins = op.fn(eng)
                if op.dma:
                    ins.then_inc(dsem[op.dsem], 16)
                elif op.sig is not None:
                    ins.then_inc(csem[e], 1)
            if e == "sp":
                for op in out_ops:
                    wait(op.dsem, dsem[op.dsem], op.dval)
                for key, op in last_on_slot.items():
                    wait(op.dsem, dsem[op.dsem], op.dval)

        with nc.Block() as block:
            @block.sync
            def _(eng):
                run_engine("sp", eng)

            @block.tensor
            def _(eng):
                run_engine("pe", eng)

            @block.scalar
            def _(eng):
                run_engine("act", eng)

            @block.vector
            def _(eng):
                run_engine("dve", eng)

            @block.gpsimd
            def _(eng):
                run_engine("pool", eng)


def _fsz(ap):
    n = 1
    for d in ap.shape[1:]:
        n *= int(d)
    return n


class H:
    def __init__(self, P):
        self.P = P

    def act(self, out, in_, func, r, w, **kw):
        self.P.add("act", lambda e: e.activation(out=out, in_=in_, func=func, **kw), r, w,
                   cost=220.0 + 1.05 * _fsz(out))

    def tt(self, eng, out, in0, in1, op, r, w):
        self.P.add(eng, lambda e: e.tensor_tensor(out=out, in0=in0, in1=in1, op=op), r, w,
                   cost=(100.0 + 1.05 * _fsz(out)) if eng == "dve" else (160.0 + 2.1 * _fsz(out)))

    def ts(self, eng, out, in0, s1, s2, op0, op1, r, w):
        if op1 is None:
            self.P.add(eng, lambda e: e.tensor_scalar(out=out, in0=in0, scalar1=s1, scalar2=None, op0=op0), r, w,
                       cost=100.0 + 1.05 * _fsz(out))
        else:
            self.P.add(eng, lambda e: e.tensor_scalar(out=out, in0=in0, scalar1=s1, scalar2=s2, op0=op0, op1=op1), r, w,
                       cost=100.0 + 1.05 * _fsz(out))

    def stt(self, eng, out, in0, scalar, in1, op0, op1, r, w):
        self.P.add(eng, lambda e: e.scalar_tensor_tensor(out=out, in0=in0, scalar=scalar, in1=in1, op0=op0, op1=op1), r, w,
                   cost=100.0 + 1.05 * _fsz(out))

    def cp(self, eng, out, in_, r, w):
        if eng == "act":
            self.P.add("act", lambda e: e.activation(out=out, in_=in_, func=AF.Copy), r, w,
                       cost=220.0 + 1.05 * _fsz(out))
        else:
            self.P.add(eng, lambda e: e.tensor_copy(out=out, in_=in_), r, w,
                       cost=(100.0 + 1.05 * _fsz(out)) if eng == "dve" else (160.0 + 2.1 * _fsz(out)))

    def memset(self, eng, ap, val, w):
        self.P.add(eng, lambda e: e.memset(ap, val), [], w, cost=160.0 + 1.0 * _fsz(ap))

    def recip(self, out, in_, r, w):
        self.P.add("dve", lambda e: e.reciprocal(out=out, in_=in_), r, w, cost=100.0 + 1.05 * _fsz(out))

    def scan(self, out, d0, d1, r, w):
        self.P.add("dve", lambda e: e.tensor_tensor_scan(out=out, data0=d0, data1=d1, initial=0.0,
                                                         op0=ALU.mult, op1=ALU.add), r, w,
                   cost=100.0 + 2.1 * _fsz(out))

    def rsum(self, out, in_, r, w):
        self.P.add("dve", lambda e: e.tensor_reduce(out=out, in_=in_, axis=AX.X, op=ALU.add), r, w,
                   cost=100.0 + 1.05 * _fsz(in_))

    def mm(self, out, lhsT, rhs, start, stop, r, w, tp=None):
        c = max(64.0, float(_fsz(rhs))) / 2.0 + 16.0
        if lhsT.dtype == F32:
            c *= 4.0
        if tp is None:
            self.P.add("pe", lambda e: e.matmul(out, lhsT=lhsT, rhs=rhs, start=start, stop=stop), r, w, mm=True,
                       cost=c)
        else:
            self.P.add("pe", lambda e: e.matmul(out, lhsT=lhsT, rhs=rhs, start=start, stop=stop,
                                                tile_position=tp), r, w, mm=True, cost=c)

    def tr(self, out, in_, ident, r, w):
        self.P.add("pe", lambda e: e.transpose(out=out, in_=in_, identity=ident), r, w, mm=True, cost=110.0)

    def dma(self, q, out, in_, r, w, isout=False, slow=False):
        nbytes = 1
        for d in out.shape:
            nbytes *= int(d)
        c = 2500.0 + 4.0 * nbytes / 150.0
        if slow:
            self.P.add(q, lambda e: e.dma_start(out=out, in_=in_, allow_slow_non_contiguous=True), r, w,
                       dma=True, isout=isout, cost=c)
        else:
            self.P.add(q, lambda e: e.dma_start(out=out, in_=in_), r, w, dma=True, isout=isout, cost=c)


def bc(ap, shape):
    a = ap
    while len(a.shape) < len(shape):
        a = a.unsqueeze(len(a.shape))
    return a.to_broadcast(list(shape))


C_ID = 0
C_BO = 128
C_RST = 256
C_VS = 384
C_ONE = 512
C_M4 = 640
C_SL = 1152
C_IU = 1280
NCST = 1408


def make_consts():
    c = np.zeros((128, NCST), np.float32)
    i = np.arange(128)
    same = (i[:, None] // 32) == (i[None, :] // 32)
    su = (same & (i[:, None] < i[None, :])).astype(np.float32)
    iu = (same & (i[:, None] <= i[None, :])).astype(np.float32)
    sl = (same & (i[:, None] > i[None, :])).astype(np.float32)
    c[:, C_ID:C_ID + 128] = np.eye(128, dtype=np.float32)
    c[:, C_BO:C_BO + 128] = ((i[:, None] // 64) == (i[None, :] // 64)).astype(np.float32)
    c[:, C_RST:C_RST + 128] = (i[None, :] % 32 != 0).astype(np.float32)
    c[:, C_VS:C_VS + 128] = (i[None, :] % 32 < 8).astype(np.float32)
    c[:, C_ONE:C_ONE + 128] = 1.0
    c[:, C_M4:C_M4 + 512] = np.concatenate([su, iu, su, iu], axis=1)
    c[:, C_SL:C_SL + 128] = sl
    c[:, C_IU:C_IU + 128] = iu
    return c


PR_G = 0
PR_MU = 8
PR_W0 = 22
PR_A0 = 26
PR_KK = 30
PR_KA = 34
PR_RK = 38
PR_LW = 42
PR_LB = 46
PR_BG = 50
PR_NW = 52
PR_OMKA = 53
NPRM = 64


def build(nc, dbg=None, phases="abc", blocks=None):
    BLKS = list(range(NBLK)) if blocks is None else list(blocks)
    gs = contextlib.ExitStack()

    def din(name, shape, dt=F32):
        return nc.dram_tensor(name, list(shape), dt, kind="ExternalInput").ap()

    def dout(name, shape):
        return nc.dram_tensor(name, list(shape), F32, kind="ExternalOutput").ap()

    def dscr(name, shape, dt):
        return nc.dram_tensor(name, list(shape), dt, kind="Internal").ap()

    xp = din("xp", [2048, D])
    xsm = din("xs", [16, 8, D])
    st_shift = din("st_shift", [16, SHIFT])
    st_wkv = din("st_wkv", [16, 8, 64, 64])
    st_gla = din("st_gla", [16, 4, 64, 128])
    st_conv = din("st_conv", [16, 2, F2])
    cst_d = din("cst", [128, NCST])
    norm_mix = din("norm_mix", [D])
    w_in = din("w_in", [D, INC])
    mu_shift = din("mu_shift", [SHIFT])
    rwkv_w0 = din("rwkv_w0", [512])
    rwkv_w2 = din("rwkv_w2", [64, 512])
    rwkv_a0 = din("rwkv_a0", [512])
    rwkv_a2 = din("rwkv_a2", [64, 512])
    rwkv_g2 = din("rwkv_g2", [128, 512])
    rwkv_k_k = din("rwkv_k_k", [512])
    rwkv_k_a = din("rwkv_k_a", [512])
    rwkv_r_k = din("rwkv_r_k", [512])
    rwkv_ln_w = din("rwkv_ln_w", [512])
    rwkv_ln_b = din("rwkv_ln_b", [512])
    gla_wg2 = din("gla_wg2", [16, 256])
    gla_bg = din("gla_bg", [256])
    gla_norm_w = din("gla_norm_w", [128])
    w_out_a = din("w_out_a", [512, D])
    w_out_b = din("w_out_b", [512, D])
    w_o = din("w_o", [D, D])
    norm_ffn = din("norm_ffn", [D])
    ffn_w_up = din("ffn_w_up", [D, F2])
    ffn_conv_w = din("ffn_conv_w", [3, F2])
    ffn_conv_b = din("ffn_conv_b", [F2])
    ffn_w_down = din("ffn_w_down", [FH, D])
    norm_final = din("norm_final", [D])

    yp = dout("yp", [2048, D])
    ysm = dout("ys", [16, 8, D])
    o_shift_p = dout("o_shift_p", [1, SHIFT])
    o_wkv_p = dout("o_wkv_p", [1, 8, 64, 64])
    o_gla_p = dout("o_gla_p", [1, 4, 64, 128])
    o_conv_p = dout("o_conv_p", [1, 2, F2])
    o_shift_s = dout("o_shift_s", [16, SHIFT])
    o_wkv_s = dout("o_wkv_s", [16, 8, 64, 64])
    o_gla_s = dout("o_gla_s", [16, 4, 64, 128])
    o_conv_s = dout("o_conv_s", [16, 2, F2])

    og_s = dscr("og_s", [NBLK, 128, 512], BF16)
    ob_s = dscr("ob_s", [NBLK, 128, 512], BF16)
    x1_s = dscr("x1_s", [NBLK * 128, D], F32)

    dbg_out = None
    if dbg is not None:
        dbg_out = dout("dbg", dbg["shape"])

    def geom(b):
        return (b >= NPB, 4, 32) if b >= NPB else (False, 1, 128)

    def load_consts(h, sb, q="sp"):
        cst = sb("cst", [128, NCST], F32)
        h.dma(q, cst[:], cst_d, [], ["cst"])
        idb = sb("idb", [128, 128], BF16)
        h.cp("dve", idb[:], cst[:, C_ID:C_ID + 128], ["cst"], ["idb"])
        return cst, idb

    def load_x_block(h, b, xt, xtn):
        samp, nseq, L = geom(b)
        if not samp:
            h.dma("sp", xt[:], xp[b * 128:(b + 1) * 128, :], [], [xtn])
        else:
            j = b - NPB
            h.memset("pool", xt[:], 0.0, [xtn])
            for q in range(4):
                h.dma("sp", xt[32 * q:32 * q + 8, :], xsm[4 * j + q], [], [(xtn, q)])

    def rms_to_fm(h, T, xt, xtn, gcol, par=0):
        xp_ = par if len(T["xn"]) > 1 else 0
        xn, ss, rstd, hT = T["xn"][xp_], T["ss"][par], T["rstd"][par], T["hT"][par]
        xnk, ssk, rsk, hTk = "xn%d" % xp_, "ss%d" % par, "rstd%d" % par, "hT%d" % par
        psT, idb, prm = T["psT"], T["idb"], T["prm"]
        h.act(xn[:], xt[:], AF.Square, [xtn], [xnk, ssk], accum_out=ss[:])
        h.ts("dve", rstd[:], ss[:], 1.0 / D, 1e-6, ALU.mult, ALU.add, [ssk], [rsk])
        h.act(rstd[:], rstd[:], AF.Sqrt, [rsk], [rsk])
        h.recip(rstd[:], rstd[:], [rsk], [rsk])
        h.act(xn[:], xt[:], AF.Copy, [xtn, rsk], [xnk], scale=rstd[:, 0:1])
        for c in range(8):
            h.tr(psT[:, c, :], xn[:, c * 128:(c + 1) * 128], idb[:], [xnk, "idb"], [("psT", c)])
        h.tt("dve", hT[:], psT[:], bc(prm[:, gcol:gcol + 8], [128, 8, 128]), ALU.mult,
             ["psT", "prm"], [hTk])
        return hT, hTk

    def alloc_rms(T, sb, nxn=2):
        T["xn"] = [sb("xn%d" % i, [128, D], BF16) for i in range(nxn)]
        T["ss"] = [sb("ss%d" % i, [128, 1], F32) for i in range(2)]
        T["rstd"] = [sb("rstd%d" % i, [128, 1], F32) for i in range(2)]
        T["hT"] = [sb("hT%d" % i, [128, 8, 128], BF16) for i in range(2)]

    def load_param_cols(h, prm, col, src, n):
        h.dma("sp", prm[:, col:col + n], src.rearrange("(c p) -> p c", p=128), [], [("prm", col)], slow=True)

    with contextlib.ExitStack() as ph:
      if "a" in phases:
        P = Prog(nc, gs, "a")
        h = H(P)

        def sb(name, shape, dt):
            return ph.enter_context(nc.sbuf_tensor("a_" + name, list(shape), dt))

        def psb(name, shape, dt):
            return ph.enter_context(nc.psum_tensor("a_" + name, list(shape), dt))

        T = {}
        cst, idb = load_consts(h, sb)
        T["idb"] = idb
        prm = sb("prm", [128, NPRM], F32)
        T["prm"] = prm
        load_param_cols(h, prm, PR_G, norm_mix, 8)
        load_param_cols(h, prm, PR_MU, mu_shift, 14)
        load_param_cols(h, prm, PR_W0, rwkv_w0, 4)
        load_param_cols(h, prm, PR_A0, rwkv_a0, 4)
        load_param_cols(h, prm, PR_KK, rwkv_k_k, 4)
        load_param_cols(h, prm, PR_KA, rwkv_k_a, 4)
        load_param_cols(h, prm, PR_RK, rwkv_r_k, 4)
        load_param_cols(h, prm, PR_LW, rwkv_ln_w, 4)
        load_param_cols(h, prm, PR_LB, rwkv_ln_b, 4)
        load_param_cols(h, prm, PR_BG, gla_bg, 2)
        load_param_cols(h, prm, PR_NW, gla_norm_w, 1)
        h.ts("dve", prm[:, PR_BG:PR_BG + 2], prm[:, PR_BG:PR_BG + 2], -1.0, None, ALU.mult, None,
             [("prm", PR_BG)], [("prm", PR_BG)])
        h.ts("dve", prm[:, PR_OMKA:PR_OMKA + 4], prm[:, PR_KA:PR_KA + 4], -1.0, 1.0, ALU.mult, ALU.add,
             [("prm", PR_KA)], [("prm", PR_OMKA)])

        NA1 = GATE0
        win = sb("win", [128, 8, NA1], BF16)
        w_in_v = w_in.rearrange("(c p) n -> p c n", p=128)
        for kc in range(8):
            h.dma("pool", win[:, kc, :], w_in_v[:, kc, 0:NA1], [], [("win", kc)])
        w2a2 = sb("w2a2", [128, 512], BF16)
        h.dma("pool", w2a2[0:64, :], rwkv_w2, [], [("w2a2", 0)])
        h.dma("pool", w2a2[64:128, :], rwkv_a2, [], [("w2a2", 1)])
        g2 = sb("g2", [128, 512], BF16)
        h.dma("pool", g2[:], rwkv_g2, [], ["g2"])
        wg2 = sb("wg2", [16, 256], BF16)
        h.dma("pool", wg2[:], gla_wg2, [], ["wg2"])

        xt = sb("xt", [128, D], F32)
        alloc_rms(T, sb, nxn=1)
        prw = sb("prw", [128, 14, 132], F32)
        lastc = sb("lastc", [128, 14, 1], F32)
        xs = sb("xs", [128, 14, 128], F32)
        Fm = [sb("F%d" % i, [128, 4, 128], F32) for i in range(10)]
        gTs = [sb("gT%d" % i, [128, 4, 128], F32) for i in range(2)]
        bons = [sb("bon%d" % i, [128, 4, 128], F32) for i in range(2)]
        ARs = [sb("AR%d" % i, [128, 4, 2, 128], BF16) for i in range(2)]
        Hm = [sb("Hb%d" % i, [128, 4, 128], BF16) for i in range(8)]
        tok = [sb("tok%d" % i, [128, 512], BF16) for i in range(4)]
        SCH = sb("SCH", [128, 8, 4, 128], BF16)
        INV = [sb("INV%d" % i, [128, 8, 128], BF16) for i in range(8)]
        lora_in = sb("lora_in", [128, 128], BF16)
        slg = sb("slg", [128, 128], BF16)
        Zbf = sb("Zbf", [128, 512], BF16)
        Yf = sb("Yf", [128, 512], F32)
        WTbf = sb("WTbf", [128, 4, 128], BF16)
        Ubf = sb("Ubf", [128, 512], BF16)
        Pst = sb("Pst", [128, 4, 64], F32)
        Pbf = sb("Pbf", [128, 4, 64], BF16)
        o1 = sb("o1", [128, 512], F32)
        oo = sb("oo", [128, 512], F32)
        stat = sb("stat", [128, 64], F32)
        ogbf = sb("ogbf", [128, 4, 128], BF16)
        obbf = sb("obbf", [128, 4, 128], BF16)
        Sin = sb("Sin", [64, 8, 64], F32)
        shs = [sb("shs%d" % i, [4, 512], F32) for i in range(2)]
        shf = sb("shf", [128, 14, 4], F32)
        sho = [sb("sho%d" % i, [4, 512], F32) for i in range(2)]
        lga = sb("lga", [16, 128], BF16)
        Sg = sb("Sg", [128, 2, 128], F32)
        Sgbf = sb("Sgbf", [128, 4, 2, 128], BF16)
        SgIn = sb("SgIn", [128, 4, 2, 128], F32)
        xg = sb("xg", [128, 12, 128], F32)
        Gf = [sb("G%d" % i, [128, 2, 128], F32) for i in range(6)]
        silu = sb("Gs", [128, 4, 128], F32)
        qdT, kiT, keT = [sb("Gh%d" % i, [128, 2, 128], BF16) for i in range(3)]
        vgbf = sb("Gv", [128, 4, 128], BF16)
        GS = sb("Ggs", [128, 4, 128], BF16)
        Vgtok = sb("Gt0", [128, 512], BF16)
        Ketok = sb("Gt1", [128, 256], BF16)
        o1g = sb("o1g", [128, 512], F32)
        oog = sb("oog", [128, 512], F32)
        statg = sb("statg", [128, 16], F32)

        T["psT"] = psb("psT", [128, 8, 128], BF16)
        psT = T["psT"]
        PS = [psb("ps%d" % i, [128, 512], F32) for i in range(7)]

        if VERBOSE:
            print("[A1] sbuf remaining after alloc:", nc.sbuf_bytes_remaining, flush=True)
        idf = cst[:, C_ID:C_ID + 128]
        bones = cst[:, C_BO:C_BO + 128]
        rstm = cst[:, C_RST:C_RST + 128]
        m4 = cst[:, C_M4:C_M4 + 512]
        msl = cst[:, C_SL:C_SL + 128]
        miu = cst[:, C_IU:C_IU + 128]

        h.memset("pool", Pst[:], 0.0, ["Pst"])
        h.memset("pool", Pbf[:], 0.0, ["Pbf"])
        h.memset("pool", Sg[:], 0.0, ["Sg"])
        h.memset("pool", lastc[:], 0.0, ["lastc"])

        def v4(ap, nseq, L):
            return ap.rearrange("p m (s l) -> p m s l", s=nseq)

        for b in BLKS:
            samp, nseq, L = geom(b)
            j = b - NPB
            valid = cst[:, (C_VS if samp else C_ONE):(C_VS if samp else C_ONE) + 128]
            load_x_block(h, b, xt, "xt")
            hT, hTk = rms_to_fm(h, T, xt, "xt", PR_G, b % 2)

            W = nseq * (L + 1)
            pv = prw[:, :, 0:W].rearrange("p m (s l) -> p m s l", s=nseq)
            for gi in range(4):
                ms = list(range(4 * gi, min(4 * gi + 4, 14)))
                pt = PS[5 + gi % 2]
                pn = "ps%d" % (5 + gi % 2)
                for mi, m in enumerate(ms):
                    for kc in range(8):
                        h.mm(pt[:, mi * 128:(mi + 1) * 128], win[:, kc, m * 128:(m + 1) * 128], hT[:, kc, :],
                             kc == 0, kc == 7, [("win", kc), hTk], [pn])
                nm = len(ms)
                h.cp("act", pv[:, ms[0]:ms[0] + nm, :, 1:L + 1],
                     pt[:, 0:nm * 128].rearrange("p (m s l) -> p m s l", m=nm, s=nseq),
                     [pn], ["prw"])
            gcols = [GLA0 + 128 * i for i in range(8)] + [GLA0 + 1040 + 128 * i for i in range(4)]
            for gi in range(3):
                pt, pn = PS[5 + gi % 2], "ps%d" % (5 + gi % 2)
                for mi in range(4):
                    c0 = gcols[4 * gi + mi]
                    for kc in range(8):
                        h.mm(pt[:, mi * 128:(mi + 1) * 128], win[:, kc, c0:c0 + 128], hT[:, kc, :],
                             kc == 0, kc == 7, [("win", kc), hTk], [pn])
                h.cp("act", xg[:, 4 * gi:4 * gi + 4, :], pt[:].rearrange("p (c l) -> p c l", c=4), [pn], [("xg", gi)])
            for kc in range(8):
                h.mm(PS[6][0:16, 0:128], win[:, kc, GLA0 + 1024:GLA0 + 1040], hT[:, kc, :], kc == 0, kc == 7,
                     [("win", kc), hTk], ["ps6"])
            h.cp("act", lga[:], PS[6][0:16, 0:128], ["ps6"], ["lga"])
            if not samp:
                h.cp("pool", pv[:, :, 0, 0:1], lastc[:], ["lastc"], ["prw"])
            else:
                for g4 in range(4):
                    ms = list(range(4 * g4, min(4 * g4 + 4, 14)))
                    nm = len(ms)
                    sh_, shk = shs[g4 % 2], "shs%d" % (g4 % 2)
                    h.dma("sp", sh_[:, 0:nm * 128], st_shift[4 * j:4 * j + 4, ms[0] * 128:(ms[0] + nm) * 128], [], [shk])
                    for mi, m in enumerate(ms):
                        h.tr(PS[5][:, mi * 4:(mi + 1) * 4], sh_[0:4, mi * 128:(mi + 1) * 128], idf[0:4, 0:4],
                             [shk, "cst"], ["ps5"])
                    h.cp("act", pv[:, ms[0]:ms[0] + nm, :, 0],
                         PS[5][:, 0:nm * 4].rearrange("p (m s) -> p m s", m=nm), ["ps5"], ["prw"])
            last_prompt = (b == NPB - 1)
            if samp or last_prompt:
                ncol = 4 if samp else 1
                if samp:
                    h.cp("pool", shf[:, :, 0:4], pv[:, :, :, 8], ["prw"], ["shf"])
                else:
                    h.cp("pool", shf[:, :, 0:1], pv[:, :, 0, 128:129], ["prw"], ["shf"])
                for g4 in range(4):
                    ms = list(range(4 * g4, min(4 * g4 + 4, 14)))
                    for mi, m in enumerate(ms):
                        h.tr(PS[6][0:ncol, mi * 128:(mi + 1) * 128], shf[:, m, 0:ncol], idf,
                             ["shf", "cst"], ["ps6"])
                    nm = len(ms)
                    so_, sok = sho[g4 % 2], "sho%d" % (g4 % 2)
                    h.cp("act", so_[0:ncol, 0:nm * 128], PS[6][0:ncol, 0:nm * 128], ["ps6"], [sok])
                    if samp:
                        h.dma("sp", o_shift_s[4 * j:4 * j + 4, ms[0] * 128:(ms[0] + nm) * 128], so_[0:4, 0:nm * 128],
                              [sok], [], isout=True)
                    else:
                        h.dma("sp", o_shift_p[0:1, ms[0] * 128:(ms[0] + nm) * 128], so_[0:1, 0:nm * 128],
                              [sok], [], isout=True)
            if not samp:
                h.cp("pool", lastc[:], pv[:, :, 0, 128:129], ["prw"], ["lastc"])
            xs4 = xs[:].rearrange("p m (s l) -> p m s l", s=nseq)
            cur = pv[:, :, :, 1:L + 1]
            prv = pv[:, :, :, 0:L]
            h.tt("dve", xs4[:, 0:9], prv[:, 0:9], cur[:, 0:9], ALU.subtract, ["prw"], [("xs", 0)])
            h.tt("pool", xs4[:, 9:14], prv[:, 9:14], cur[:, 9:14], ALU.subtract, ["prw"], [("xs", 1)])
            for m in range(14):
                h.stt("dve", xs4[:, m], xs4[:, m], prm[:, PR_MU + m:PR_MU + m + 1], cur[:, m], ALU.mult, ALU.add,
                      [("xs", 0 if m < 9 else 1), "prm", "prw"], [("xs", 2 + m)])
            rT = xs[:, 0:4, :]
            kT = xs[:, 4:8, :]
            vT = xs[:, 8:12, :]

            sw, aa, cum, E, Einv, Eprev, Eend, kk, kh, tmp = Fm
            n = lambda i: "F%d" % i
            N_SW, N_AA, N_CUM, N_E, N_EINV, N_EPREV, N_EEND, N_KK, N_KH, N_TMP = [n(i) for i in range(10)]
            gT, N_GT = gTs[b % 2], "gT%d" % (b % 2)
            bon, N_BON = bons[b % 2], "bon%d" % (b % 2)
            AR, N_AR = ARs[b % 2], "AR%d" % (b % 2)
            bT, kTb, BpT, KpT, vbf = Hm[0], Hm[1], Hm[2], Hm[3], Hm[4]
            h.act(lora_in[0:64, :], xs[0:64, 12, :], AF.Tanh, ["xs"], [("lora_in", 0)])
            h.cp("act", lora_in[64:128, :], xs[64:128, 12, :], ["xs"], [("lora_in", 1)])
            h.act(slg[:], xs[:, 13, :], AF.Sigmoid, ["xs"], ["slg"])
            for hg in range(4):
                h.mm(PS[5][:, hg * 128:(hg + 1) * 128], w2a2[0:64, hg * 128:(hg + 1) * 128], lora_in[0:64, :],
                     True, True, [("w2a2", 0), ("lora_in", 0)], ["ps5"])
            for hg in range(4):
                h.mm(PS[6][:, hg * 128:(hg + 1) * 128], w2a2[64:128, hg * 128:(hg + 1) * 128], lora_in[64:128, :],
                     True, True, [("w2a2", 1), ("lora_in", 1)], ["ps6"])
            for hg in range(4):
                h.act(sw[:, hg, :], PS[5][:, hg * 128:(hg + 1) * 128], AF.Sigmoid, ["ps5", "prm"], [(N_SW, hg)],
                      bias=prm[:, PR_W0 + hg:PR_W0 + hg + 1])
                h.act(aa[:, hg, :], PS[6][:, hg * 128:(hg + 1) * 128], AF.Sigmoid, ["ps6", "prm"], [(N_AA, hg)],
                      bias=prm[:, PR_A0 + hg:PR_A0 + hg + 1])
            for hg in range(4):
                h.mm(PS[5][:, hg * 128:(hg + 1) * 128], g2[:, hg * 128:(hg + 1) * 128], slg[:],
                     True, True, ["g2", "slg"], ["ps5"])
            h.cp("act", gT[:], PS[5][:].rearrange("p (c l) -> p c l", c=4), ["ps5"], [N_GT])
            h.stt("dve", sw[:], sw[:], C0, valid.unsqueeze(1).to_broadcast([128, 4, 128]),
                  ALU.mult, ALU.mult, [N_SW, "cst"], [N_SW])
            for hg in range(4):
                h.scan(cum[:, hg, :], rstm, sw[:, hg, :], [N_SW, "cst"], [(N_CUM, hg)])
            h.act(E[:], cum[:], AF.Exp, [N_CUM], [N_E])
            h.act(Einv[:], cum[:], AF.Exp, [N_CUM], [N_EINV], scale=-1.0)
            h.tt("pool", tmp[:], cum[:], sw[:], ALU.subtract, [N_CUM, N_SW], [N_TMP])
            h.act(Eprev[:], tmp[:], AF.Exp, [N_TMP], [N_EPREV])
            cum4 = cum[:].rearrange("p g (c l) -> p g c l", c=4)
            h.tt("pool", tmp[:].rearrange("p g (c l) -> p g c l", c=4),
                 cum4[:, :, :, 31:32].to_broadcast([128, 4, 4, 32]), cum4, ALU.subtract, [N_CUM], [N_TMP])
            h.act(Eend[:], tmp[:], AF.Exp, [N_TMP], [N_EEND])
            if samp:
                h.tt("pool", Eend[:], Eend[:], valid.unsqueeze(1).to_broadcast([128, 4, 128]), ALU.mult,
                     [N_EEND, "cst"], [N_EEND])
            h.tt("dve", kk[:], kT, bc(prm[:, PR_KK:PR_KK + 4], [128, 4, 128]), ALU.mult, ["xs", "prm"], [N_KK])
            h.act(tmp[:], kk[:], AF.Square, [N_KK], [N_TMP])
            for hg in range(4):
                h.mm(PS[6][:, hg * 128:(hg + 1) * 128], bones, tmp[:, hg, :], True, True, ["cst", N_TMP], ["ps6"])
            h.act(tmp[:], PS[6][:].rearrange("p (c l) -> p c l", c=4), AF.Sqrt, ["ps6"], [N_TMP])
            h.ts("dve", tmp[:], tmp[:], 1e-12, None, ALU.max, None, [N_TMP], [N_TMP])
            h.recip(tmp[:], tmp[:], [N_TMP], [N_TMP])
            h.tt("dve", kk[:], kk[:], tmp[:], ALU.mult, [N_KK, N_TMP], [N_KK])
            h.tt("pool", kh[:], aa[:], bc(prm[:, PR_KA:PR_KA + 4], [128, 4, 128]), ALU.mult, [N_AA, "prm"], [N_KH])
            h.tt("pool", kh[:], kh[:], bc(prm[:, PR_OMKA:PR_OMKA + 4], [128, 4, 128]), ALU.add, [N_KH, "prm"], [N_KH])
            h.tt("pool", kh[:], kh[:], kT, ALU.mult, [N_KH, "xs"], [N_KH])
            h.tt("dve", tmp[:], rT, bc(prm[:, PR_RK:PR_RK + 4], [128, 4, 128]), ALU.mult, ["xs", "prm"], [N_TMP])
            h.tt("dve", tmp[:], tmp[:], kh[:], ALU.mult, [N_TMP, N_KH], [N_TMP])
            for hg in range(4):
                h.mm(PS[5][:, hg * 128:(hg + 1) * 128], bones, tmp[:, hg, :], True, True, ["cst", N_TMP], ["ps5"])
            h.tt("dve", bon[:], PS[5][:].rearrange("p (c l) -> p c l", c=4), vT, ALU.mult, ["ps5", "xs"], [N_BON])
            h.tt("pool", aa[:], aa[:], kk[:], ALU.mult, [N_AA, N_KK], [N_AA])
            h.tt("dve", AR[:, :, 1, :], rT, E[:], ALU.mult, ["xs", N_E], [(N_AR, 1)])
            h.stt("dve", AR[:, :, 0, :], kk[:], -1.0, Eprev[:], ALU.mult, ALU.mult, [N_KK, N_EPREV], [(N_AR, 0)])
            h.tt("dve", bT[:], aa[:], Einv[:], ALU.mult, [N_AA, N_EINV], ["Hb0"])
            h.tt("dve", kTb[:], kh[:], Einv[:], ALU.mult, [N_KH, N_EINV], ["Hb1"])
            h.tt("pool", BpT[:], aa[:], Eend[:], ALU.mult, [N_AA, N_EEND], ["Hb2"])
            h.tt("pool", KpT[:], kh[:], Eend[:], ALU.mult, [N_KH, N_EEND], ["Hb3"])
            h.cp("pool", vbf[:], vT, ["xs"], ["Hb4"])
            h.cp("pool", stat[:, 0:16].rearrange("p (g c) -> p g c", g=4),
                 E[:].rearrange("p g (c l) -> p g c l", c=4)[:, :, :, 31], [N_E], [("stat", 0)])
            gam = stat[:, 0:16].rearrange("p (g c) -> p g c", g=4)

            for hd in range(8):
                hg, pb = hd // 2, 64 * (hd % 2)
                px = PS[hd % 2]
                pxn = "ps%d" % (hd % 2)
                h.mm(px[:, 0:256], bT[pb:pb + 64, hg, :], AR[pb:pb + 64, hg, :, :].rearrange("p a l -> p (a l)"),
                     True, True, ["Hb0", N_AR], [pxn])
                h.mm(px[:, 256:512], kTb[pb:pb + 64, hg, :], AR[pb:pb + 64, hg, :, :].rearrange("p a l -> p (a l)"),
                     True, True, ["Hb1", N_AR], [pxn])
                h.tt("dve", SCH[:, hd, :, :].rearrange("p a l -> p (a l)"), px[:], m4, ALU.mult,
                     [pxn, "cst"], [("SCH", hd)])
            Ac, Nn, An, Xc, Xtc, Xn, Xtn, Nc2 = INV
            NI = ["INV%d" % i for i in range(8)]
            Ac4 = Ac[:].rearrange("p (g a) l -> p g a l", a=2)
            for h2 in range(2):
                pt = PS[2 + h2]
                ptn = "ps%d" % (2 + h2)
                pb = 64 * h2
                for hg in range(4):
                    h.mm(pt[:, hg * 128:(hg + 1) * 128], AR[pb:pb + 64, hg, 0, :], bT[pb:pb + 64, hg, :],
                         True, True, [N_AR, "Hb0"], [ptn])
                h.tt("dve", Ac4[:, :, h2, :], pt[:].rearrange("p (q l) -> p q l", q=4),
                     msl.unsqueeze(1).to_broadcast([128, 4, 128]), ALU.mult, [ptn, "cst"], [NI[0]])
            h.tt("pool", Xc[:], SCH[:, :, 0, :], idb[:].unsqueeze(1).to_broadcast([128, 8, 128]), ALU.add,
                 ["SCH", "idb"], [NI[3]])
            h.tt("pool", Xtc[:], Ac[:], idb[:].unsqueeze(1).to_broadcast([128, 8, 128]), ALU.add,
                 [NI[0], "idb"], [NI[4]])

            Ncur_ap = lambda hd: SCH[:, hd, 0, :]
            Ncur_key = "SCH"
            Acur, Acur_key = Ac, NI[0]
            Xcur, Xcur_key, Xtcur, Xtcur_key = Xc, NI[3], Xtc, NI[4]
            Nnext = [(Nn, NI[1]), (Nc2, NI[7])]
            Anext = [(An, NI[2]), (Ac, NI[0])]
            Xnext = [(Xn, NI[5]), (Xc, NI[3])]
            Xtnext = [(Xtn, NI[6]), (Xtc, NI[4])]
            for lvl in range(4):
                last = (lvl == 3)
                Nx, Nxk = Nnext[lvl % 2]
                Ax, Axk = Anext[lvl % 2]
                Xx, Xxk = Xnext[lvl % 2]
                Xtx, Xtxk = Xtnext[lvl % 2]
                for g2i in range(2):
                    pa, pan = PS[2 * g2i], "ps%d" % (2 * g2i)
                    pbk, pbn = PS[2 * g2i + 1], "ps%d" % (2 * g2i + 1)
                    for q in range(4):
                        hd = 4 * g2i + q
                        h.mm(pa[:, q * 128:(q + 1) * 128], Acur[:, hd, :], Ncur_ap(hd), True, True,
                             [Acur_key, Ncur_key], [pan])
                    h.cp("act", Nx[:, 4 * g2i:4 * g2i + 4, :], pa[:].rearrange("p (q l) -> p q l", q=4),
                         [pan], [(Nxk, g2i)])
                    if not last:
                        for q in range(4):
                            hd = 4 * g2i + q
                            h.mm(pbk[:, q * 128:(q + 1) * 128], Ncur_ap(hd), Acur[:, hd, :], True, True,
                                 [Acur_key, Ncur_key], [pbn])
                        h.cp("act", Ax[:, 4 * g2i:4 * g2i + 4, :], pbk[:].rearrange("p (q l) -> p q l", q=4),
                             [pbn], [(Axk, g2i)])
                for g2i in range(2):
                    pa, pan = PS[4], "ps4"
                    for q in range(4):
                        hd = 4 * g2i + q
                        h.mm(pa[:, q * 128:(q + 1) * 128], Xtcur[:, hd, :], Nx[:, hd, :], True, True,
                             [Xtcur_key, (Nxk, g2i)], [pan])
                    h.tt("dve", Xx[:, 4 * g2i:4 * g2i + 4, :], pa[:].rearrange("p (q l) -> p q l", q=4),
                         Xcur[:, 4 * g2i:4 * g2i + 4, :], ALU.add, [pan, Xcur_key], [(Xxk, g2i)])
                if not last:
                    for g2i in range(2):
                        pa, pan = PS[2 * g2i], "ps%d" % (2 * g2i)
                        for q in range(4):
                            hd = 4 * g2i + q
                            h.mm(pa[:, q * 128:(q + 1) * 128], Nx[:, hd, :], Xtcur[:, hd, :], True, True,
                                 [Xtcur_key, (Nxk, g2i)], [pan])
                        h.tt("dve", Xtx[:, 4 * g2i:4 * g2i + 4, :], pa[:].rearrange("p (q l) -> p q l", q=4),
                             Xtcur[:, 4 * g2i:4 * g2i + 4, :], ALU.add, [pan, Xtcur_key], [(Xtxk, g2i)])
                Ncur_ap = (lambda t: (lambda hd: t[:, hd, :]))(Nx)
                Ncur_key = Nxk
                Acur, Acur_key = Ax, Axk
                Xcur, Xcur_key = Xx, Xxk
                Xtcur, Xtcur_key = Xtx, Xtxk
            X4, X4k = Xcur, Xcur_key

            Atok, Bptok, Kptok, Vtok = tok
            srcs = [(AR[:, :, 0, :], N_AR, Atok, "tok0"), (BpT[:], "Hb2", Bptok, "tok1"),
                    (KpT[:], "Hb3", Kptok, "tok2"), (vbf[:], "Hb4", Vtok, "tok3")]
            for si in range(0, 4, 2):
                for u in range(2):
                    src, srck, dst, dstk = srcs[si + u]
                    for hg in range(4):
                        h.tr(psT[:, u * 4 + hg, :], src[:, hg, :], idb[:], [srck, "idb"], [("psT", u * 4 + hg)])
                for u in range(2):
                    src, srck, dst, dstk = srcs[si + u]
                    h.cp("act", dst[:], psT[:, u * 4:(u + 1) * 4, :].rearrange("p c l -> p (c l)"),
                         ["psT"], [dstk])

            for hd in range(8):
                h.mm(PS[0][:, hd * 64:(hd + 1) * 64], SCH[:, hd, 2, :], Vtok[:, hd * 64:(hd + 1) * 64],
                     True, True, ["SCH", "tok3"], ["ps0"])
            h.cp("act", Zbf[:], PS[0][:], ["ps0"], ["Zbf"])
            for hd in range(8):
                h.mm(PS[1][:, hd * 64:(hd + 1) * 64], X4[:, hd, :], Zbf[:, hd * 64:(hd + 1) * 64],
                     True, True, [X4k, "Zbf"], ["ps1"])
            h.cp("act", Yf[:], PS[1][:], ["ps1"], ["Yf"])
            for hd in range(8):
                hg, pb = hd // 2, 64 * (hd % 2)
                h.mm(PS[2][pb:pb + 64, hg * 128:(hg + 1) * 128], Atok[:, hd * 64:(hd + 1) * 64], X4[:, hd, :],
                     True, True, ["tok0", X4k], ["ps2"])
            h.cp("act", WTbf[:], PS[2][:].rearrange("p (c l) -> p c l", c=4), ["ps2"], ["WTbf"])

            psUs, psUn = (PS[0], PS[1]), ("ps0", "ps1")
            psOs, psOn = (PS[2], PS[3]), ("ps2", "ps3")
            psPn = PS[4]
            Ubf4 = Ubf[:].rearrange("p (g a v) -> p g a v", g=4, a=2)
            Yf4 = Yf[:].rearrange("p (g a v) -> p g a v", g=4, a=2)
            for c in range(4):
                s0 = 32 * c
                if samp:
                    seq = 4 * j + c
                    h.dma("sp", Sin[:], st_wkv[seq].rearrange("h v k -> v h k"), [], ["Sin"])
                    for hg in range(4):
                        h.tr(PS[4][:, hg * 64:(hg + 1) * 64],
                             Sin[:, 2 * hg:2 * hg + 2, :].rearrange("p a k -> p (a k)"), idf[0:64, 0:64],
                             ["Sin", "cst"], ["ps4"])
                    h.cp("act", Pst[:], PS[4][:, 0:256].rearrange("p (c v) -> p c v", c=4), ["ps4"], ["Pst"])
                    h.cp("dve", Pbf[:], Pst[:], ["Pst"], ["Pbf"])
                for hd in range(8):
                    hg, h2 = hd // 2, hd % 2
                    pb = 64 * h2
                    h.mm(psUs[h2][s0:s0 + 32, hg * 64:(hg + 1) * 64], WTbf[pb:pb + 64, hg, s0:s0 + 32],
                         Pbf[pb:pb + 64, hg, :], True, True, ["WTbf", "Pbf"], [psUn[h2]], tp=(pb, s0))
                for hd in range(8):
                    hg, h2 = hd // 2, hd % 2
                    pb = 64 * h2
                    h.mm(psOs[h2][s0:s0 + 32, hg * 64:(hg + 1) * 64], AR[pb:pb + 64, hg, 1, s0:s0 + 32],
                         Pbf[pb:pb + 64, hg, :], True, True, [N_AR, "Pbf"], [psOn[h2]], tp=(pb, s0))
                for h2 in range(2):
                    h.tt("dve", Ubf4[s0:s0 + 32, :, h2, :],
                         psUs[h2][s0:s0 + 32, 0:256].rearrange("p (g v) -> p g v", g=4),
                         Yf4[s0:s0 + 32, :, h2, :], ALU.add, [psUn[h2], "Yf"], [("Ubf", c)])
                for hd in range(8):
                    hg, pb = hd // 2, 64 * (hd % 2)
                    h.mm(psPn[pb:pb + 64, hg * 64:(hg + 1) * 64], Bptok[s0:s0 + 32, hd * 64:(hd + 1) * 64],
                         Ubf[s0:s0 + 32, hd * 64:(hd + 1) * 64], True, False, ["tok1", ("Ubf", c)], ["ps4"],
                         tp=(s0, pb))
                    h.mm(psPn[pb:pb + 64, hg * 64:(hg + 1) * 64], Kptok[s0:s0 + 32, hd * 64:(hd + 1) * 64],
                         Vtok[s0:s0 + 32, hd * 64:(hd + 1) * 64], False, True, ["tok2", "tok3"], ["ps4"],
                         tp=(s0, pb))
                for hg in range(4):
                    h.stt("dve", Pst[:, hg, :], Pst[:, hg, :], gam[:, hg, c:c + 1], psPn[:, hg * 64:(hg + 1) * 64],
                          ALU.mult, ALU.add, ["Pst", ("stat", 0), "ps4"], ["Pst"])
                if not samp and not (b == NPB - 1 and c == 3):
                    h.cp("act", Pbf[:], Pst[:], ["Pst"], ["Pbf"])
                if samp or (b == NPB - 1 and c == 3):
                    for hg in range(4):
                        h.tr(PS[4][0:64, hg * 128:(hg + 1) * 128], Pst[:, hg, :], idf, ["Pst", "cst"], ["ps4"])
                    h.cp("act", Sin[:].rearrange("p h k -> p (h k)"), PS[4][0:64, :], ["ps4"], ["Sin"])
                    dst = o_wkv_s[4 * j + c] if samp else o_wkv_p[0]
                    h.dma("sp", dst.rearrange("h v k -> v h k"), Sin[:], ["Sin"], [], isout=True)
            for hd in range(8):
                h.mm(PS[0][:, hd * 64:(hd + 1) * 64], SCH[:, hd, 1, :], Ubf[:, hd * 64:(hd + 1) * 64],
                     True, False, ["SCH", "Ubf"], ["ps0"])
                h.mm(PS[0][:, hd * 64:(hd + 1) * 64], SCH[:, hd, 3, :], Vtok[:, hd * 64:(hd + 1) * 64],
                     False, True, ["SCH", "tok3"], ["ps0"])
            o14 = o1[:].rearrange("p (g a v) -> p g a v", g=4, a=2)
            for h2 in range(2):
                h.cp("act", o14[:, :, h2, :], psOs[h2][:, 0:256].rearrange("p (g v) -> p g v", g=4),
                     [psOn[h2]], ["o1"])
            h.tt("dve", oo[:], o1[:], PS[0][:], ALU.add, ["o1", "ps0"], ["oo"])

            oo3 = oo[:].rearrange("p (h v) -> p h v", h=8)
            osq = o1
            h.act(osq[:], oo[:], AF.Square, ["oo"], ["o1"])
            s1, s2, mean, msq, rs, nb = (stat[:, 16:24], stat[:, 24:32], stat[:, 32:40], stat[:, 40:48],
                                         stat[:, 48:56], stat[:, 56:64])
            h.rsum(s1, oo3, ["oo"], [("stat", 1)])
            h.rsum(s2, osq[:].rearrange("p (h v) -> p h v", h=8), ["o1"], [("stat", 2)])
            h.ts("dve", mean, s1, 1.0 / 64, None, ALU.mult, None, [("stat", 1)], [("stat", 3)])
            h.tt("dve", msq, mean, mean, ALU.mult, [("stat", 3)], [("stat", 4)])
            h.stt("dve", rs, s2, 1.0 / 64, msq, ALU.mult, ALU.subtract, [("stat", 2), ("stat", 4)], [("stat", 5)])
            h.ts("dve", rs, rs, 64e-5, None, ALU.add, None, [("stat", 5)], [("stat", 5)])
            h.act(rs, rs, AF.Sqrt, [("stat", 5)], [("stat", 5)])
            h.recip(rs, rs, [("stat", 5)], [("stat", 5)])
            h.tt("dve", oo3, oo3, mean.unsqueeze(2).to_broadcast([128, 8, 64]), ALU.subtract,
                 ["oo", ("stat", 3)], ["oo"])
            h.tt("dve", oo3, oo3, rs.unsqueeze(2).to_broadcast([128, 8, 64]), ALU.mult,
                 ["oo", ("stat", 5)], ["oo"])
            for hg in range(4):
                h.tr(PS[0][:, hg * 128:(hg + 1) * 128], oo[:, hg * 128:(hg + 1) * 128], idf, ["oo", "cst"], ["ps0"])
            ps0v = PS[0][:].rearrange("p (c l) -> p c l", c=4)
            t2 = o1[:].rearrange("p (c l) -> p c l", c=4)
            h.tt("dve", t2, ps0v, bc(prm[:, PR_LW:PR_LW + 4], [128, 4, 128]), ALU.mult, ["ps0", "prm"], ["o1"])
            h.tt("pool", t2, t2, bc(prm[:, PR_LB:PR_LB + 4], [128, 4, 128]), ALU.add, ["o1", "prm"], ["o1"])
            h.tt("pool", t2, t2, bon[:], ALU.add, ["o1", N_BON], ["o1"])
            h.tt("dve", ogbf[:], t2, gT[:], ALU.mult, ["o1", N_GT], ["ogbf"])
            h.dma("sp", og_s[b], ogbf[:].rearrange("p c l -> p (c l)"), ["ogbf"], [])

            qT, kgT, vgT, ogT = xg[:, 0:2, :], xg[:, 2:4, :], xg[:, 4:8, :], xg[:, 8:12, :]
            for c2 in range(2):
                h.mm(PS[5][:, c2 * 128:(c2 + 1) * 128], wg2[0:16, c2 * 128:(c2 + 1) * 128], lga[:], True, True,
                     ["wg2", "lga"], ["ps5"])
            la_, cg, Eg, Eginv, Egend, gtmp = Gf
            for c2 in range(2):
                h.act(la_[:, c2, :], PS[5][:, c2 * 128:(c2 + 1) * 128], AF.Exp, ["ps5", "prm"], [("G0", c2)],
                      scale=-1.0, bias=prm[:, PR_BG + c2:PR_BG + c2 + 1])
            h.ts("dve", la_[:], la_[:], 1.0, None, ALU.add, None, ["G0"], ["G0"])
            h.act(la_[:], la_[:], AF.Ln, ["G0"], ["G0"])
            h.stt("dve", la_[:], la_[:], -1.0 / 16.0,
                  valid.unsqueeze(1).to_broadcast([128, 2, 128]), ALU.mult, ALU.mult, ["G0", "cst"], ["G0"])
            for c2 in range(2):
                h.scan(cg[:, c2, :], rstm, la_[:, c2, :], ["G0", "cst"], [("G1", c2)])
            h.act(Eg[:], cg[:], AF.Exp, ["G1"], ["G2"])
            h.act(Eginv[:], cg[:], AF.Exp, ["G1"], ["G3"], scale=-1.0)
            cg4 = cg[:].rearrange("p g (c l) -> p g c l", c=4)
            h.tt("pool", gtmp[:].rearrange("p g (c l) -> p g c l", c=4),
                 cg4[:, :, :, 31:32].to_broadcast([128, 2, 4, 32]), cg4, ALU.subtract, ["G1"], ["G5"])
            h.act(Egend[:], gtmp[:], AF.Exp, ["G5"], ["G4"])
            if samp:
                h.tt("pool", Egend[:], Egend[:], valid.unsqueeze(1).to_broadcast([128, 2, 128]),
                     ALU.mult, ["G4", "cst"], ["G4"])
            h.cp("pool", statg[:, 0:8].rearrange("p (g c) -> p g c", g=2),
                 Eg[:].rearrange("p g (c l) -> p g c l", c=4)[:, :, :, 31], ["G2"], [("statg", 0)])
            gamg = statg[:, 0:8].rearrange("p (g c) -> p g c", g=2)
            h.stt("dve", qdT[:], qT, 0.125, Eg[:], ALU.mult, ALU.mult, [("xg", 0), "G2"], ["Gh0"])
            h.tt("pool", kiT[:], kgT, Eginv[:], ALU.mult, [("xg", 0), "G3"], ["Gh1"])
            h.tt("pool", keT[:], kgT, Egend[:], ALU.mult, [("xg", 0), "G4"], ["Gh2"])
            h.cp("pool", vgbf[:], vgT, [("xg", 1)], ["Gv"])
            h.act(silu[:], ogT, AF.Silu, [("xg", 2)], ["Gs"])
            GS4 = GS[:].rearrange("p (g a) l -> p g a l", a=2)
            for h2 in range(2):
                pb = 64 * h2
                pt, ptn = PS[5 + h2], "ps%d" % (5 + h2)
                for c2 in range(2):
                    h.mm(pt[:, c2 * 128:(c2 + 1) * 128], kiT[pb:pb + 64, c2, :], qdT[pb:pb + 64, c2, :], True, True,
                         ["Gh1", "Gh0"], [ptn])
                h.tt("dve", GS4[:, :, h2, :], pt[:, 0:256].rearrange("p (q l) -> p q l", q=2),
                     miu.unsqueeze(1).to_broadcast([128, 2, 128]), ALU.mult, [ptn, "cst"], ["Ggs"])
            for hg in range(4):
                h.tr(psT[:, hg, :], vgbf[:, hg, :], idb[:], ["Gv", "idb"], [("psT", hg)])
            for c2 in range(2):
                h.tr(psT[:, 4 + c2, :], keT[:, c2, :], idb[:], ["Gh2", "idb"], [("psT", 4 + c2)])
            h.cp("act", Vgtok[:], psT[:, 0:4, :].rearrange("p c l -> p (c l)"), ["psT"], ["Gt0"])
            h.cp("act", Ketok[:], psT[:, 4:6, :].rearrange("p c l -> p (c l)"), ["psT"], ["Gt1"])
            if samp:
                h.dma("sp", SgIn[:], st_gla[4 * j:4 * j + 4].rearrange("q (c2 h2) k v -> (h2 k) q c2 v", h2=2),
                      [], ["SgIn"])
                h.cp("act", Sgbf[:], SgIn[:], ["SgIn"], ["Sgbf"])
            else:
                h.cp("act", Sgbf[:, 0, :, :], Sg[:], ["Sg"], [("Sgbf", 0)])
            for c in range(4):
                s0 = 32 * c
                pt, pn = PS[5 + c % 2], "ps%d" % (5 + c % 2)
                for hd in range(4):
                    c2, pb = hd // 2, 64 * (hd % 2)
                    off = c2 * 128
                    h.mm(pt[pb:pb + 64, off:off + 128], Ketok[s0:s0 + 32, hd * 64:(hd + 1) * 64],
                         Vgtok[s0:s0 + 32, hd * 128:(hd + 1) * 128], True, True, ["Gt1", "Gt0"], [pn],
                         tp=(s0, pb))
                dv = pt[:, 0:256].rearrange("p (g v) -> p g v", g=2)
                gb = gamg[:, :, c:c + 1].to_broadcast([128, 2, 128])
                for c2 in range(2):
                    if samp:
                        h.stt("dve", SgIn[:, c, c2, :], SgIn[:, c, c2, :], gamg[:, c2, c:c + 1], dv[:, c2, :],
                              ALU.mult, ALU.add, [("SgIn", c), ("statg", 0), pn], [("SgIn", c)])
                    else:
                        h.stt("dve", Sg[:, c2, :], Sg[:, c2, :], gamg[:, c2, c:c + 1], dv[:, c2, :],
                              ALU.mult, ALU.add, ["Sg", ("statg", 0), pn], ["Sg"])
                if (not samp) and c < 3:
                    h.cp("act", Sgbf[:, c + 1, :, :], Sg[:], ["Sg"], [("Sgbf", c + 1)])
            if samp:
                h.dma("sp", o_gla_s[4 * j:4 * j + 4].rearrange("q (c2 h2) k v -> (h2 k) q c2 v", h2=2),
                      SgIn[:], ["SgIn"], [], isout=True)
            elif b == NPB - 1:
                h.dma("sp", o_gla_p[0].rearrange("(c2 h2) k v -> (h2 k) c2 v", h2=2), Sg[:], ["Sg"], [],
                      isout=True)
            for c in range(4):
                s0 = 32 * c
                for hd in range(4):
                    c2, h2 = hd // 2, hd % 2
                    pb = 64 * h2
                    h.mm(PS[5 + h2][s0:s0 + 32, c2 * 128:(c2 + 1) * 128], qdT[pb:pb + 64, c2, s0:s0 + 32],
                         Sgbf[pb:pb + 64, c, c2, :], True, True, ["Gh0", "Sgbf"], ["ps%d" % (5 + h2)], tp=(pb, s0))
            o1gv = o1g[:].rearrange("p (g a v) -> p g a v", g=2, a=2)
            for h2 in range(2):
                h.cp("act", o1gv[:, :, h2, :], PS[5 + h2][:, 0:256].rearrange("p (g v) -> p g v", g=2),
                     ["ps%d" % (5 + h2)], ["o1g"])
            for hd in range(4):
                h.mm(PS[5][:, hd * 128:(hd + 1) * 128], GS[:, hd, :], Vgtok[:, hd * 128:(hd + 1) * 128], True, True,
                     ["Ggs", "Gt0"], ["ps5"])
            h.tt("dve", oog[:], o1g[:], PS[5][:], ALU.add, ["o1g", "ps5"], ["oog"])
            oo4 = oog[:].rearrange("p (h v) -> p h v", h=4)
            h.act(o1g[:], oog[:], AF.Square, ["oog"], ["o1g"])
            gs2, grs = statg[:, 8:12], statg[:, 12:16]
            h.rsum(gs2, o1g[:].rearrange("p (h v) -> p h v", h=4), ["o1g"], [("statg", 1)])
            h.ts("dve", grs, gs2, 1.0 / 128, 1e-6, ALU.mult, ALU.add, [("statg", 1)], [("statg", 2)])
            h.act(grs, grs, AF.Sqrt, [("statg", 2)], [("statg", 2)])
            h.recip(grs, grs, [("statg", 2)], [("statg", 2)])
            h.tt("dve", oo4, oo4, grs.unsqueeze(2).to_broadcast([128, 4, 128]), ALU.mult, ["oog", ("statg", 2)], ["oog"])
            for hd in range(4):
                h.tr(PS[6][:, hd * 128:(hd + 1) * 128], oog[:, hd * 128:(hd + 1) * 128], idf, ["oog", "cst"], ["ps6"])
            h.stt("dve", obbf[:], PS[6][:].rearrange("p (c l) -> p c l", c=4), prm[:, PR_NW:PR_NW + 1], silu[:],
                  ALU.mult, ALU.mult, ["ps6", "prm", "Gs"], ["obbf"])
            h.dma("sp", ob_s[b], obbf[:].rearrange("p c l -> p (c l)"), ["obbf"], [])

            if dbg is not None and dbg.get("blk") == b and dbg.get("phase") == "a1":
                src, keys = dbg["fn"](dict(locals()))
                h.dma("sp", dbg_out, src, keys, [], isout=True)
        P.emit()

    with contextlib.ExitStack() as ph:
      if "b" in phases:
        P = Prog(nc, gs, "b")
        h = H(P)

        def sb(name, shape, dt):
            return ph.enter_context(nc.sbuf_tensor("b_" + name, list(shape), dt))

        def psb(name, shape, dt):
            return ph.enter_context(nc.psum_tensor("b_" + name, list(shape), dt))

        T = {}
        cst, idb = load_consts(h, sb)
        T["idb"] = idb
        prm = sb("prm", [128, NPRM], F32)
        T["prm"] = prm
        load_param_cols(h, prm, PR_G, norm_mix, 8)
        wing = sb("wing", [128, 8, 2048], BF16)
        w_in_v = w_in.rearrange("(c p) n -> p c n", p=128)
        for kc in range(8):
            h.dma("pool", wing[:, kc, :], w_in_v[:, kc, GATE0:INC], [], [("wing", kc)])
        woa = sb("woa", [128, 4, D], BF16)
        wob = sb("wob", [128, 4, D], BF16)
        wo = sb("wo", [128, 8, D], BF16)
        h.dma("pool", woa[:], w_out_a.rearrange("(c p) n -> p c n", p=128), [], ["woa"])
        h.dma("pool", wob[:], w_out_b.rearrange("(c p) n -> p c n", p=128), [], ["wob"])
        for kc in range(8):
            h.dma("pool", wo[:, kc, :], w_o.rearrange("(c p) n -> p c n", p=128)[:, kc, :], [], [("wo", kc)])
        xts = [sb("xt%d" % i, [128, D], F32) for i in range(4)]
        alloc_rms(T, sb)
        ogbs = [sb("ogb%d" % i, [128, 4, 128], BF16) for i in range(2)]
        obbs = [sb("obb%d" % i, [128, 4, 128], BF16) for i in range(2)]
        sgas = [sb("sga%d" % i, [128, 8, 128], F32) for i in range(2)]
        sgbs = [sb("sgb%d" % i, [128, 8, 128], F32) for i in range(2)]
        tas = [sb("ta%d" % i, [128, 8, 128], F32) for i in range(2)]
        mgs = [sb("mg%d" % i, [128, 8, 128], BF16) for i in range(2)]
        x1ts = [sb("x1t%d" % i, [128, D], F32) for i in range(2)]
        T["psT"] = psb("psT", [128, 8, 128], BF16)
        PS = [psb("ps%d" % i, [128, 512], F32) for i in range(7)]
        for b in BLKS:
            xt, xtn = xts[b % 4], "xt%d" % (b % 4)
            load_x_block(h, b, xt, xtn)
            p2 = b % 2
            ogb, obb, sga, sgb, ta, mg, x1t = ogbs[p2], obbs[p2], sgas[p2], sgbs[p2], tas[p2], mgs[p2], x1ts[p2]
            K_ogb, K_obb, K_sga, K_sgb, K_ta, K_mg, K_x1t = ["%s%d" % (nm, p2) for nm in
                                                            ("ogb", "obb", "sga", "sgb", "ta", "mg", "x1t")]
            h.dma("sp", ogb[:].rearrange("p c l -> p (c l)"), og_s[b], [], [K_ogb])
            h.dma("sp", obb[:].rearrange("p c l -> p (c l)"), ob_s[b], [], [K_obb])
            hT, hTk = rms_to_fm(h, T, xt, xtn, PR_G, b % 2)
            for half, dst, dk in ((0, sga, K_sga), (1, sgb, K_sgb)):
                for gi in range(2):
                    pt, pn = PS[gi], "ps%d" % gi
                    for mi in range(4):
                        c0 = half * 1024 + (4 * gi + mi) * 128
                        for kc in range(8):
                            h.mm(pt[:, mi * 128:(mi + 1) * 128], wing[:, kc, c0:c0 + 128], hT[:, kc, :],
                                 kc == 0, kc == 7, [("wing", kc), hTk], [pn])
                    h.act(dst[:, 4 * gi:4 * gi + 4, :], pt[:].rearrange("p (c l) -> p c l", c=4), AF.Sigmoid,
                          [pn], [(dk, gi)])
            for gi in range(2):
                pt, pn = PS[2 + gi], "ps%d" % (2 + gi)
                for mi in range(4):
                    m = 4 * gi + mi
                    for kc in range(4):
                        h.mm(pt[:, mi * 128:(mi + 1) * 128], woa[:, kc, m * 128:(m + 1) * 128], ogb[:, kc, :],
                             kc == 0, kc == 3, ["woa", K_ogb], [pn])
                h.tt("dve", ta[:, 4 * gi:4 * gi + 4, :], pt[:].rearrange("p (c l) -> p c l", c=4),
                     sga[:, 4 * gi:4 * gi + 4, :], ALU.mult, [pn, (K_sga, gi)], [(K_ta, gi)])
            for gi in range(2):
                pt, pn = PS[4 + gi], "ps%d" % (4 + gi)
                for mi in range(4):
                    m = 4 * gi + mi
                    for kc in range(4):
                        h.mm(pt[:, mi * 128:(mi + 1) * 128], wob[:, kc, m * 128:(m + 1) * 128], obb[:, kc, :],
                             kc == 0, kc == 3, ["wob", K_obb], [pn])
                h.tt("dve", sgb[:, 4 * gi:4 * gi + 4, :], pt[:].rearrange("p (c l) -> p c l", c=4),
                     sgb[:, 4 * gi:4 * gi + 4, :], ALU.mult, [pn, (K_sgb, gi)], [(K_sgb, gi)])
                h.tt("pool", mg[:, 4 * gi:4 * gi + 4, :], ta[:, 4 * gi:4 * gi + 4, :],
                     sgb[:, 4 * gi:4 * gi + 4, :], ALU.add, [(K_ta, gi), (K_sgb, gi)], [(K_mg, gi)])
            for nh in range(2):
                pt, pn = PS[2 + nh], "ps%d" % (2 + nh)
                for kc in range(8):
                    h.mm(pt[:], mg[:, kc, :], wo[:, kc, nh * 512:(nh + 1) * 512], kc == 0, kc == 7,
                         [K_mg, ("wo", kc)], [pn])
                h.tt("dve", x1t[:, nh * 512:(nh + 1) * 512], pt[:], xt[:, nh * 512:(nh + 1) * 512], ALU.add,
                     [pn, xtn], [(K_x1t, nh)])
            h.dma("sp", x1_s[b * 128:(b + 1) * 128, :], x1t[:], [K_x1t], [])
        P.emit()

    with contextlib.ExitStack() as ph:
      if "c" in phases:
        P = Prog(nc, gs, "c")
        h = H(P)

        def sb(name, shape, dt):
            return ph.enter_context(nc.sbuf_tensor("c_" + name, list(shape), dt))

        def psb(name, shape, dt):
            return ph.enter_context(nc.psum_tensor("c_" + name, list(shape), dt))

        T = {}
        cst, idb = load_consts(h, sb)
        T["idb"] = idb
        idf = cst[:, C_ID:C_ID + 128]
        prm = sb("prm", [128, 8], F32)
        T["prm"] = prm
        load_param_cols(h, prm, 0, norm_ffn, 8)
        cw = sb("cw", [128, 4, 44], F32)
        for jx in range(3):
            h.dma("sp", cw[:, jx, :], ffn_conv_w[jx].rearrange("(c p) -> p c", p=128), [], [("cw", jx)], slow=True)
        h.dma("sp", cw[:, 3, :], ffn_conv_b.rearrange("(c p) -> p c", p=128), [], [("cw", 3)], slow=True)
        nfb = sb("nfb", [128, D], F32)
        h.dma("sp", nfb[:], norm_final.partition_broadcast(128), [], ["nfb"])
        wup = sb("wup", [128, 8, F2], BF16)
        w_up_v = ffn_w_up.rearrange("(c p) n -> p c n", p=128)
        for kc in range(8):
            h.dma("pool", wup[:, kc, :], w_up_v[:, kc, :], [], [("wup", kc)])
        wdn = sb("wdn", [128, 22, D], BF16)
        w_dn_v = ffn_w_down.rearrange("(c p) n -> p c n", p=128)
        for kc in range(22):
            h.dma("pool", wdn[:, kc, :], w_dn_v[:, kc, :], [], [("wdn", kc)])
        xts = [sb("xt0", [128, D], F32), sb("xt1", [128, D], F32)]
        alloc_rms(T, sb)
        ub = [sb("ub0", [128, 4, 136], F32), sb("ub1", [128, 4, 136], F32)]
        ucar = sb("ucar", [128, 44, 2], F32)
        cin = sb("cin", [128, 44, 4, 2], F32)
        cout = sb("cout", [128, 44, 8], F32)
        cstg = [sb("cstg%d" % i, [8, 512], F32) for i in range(2)]
        csto = [sb("csto%d" % i, [8, 512], F32) for i in range(2)]
        fss = [sb("fss%d" % i, [128, 1], F32) for i in range(2)]
        frs = [sb("frs%d" % i, [128, 1], F32) for i in range(2)]
        cc = [sb("cc0", [128, 4, 128], F32), sb("cc1", [128, 4, 128], F32)]
        g1 = [sb("g10", [128, 2, 128], F32), sb("g11", [128, 2, 128], F32)]
        g2t = [sb("g20", [128, 2, 128], F32), sb("g21", [128, 2, 128], F32)]
        actTs = [sb("actT%d" % i, [128, 22, 128], BF16) for i in range(2)]
        x2s = [sb("x2%d" % i, [128, D], F32) for i in range(2)]
        yts = [sb("yt0", [128, D], F32)] * 2
        T["psT"] = psb("psT", [128, 8, 128], BF16)
        PS = [psb("ps%d" % i, [128, 512], F32) for i in range(7)]
        h.memset("pool", ucar[:], 0.0, ["ucar"])
        for b in BLKS:
            samp, nseq, L = geom(b)
            j = b - NPB
            xt, xtn = xts[b % 2], "xt%d" % (b % 2)
            actT, actk = actTs[b % 2], "actT%d" % (b % 2)
            x2, x2k = x2s[b % 2], "x2%d" % (b % 2)
            yt, ytk = yts[0], "yt0"
            h.dma("sp", xt[:], x1_s[b * 128:(b + 1) * 128, :], [], [xtn])
            hT, hTk = rms_to_fm(h, T, xt, xtn, 0, b % 2)
            W = nseq * (L + 2)
            if samp:
                stc_v = st_conv[4 * j:4 * j + 4].rearrange("q t f -> (q t) f")
                for g4 in range(11):
                    cg_, cgk = cstg[g4 % 2], "cstg%d" % (g4 % 2)
                    h.dma("sp", cg_[:], stc_v[:, g4 * 512:(g4 + 1) * 512], [], [cgk])
                    for mi in range(4):
                        m = 4 * g4 + mi
                        h.tr(PS[4][:, mi * 8:(mi + 1) * 8], cg_[0:8, mi * 128:(mi + 1) * 128], idf[0:8, 0:8],
                             [cgk, "cst"], ["ps4"])
                    h.cp("act", cin[:, 4 * g4:4 * g4 + 4, :, :].rearrange("p m q t -> p m (q t)"),
                         PS[4][:, 0:32].rearrange("p (m x) -> p m x", m=4), ["ps4"], [("cin", g4)])
            last_prompt = (b == NPB - 1)
            for gi in range(11):
                u, un = ub[gi % 2], "ub%d" % (gi % 2)
                uv = u[:, :, 0:W].rearrange("p m (s l) -> p m s l", s=nseq)
                pt, pn = PS[gi % 2], "ps%d" % (gi % 2)
                chunks = [2 * gi, 2 * gi + 1, 22 + 2 * gi, 23 + 2 * gi]
                for mi, m in enumerate(chunks):
                    for kc in range(8):
                        h.mm(pt[:, mi * 128:(mi + 1) * 128], wup[:, kc, m * 128:(m + 1) * 128], hT[:, kc, :],
                             kc == 0, kc == 7, [("wup", kc), hTk], [pn])
                h.cp("act", uv[:, :, :, 2:L + 2], pt[:].rearrange("p (m s l) -> p m s l", m=4, s=nseq),
                     [pn], [un])
                for half in range(2):
                    m0 = chunks[2 * half]
                    if samp:
                        h.cp("pool", uv[:, 2 * half:2 * half + 2, :, 0:2], cin[:, m0:m0 + 2, :, :],
                             [("cin", m0 // 4)], [un])
                    else:
                        h.cp("pool", uv[:, 2 * half:2 * half + 2, 0, 0:2], ucar[:, m0:m0 + 2, :],
                             [("ucar", gi)], [un])
                for half in range(2):
                    m0 = chunks[2 * half]
                    if samp:
                        h.cp("pool", cout[:, m0:m0 + 2, :].rearrange("p m (q t) -> p m q t", q=4),
                             uv[:, 2 * half:2 * half + 2, :, 8:10], [un], [("cout", gi)])
                    else:
                        h.cp("pool", ucar[:, m0:m0 + 2, :], uv[:, 2 * half:2 * half + 2, 0, 128:130],
                             [un], [("ucar", gi)])
                        if last_prompt:
                            h.cp("pool", cout[:, m0:m0 + 2, 0:2], uv[:, 2 * half:2 * half + 2, 0, 128:130],
                                 [un], [("cout", gi)])
                c_, cn = cc[gi % 2], "cc%d" % (gi % 2)
                for mi, m in enumerate(chunks):
                    eng = "dve"
                    c4 = c_[:, mi, :].rearrange("p (s l) -> p s l", s=nseq)
                    h.ts(eng, c4, uv[:, mi, :, 2:L + 2], cw[:, 2, m:m + 1], cw[:, 3, m:m + 1], ALU.mult, ALU.add,
                         [un, "cw"], [(cn, mi)])
                    h.stt(eng, c4, uv[:, mi, :, 1:L + 1], cw[:, 1, m:m + 1], c4, ALU.mult, ALU.add,
                          [un, "cw", (cn, mi)], [(cn, mi)])
                    h.stt(eng, c4, uv[:, mi, :, 0:L], cw[:, 0, m:m + 1], c4, ALU.mult, ALU.add,
                          [un, "cw", (cn, mi)], [(cn, mi)])
                ga, gan = g1[gi % 2], "g1%d" % (gi % 2)
                gb_, gbn = g2t[gi % 2], "g2%d" % (gi % 2)
                gate = c_[:, 2:4, :]
                val = c_[:, 0:2, :]
                h.act(ga[:], gate, AF.Square, [(cn, 2), (cn, 3)], [gan])
                h.ts("dve", ga[:], ga[:], 0.044715, 1.0, ALU.mult, ALU.add, [gan], [gan])
                h.tt("pool", ga[:], ga[:], gate, ALU.mult, [gan, (cn, 2), (cn, 3)], [gan])
                h.act(gb_[:], ga[:], AF.Sigmoid, [gan], [gbn], scale=GELU_S)
                h.tt("pool", gb_[:], gb_[:], gate, ALU.mult, [gbn, (cn, 2), (cn, 3)], [gbn])
                h.tt("dve", actT[:, 2 * gi:2 * gi + 2, :], gb_[:], val, ALU.mult, [gbn, (cn, 0), (cn, 1)],
                     [(actk, gi)])
            if samp or last_prompt:
                ncol = 8 if samp else 2
                for g4 in range(11):
                    for mi in range(4):
                        m = 4 * g4 + mi
                        h.tr(PS[4][0:ncol, mi * 128:(mi + 1) * 128], cout[:, m, 0:ncol], idf, ["cout", "cst"],
                             ["ps4"])
                    co_, cok = csto[g4 % 2], "csto%d" % (g4 % 2)
                    h.cp("act", co_[0:ncol, :], PS[4][0:ncol, :], ["ps4"], [cok])
                    if samp:
                        h.dma("sp", o_conv_s[4 * j:4 * j + 4].rearrange("q t f -> (q t) f")[:, g4 * 512:(g4 + 1) * 512],
                              co_[:], [cok], [], isout=True)
                    else:
                        h.dma("sp", o_conv_p[0][:, g4 * 512:(g4 + 1) * 512], co_[0:2, :], [cok], [], isout=True)
            for nh in range(2):
                pt, pn = PS[2 + nh], "ps%d" % (2 + nh)
                for kc in range(22):
                    h.mm(pt[:], actT[:, kc, :], wdn[:, kc, nh * 512:(nh + 1) * 512], kc == 0, kc == 21,
                         [(actk, kc // 2), ("wdn", kc)], [pn])
                h.tt("dve", x2[:, nh * 512:(nh + 1) * 512], pt[:], xt[:, nh * 512:(nh + 1) * 512], ALU.add,
                     [pn, xtn], [(x2k, nh)])
            ss2, rs2 = fss[b % 2], frs[b % 2]
            ssk, rsk = "fss%d" % (b % 2), "frs%d" % (b % 2)
            h.act(yt[:], x2[:], AF.Square, [x2k], [ytk, ssk], accum_out=ss2[:])
            h.ts("dve", rs2[:], ss2[:], 1.0 / D, 1e-6, ALU.mult, ALU.add, [ssk], [rsk])
            h.act(rs2[:], rs2[:], AF.Sqrt, [rsk], [rsk])
            h.recip(rs2[:], rs2[:], [rsk], [rsk])
            h.stt("dve", yt[:], x2[:], rs2[:, 0:1], nfb[:], ALU.mult, ALU.mult, [x2k, rsk, "nfb"], [ytk])
            if samp:
                for q in range(4):
                    h.dma("sp", ysm[4 * j + q], yt[32 * q:32 * q + 8, :], [ytk], [], isout=True)
            else:
                h.dma("sp", yp[b * 128:(b + 1) * 128, :], yt[:], [ytk], [], isout=True)
        P.emit()
    gs.close()
    return nc


_CACHE = {}


def kernel(**inputs):
    f32 = lambda a: np.ascontiguousarray(np.asarray(a), dtype=np.float32)
    if "nc" not in _CACHE:
        nc = bass.Bass("TRN2", target_bir_lowering=False)
        build(nc)
        _CACHE["nc"] = nc
    nc = _CACHE["nc"]
    cst = make_consts()
    shared = {
        "cst": cst,
        "norm_mix": f32(inputs["norm_mix"][0]), "w_in": f32(inputs["w_in"][0]),
        "mu_shift": f32(inputs["mu_shift"][0]), "rwkv_w0": f32(inputs["rwkv_w0"][0]),
        "rwkv_w2": f32(inputs["rwkv_w2"][0]), "rwkv_a0": f32(inputs["rwkv_a0"][0]),
        "rwkv_a2": f32(inputs["rwkv_a2"][0]), "rwkv_g2": f32(inputs["rwkv_g2"][0]),
        "rwkv_k_k": f32(inputs["rwkv_k_k"][0]), "rwkv_k_a": f32(inputs["rwkv_k_a"][0]),
        "rwkv_r_k": f32(inputs["rwkv_r_k"][0]).reshape(512), "rwkv_ln_w": f32(inputs["rwkv_ln_w"][0]),
        "rwkv_ln_b": f32(inputs["rwkv_ln_b"][0]), "gla_wg2": f32(inputs["gla_wg2"][0]),
        "gla_bg": f32(inputs["gla_bg"][0]), "gla_norm_w": f32(inputs["gla_norm_w"][0]),
        "w_out_a": f32(inputs["w_out_a"][0]), "w_out_b": f32(inputs["w_out_b"][0]),
        "w_o": f32(inputs["w_o"][0]), "norm_ffn": f32(inputs["norm_ffn"][0]),
        "ffn_w_up": f32(inputs["ffn_w_up"][0]), "ffn_conv_w": f32(inputs["ffn_conv_w"][0]),
        "ffn_conv_b": f32(inputs["ffn_conv_b"][0]), "ffn_w_down": f32(inputs["ffn_w_down"][0]),
        "norm_final": f32(inputs["norm_final"]),
    }
    in_maps = []
    for c in range(NCORES):
        m = dict(shared)
        sl = slice(16 * c, 16 * c + 16)
        m["xp"] = f32(inputs["x_prompt"][c])
        m["xs"] = f32(inputs["x_sample"][sl])
        m["st_shift"] = f32(inputs["state_rwkv_shift"][0, sl])
        m["st_wkv"] = f32(inputs["state_rwkv_wkv"][0, sl])
        m["st_gla"] = f32(inputs["state_gla"][0, sl])
        m["st_conv"] = f32(inputs["state_ffn_conv"][0, sl])
        in_maps.append(m)
    res = run_bass_kernel_spmd(nc, in_maps, core_ids=list(range(NCORES)))
    R = res.results
    cat = lambda k: np.concatenate([np.asarray(r[k]) for r in R], axis=0)
    y_p = np.stack([np.asarray(r["yp"]) for r in R], axis=0)
    y_s = cat("ys")
    outs = (
        y_p, y_s,
        cat("o_shift_p")[None], cat("o_wkv_p")[None], cat("o_gla_p")[None], cat("o_conv_p")[None],
        cat("o_shift_s")[None], cat("o_wkv_s")[None], cat("o_gla_s")[None], cat("o_conv_s")[None],
    )
    return tuple(np.ascontiguousarray(o, dtype=np.float32) for o in outs)
```

```python
import contextlib
import numpy as np
import concourse.bass as bass
import concourse.mybir as mybir
from concourse.bass_utils import run_bass_kernel_spmd

F32 = mybir.dt.float32
BF16 = mybir.dt.bfloat16
AF = mybir.ActivationFunctionType
ALU = mybir.AluOpType
AX = mybir.AxisListType

NCORES = 8
D = 1024
NPB = 16
NSB = 4
NBLK = NPB + NSB
SHIFT = 1792
INC = 5392
FH = 2816
F2 = 5632
GLA0 = 1792
GATE0 = 3344
C0 = -0.6065306597126334
GELU_S = 1.5957691216057308

ENGS = ("pe", "act", "dve", "pool", "sp")
MAXOPS = None
SCHED = True
VERBOSE = False
PROGS = []
WINDOW = 300
FILL_MIN = 250.0
FILL_MARGIN = 80.0
FILL_MAX = 24
FILLERS = False
LINES = []
TAGS = []


class Op:
    __slots__ = ("eng", "fn", "reads", "writes", "dma", "deps", "sig", "idx",
                 "dsem", "dval", "dprev", "isout", "mm", "odeps", "cost", "start", "fin")

    def __init__(self, eng, fn, reads, writes, dma, isout, mm, cost=300.0):
        self.odeps = []
        self.cost = cost
        self.eng = eng
        self.fn = fn
        self.reads = reads
        self.writes = writes
        self.dma = dma
        self.deps = []
        self.sig = None
        self.dsem = None
        self.dval = None
        self.dprev = None
        self.isout = isout
        self.mm = mm


def _norm(k):
    return k if isinstance(k, tuple) else (k, None)


class Prog:
    def __init__(self, nc, semstack, tag, n_dma_sems=6):
        self.nc = nc
        self.ops = []
        self.n_dma_sems = n_dma_sems
        self.st = {}
        self.semstack = semstack
        self.tag = tag
        self.filler = None

    @staticmethod
    def _conf(a, b):
        return a is None or b is None or a == b

    def add(self, eng, fn, reads=(), writes=(), dma=False, isout=False, mm=False, cost=300.0):
        op = Op(eng, fn, [_norm(k) for k in reads], [_norm(k) for k in writes], dma, isout, mm, cost)
        odeps = {}
        op.idx = len(self.ops)
        if MAXOPS is not None:
            import sys as _s
            f = _s._getframe(1)
            while f is not None and f.f_code.co_name != "build":
                f = f.f_back
            LINES.append(f.f_lineno if f is not None else -1)
            TAGS.append(f.f_locals.get("b", -1) if f is not None else -1)
        deps = {}
        for (name, sub) in op.reads:
            s = self.st.setdefault(name, {"w": {}, "r": {}})
            for ws, wop in s["w"].items():
                if self._conf(ws, sub):
                    deps[wop.idx] = wop
        for (name, sub) in op.writes:
            s = self.st.setdefault(name, {"w": {}, "r": {}})
            for ws, wop in s["w"].items():
                if self._conf(ws, sub):
                    if not (op.mm and wop.mm):
                        deps[wop.idx] = wop
                    else:
                        odeps[wop.idx] = wop
            for rs, rops in s["r"].items():
                if self._conf(rs, sub):
                    for rop in rops:
                        deps[rop.idx] = rop
        for (name, sub) in op.reads:
            self.st[name]["r"].setdefault(sub, []).append(op)
        for (name, sub) in op.writes:
            s = self.st[name]
            if sub is None:
                s["w"] = {None: op}
                s["r"] = {}
            else:
                s["w"][sub] = op
                s["r"][sub] = []
        deps.pop(op.idx, None)
        op.deps = list(deps.values())
        op.odeps = [o for k, o in odeps.items() if k not in deps]
        self.ops.append(op)
        return op

    def schedule(self, window=None):
        window = window or WINDOW
        ops = self.ops
        n = len(ops)
        ndep = [0] * n
        users = [[] for _ in range(n)]
        for op in ops:
            ds = {d.idx for d in op.deps} | {d.idx for d in op.odeps}
            ndep[op.idx] = len(ds)
            for d in ds:
                users[d].append(op.idx)
        ready_t = [0.0] * n
        per = {e: [op.idx for op in ops if op.eng == e] for e in ENGS}
        head = {e: 0 for e in ENGS}
        done = [False] * n
        t_e = {e: 0.0 for e in ENGS}
        order = []
        remaining = n
        LAT = 120.0
        while remaining:
            best = None
            for e in ENGS:
                lst = per[e]
                hp = head[e]
                while hp < len(lst) and done[lst[hp]]:
                    hp += 1
                head[e] = hp
                if hp >= len(lst):
                    continue
                cnt = 0
                k = hp
                cand = None
                rdy = []
                while k < len(lst) and cnt < window:
                    i = lst[k]
                    if not done[i]:
                        cnt += 1
                        if ndep[i] == 0:
                            st = max(t_e[e], ready_t[i])
                            key = (st + 2.0 * (cnt - 1), i)
                            rdy.append((st, i))
                            if cand is None or key < cand[0]:
                                cand = (key, i, st)
                    k += 1
                if cand is not None:
                    bst = cand[2]
                    for (st, i) in rdy:
                        if i != cand[1] and st + (60.0 if ops[i].dma else ops[i].cost) <= bst:
                            cand = ((st, i), i, st)
                            break
                    if best is None or cand[0] < best[0]:
                        best = (cand[0], cand[1], cand[2], e)
            assert best is not None, "scheduler deadlock"
            _, i, st, e = best
            op = ops[i]
            op.start = st
            if op.dma:
                t_e[e] = st + 60.0
            else:
                t_e[e] = st + op.cost
            op.fin = st + op.cost
            done[i] = True
            remaining -= 1
            order.append(op)
            for u in users[i]:
                ndep[u] -= 1
                if ready_t[u] < op.fin + LAT:
                    ready_t[u] = op.fin + LAT
        if self.filler is not None:
            fn, fcost = self.filler
            out = []
            pe_end = None
            nf = 0
            for op in order:
                if op.eng == "pe":
                    if pe_end is not None:
                        gap = op.start - pe_end
                        if gap > FILL_MIN:
                            k = min(FILL_MAX, int((gap - FILL_MARGIN) / fcost))
                            for _ in range(max(0, k)):
                                f = Op("pe", fn, [], [], False, False, True, fcost)
                                f.idx = -1
                                f.start = pe_end
                                f.fin = pe_end + fcost
                                out.append(f)
                                nf += 1
                    pe_end = op.start + op.cost
                out.append(op)
            order = out
            if VERBOSE:
                print("[sched %s] fillers inserted: %d" % (self.tag, nf), flush=True)
        self.ops = order
        self.est = max(op.fin for op in order) if order else 0.0
        if VERBOSE:
            PROGS.append(self)
            busy = {e: sum(o.cost for o in order if o.eng == e and not o.dma) for e in ENGS}
            print("[sched %s] n=%d est=%.1f us busy(us): %s" % (
                self.tag, n, self.est / 1e3, " ".join("%s=%.0f" % (e, busy[e] / 1e3) for e in ENGS)), flush=True)

    def emit(self):
        nc = self.nc
        if MAXOPS is not None:
            self.ops = self.ops[:MAXOPS]
        if SCHED:
            self.schedule()
        ops = self.ops
        needed = set()
        for op in ops:
            for d in op.deps:
                needed.add(d.idx)
        cnt = {e: 0 for e in ENGS}
        for op in ops:
            if not op.dma and op.idx in needed:
                cnt[op.eng] += 1
                op.sig = cnt[op.eng]
        dcount = {e: 0 for e in ENGS}
        last_on_slot = {}
        for op in ops:
            if not op.dma:
                continue
            j = dcount[op.eng]
            dcount[op.eng] += 1
            slot = j % self.n_dma_sems
            op.dsem = (op.eng, slot)
            op.dval = 16 * (j // self.n_dma_sems + 1)
            op.dprev = last_on_slot.get(op.dsem)
            last_on_slot[op.dsem] = op
        out_ops = [op for op in ops if op.dma and op.isout]
        per_eng = {e: [op for op in ops if op.eng == e] for e in ENGS}
        es = self.semstack
        csem = {e: es.enter_context(nc.semaphore("cs%s_%s" % (self.tag, e)))
                for e in ENGS if e != "sp"}
        dsem = {}
        for e in ENGS:
            for s in range(min(self.n_dma_sems, dcount[e])):
                dsem[(e, s)] = es.enter_context(nc.semaphore("ds%s_%s%d" % (self.tag, e, s)))

        def run_engine(e, eng):
            known = {}

            def wait(key, sem, val):
                if known.get(key, 0) >= val:
                    return
                known[key] = val
                eng.wait_ge(sem, val)

            for op in per_eng[e]:
                for d in op.deps:
                    if d.dma:
                        wait(d.dsem, dsem[d.dsem], d.dval)
                    else:
                        wait(d.eng, csem[d.eng], d.sig)
                if op.dma and op.dprev is not None:
                    wait(op.dsem, dsem[op.dsem], op.dprev.dval)
                ins = op.fn(eng)
                if op.dma:
                    ins.then_inc(dsem[op.dsem], 16)
                elif op.sig is not None:
                    ins.then_inc(csem[e], 1)
            if e == "sp":
                for op in out_ops:
                    wait(op.dsem, dsem[op.dsem], op.dval)
                for key, op in last_on_slot.items():
                    wait(op.dsem, dsem[op.dsem], op.dval)

        with nc.Block() as block:
            @block.sync
            def _(eng):
                run_engine("sp", eng)

            @block.tensor
            def _(eng):
                run_engine("pe", eng)

            @block.scalar
            def _(eng):
                run_engine("act", eng)

            @block.vector
            def _(eng):
                run_engine("dve", eng)

            @block.gpsimd
            def _(eng):
                run_engine("pool", eng)


def _fsz(ap):
    n = 1
    for d in ap.shape[1:]:
        n *= int(d)
    return n


class H:
    def __init__(self, P):
        self.P = P

    def act(self, out, in_, func, r, w, **kw):
        self.P.add("act", lambda e: e.activation(out=out, in_=in_, func=func, **kw), r, w,
                   cost=220.0 + 1.05 * _fsz(out))

    def tt(self, eng, out, in0, in1, op, r, w):
        self.P.add(eng, lambda e: e.tensor_tensor(out=out, in0=in0, in1=in1, op=op), r, w,
                   cost=(100.0 + 1.05 * _fsz(out)) if eng == "dve" else (160.0 + 2.1 * _fsz(out)))

    def ts(self, eng, out, in0, s1, s2, op0, op1, r, w):
        if op1 is None:
            self.P.add(eng, lambda e: e.tensor_scalar(out=out, in0=in0, scalar1=s1, scalar2=None, op0=op0), r, w,
                       cost=100.0 + 1.05 * _fsz(out))
        else:
            self.P.add(eng, lambda e: e.tensor_scalar(out=out, in0=in0, scalar1=s1, scalar2=s2, op0=op0, op1=op1), r, w,
                       cost=100.0 + 1.05 * _fsz(out))

    def stt(self, eng, out, in0, scalar, in1, op0, op1, r, w):
        self.P.add(eng, lambda e: e.scalar_tensor_tensor(out=out, in0=in0, scalar=scalar, in1=in1, op0=op0, op1=op1), r, w,
                   cost=100.0 + 1.05 * _fsz(out))

    def cp(self, eng, out, in_, r, w):
        if eng == "act":
            self.P.add("act", lambda e: e.activation(out=out, in_=in_, func=AF.Copy), r, w,
                       cost=220.0 + 1.05 * _fsz(out))
        else:
            self.P.add(eng, lambda e: e.tensor_copy(out=out, in_=in_), r, w,
                       cost=(100.0 + 1.05 * _fsz(out)) if eng == "dve" else (160.0 + 2.1 * _fsz(out)))

    def memset(self, eng, ap, val, w):
        self.P.add(eng, lambda e: e.memset(ap, val), [], w, cost=160.0 + 1.0 * _fsz(ap))

    def recip(self, out, in_, r, w):
        self.P.add("dve", lambda e: e.reciprocal(out=out, in_=in_), r, w, cost=100.0 + 1.05 * _fsz(out))

    def scan(self, out, d0, d1, r, w):
        self.P.add("dve", lambda e: e.tensor_tensor_scan(out=out, data0=d0, data1=d1, initial=0.0,
                                                         op0=ALU.mult, op1=ALU.add), r, w,
                   cost=100.0 + 2.1 * _fsz(out))

    def rsum(self, out, in_, r, w):
        self.P.add("dve", lambda e: e.tensor_reduce(out=out, in_=in_, axis=AX.X, op=ALU.add), r, w,
                   cost=100.0 + 1.05 * _fsz(in_))

    def mm(self, out, lhsT, rhs, start, stop, r, w, tp=None):
        c = max(64.0, float(_fsz(rhs))) / 2.0 + 16.0
        if lhsT.dtype == F32:
            c *= 4.0
        if tp is None:
            self.P.add("pe", lambda e: e.matmul(out, lhsT=lhsT, rhs=rhs, start=start, stop=stop), r, w, mm=True,
                       cost=c)
        else:
            self.P.add("pe", lambda e: e.matmul(out, lhsT=lhsT, rhs=rhs, start=start, stop=stop,
                                                tile_position=tp), r, w, mm=True, cost=c)

    def tr(self, out, in_, ident, r, w):
        self.P.add("pe", lambda e: e.transpose(out=out, in_=in_, identity=ident), r, w, mm=True, cost=110.0)

    def dma(self, q, out, in_, r, w, isout=False, slow=False):
        nbytes = 1
        for d in out.shape:
            nbytes *= int(d)
        c = 2500.0 + 4.0 * nbytes / 150.0
        if slow:
            self.P.add(q, lambda e: e.dma_start(out=out, in_=in_, allow_slow_non_contiguous=True), r, w,
                       dma=True, isout=isout, cost=c)
        else:
            self.P.add(q, lambda e: e.dma_start(out=out, in_=in_), r, w, dma=True, isout=isout, cost=c)


def bc(ap, shape):
    a = ap
    while len(a.shape) < len(shape):
        a = a.unsqueeze(len(a.shape))
    return a.to_broadcast(list(shape))


C_ID = 0
C_BO = 128
C_RST = 256
C_VS = 384
C_ONE = 512
C_M4 = 640
C_SL = 1152
C_IU = 1280
NCST = 1408


def make_consts():
    c = np.zeros((128, NCST), np.float32)
    i = np.arange(128)
    same = (i[:, None] // 32) == (i[None, :] // 32)
    su = (same & (i[:, None] < i[None, :])).astype(np.float32)
    iu = (same & (i[:, None] <= i[None, :])).astype(np.float32)
    sl = (same & (i[:, None] > i[None, :])).astype(np.float32)
    c[:, C_ID:C_ID + 128] = np.eye(128, dtype=np.float32)
    c[:, C_BO:C_BO + 128] = ((i[:, None] // 64) == (i[None, :] // 64)).astype(np.float32)
    c[:, C_RST:C_RST + 128] = (i[None, :] % 32 != 0).astype(np.float32)
    c[:, C_VS:C_VS + 128] = (i[None, :] % 32 < 8).astype(np.float32)
    c[:, C_ONE:C_ONE + 128] = 1.0
    c[:, C_M4:C_M4 + 512] = np.concatenate([su, iu, su, iu], axis=1)
    c[:, C_SL:C_SL + 128] = sl
    c[:, C_IU:C_IU + 128] = iu
    return c


PR_G = 0
PR_MU = 8
PR_W0 = 22
PR_A0 = 26
PR_KK = 30
PR_KA = 34
PR_RK = 38
PR_LW = 42
PR_LB = 46
PR_BG = 50
PR_NW = 52
PR_OMKA = 53
NPRM = 64


def build(nc, dbg=None, phases="abc", blocks=None):
    BLKS = list(range(NBLK)) if blocks is None else list(blocks)
    gs = contextlib.ExitStack()

    def din(name, shape, dt=F32):
        return nc.dram_tensor(name, list(shape), dt, kind="ExternalInput").ap()

    def dout(name, shape):
        return nc.dram_tensor(name, list(shape), F32, kind="ExternalOutput").ap()

    def dscr(name, shape, dt):
        return nc.dram_tensor(name, list(shape), dt, kind="Internal").ap()

    xp = din("xp", [2048, D])
    xsm = din("xs", [16, 8, D])
    st_shift = din("st_shift", [16, SHIFT])
    st_wkv = din("st_wkv", [16, 8, 64, 64])
    st_gla = din("st_gla", [16, 4, 64, 128])
    st_conv = din("st_conv", [16, 2, F2])
    cst_d = din("cst", [128, NCST])
    norm_mix = din("norm_mix", [D])
    w_in = din("w_in", [D, INC])
    mu_shift = din("mu_shift", [SHIFT])
    rwkv_w0 = din("rwkv_w0", [512])
    rwkv_w2 = din("rwkv_w2", [64, 512])
    rwkv_a0 = din("rwkv_a0", [512])
    rwkv_a2 = din("rwkv_a2", [64, 512])
    rwkv_g2 = din("rwkv_g2", [128, 512])
    rwkv_k_k = din("rwkv_k_k", [512])
    rwkv_k_a = din("rwkv_k_a", [512])
    rwkv_r_k = din("rwkv_r_k", [512])
    rwkv_ln_w = din("rwkv_ln_w", [512])
    rwkv_ln_b = din("rwkv_ln_b", [512])
    gla_wg2 = din("gla_wg2", [16, 256])
    gla_bg = din("gla_bg", [256])
    gla_norm_w = din("gla_norm_w", [128])
    w_out_a = din("w_out_a", [512, D])
    w_out_b = din("w_out_b", [512, D])
    w_o = din("w_o", [D, D])
    norm_ffn = din("norm_ffn", [D])
    ffn_w_up = din("ffn_w_up", [D, F2])
    ffn_conv_w = din("ffn_conv_w", [3, F2])
    ffn_conv_b = din("ffn_conv_b", [F2])
    ffn_w_down = din("ffn_w_down", [FH, D])
    norm_final = din("norm_final", [D])

    yp = dout("yp", [2048, D])
    ysm = dout("ys", [16, 8, D])
    o_shift_p = dout("o_shift_p", [1, SHIFT])
    o_wkv_p = dout("o_wkv_p", [1, 8, 64, 64])
    o_gla_p = dout("o_gla_p", [1, 4, 64, 128])
    o_conv_p = dout("o_conv_p", [1, 2, F2])
    o_shift_s = dout("o_shift_s", [16, SHIFT])
    o_wkv_s = dout("o_wkv_s", [16, 8, 64, 64])
    o_gla_s = dout("o_gla_s", [16, 4, 64, 128])
    o_conv_s = dout("o_conv_s", [16, 2, F2])

    og_s = dscr("og_s", [NBLK, 128, 512], BF16)
    ob_s = dscr("ob_s", [NBLK, 128, 512], BF16)
    x1_s = dscr("x1_s", [NBLK * 128, D], F32)

    dbg_out = None
    if dbg is not None:
        dbg_out = dout("dbg", dbg["shape"])

    def geom(b):
        return (b >= NPB, 4, 32) if b >= NPB else (False, 1, 128)

    def load_consts(h, sb, q="sp"):
        cst = sb("cst", [128, NCST], F32)
        h.dma(q, cst[:], cst_d, [], ["cst"])
        idb = sb("idb", [128, 128], BF16)
        h.cp("dve", idb[:], cst[:, C_ID:C_ID + 128], ["cst"], ["idb"])
        return cst, idb

    def load_x_block(h, b, xt, xtn):
        samp, nseq, L = geom(b)
        if not samp:
            h.dma("sp", xt[:], xp[b * 128:(b + 1) * 128, :], [], [xtn])
        else:
            j = b - NPB
            h.memset("pool", xt[:], 0.0, [xtn])
            for q in range(4):
                h.dma("sp", xt[32 * q:32 * q + 8, :], xsm[4 * j + q], [], [(xtn, q)])

    def rms_to_fm(h, T, xt, xtn, gcol, par=0):
        xp_ = par if len(T["xn"]) > 1 else 0
        xn, ss, rstd, hT = T["xn"][xp_], T["ss"][par], T["rstd"][par], T["hT"][par]
        xnk, ssk, rsk, hTk = "xn%d" % xp_, "ss%d" % par, "rstd%d" % par, "hT%d" % par
        psT, idb, prm = T["psT"], T["idb"], T["prm"]
        h.act(xn[:], xt[:], AF.Square, [xtn], [xnk, ssk], accum_out=ss[:])
        h.ts("dve", rstd[:], ss[:], 1.0 / D, 1e-6, ALU.mult, ALU.add, [ssk], [rsk])
        h.act(rstd[:], rstd[:], AF.Sqrt, [rsk], [rsk])
        h.recip(rstd[:], rstd[:], [rsk], [rsk])
        h.act(xn[:], xt[:], AF.Copy, [xtn, rsk], [xnk], scale=rstd[:, 0:1])
        for c in range(8):
            h.tr(psT[:, c, :], xn[:, c * 128:(c + 1) * 128], idb[:], [xnk, "idb"], [("psT", c)])
        h.tt("dve", hT[:], psT[:], bc(prm[:, gcol:gcol + 8], [128, 8, 128]), ALU.mult,
             ["psT", "prm"], [hTk])
        return hT, hTk

    def alloc_rms(T, sb, nxn=2):
        T["xn"] = [sb("xn%d" % i, [128, D], BF16) for i in range(nxn)]
        T["ss"] = [sb("ss%d" % i, [128, 1], F32) for i in range(2)]
        T["rstd"] = [sb("rstd%d" % i, [128, 1], F32) for i in range(2)]
        T["hT"] = [sb("hT%d" % i, [128, 8, 128], BF16) for i in range(2)]

    def load_param_cols(h, prm, col, src, n):
        h.dma("sp", prm[:, col:col + n], src.rearrange("(c p) -> p c", p=128), [], [("prm", col)], slow=True)

    with contextlib.ExitStack() as ph:
      if "a" in phases:
        P = Prog(nc, gs, "a")
        h = H(P)

        def sb(name, shape, dt):
            return ph.enter_context(nc.sbuf_tensor("a_" + name, list(shape), dt))

        def psb(name, shape, dt):
            return ph.enter_context(nc.psum_tensor("a_" + name, list(shape), dt))

        T = {}
        cst, idb = load_consts(h, sb)
        T["idb"] = idb
        prm = sb("prm", [128, NPRM], F32)
        T["prm"] = prm
        load_param_cols(h, prm, PR_G, norm_mix, 8)
        load_param_cols(h, prm, PR_MU, mu_shift, 14)
        load_param_cols(h, prm, PR_W0, rwkv_w0, 4)
        load_param_cols(h, prm, PR_A0, rwkv_a0, 4)
        load_param_cols(h, prm, PR_KK, rwkv_k_k, 4)
        load_param_cols(h, prm, PR_KA, rwkv_k_a, 4)
        load_param_cols(h, prm, PR_RK, rwkv_r_k, 4)
        load_param_cols(h, prm, PR_LW, rwkv_ln_w, 4)
        load_param_cols(h, prm, PR_LB, rwkv_ln_b, 4)
        load_param_cols(h, prm, PR_BG, gla_bg, 2)
        load_param_cols(h, prm, PR_NW, gla_norm_w, 1)
        h.ts("dve", prm[:, PR_BG:PR_BG + 2], prm[:, PR_BG:PR_BG + 2], -1.0, None, ALU.mult, None,
             [("prm", PR_BG)], [("prm", PR_BG)])
        h.ts("dve", prm[:, PR_OMKA:PR_OMKA + 4], prm[:, PR_KA:PR_KA + 4], -1.0, 1.0, ALU.mult, ALU.add,
             [("prm", PR_KA)], [("prm", PR_OMKA)])

        NA1 = GATE0
        win = sb("win", [128, 8, NA1], BF16)
        w_in_v = w_in.rearrange("(c p) n -> p c n", p=128)
        for kc in range(8):
            h.dma("pool", win[:, kc, :], w_in_v[:, kc, 0:NA1], [], [("win", kc)])
        w2a2 = sb("w2a2", [128, 512], BF16)
        h.dma("pool", w2a2[0:64, :], rwkv_w2, [], [("w2a2", 0)])
        h.dma("pool", w2a2[64:128, :], rwkv_a2, [], [("w2a2", 1)])
        g2 = sb("g2", [128, 512], BF16)
        h.dma("pool", g2[:], rwkv_g2, [], ["g2"])
        wg2 = sb("wg2", [16, 256], BF16)
        h.dma("pool", wg2[:], gla_wg2, [], ["wg2"])

        xt = sb("xt", [128, D], F32)
        alloc_rms(T, sb, nxn=1)
        prw = sb("prw", [128, 14, 132], F32)
        lastc = sb("lastc", [128, 14, 1], F32)
        xs = sb("xs", [128, 14, 128], F32)
        Fm = [sb("F%d" % i, [128, 4, 128], F32) for i in range(10)]
        gTs = [sb("gT%d" % i, [128, 4, 128], F32) for i in range(2)]
        bons = [sb("bon%d" % i, [128, 4, 128], F32) for i in range(2)]
        ARs = [sb("AR%d" % i, [128, 4, 2, 128], BF16) for i in range(2)]
        Hm = [sb("Hb%d" % i, [128, 4, 128], BF16) for i in range(8)]
        tok = [sb("tok%d" % i, [128, 512], BF16) for i in range(4)]
        SCH = sb("SCH", [128, 8, 4, 128], BF16)
        INV = [sb("INV%d" % i, [128, 8, 128], BF16) for i in range(8)]
        lora_in = sb("lora_in", [128, 128], BF16)
        slg = sb("slg", [128, 128], BF16)
        Zbf = sb("Zbf", [128, 512], BF16)
        Yf = sb("Yf", [128, 512], F32)
        WTbf = sb("WTbf", [128, 4, 128], BF16)
        Ubf = sb("Ubf", [128, 512], BF16)
        Pst = sb("Pst", [128, 4, 64], F32)
        Pbf = sb("Pbf", [128, 4, 64], BF16)
        o1 = sb("o1", [128, 512], F32)
        oo = sb("oo", [128, 512], F32)
        stat = sb("stat", [128, 64], F32)
        ogbf = sb("ogbf", [128, 4, 128], BF16)
        obbf = sb("obbf", [128, 4, 128], BF16)
        Sin = sb("Sin", [64, 8, 64], F32)
        shs = [sb("shs%d" % i, [4, 512], F32) for i in range(2)]
        shf = sb("shf", [128, 14, 4], F32)
        sho = [sb("sho%d" % i, [4, 512], F32) for i in range(2)]
        lga = sb("lga", [16, 128], BF16)
        Sg = sb("Sg", [128, 2, 128], F32)
        Sgbf = sb("Sgbf", [128, 4, 2, 128], BF16)
        SgIn = sb("SgIn", [128, 4, 2, 128], F32)
        xg = sb("xg", [128, 12, 128], F32)
        Gf = [sb("G%d" % i, [128, 2, 128], F32) for i in range(6)]
        silu = sb("Gs", [128, 4, 128], F32)
        qdT, kiT, keT = [sb("Gh%d" % i, [128, 2, 128], BF16) for i in range(3)]
        vgbf = sb("Gv", [128, 4, 128], BF16)
        GS = sb("Ggs", [128, 4, 128], BF16)
        Vgtok = sb("Gt0", [128, 512], BF16)
        Ketok = sb("Gt1", [128, 256], BF16)
        o1g = sb("o1g", [128, 512], F32)
        oog = sb("oog", [128, 512], F32)
        statg = sb("statg", [128, 16], F32)

        T["psT"] = psb("psT", [128, 8, 128], BF16)
        psT = T["psT"]
        PS = [psb("ps%d" % i, [128, 512], F32) for i in range(7)]

        if VERBOSE:
            print("[A1] sbuf remaining after alloc:", nc.sbuf_bytes_remaining, flush=True)
        idf = cst[:, C_ID:C_ID + 128]
        bones = cst[:, C_BO:C_BO + 128]
        rstm = cst[:, C_RST:C_RST + 128]
        m4 = cst[:, C_M4:C_M4 + 512]
        msl = cst[:, C_SL:C_SL + 128]
        miu = cst[:, C_IU:C_IU + 128]

        h.memset("pool", Pst[:], 0.0, ["Pst"])
        h.memset("pool", Pbf[:], 0.0, ["Pbf"])
        h.memset("pool", Sg[:], 0.0, ["Sg"])
        h.memset("pool", lastc[:], 0.0, ["lastc"])

        def v4(ap, nseq, L):
            return ap.rearrange("p m (s l) -> p m s l", s=nseq)

        for b in BLKS:
            samp, nseq, L = geom(b)
            j = b - NPB
            valid = cst[:, (C_VS if samp else C_ONE):(C_VS if samp else C_ONE) + 128]
            load_x_block(h, b, xt, "xt")
            hT, hTk = rms_to_fm(h, T, xt, "xt", PR_G, b % 2)

            W = nseq * (L + 1)
            pv = prw[:, :, 0:W].rearrange("p m (s l) -> p m s l", s=nseq)
            for gi in range(4):
                ms = list(range(4 * gi, min(4 * gi + 4, 14)))
                pt = PS[5 + gi % 2]
                pn = "ps%d" % (5 + gi % 2)
                for mi, m in enumerate(ms):
                    for kc in range(8):
                        h.mm(pt[:, mi * 128:(mi + 1) * 128], win[:, kc, m * 128:(m + 1) * 128], hT[:, kc, :],
                             kc == 0, kc == 7, [("win", kc), hTk], [pn])
                nm = len(ms)
                h.cp("act", pv[:, ms[0]:ms[0] + nm, :, 1:L + 1],
                     pt[:, 0:nm * 128].rearrange("p (m s l) -> p m s l", m=nm, s=nseq),
                     [pn], ["prw"])
            gcols = [GLA0 + 128 * i for i in range(8)] + [GLA0 + 1040 + 128 * i for i in range(4)]
            for gi in range(3):
                pt, pn = PS[5 + gi % 2], "ps%d" % (5 + gi % 2)
                for mi in range(4):
                    c0 = gcols[4 * gi + mi]
                    for kc in range(8):
                        h.mm(pt[:, mi * 128:(mi + 1) * 128], win[:, kc, c0:c0 + 128], hT[:, kc, :],
                             kc == 0, kc == 7, [("win", kc), hTk], [pn])
                h.cp("act", xg[:, 4 * gi:4 * gi + 4, :], pt[:].rearrange("p (c l) -> p c l", c=4), [pn], [("xg", gi)])
            for kc in range(8):
                h.mm(PS[6][0:16, 0:128], win[:, kc, GLA0 + 1024:GLA0 + 1040], hT[:, kc, :], kc == 0, kc == 7,
                     [("win", kc), hTk], ["ps6"])
            h.cp("act", lga[:], PS[6][0:16, 0:128], ["ps6"], ["lga"])
            if not samp:
                h.cp("pool", pv[:, :, 0, 0:1], lastc[:], ["lastc"], ["prw"])
            else:
                for g4 in range(4):
                    ms = list(range(4 * g4, min(4 * g4 + 4, 14)))
                    nm = len(ms)
                    sh_, shk = shs[g4 % 2], "shs%d" % (g4 % 2)
                    h.dma("sp", sh_[:, 0:nm * 128], st_shift[4 * j:4 * j + 4, ms[0] * 128:(ms[0] + nm) * 128], [], [shk])
                    for mi, m in enumerate(ms):
                        h.tr(PS[5][:, mi * 4:(mi + 1) * 4], sh_[0:4, mi * 128:(mi + 1) * 128], idf[0:4, 0:4],
                             [shk, "cst"], ["ps5"])
                    h.cp("act", pv[:, ms[0]:ms[0] + nm, :, 0],
                         PS[5][:, 0:nm * 4].rearrange("p (m s) -> p m s", m=nm), ["ps5"], ["prw"])
            last_prompt = (b == NPB - 1)
            if samp or last_prompt:
                ncol = 4 if samp else 1
                if samp:
                    h.cp("pool", shf[:, :, 0:4], pv[:, :, :, 8], ["prw"], ["shf"])
                else:
                    h.cp("pool", shf[:, :, 0:1], pv[:, :, 0, 128:129], ["prw"], ["shf"])
                for g4 in range(4):
                    ms = list(range(4 * g4, min(4 * g4 + 4, 14)))
                    for mi, m in enumerate(ms):
                        h.tr(PS[6][0:ncol, mi * 128:(mi + 1) * 128], shf[:, m, 0:ncol], idf,
                             ["shf", "cst"], ["ps6"])
                    nm = len(ms)
                    so_, sok = sho[g4 % 2], "sho%d" % (g4 % 2)
                    h.cp("act", so_[0:ncol, 0:nm * 128], PS[6][0:ncol, 0:nm * 128], ["ps6"], [sok])
                    if samp:
                        h.dma("sp", o_shift_s[4 * j:4 * j + 4, ms[0] * 128:(ms[0] + nm) * 128], so_[0:4, 0:nm * 128],
                              [sok], [], isout=True)
                    else:
                        h.dma("sp", o_shift_p[0:1, ms[0] * 128:(ms[0] + nm) * 128], so_[0:1, 0:nm * 128],
                              [sok], [], isout=True)
            if not samp:
                h.cp("pool", lastc[:], pv[:, :, 0, 128:129], ["prw"], ["lastc"])
            xs4 = xs[:].rearrange("p m (s l) -> p m s l", s=nseq)
            cur = pv[:, :, :, 1:L + 1]
            prv = pv[:, :, :, 0:L]
            h.tt("dve", xs4[:, 0:9], prv[:, 0:9], cur[:, 0:9], ALU.subtract, ["prw"], [("xs", 0)])
            h.tt("pool", xs4[:, 9:14], prv[:, 9:14], cur[:, 9:14], ALU.subtract, ["prw"], [("xs", 1)])
            for m in range(14):
                h.stt("dve", xs4[:, m], xs4[:, m], prm[:, PR_MU + m:PR_MU + m + 1], cur[:, m], ALU.mult, ALU.add,
                      [("xs", 0 if m < 9 else 1), "prm", "prw"], [("xs", 2 + m)])
            rT = xs[:, 0:4, :]
            kT = xs[:, 4:8, :]
            vT = xs[:, 8:12, :]

            sw, aa, cum, E, Einv, Eprev, Eend, kk, kh, tmp = Fm
            n = lambda i: "F%d" % i
            N_SW, N_AA, N_CUM, N_E, N_EINV, N_EPREV, N_EEND, N_KK, N_KH, N_TMP = [n(i) for i in range(10)]
            gT, N_GT = gTs[b % 2], "gT%d" % (b % 2)
            bon, N_BON = bons[b % 2], "bon%d" % (b % 2)
            AR, N_AR = ARs[b % 2], "AR%d" % (b % 2)
            bT, kTb, BpT, KpT, vbf = Hm[0], Hm[1], Hm[2], Hm[3], Hm[4]
            h.act(lora_in[0:64, :], xs[0:64, 12, :], AF.Tanh, ["xs"], [("lora_in", 0)])
            h.cp("act", lora_in[64:128, :], xs[64:128, 12, :], ["xs"], [("lora_in", 1)])
            h.act(slg[:], xs[:, 13, :], AF.Sigmoid, ["xs"], ["slg"])
            for hg in range(4):
                h.mm(PS[5][:, hg * 128:(hg + 1) * 128], w2a2[0:64, hg * 128:(hg + 1) * 128], lora_in[0:64, :],
                     True, True, [("w2a2", 0), ("lora_in", 0)], ["ps5"])
            for hg in range(4):
                h.mm(PS[6][:, hg * 128:(hg + 1) * 128], w2a2[64:128, hg * 128:(hg + 1) * 128], lora_in[64:128, :],
                     True, True, [("w2a2", 1), ("lora_in", 1)], ["ps6"])
            for hg in range(4):
                h.act(sw[:, hg, :], PS[5][:, hg * 128:(hg + 1) * 128], AF.Sigmoid, ["ps5", "prm"], [(N_SW, hg)],
                      bias=prm[:, PR_W0 + hg:PR_W0 + hg + 1])
                h.act(aa[:, hg, :], PS[6][:, hg * 128:(hg + 1) * 128], AF.Sigmoid, ["ps6", "prm"], [(N_AA, hg)],
                      bias=prm[:, PR_A0 + hg:PR_A0 + hg + 1])
            for hg in range(4):
                h.mm(PS[5][:, hg * 128:(hg + 1) * 128], g2[:, hg * 128:(hg + 1) * 128], slg[:],
                     True, True, ["g2", "slg"], ["ps5"])
            h.cp("act", gT[:], PS[5][:].rearrange("p (c l) -> p c l", c=4), ["ps5"], [N_GT])
            h.stt("dve", sw[:], sw[:], C0, valid.unsqueeze(1).to_broadcast([128, 4, 128]),
                  ALU.mult, ALU.mult, [N_SW, "cst"], [N_SW])
            for hg in range(4):
                h.scan(cum[:, hg, :], rstm, sw[:, hg, :], [N_SW, "cst"], [(N_CUM, hg)])
            h.act(E[:], cum[:], AF.Exp, [N_CUM], [N_E])
            h.act(Einv[:], cum[:], AF.Exp, [N_CUM], [N_EINV], scale=-1.0)
            h.tt("pool", tmp[:], cum[:], sw[:], ALU.subtract, [N_CUM, N_SW], [N_TMP])
            h.act(Eprev[:], tmp[:], AF.Exp, [N_TMP], [N_EPREV])
            cum4 = cum[:].rearrange("p g (c l) -> p g c l", c=4)
            h.tt("pool", tmp[:].rearrange("p g (c l) -> p g c l", c=4),
                 cum4[:, :, :, 31:32].to_broadcast([128, 4, 4, 32]), cum4, ALU.subtract, [N_CUM], [N_TMP])
            h.act(Eend[:], tmp[:], AF.Exp, [N_TMP], [N_EEND])
            if samp:
                h.tt("pool", Eend[:], Eend[:], valid.unsqueeze(1).to_broadcast([128, 4, 128]), ALU.mult,
                     [N_EEND, "cst"], [N_EEND])
            h.tt("dve", kk[:], kT, bc(prm[:, PR_KK:PR_KK + 4], [128, 4, 128]), ALU.mult, ["xs", "prm"], [N_KK])
            h.act(tmp[:], kk[:], AF.Square, [N_KK], [N_TMP])
            for hg in range(4):
                h.mm(PS[6][:, hg * 128:(hg + 1) * 128], bones, tmp[:, hg, :], True, True, ["cst", N_TMP], ["ps6"])
            h.act(tmp[:], PS[6][:].rearrange("p (c l) -> p c l", c=4), AF.Sqrt, ["ps6"], [N_TMP])
            h.ts("dve", tmp[:], tmp[:], 1e-12, None, ALU.max, None, [N_TMP], [N_TMP])
            h.recip(tmp[:], tmp[:], [N_TMP], [N_TMP])
            h.tt("dve", kk[:], kk[:], tmp[:], ALU.mult, [N_KK, N_TMP], [N_KK])
            h.tt("pool", kh[:], aa[:], bc(prm[:, PR_KA:PR_KA + 4], [128, 4, 128]), ALU.mult, [N_AA, "prm"], [N_KH])
            h.tt("pool", kh[:], kh[:], bc(prm[:, PR_OMKA:PR_OMKA + 4], [128, 4, 128]), ALU.add, [N_KH, "prm"], [N_KH])
            h.tt("pool", kh[:], kh[:], kT, ALU.mult, [N_KH, "xs"], [N_KH])
            h.tt("dve", tmp[:], rT, bc(prm[:, PR_RK:PR_RK + 4], [128, 4, 128]), ALU.mult, ["xs", "prm"], [N_TMP])
            h.tt("dve", tmp[:], tmp[:], kh[:], ALU.mult, [N_TMP, N_KH], [N_TMP])
            for hg in range(4):
                h.mm(PS[5][:, hg * 128:(hg + 1) * 128], bones, tmp[:, hg, :], True, True, ["cst", N_TMP], ["ps5"])
            h.tt("dve", bon[:], PS[5][:].rearrange("p (c l) -> p c l", c=4), vT, ALU.mult, ["ps5", "xs"], [N_BON])
            h.tt("pool", aa[:], aa[:], kk[:], ALU.mult, [N_AA, N_KK], [N_AA])
            h.tt("dve", AR[:, :, 1, :], rT, E[:], ALU.mult, ["xs", N_E], [(N_AR, 1)])
            h.stt("dve", AR[:, :, 0, :], kk[:], -1.0, Eprev[:], ALU.mult, ALU.mult, [N_KK, N_EPREV], [(N_AR, 0)])
            h.tt("dve", bT[:], aa[:], Einv[:], ALU.mult, [N_AA, N_EINV], ["Hb0"])
            h.tt("dve", kTb[:], kh[:], Einv[:], ALU.mult, [N_KH, N_EINV], ["Hb1"])
            h.tt("pool", BpT[:], aa[:], Eend[:], ALU.mult, [N_AA, N_EEND], ["Hb2"])
            h.tt("pool", KpT[:], kh[:], Eend[:], ALU.mult, [N_KH, N_EEND], ["Hb3"])
            h.cp("pool", vbf[:], vT, ["xs"], ["Hb4"])
            h.cp("pool", stat[:, 0:16].rearrange("p (g c) -> p g c", g=4),
                 E[:].rearrange("p g (c l) -> p g c l", c=4)[:, :, :, 31], [N_E], [("stat", 0)])
            gam = stat[:, 0:16].rearrange("p (g c) -> p g c", g=4)

            for hd in range(8):
                hg, pb = hd // 2, 64 * (hd % 2)
                px = PS[hd % 2]
                pxn = "ps%d" % (hd % 2)
                h.mm(px[:, 0:256], bT[pb:pb + 64, hg, :], AR[pb:pb + 64, hg, :, :].rearrange("p a l -> p (a l)"),
                     True, True, ["Hb0", N_AR], [pxn])
                h.mm(px[:, 256:512], kTb[pb:pb + 64, hg, :], AR[pb:pb + 64, hg, :, :].rearrange("p a l -> p (a l)"),
                     True, True, ["Hb1", N_AR], [pxn])
                h.tt("dve", SCH[:, hd, :, :].rearrange("p a l -> p (a l)"), px[:], m4, ALU.mult,
                     [pxn, "cst"], [("SCH", hd)])
            Ac, Nn, An, Xc, Xtc, Xn, Xtn, Nc2 = INV
            NI = ["INV%d" % i for i in range(8)]
            Ac4 = Ac[:].rearrange("p (g a) l -> p g a l", a=2)
            for h2 in range(2):
                pt = PS[2 + h2]
                ptn = "ps%d" % (2 + h2)
                pb = 64 * h2
                for hg in range(4):
                    h.mm(pt[:, hg * 128:(hg + 1) * 128], AR[pb:pb + 64, hg, 0, :], bT[pb:pb + 64, hg, :],
                         True, True, [N_AR, "Hb0"], [ptn])
                h.tt("dve", Ac4[:, :, h2, :], pt[:].rearrange("p (q l) -> p q l", q=4),
                     msl.unsqueeze(1).to_broadcast([128, 4, 128]), ALU.mult, [ptn, "cst"], [NI[0]])
            h.tt("pool", Xc[:], SCH[:, :, 0, :], idb[:].unsqueeze(1).to_broadcast([128, 8, 128]), ALU.add,
                 ["SCH", "idb"], [NI[3]])
            h.tt("pool", Xtc[:], Ac[:], idb[:].unsqueeze(1).to_broadcast([128, 8, 128]), ALU.add,
                 [NI[0], "idb"], [NI[4]])

            Ncur_ap = lambda hd: SCH[:, hd, 0, :]
            Ncur_key = "SCH"
            Acur, Acur_key = Ac, NI[0]
            Xcur, Xcur_key, Xtcur, Xtcur_key = Xc, NI[3], Xtc, NI[4]
            Nnext = [(Nn, NI[1]), (Nc2, NI[7])]
            Anext = [(An, NI[2]), (Ac, NI[0])]
            Xnext = [(Xn, NI[5]), (Xc, NI[3])]
            Xtnext = [(Xtn, NI[6]), (Xtc, NI[4])]
            for lvl in range(4):
                last = (lvl == 3)
                Nx, Nxk = Nnext[lvl % 2]
                Ax, Axk = Anext[lvl % 2]
                Xx, Xxk = Xnext[lvl % 2]
                Xtx, Xtxk = Xtnext[lvl % 2]
                for g2i in range(2):
                    pa, pan = PS[2 * g2i], "ps%d" % (2 * g2i)
                    pbk, pbn = PS[2 * g2i + 1], "ps%d" % (2 * g2i + 1)
                    for q in range(4):
                        hd = 4 * g2i + q
                        h.mm(pa[:, q * 128:(q + 1) * 128], Acur[:, hd, :], Ncur_ap(hd), True, True,
                             [Acur_key, Ncur_key], [pan])
                    h.cp("act", Nx[:, 4 * g2i:4 * g2i + 4, :], pa[:].rearrange("p (q l) -> p q l", q=4),
                         [pan], [(Nxk, g2i)])
                    if not last:
                        for q in range(4):
                            hd = 4 * g2i + q
                            h.mm(pbk[:, q * 128:(q + 1) * 128], Ncur_ap(hd), Acur[:, hd, :], True, True,
                                 [Acur_key, Ncur_key], [pbn])
                        h.cp("act", Ax[:, 4 * g2i:4 * g2i + 4, :], pbk[:].rearrange("p (q l) -> p q l", q=4),
                             [pbn], [(Axk, g2i)])
                for g2i in range(2):
                    pa, pan = PS[4], "ps4"
                    for q in range(4):
                        hd = 4 * g2i + q
                        h.mm(pa[:, q * 128:(q + 1) * 128], Xtcur[:, hd, :], Nx[:, hd, :], True, True,
                             [Xtcur_key, (Nxk, g2i)], [pan])
                    h.tt("dve", Xx[:, 4 * g2i:4 * g2i + 4, :], pa[:].rearrange("p (q l) -> p q l", q=4),
                         Xcur[:, 4 * g2i:4 * g2i + 4, :], ALU.add, [pan, Xcur_key], [(Xxk, g2i)])
                if not last:
                    for g2i in range(2):
                        pa, pan = PS[2 * g2i], "ps%d" % (2 * g2i)
                        for q in range(4):
                            hd = 4 * g2i + q
                            h.mm(pa[:, q * 128:(q + 1) * 128], Nx[:, hd, :], Xtcur[:, hd, :], True, True,
                                 [Xtcur_key, (Nxk, g2i)], [pan])
                        h.tt("dve", Xtx[:, 4 * g2i:4 * g2i + 4, :], pa[:].rearrange("p (q l) -> p q l", q=4),
                             Xtcur[:, 4 * g2i:4 * g2i + 4, :], ALU.add, [pan, Xtcur_key], [(Xtxk, g2i)])
                Ncur_ap = (lambda t: (lambda hd: t[:, hd, :]))(Nx)
                Ncur_key = Nxk
                Acur, Acur_key = Ax, Axk
                Xcur, Xcur_key = Xx, Xxk
                Xtcur, Xtcur_key = Xtx, Xtxk
            X4, X4k = Xcur, Xcur_key

            Atok, Bptok, Kptok, Vtok = tok
            srcs = [(AR[:, :, 0, :], N_AR, Atok, "tok0"), (BpT[:], "Hb2", Bptok, "tok1"),
                    (KpT[:], "Hb3", Kptok, "tok2"), (vbf[:], "Hb4", Vtok, "tok3")]
            for si in range(0, 4, 2):
                for u in range(2):
                    src, srck, dst, dstk = srcs[si + u]
                    for hg in range(4):
                        h.tr(psT[:, u * 4 + hg, :], src[:, hg, :], idb[:], [srck, "idb"], [("psT", u * 4 + hg)])
                for u in range(2):
                    src, srck, dst, dstk = srcs[si + u]
                    h.cp("act", dst[:], psT[:, u * 4:(u + 1) * 4, :].rearrange("p c l -> p (c l)"),
                         ["psT"], [dstk])

            for hd in range(8):
                h.mm(PS[0][:, hd * 64:(hd + 1) * 64], SCH[:, hd, 2, :], Vtok[:, hd * 64:(hd + 1) * 64],
                     True, True, ["SCH", "tok3"], ["ps0"])
            h.cp("act", Zbf[:], PS[0][:], ["ps0"], ["Zbf"])
            for hd in range(8):
                h.mm(PS[1][:, hd * 64:(hd + 1) * 64], X4[:, hd, :], Zbf[:, hd * 64:(hd + 1) * 64],
                     True, True, [X4k, "Zbf"], ["ps1"])
            h.cp("act", Yf[:], PS[1][:], ["ps1"], ["Yf"])
            for hd in range(8):
                hg, pb = hd // 2, 64 * (hd % 2)
                h.mm(PS[2][pb:pb + 64, hg * 128:(hg + 1) * 128], Atok[:, hd * 64:(hd + 1) * 64], X4[:, hd, :],
                     True, True, ["tok0", X4k], ["ps2"])
            h.cp("act", WTbf[:], PS[2][:].rearrange("p (c l) -> p c l", c=4), ["ps2"], ["WTbf"])

            psUs, psUn = (PS[0], PS[1]), ("ps0", "ps1")
            psOs, psOn = (PS[2], PS[3]), ("ps2", "ps3")
            psPn = PS[4]
            Ubf4 = Ubf[:].rearrange("p (g a v) -> p g a v", g=4, a=2)
            Yf4 = Yf[:].rearrange("p (g a v) -> p g a v", g=4, a=2)
            for c in range(4):
                s0 = 32 * c
                if samp:
                    seq = 4 * j + c
                    h.dma("sp", Sin[:], st_wkv[seq].rearrange("h v k -> v h k"), [], ["Sin"])
                    for hg in range(4):
                        h.tr(PS[4][:, hg * 64:(hg + 1) * 64],
                             Sin[:, 2 * hg:2 * hg + 2, :].rearrange("p a k -> p (a k)"), idf[0:64, 0:64],
                             ["Sin", "cst"], ["ps4"])
                    h.cp("act", Pst[:], PS[4][:, 0:256].rearrange("p (c v) -> p c v", c=4), ["ps4"], ["Pst"])
                    h.cp("dve", Pbf[:], Pst[:], ["Pst"], ["Pbf"])
                for hd in range(8):
                    hg, h2 = hd // 2, hd % 2
                    pb = 64 * h2
                    h.mm(psUs[h2][s0:s0 + 32, hg * 64:(hg + 1) * 64], WTbf[pb:pb + 64, hg, s0:s0 + 32],
                         Pbf[pb:pb + 64, hg, :], True, True, ["WTbf", "Pbf"], [psUn[h2]], tp=(pb, s0))
                for hd in range(8):
                    hg, h2 = hd // 2, hd % 2
                    pb = 64 * h2
                    h.mm(psOs[h2][s0:s0 + 32, hg * 64:(hg + 1) * 64], AR[pb:pb + 64, hg, 1, s0:s0 + 32],
                         Pbf[pb:pb + 64, hg, :], True, True, [N_AR, "Pbf"], [psOn[h2]], tp=(pb, s0))
                for h2 in range(2):
                    h.tt("dve", Ubf4[s0:s0 + 32, :, h2, :],
                         psUs[h2][s0:s0 + 32, 0:256].rearrange("p (g v) -> p g v", g=4),
                         Yf4[s0:s0 + 32, :, h2, :], ALU.add, [psUn[h2], "Yf"], [("Ubf", c)])
                for hd in range(8):
                    hg, pb = hd // 2, 64 * (hd % 2)
                    h.mm(psPn[pb:pb + 64, hg * 64:(hg + 1) * 64], Bptok[s0:s0 + 32, hd * 64:(hd + 1) * 64],
                         Ubf[s0:s0 + 32, hd * 64:(hd + 1) * 64], True, False, ["tok1", ("Ubf", c)], ["ps4"],
                         tp=(s0, pb))
                    h.mm(psPn[pb:pb + 64, hg * 64:(hg + 1) * 64], Kptok[s0:s0 + 32, hd * 64:(hd + 1) * 64],
                         Vtok[s0:s0 + 32, hd * 64:(hd + 1) * 64], False, True, ["tok2", "tok3"], ["ps4"],
                         tp=(s0, pb))
                for hg in range(4):
                    h.stt("dve", Pst[:, hg, :], Pst[:, hg, :], gam[:, hg, c:c + 1], psPn[:, hg * 64:(hg + 1) * 64],
                          ALU.mult, ALU.add, ["Pst", ("stat", 0), "ps4"], ["Pst"])
                if not samp and not (b == NPB - 1 and c == 3):
                    h.cp("act", Pbf[:], Pst[:], ["Pst"], ["Pbf"])
                if samp or (b == NPB - 1 and c == 3):
                    for hg in range(4):
                        h.tr(PS[4][0:64, hg * 128:(hg + 1) * 128], Pst[:, hg, :], idf, ["Pst", "cst"], ["ps4"])
                    h.cp("act", Sin[:].rearrange("p h k -> p (h k)"), PS[4][0:64, :], ["ps4"], ["Sin"])
                    dst = o_wkv_s[4 * j + c] if samp else o_wkv_p[0]
                    h.dma("sp", dst.rearrange("h v k -> v h k"), Sin[:], ["Sin"], [], isout=True)
            for hd in range(8):
                h.mm(PS[0][:, hd * 64:(hd + 1) * 64], SCH[:, hd, 1, :], Ubf[:, hd * 64:(hd + 1) * 64],
                     True, False, ["SCH", "Ubf"], ["ps0"])
                h.mm(PS[0][:, hd * 64:(hd + 1) * 64], SCH[:, hd, 3, :], Vtok[:, hd * 64:(hd + 1) * 64],
                     False, True, ["SCH", "tok3"], ["ps0"])
            o14 = o1[:].rearrange("p (g a v) -> p g a v", g=4, a=2)
            for h2 in range(2):
                h.cp("act", o14[:, :, h2, :], psOs[h2][:, 0:256].rearrange("p (g v) -> p g v", g=4),
                     [psOn[h2]], ["o1"])
            h.tt("dve", oo[:], o1[:], PS[0][:], ALU.add, ["o1", "ps0"], ["oo"])

            oo3 = oo[:].rearrange("p (h v) -> p h v", h=8)
            osq = o1
            h.act(osq[:], oo[:], AF.Square, ["oo"], ["o1"])
            s1, s2, mean, msq, rs, nb = (stat[:, 16:24], stat[:, 24:32], stat[:, 32:40], stat[:, 40:48],
                                         stat[:, 48:56], stat[:, 56:64])
            h.rsum(s1, oo3, ["oo"], [("stat", 1)])
            h.rsum(s2, osq[:].rearrange("p (h v) -> p h v", h=8), ["o1"], [("stat", 2)])
            h.ts("dve", mean, s1, 1.0 / 64, None, ALU.mult, None, [("stat", 1)], [("stat", 3)])
            h.tt("dve", msq, mean, mean, ALU.mult, [("stat", 3)], [("stat", 4)])
            h.stt("dve", rs, s2, 1.0 / 64, msq, ALU.mult, ALU.subtract, [("stat", 2), ("stat", 4)], [("stat", 5)])
            h.ts("dve", rs, rs, 64e-5, None, ALU.add, None, [("stat", 5)], [("stat", 5)])
            h.act(rs, rs, AF.Sqrt, [("stat", 5)], [("stat", 5)])
            h.recip(rs, rs, [("stat", 5)], [("stat", 5)])
            h.tt("dve", oo3, oo3, mean.unsqueeze(2).to_broadcast([128, 8, 64]), ALU.subtract,
                 ["oo", ("stat", 3)], ["oo"])
            h.tt("dve", oo3, oo3, rs.unsqueeze(2).to_broadcast([128, 8, 64]), ALU.mult,
                 ["oo", ("stat", 5)], ["oo"])
            for hg in range(4):
                h.tr(PS[0][:, hg * 128:(hg + 1) * 128], oo[:, hg * 128:(hg + 1) * 128], idf, ["oo", "cst"], ["ps0"])
            ps0v = PS[0][:].rearrange("p (c l) -> p c l", c=4)
            t2 = o1[:].rearrange("p (c l) -> p c l", c=4)
            h.tt("dve", t2, ps0v, bc(prm[:, PR_LW:PR_LW + 4], [128, 4, 128]), ALU.mult, ["ps0", "prm"], ["o1"])
            h.tt("pool", t2, t2, bc(prm[:, PR_LB:PR_LB + 4], [128, 4, 128]), ALU.add, ["o1", "prm"], ["o1"])
            h.tt("pool", t2, t2, bon[:], ALU.add, ["o1", N_BON], ["o1"])
            h.tt("dve", ogbf[:], t2, gT[:], ALU.mult, ["o1", N_GT], ["ogbf"])
            h.dma("sp", og_s[b], ogbf[:].rearrange("p c l -> p (c l)"), ["ogbf"], [])

            qT, kgT, vgT, ogT = xg[:, 0:2, :], xg[:, 2:4, :], xg[:, 4:8, :], xg[:, 8:12, :]
            for c2 in range(2):
                h.mm(PS[5][:, c2 * 128:(c2 + 1) * 128], wg2[0:16, c2 * 128:(c2 + 1) * 128], lga[:], True, True,
                     ["wg2", "lga"], ["ps5"])
            la_, cg, Eg, Eginv, Egend, gtmp = Gf
            for c2 in range(2):
                h.act(la_[:, c2, :], PS[5][:, c2 * 128:(c2 + 1) * 128], AF.Exp, ["ps5", "prm"], [("G0", c2)],
                      scale=-1.0, bias=prm[:, PR_BG + c2:PR_BG + c2 + 1])
            h.ts("dve", la_[:], la_[:], 1.0, None, ALU.add, None, ["G0"], ["G0"])
            h.act(la_[:], la_[:], AF.Ln, ["G0"], ["G0"])
            h.stt("dve", la_[:], la_[:], -1.0 / 16.0,
                  valid.unsqueeze(1).to_broadcast([128, 2, 128]), ALU.mult, ALU.mult, ["G0", "cst"], ["G0"])
            for c2 in range(2):
                h.scan(cg[:, c2, :], rstm, la_[:, c2, :], ["G0", "cst"], [("G1", c2)])
            h.act(Eg[:], cg[:], AF.Exp, ["G1"], ["G2"])
            h.act(Eginv[:], cg[:], AF.Exp, ["G1"], ["G3"], scale=-1.0)
            cg4 = cg[:].rearrange("p g (c l) -> p g c l", c=4)
            h.tt("pool", gtmp[:].rearrange("p g (c l) -> p g c l", c=4),
                 cg4[:, :, :, 31:32].to_broadcast([128, 2, 4, 32]), cg4, ALU.subtract, ["G1"], ["G5"])
            h.act(Egend[:], gtmp[:], AF.Exp, ["G5"], ["G4"])
            if samp:
                h.tt("pool", Egend[:], Egend[:], valid.unsqueeze(1).to_broadcast([128, 2, 128]),
                     ALU.mult, ["G4", "cst"], ["G4"])
            h.cp("pool", statg[:, 0:8].rearrange("p (g c) -> p g c", g=2),
                 Eg[:].rearrange("p g (c l) -> p g c l", c=4)[:, :, :, 31], ["G2"], [("statg", 0)])
            gamg = statg[:, 0:8].rearrange("p (g c) -> p g c", g=2)
            h.stt("dve", qdT[:], qT, 0.125, Eg[:], ALU.mult, ALU.mult, [("xg", 0), "G2"], ["Gh0"])
            h.tt("pool", kiT[:], kgT, Eginv[:], ALU.mult, [("xg", 0), "G3"], ["Gh1"])
            h.tt("pool", keT[:], kgT, Egend[:], ALU.mult, [("xg", 0), "G4"], ["Gh2"])
            h.cp("pool", vgbf[:], vgT, [("xg", 1)], ["Gv"])
            h.act(silu[:], ogT, AF.Silu, [("xg", 2)], ["Gs"])
            GS4 = GS[:].rearrange("p (g a) l -> p g a l", a=2)
            for h2 in range(2):
                pb = 64 * h2
                pt, ptn = PS[5 + h2], "ps%d" % (5 + h2)
                for c2 in range(2):
                    h.mm(pt[:, c2 * 128:(c2 + 1) * 128], kiT[pb:pb + 64, c2, :], qdT[pb:pb + 64, c2, :], True, True,
                         ["Gh1", "Gh0"], [ptn])
                h.tt("dve", GS4[:, :, h2, :], pt[:, 0:256].rearrange("p (q l) -> p q l", q=2),
                     miu.unsqueeze(1).to_broadcast([128, 2, 128]), ALU.mult, [ptn, "cst"], ["Ggs"])
            for hg in range(4):
                h.tr(psT[:, hg, :], vgbf[:, hg, :], idb[:], ["Gv", "idb"], [("psT", hg)])
            for c2 in range(2):
                h.tr(psT[:, 4 + c2, :], keT[:, c2, :], idb[:], ["Gh2", "idb"], [("psT", 4 + c2)])
            h.cp("act", Vgtok[:], psT[:, 0:4, :].rearrange("p c l -> p (c l)"), ["psT"], ["Gt0"])
            h.cp("act", Ketok[:], psT[:, 4:6, :].rearrange("p c l -> p (c l)"), ["psT"], ["Gt1"])
            if samp:
                h.dma("sp", SgIn[:], st_gla[4 * j:4 * j + 4].rearrange("q (c2 h2) k v -> (h2 k) q c2 v", h2=2),
                      [], ["SgIn"])
                h.cp("act", Sgbf[:], SgIn[:], ["SgIn"], ["Sgbf"])
            else:
                h.cp("act", Sgbf[:, 0, :, :], Sg[:], ["Sg"], [("Sgbf", 0)])
            for c in range(4):
                s0 = 32 * c
                pt, pn = PS[5 + c % 2], "ps%d" % (5 + c % 2)
                for hd in range(4):
                    c2, pb = hd // 2, 64 * (hd % 2)
                    off = c2 * 128
                    h.mm(pt[pb:pb + 64, off:off + 128], Ketok[s0:s0 + 32, hd * 64:(hd + 1) * 64],
                         Vgtok[s0:s0 + 32, hd * 128:(hd + 1) * 128], True, True, ["Gt1", "Gt0"], [pn],
                         tp=(s0, pb))
                dv = pt[:, 0:256].rearrange("p (g v) -> p g v", g=2)
                gb = gamg[:, :, c:c + 1].to_broadcast([128, 2, 128])
                for c2 in range(2):
                    if samp:
                        h.stt("dve", SgIn[:, c, c2, :], SgIn[:, c, c2, :], gamg[:, c2, c:c + 1], dv[:, c2, :],
                              ALU.mult, ALU.add, [("SgIn", c), ("statg", 0), pn], [("SgIn", c)])
                    else:
                        h.stt("dve", Sg[:, c2, :], Sg[:, c2, :], gamg[:, c2, c:c + 1], dv[:, c2, :],
                              ALU.mult, ALU.add, ["Sg", ("statg", 0), pn], ["Sg"])
                if (not samp) and c < 3:
                    h.cp("act", Sgbf[:, c + 1, :, :], Sg[:], ["Sg"], [("Sgbf", c + 1)])
            if samp:
                h.dma("sp", o_gla_s[4 * j:4 * j + 4].rearrange("q (c2 h2) k v -> (h2 k) q c2 v", h2=2),
                      SgIn[:], ["SgIn"], [], isout=True)
            elif b == NPB - 1:
                h.dma("sp", o_gla_p[0].rearrange("(c2 h2) k v -> (h2 k) c2 v", h2=2), Sg[:], ["Sg"], [],
                      isout=True)
            for c in range(4):
                s0 = 32 * c
                for hd in range(4):
                    c2, h2 = hd // 2, hd % 2
                    pb = 64 * h2
                    h.mm(PS[5 + h2][s0:s0 + 32, c2 * 128:(c2 + 1) * 128], qdT[pb:pb + 64, c2, s0:s0 + 32],
                         Sgbf[pb:pb + 64, c, c2, :], True, True, ["Gh0", "Sgbf"], ["ps%d" % (5 + h2)], tp=(pb, s0))
            o1gv = o1g[:].rearrange("p (g a v) -> p g a v", g=2, a=2)
            for h2 in range(2):
                h.cp("act", o1gv[:, :, h2, :], PS[5 + h2][:, 0:256].rearrange("p (g v) -> p g v", g=2),
                     ["ps%d" % (5 + h2)], ["o1g"])
            for hd in range(4):
                h.mm(PS[5][:, hd * 128:(hd + 1) * 128], GS[:, hd, :], Vgtok[:, hd * 128:(hd + 1) * 128], True, True,
                     ["Ggs", "Gt0"], ["ps5"])
            h.tt("dve", oog[:], o1g[:], PS[5][:], ALU.add, ["o1g", "ps5"], ["oog"])
            oo4 = oog[:].rearrange("p (h v) -> p h v", h=4)
            h.act(o1g[:], oog[:], AF.Square, ["oog"], ["o1g"])
            gs2, grs = statg[:, 8:12], statg[:, 12:16]
            h.rsum(gs2, o1g[:].rearrange("p (h v) -> p h v", h=4), ["o1g"], [("statg", 1)])
            h.ts("dve", grs, gs2, 1.0 / 128, 1e-6, ALU.mult, ALU.add, [("statg", 1)], [("statg", 2)])
            h.act(grs, grs, AF.Sqrt, [("statg", 2)], [("statg", 2)])
            h.recip(grs, grs, [("statg", 2)], [("statg", 2)])
            h.tt("dve", oo4, oo4, grs.unsqueeze(2).to_broadcast([128, 4, 128]), ALU.mult, ["oog", ("statg", 2)], ["oog"])
            for hd in range(4):
                h.tr(PS[6][:, hd * 128:(hd + 1) * 128], oog[:, hd * 128:(hd + 1) * 128], idf, ["oog", "cst"], ["ps6"])
            h.stt("dve", obbf[:], PS[6][:].rearrange("p (c l) -> p c l", c=4), prm[:, PR_NW:PR_NW + 1], silu[:],
                  ALU.mult, ALU.mult, ["ps6", "prm", "Gs"], ["obbf"])
            h.dma("sp", ob_s[b], obbf[:].rearrange("p c l -> p (c l)"), ["obbf"], [])

            if dbg is not None and dbg.get("blk") == b and dbg.get("phase") == "a1":
                src, keys = dbg["fn"](dict(locals()))
                h.dma("sp", dbg_out, src, keys, [], isout=True)
        P.emit()

    with contextlib.ExitStack() as ph:
      if "b" in phases:
        P = Prog(nc, gs, "b")
        h = H(P)

        def sb(name, shape, dt):
            return ph.enter_context(nc.sbuf_tensor("b_" + name, list(shape), dt))

        def psb(name, shape, dt):
            return ph.enter_context(nc.psum_tensor("b_" + name, list(shape), dt))

        T = {}
        cst, idb = load_consts(h, sb)
        T["idb"] = idb
        prm = sb("prm", [128, NPRM], F32)
        T["prm"] = prm
        load_param_cols(h, prm, PR_G, norm_mix, 8)
        wing = sb("wing", [128, 8, 2048], BF16)
        w_in_v = w_in.rearrange("(c p) n -> p c n", p=128)
        for kc in range(8):
            h.dma("pool", wing[:, kc, :], w_in_v[:, kc, GATE0:INC], [], [("wing", kc)])
        woa = sb("woa", [128, 4, D], BF16)
        wob = sb("wob", [128, 4, D], BF16)
        wo = sb("wo", [128, 8, D], BF16)
        h.dma("pool", woa[:], w_out_a.rearrange("(c p) n -> p c n", p=128), [], ["woa"])
        h.dma("pool", wob[:], w_out_b.rearrange("(c p) n -> p c n", p=128), [], ["wob"])
        for kc in range(8):
            h.dma("pool", wo[:, kc, :], w_o.rearrange("(c p) n -> p c n", p=128)[:, kc, :], [], [("wo", kc)])
        xts = [sb("xt%d" % i, [128, D], F32) for i in range(4)]
        alloc_rms(T, sb)
        ogbs = [sb("ogb%d" % i, [128, 4, 128], BF16) for i in range(2)]
        obbs = [sb("obb%d" % i, [128, 4, 128], BF16) for i in range(2)]
        sgas = [sb("sga%d" % i, [128, 8, 128], F32) for i in range(2)]
        sgbs = [sb("sgb%d" % i, [128, 8, 128], F32) for i in range(2)]
        tas = [sb("ta%d" % i, [128, 8, 128], F32) for i in range(2)]
        mgs = [sb("mg%d" % i, [128, 8, 128], BF16) for i in range(2)]
        x1ts = [sb("x1t%d" % i, [128, D], F32) for i in range(2)]
        T["psT"] = psb("psT", [128, 8, 128], BF16)
        PS = [psb("ps%d" % i, [128, 512], F32) for i in range(7)]
        if FILLERS:
            _pf, _id = PS[6], idb
            P.filler = ((lambda e: e.matmul(_pf[:, 0:128], lhsT=_id[:], rhs=_id[:], start=True, stop=True)), 70.0)
        for b in BLKS:
            xt, xtn = xts[b % 4], "xt%d" % (b % 4)
            load_x_block(h, b, xt, xtn)
            p2 = b % 2
            ogb, obb, sga, sgb, ta, mg, x1t = ogbs[p2], obbs[p2], sgas[p2], sgbs[p2], tas[p2], mgs[p2], x1ts[p2]
            K_ogb, K_obb, K_sga, K_sgb, K_ta, K_mg, K_x1t = ["%s%d" % (nm, p2) for nm in
                                                            ("ogb", "obb", "sga", "sgb", "ta", "mg", "x1t")]
            h.dma("sp", ogb[:].rearrange("p c l -> p (c l)"), og_s[b], [], [K_ogb])
            h.dma("sp", obb[:].rearrange("p c l -> p (c l)"), ob_s[b], [], [K_obb])
            hT, hTk = rms_to_fm(h, T, xt, xtn, PR_G, b % 2)
            for half, dst, dk in ((0, sga, K_sga), (1, sgb, K_sgb)):
                for gi in range(2):
                    pt, pn = PS[gi], "ps%d" % gi
                    for mi in range(4):
                        c0 = half * 1024 + (4 * gi + mi) * 128
                        for kc in range(8):
                            h.mm(pt[:, mi * 128:(mi + 1) * 128], wing[:, kc, c0:c0 + 128], hT[:, kc, :],
                                 kc == 0, kc == 7, [("wing", kc), hTk], [pn])
                    h.act(dst[:, 4 * gi:4 * gi + 4, :], pt[:].rearrange("p (c l) -> p c l", c=4), AF.Sigmoid,
                          [pn], [(dk, gi)])
            for gi in range(2):
                pt, pn = PS[2 + gi], "ps%d" % (2 + gi)
                for mi in range(4):
                    m = 4 * gi + mi
                    for kc in range(4):
                        h.mm(pt[:, mi * 128:(mi + 1) * 128], woa[:, kc, m * 128:(m + 1) * 128], ogb[:, kc, :],
                             kc == 0, kc == 3, ["woa", K_ogb], [pn])
                h.tt("dve", ta[:, 4 * gi:4 * gi + 4, :], pt[:].rearrange("p (c l) -> p c l", c=4),
                     sga[:, 4 * gi:4 * gi + 4, :], ALU.mult, [pn, (K_sga, gi)], [(K_ta, gi)])
            for gi in range(2):
                pt, pn = PS[4 + gi], "ps%d" % (4 + gi)
                for mi in range(4):
                    m = 4 * gi + mi
                    for kc in range(4):
                        h.mm(pt[:, mi * 128:(mi + 1) * 128], wob[:, kc, m * 128:(m + 1) * 128], obb[:, kc, :],
                             kc == 0, kc == 3, ["wob", K_obb], [pn])
                h.tt("dve", sgb[:, 4 * gi:4 * gi + 4, :], pt[:].rearrange("p (c l) -> p c l", c=4),
                     sgb[:, 4 * gi:4 * gi + 4, :], ALU.mult, [pn, (K_sgb, gi)], [(K_sgb, gi)])
                h.tt("pool", mg[:, 4 * gi:4 * gi + 4, :], ta[:, 4 * gi:4 * gi + 4, :],
                     sgb[:, 4 * gi:4 * gi + 4, :], ALU.add, [(K_ta, gi), (K_sgb, gi)], [(K_mg, gi)])
            for nh in range(2):
                pt, pn = PS[2 + nh], "ps%d" % (2 + nh)
                for kc in range(8):
                    h.mm(pt[:], mg[:, kc, :], wo[:, kc, nh * 512:(nh + 1) * 512], kc == 0, kc == 7,
                         [K_mg, ("wo", kc)], [pn])
                h.tt("dve", x1t[:, nh * 512:(nh + 1) * 512], pt[:], xt[:, nh * 512:(nh + 1) * 512], ALU.add,
                     [pn, xtn], [(K_x1t, nh)])
            h.dma("sp", x1_s[b * 128:(b + 1) * 128, :], x1t[:], [K_x1t], [])
        P.emit()

    with contextlib.ExitStack() as ph:
      if "c" in phases:
        P = Prog(nc, gs, "c")
        h = H(P)

        def sb(name, shape, dt):
            return ph.enter_context(nc.sbuf_tensor("c_" + name, list(shape), dt))

        def psb(name, shape, dt):
            return ph.enter_context(nc.psum_tensor("c_" + name, list(shape), dt))

        T = {}
        cst, idb = load_consts(h, sb)
        T["idb"] = idb
        idf = cst[:, C_ID:C_ID + 128]
        prm = sb("prm", [128, 8], F32)
        T["prm"] = prm
        load_param_cols(h, prm, 0, norm_ffn, 8)
        cw = sb("cw", [128, 4, 44], F32)
        for jx in range(3):
            h.dma("sp", cw[:, jx, :], ffn_conv_w[jx].rearrange("(c p) -> p c", p=128), [], [("cw", jx)], slow=True)
        h.dma("sp", cw[:, 3, :], ffn_conv_b.rearrange("(c p) -> p c", p=128), [], [("cw", 3)], slow=True)
        nfb = sb("nfb", [128, D], F32)
        h.dma("sp", nfb[:], norm_final.partition_broadcast(128), [], ["nfb"])
        wup = sb("wup", [128, 8, F2], BF16)
        w_up_v = ffn_w_up.rearrange("(c p) n -> p c n", p=128)
        for kc in range(8):
            h.dma("pool", wup[:, kc, :], w_up_v[:, kc, :], [], [("wup", kc)])
        wdn = sb("wdn", [128, 22, D], BF16)
        w_dn_v = ffn_w_down.rearrange("(c p) n -> p c n", p=128)
        for kc in range(22):
            h.dma("pool", wdn[:, kc, :], w_dn_v[:, kc, :], [], [("wdn", kc)])
        xts = [sb("xt0", [128, D], F32), sb("xt1", [128, D], F32)]
        alloc_rms(T, sb)
        ub = [sb("ub0", [128, 4, 136], F32), sb("ub1", [128, 4, 136], F32)]
        ucar = sb("ucar", [128, 44, 2], F32)
        cin = sb("cin", [128, 44, 4, 2], F32)
        cout = sb("cout", [128, 44, 8], F32)
        cstg = [sb("cstg%d" % i, [8, 512], F32) for i in range(2)]
        csto = [sb("csto%d" % i, [8, 512], F32) for i in range(2)]
        fss = [sb("fss%d" % i, [128, 1], F32) for i in range(2)]
        frs = [sb("frs%d" % i, [128, 1], F32) for i in range(2)]
        cc = [sb("cc0", [128, 4, 128], F32), sb("cc1", [128, 4, 128], F32)]
        g1 = [sb("g10", [128, 2, 128], F32), sb("g11", [128, 2, 128], F32)]
        g2t = [sb("g20", [128, 2, 128], F32), sb("g21", [128, 2, 128], F32)]
        actTs = [sb("actT%d" % i, [128, 22, 128], BF16) for i in range(2)]
        x2s = [sb("x2%d" % i, [128, D], F32) for i in range(2)]
        yts = [sb("yt0", [128, D], F32)] * 2
        T["psT"] = psb("psT", [128, 8, 128], BF16)
        PS = [psb("ps%d" % i, [128, 512], F32) for i in range(7)]
        h.memset("pool", ucar[:], 0.0, ["ucar"])
        if FILLERS:
            _pf2, _id2 = PS[6], idb
            P.filler = ((lambda e: e.matmul(_pf2[:, 0:128], lhsT=_id2[:], rhs=_id2[:], start=True, stop=True)), 70.0)
        for b in BLKS:
            samp, nseq, L = geom(b)
            j = b - NPB
            xt, xtn = xts[b % 2], "xt%d" % (b % 2)
            actT, actk = actTs[b % 2], "actT%d" % (b % 2)
            x2, x2k = x2s[b % 2], "x2%d" % (b % 2)
            yt, ytk = yts[0], "yt0"
            h.dma("sp", xt[:], x1_s[b * 128:(b + 1) * 128, :], [], [xtn])
            hT, hTk = rms_to_fm(h, T, xt, xtn, 0, b % 2)
            W = nseq * (L + 2)
            if samp:
                stc_v = st_conv[4 * j:4 * j + 4].rearrange("q t f -> (q t) f")
                for g4 in range(11):
                    cg_, cgk = cstg[g4 % 2], "cstg%d" % (g4 % 2)
                    h.dma("sp", cg_[:], stc_v[:, g4 * 512:(g4 + 1) * 512], [], [cgk])
                    for mi in range(4):
                        m = 4 * g4 + mi
                        h.tr(PS[4][:, mi * 8:(mi + 1) * 8], cg_[0:8, mi * 128:(mi + 1) * 128], idf[0:8, 0:8],
                             [cgk, "cst"], ["ps4"])
                    h.cp("act", cin[:, 4 * g4:4 * g4 + 4, :, :].rearrange("p m q t -> p m (q t)"),
                         PS[4][:, 0:32].rearrange("p (m x) -> p m x", m=4), ["ps4"], [("cin", g4)])
            last_prompt = (b == NPB - 1)
            for gi in range(11):
                u, un = ub[gi % 2], "ub%d" % (gi % 2)
                uv = u[:, :, 0:W].rearrange("p m (s l) -> p m s l", s=nseq)
                pt, pn = PS[gi % 2], "ps%d" % (gi % 2)
                chunks = [2 * gi, 2 * gi + 1, 22 + 2 * gi, 23 + 2 * gi]
                for mi, m in enumerate(chunks):
                    for kc in range(8):
                        h.mm(pt[:, mi * 128:(mi + 1) * 128], wup[:, kc, m * 128:(m + 1) * 128], hT[:, kc, :],
                             kc == 0, kc == 7, [("wup", kc), hTk], [pn])
                h.cp("act", uv[:, :, :, 2:L + 2], pt[:].rearrange("p (m s l) -> p m s l", m=4, s=nseq),
                     [pn], [un])
                for half in range(2):
                    m0 = chunks[2 * half]
                    if samp:
                        h.cp("pool", uv[:, 2 * half:2 * half + 2, :, 0:2], cin[:, m0:m0 + 2, :, :],
                             [("cin", m0 // 4)], [un])
                    else:
                        h.cp("pool", uv[:, 2 * half:2 * half + 2, 0, 0:2], ucar[:, m0:m0 + 2, :],
                             [("ucar", gi)], [un])
                for half in range(2):
                    m0 = chunks[2 * half]
                    if samp:
                        h.cp("pool", cout[:, m0:m0 + 2, :].rearrange("p m (q t) -> p m q t", q=4),
                             uv[:, 2 * half:2 * half + 2, :, 8:10], [un], [("cout", gi)])
                    else:
                        h.cp("pool", ucar[:, m0:m0 + 2, :], uv[:, 2 * half:2 * half + 2, 0, 128:130],
                             [un], [("ucar", gi)])
                        if last_prompt:
                            h.cp("pool", cout[:, m0:m0 + 2, 0:2], uv[:, 2 * half:2 * half + 2, 0, 128:130],
                                 [un], [("cout", gi)])
                c_, cn = cc[gi % 2], "cc%d" % (gi % 2)
                for mi, m in enumerate(chunks):
                    eng = "dve"
                    c4 = c_[:, mi, :].rearrange("p (s l) -> p s l", s=nseq)
                    h.act(c4, uv[:, mi, :, 2:L + 2], AF.Identity, [un, "cw"], [(cn, mi)],
                          scale=cw[:, 2, m:m + 1], bias=cw[:, 3, m:m + 1])
                    h.stt(eng, c4, uv[:, mi, :, 1:L + 1], cw[:, 1, m:m + 1], c4, ALU.mult, ALU.add,
                          [un, "cw", (cn, mi)], [(cn, mi)])
                    h.stt(eng, c4, uv[:, mi, :, 0:L], cw[:, 0, m:m + 1], c4, ALU.mult, ALU.add,
                          [un, "cw", (cn, mi)], [(cn, mi)])
                ga, gan = g1[gi % 2], "g1%d" % (gi % 2)
                gb_, gbn = g2t[gi % 2], "g2%d" % (gi % 2)
                gate = c_[:, 2:4, :]
                val = c_[:, 0:2, :]
                h.act(ga[:], gate, AF.Square, [(cn, 2), (cn, 3)], [gan])
                h.ts("dve", ga[:], ga[:], 0.044715, 1.0, ALU.mult, ALU.add, [gan], [gan])
                h.tt("pool", ga[:], ga[:], gate, ALU.mult, [gan, (cn, 2), (cn, 3)], [gan])
                h.act(gb_[:], ga[:], AF.Sigmoid, [gan], [gbn], scale=GELU_S)
                h.tt("pool", gb_[:], gb_[:], gate, ALU.mult, [gbn, (cn, 2), (cn, 3)], [gbn])
                h.tt("dve", actT[:, 2 * gi:2 * gi + 2, :], gb_[:], val, ALU.mult, [gbn, (cn, 0), (cn, 1)],
                     [(actk, gi)])
            if samp or last_prompt:
                ncol = 8 if samp else 2
                for g4 in range(11):
                    for mi in range(4):
                        m = 4 * g4 + mi
                        h.tr(PS[4][0:ncol, mi * 128:(mi + 1) * 128], cout[:, m, 0:ncol], idf, ["cout", "cst"],
                             ["ps4"])
                    co_, cok = csto[g4 % 2], "csto%d" % (g4 % 2)
                    h.cp("act", co_[0:ncol, :], PS[4][0:ncol, :], ["ps4"], [cok])
                    if samp:
                        h.dma("sp", o_conv_s[4 * j:4 * j + 4].rearrange("q t f -> (q t) f")[:, g4 * 512:(g4 + 1) * 512],
                              co_[:], [cok], [], isout=True)
                    else:
                        h.dma("sp", o_conv_p[0][:, g4 * 512:(g4 + 1) * 512], co_[0:2, :], [cok], [], isout=True)
            for nh in range(2):
                pt, pn = PS[2 + nh], "ps%d" % (2 + nh)
                for kc in range(22):
                    h.mm(pt[:], actT[:, kc, :], wdn[:, kc, nh * 512:(nh + 1) * 512], kc == 0, kc == 21,
                         [(actk, kc // 2), ("wdn", kc)], [pn])
                h.tt("dve", x2[:, nh * 512:(nh + 1) * 512], pt[:], xt[:, nh * 512:(nh + 1) * 512], ALU.add,
                     [pn, xtn], [(x2k, nh)])
            ss2, rs2 = fss[b % 2], frs[b % 2]
            ssk, rsk = "fss%d" % (b % 2), "frs%d" % (b % 2)
            h.act(yt[:], x2[:], AF.Square, [x2k], [ytk, ssk], accum_out=ss2[:])
            h.ts("dve", rs2[:], ss2[:], 1.0 / D, 1e-6, ALU.mult, ALU.add, [ssk], [rsk])
            h.act(rs2[:], rs2[:], AF.Sqrt, [rsk], [rsk])
            h.recip(rs2[:], rs2[:], [rsk], [rsk])
            h.stt("dve", yt[:], x2[:], rs2[:, 0:1], nfb[:], ALU.mult, ALU.mult, [x2k, rsk, "nfb"], [ytk])
            if samp:
                for q in range(4):
                    h.dma("sp", ysm[4 * j + q], yt[32 * q:32 * q + 8, :], [ytk], [], isout=True)
            else:
                h.dma("sp", yp[b * 128:(b + 1) * 128, :], yt[:], [ytk], [], isout=True)
        P.emit()
    gs.close()
    return nc


_CACHE = {}


def kernel(**inputs):
    f32 = lambda a: np.ascontiguousarray(np.asarray(a), dtype=np.float32)
    if "nc" not in _CACHE:
        nc = bass.Bass("TRN2", target_bir_lowering=False)
        build(nc)
        _CACHE["nc"] = nc
    nc = _CACHE["nc"]
    cst = make_consts()
    shared = {
        "cst": cst,
        "norm_mix": f32(inputs["norm_mix"][0]), "w_in": f32(inputs["w_in"][0]),
        "mu_shift": f32(inputs["mu_shift"][0]), "rwkv_w0": f32(inputs["rwkv_w0"][0]),
        "rwkv_w2": f32(inputs["rwkv_w2"][0]), "rwkv_a0": f32(inputs["rwkv_a0"][0]),
        "rwkv_a2": f32(inputs["rwkv_a2"][0]), "rwkv_g2": f32(inputs["rwkv_g2"][0]),
        "rwkv_k_k": f32(inputs["rwkv_k_k"][0]), "rwkv_k_a": f32(inputs["rwkv_k_a"][0]),
        "rwkv_r_k": f32(inputs["rwkv_r_k"][0]).reshape(512), "rwkv_ln_w": f32(inputs["rwkv_ln_w"][0]),
        "rwkv_ln_b": f32(inputs["rwkv_ln_b"][0]), "gla_wg2": f32(inputs["gla_wg2"][0]),
        "gla_bg": f32(inputs["gla_bg"][0]), "gla_norm_w": f32(inputs["gla_norm_w"][0]),
        "w_out_a": f32(inputs["w_out_a"][0]), "w_out_b": f32(inputs["w_out_b"][0]),
        "w_o": f32(inputs["w_o"][0]), "norm_ffn": f32(inputs["norm_ffn"][0]),
        "ffn_w_up": f32(inputs["ffn_w_up"][0]), "ffn_conv_w": f32(inputs["ffn_conv_w"][0]),
        "ffn_conv_b": f32(inputs["ffn_conv_b"][0]), "ffn_w_down": f32(inputs["ffn_w_down"][0]),
        "norm_final": f32(inputs["norm_final"]),
    }
    in_maps = []
    for c in range(NCORES):
        m = dict(shared)
        sl = slice(16 * c, 16 * c + 16)
        m["xp"] = f32(inputs["x_prompt"][c])
        m["xs"] = f32(inputs["x_sample"][sl])
        m["st_shift"] = f32(inputs["state_rwkv_shift"][0, sl])
        m["st_wkv"] = f32(inputs["state_rwkv_wkv"][0, sl])
        m["st_gla"] = f32(inputs["state_gla"][0, sl])
        m["st_conv"] = f32(inputs["state_ffn_conv"][0, sl])
        in_maps.append(m)
    res = run_bass_kernel_spmd(nc, in_maps, core_ids=list(range(NCORES)))
    R = res.results
    cat = lambda k: np.concatenate([np.asarray(r[k]) for r in R], axis=0)
    y_p = np.stack([np.asarray(r["yp"]) for r in R], axis=0)
    y_s = cat("ys")
    outs = (
        y_p, y_s,
        cat("o_shift_p")[None], cat("o_wkv_p")[None], cat("o_gla_p")[None], cat("o_conv_p")[None],
        cat("o_shift_s")[None], cat("o_wkv_s")[None], cat("o_gla_s")[None], cat("o_conv_s")[None],
    )
    return tuple(np.ascontiguousarray(o, dtype=np.float32) for o in outs)
```

```python
import contextlib
import numpy as np
import concourse.bass as bass
import concourse.mybir as mybir
from concourse.bass_utils import run_bass_kernel_spmd

F32 = mybir.dt.float32
BF16 = mybir.dt.bfloat16
AF = mybir.ActivationFunctionType
ALU = mybir.AluOpType
AX = mybir.AxisListType

NCORES = 8
D = 1024
NPB = 16
NSB = 4
NBLK = NPB + NSB
SHIFT = 1792
INC = 5392
FH = 2816
F2 = 5632
GLA0 = 1792
GATE0 = 3344
C0 = -0.6065306597126334
GELU_S = 1.5957691216057308

ENGS = ("pe", "act", "dve", "pool", "sp")
MAXOPS = None
SCHED = True
VERBOSE = False
PROGS = []
WINDOW = 300
SCHED_LAT = 120.0
ODEP_LAT = 0.0
FILL_MIN = 250.0
FILL_MARGIN = 80.0
FILL_MAX = 24
FILLERS = False
LINES = []
TAGS = []


class Op:
    __slots__ = ("eng", "fn", "reads", "writes", "dma", "deps", "sig", "idx",
                 "dsem", "dval", "dprev", "isout", "mm", "odeps", "cost", "start", "fin")

    def __init__(self, eng, fn, reads, writes, dma, isout, mm, cost=300.0):
        self.odeps = []
        self.cost = cost
        self.eng = eng
        self.fn = fn
        self.reads = reads
        self.writes = writes
        self.dma = dma
        self.deps = []
        self.sig = None
        self.dsem = None
        self.dval = None
        self.dprev = None
        self.isout = isout
        self.mm = mm


def _norm(k):
    return k if isinstance(k, tuple) else (k, None)


class Prog:
    def __init__(self, nc, semstack, tag, n_dma_sems=6):
        self.nc = nc
        self.ops = []
        self.n_dma_sems = n_dma_sems
        self.st = {}
        self.semstack = semstack
        self.tag = tag
        self.filler = None

    @staticmethod
    def _conf(a, b):
        return a is None or b is None or a == b

    def add(self, eng, fn, reads=(), writes=(), dma=False, isout=False, mm=False, cost=300.0):
        op = Op(eng, fn, [_norm(k) for k in reads], [_norm(k) for k in writes], dma, isout, mm, cost)
        odeps = {}
        op.idx = len(self.ops)
        if MAXOPS is not None:
            import sys as _s
            f = _s._getframe(1)
            while f is not None and f.f_code.co_name != "build":
                f = f.f_back
            LINES.append(f.f_lineno if f is not None else -1)
            TAGS.append(f.f_locals.get("b", -1) if f is not None else -1)
        deps = {}
        for (name, sub) in op.reads:
            s = self.st.setdefault(name, {"w": {}, "r": {}})
            for ws, wop in s["w"].items():
                if self._conf(ws, sub):
                    deps[wop.idx] = wop
        for (name, sub) in op.writes:
            s = self.st.setdefault(name, {"w": {}, "r": {}})
            for ws, wop in s["w"].items():
                if self._conf(ws, sub):
                    if not (op.mm and wop.mm):
                        deps[wop.idx] = wop
                    else:
                        odeps[wop.idx] = wop
            for rs, rops in s["r"].items():
                if self._conf(rs, sub):
                    for rop in rops:
                        deps[rop.idx] = rop
        for (name, sub) in op.reads:
            self.st[name]["r"].setdefault(sub, []).append(op)
        for (name, sub) in op.writes:
            s = self.st[name]
            if sub is None:
                s["w"] = {None: op}
                s["r"] = {}
            else:
                s["w"][sub] = op
                s["r"][sub] = []
        deps.pop(op.idx, None)
        op.deps = list(deps.values())
        op.odeps = [o for k, o in odeps.items() if k not in deps]
        self.ops.append(op)
        return op

    def schedule(self, window=None):
        window = window or WINDOW
        ops = self.ops
        n = len(ops)
        ndep = [0] * n
        users = [[] for _ in range(n)]
        truedep = set()
        for op in ops:
            for d in op.deps:
                truedep.add((op.idx, d.idx))
            ds = {d.idx for d in op.deps} | {d.idx for d in op.odeps}
            ndep[op.idx] = len(ds)
            for d in ds:
                users[d].append(op.idx)
        ready_t = [0.0] * n
        per = {e: [op.idx for op in ops if op.eng == e] for e in ENGS}
        head = {e: 0 for e in ENGS}
        done = [False] * n
        t_e = {e: 0.0 for e in ENGS}
        order = []
        remaining = n
        LAT = SCHED_LAT
        while remaining:
            best = None
            for e in ENGS:
                lst = per[e]
                hp = head[e]
                while hp < len(lst) and done[lst[hp]]:
                    hp += 1
                head[e] = hp
                if hp >= len(lst):
                    continue
                cnt = 0
                k = hp
                cand = None
                rdy = []
                while k < len(lst) and cnt < window:
                    i = lst[k]
                    if not done[i]:
                        cnt += 1
                        if ndep[i] == 0:
                            st = max(t_e[e], ready_t[i])
                            key = (st + 2.0 * (cnt - 1), i)
                            rdy.append((st, i))
                            if cand is None or key < cand[0]:
                                cand = (key, i, st)
                    k += 1
                if cand is not None:
                    bst = cand[2]
                    for (st, i) in rdy:
                        if i != cand[1] and st + (60.0 if ops[i].dma else ops[i].cost) <= bst:
                            cand = ((st, i), i, st)
                            break
                    if best is None or cand[0] < best[0]:
                        best = (cand[0], cand[1], cand[2], e)
            assert best is not None, "scheduler deadlock"
            _, i, st, e = best
            op = ops[i]
            op.start = st
            if op.dma:
                t_e[e] = st + 60.0
            else:
                t_e[e] = st + op.cost
            op.fin = st + op.cost
            done[i] = True
            remaining -= 1
            order.append(op)
            for u in users[i]:
                ndep[u] -= 1
                lat_ = LAT if ((u, i) in truedep) else ODEP_LAT
                if ready_t[u] < op.fin + lat_:
                    ready_t[u] = op.fin + lat_
        if self.filler is not None:
            fn, fcost = self.filler
            out = []
            pe_end = None
            nf = 0
            for op in order:
                if op.eng == "pe":
                    if pe_end is not None:
                        gap = op.start - pe_end
                        if gap > FILL_MIN:
                            k = min(FILL_MAX, int((gap - FILL_MARGIN) / fcost))
                            for _ in range(max(0, k)):
                                f = Op("pe", fn, [], [], False, False, True, fcost)
                                f.idx = -1
                                f.start = pe_end
                                f.fin = pe_end + fcost
                                out.append(f)
                                nf += 1
                    pe_end = op.start + op.cost
                out.append(op)
            order = out
            if VERBOSE:
                print("[sched %s] fillers inserted: %d" % (self.tag, nf), flush=True)
        self.ops = order
        self.est = max(op.fin for op in order) if order else 0.0
        if VERBOSE:
            PROGS.append(self)
            busy = {e: sum(o.cost for o in order if o.eng == e and not o.dma) for e in ENGS}
            print("[sched %s] n=%d est=%.1f us busy(us): %s" % (
                self.tag, n, self.est / 1e3, " ".join("%s=%.0f" % (e, busy[e] / 1e3) for e in ENGS)), flush=True)

    def emit(self):
        nc = self.nc
        if MAXOPS is not None:
            self.ops = self.ops[:MAXOPS]
        if SCHED:
            self.schedule()
        ops = self.ops
        needed = set()
        for op in ops:
            for d in op.deps:
                needed.add(d.idx)
        cnt = {e: 0 for e in ENGS}
        for op in ops:
            if not op.dma and op.idx in needed:
                cnt[op.eng] += 1
                op.sig = cnt[op.eng]
        dcount = {e: 0 for e in ENGS}
        last_on_slot = {}
        for op in ops:
            if not op.dma:
                continue
            j = dcount[op.eng]
            dcount[op.eng] += 1
            slot = j % self.n_dma_sems
            op.dsem = (op.eng, slot)
            op.dval = 16 * (j // self.n_dma_sems + 1)
            op.dprev = last_on_slot.get(op.dsem)
            last_on_slot[op.dsem] = op
        out_ops = [op for op in ops if op.dma and op.isout]
        per_eng = {e: [op for op in ops if op.eng == e] for e in ENGS}
        es = self.semstack
        csem = {e: es.enter_context(nc.semaphore("cs%s_%s" % (self.tag, e)))
                for e in ENGS if e != "sp"}
        dsem = {}
        for e in ENGS:
            for s in range(min(self.n_dma_sems, dcount[e])):
                dsem[(e, s)] = es.enter_context(nc.semaphore("ds%s_%s%d" % (self.tag, e, s)))

        def run_engine(e, eng):
            known = {}

            def wait(key, sem, val):
                if known.get(key, 0) >= val:
                    return
                known[key] = val
                eng.wait_ge(sem, val)

            for op in per_eng[e]:
                for d in op.deps:
                    if d.dma:
                        wait(d.dsem, dsem[d.dsem], d.dval)
                    else:
                        wait(d.eng, csem[d.eng], d.sig)
                if op.dma and op.dprev is not None:
                    wait(op.dsem, dsem[op.dsem], op.dprev.dval)
                ins = op.fn(eng)
                if op.dma:
                    ins.then_inc(dsem[op.dsem], 16)
                elif op.sig is not None:
                    ins.then_inc(csem[e], 1)
            if e == "sp":
                for op in out_ops:
                    wait(op.dsem, dsem[op.dsem], op.dval)
                for key, op in last_on_slot.items():
                    wait(op.dsem, dsem[op.dsem], op.dval)

        with nc.Block() as block:
            @block.sync
            def _(eng):
                run_engine("sp", eng)

            @block.tensor
            def _(eng):
                run_engine("pe", eng)

            @block.scalar
            def _(eng):
                run_engine("act", eng)

            @block.vector
            def _(eng):
                run_engine("dve", eng)

            @block.gpsimd
            def _(eng):
                run_engine("pool", eng)


def _fsz(ap):
    n = 1
    for d in ap.shape[1:]:
        n *= int(d)
    return n


class H:
    def __init__(self, P):
        self.P = P

    def act(self, out, in_, func, r, w, **kw):
        self.P.add("act", lambda e: e.activation(out=out, in_=in_, func=func, **kw), r, w,
                   cost=220.0 + 1.05 * _fsz(out))

    def tt(self, eng, out, in0, in1, op, r, w):
        self.P.add(eng, lambda e: e.tensor_tensor(out=out, in0=in0, in1=in1, op=op), r, w,
                   cost=(100.0 + 1.05 * _fsz(out)) if eng == "dve" else (160.0 + 2.1 * _fsz(out)))

    def ts(self, eng, out, in0, s1, s2, op0, op1, r, w):
        if op1 is None:
            self.P.add(eng, lambda e: e.tensor_scalar(out=out, in0=in0, scalar1=s1, scalar2=None, op0=op0), r, w,
                       cost=100.0 + 1.05 * _fsz(out))
        else:
            self.P.add(eng, lambda e: e.tensor_scalar(out=out, in0=in0, scalar1=s1, scalar2=s2, op0=op0, op1=op1), r, w,
                       cost=100.0 + 1.05 * _fsz(out))

    def stt(self, eng, out, in0, scalar, in1, op0, op1, r, w):
        self.P.add(eng, lambda e: e.scalar_tensor_tensor(out=out, in0=in0, scalar=scalar, in1=in1, op0=op0, op1=op1), r, w,
                   cost=100.0 + 1.05 * _fsz(out))

    def cp(self, eng, out, in_, r, w):
        if eng == "act":
            self.P.add("act", lambda e: e.activation(out=out, in_=in_, func=AF.Copy), r, w,
                       cost=220.0 + 1.05 * _fsz(out))
        else:
            self.P.add(eng, lambda e: e.tensor_copy(out=out, in_=in_), r, w,
                       cost=(100.0 + 1.05 * _fsz(out)) if eng == "dve" else (160.0 + 2.1 * _fsz(out)))

    def memset(self, eng, ap, val, w):
        self.P.add(eng, lambda e: e.memset(ap, val), [], w, cost=160.0 + 1.0 * _fsz(ap))

    def recip(self, out, in_, r, w):
        self.P.add("dve", lambda e: e.reciprocal(out=out, in_=in_), r, w, cost=100.0 + 1.05 * _fsz(out))

    def scan(self, out, d0, d1, r, w):
        self.P.add("dve", lambda e: e.tensor_tensor_scan(out=out, data0=d0, data1=d1, initial=0.0,
                                                         op0=ALU.mult, op1=ALU.add), r, w,
                   cost=100.0 + 2.1 * _fsz(out))

    def rsum(self, out, in_, r, w):
        self.P.add("dve", lambda e: e.tensor_reduce(out=out, in_=in_, axis=AX.X, op=ALU.add), r, w,
                   cost=100.0 + 1.05 * _fsz(in_))

    def mm(self, out, lhsT, rhs, start, stop, r, w, tp=None):
        c = max(64.0, float(_fsz(rhs))) / 2.0 + 16.0
        if lhsT.dtype == F32:
            c *= 4.0
        if tp is None:
            self.P.add("pe", lambda e: e.matmul(out, lhsT=lhsT, rhs=rhs, start=start, stop=stop), r, w, mm=True,
                       cost=c)
        else:
            self.P.add("pe", lambda e: e.matmul(out, lhsT=lhsT, rhs=rhs, start=start, stop=stop,
                                                tile_position=tp), r, w, mm=True, cost=c)

    def tr(self, out, in_, ident, r, w):
        self.P.add("pe", lambda e: e.transpose(out=out, in_=in_, identity=ident), r, w, mm=True, cost=110.0)

    def dma(self, q, out, in_, r, w, isout=False, slow=False):
        nbytes = 1
        for d in out.shape:
            nbytes *= int(d)
        c = 2500.0 + 4.0 * nbytes / 150.0
        if slow:
            self.P.add(q, lambda e: e.dma_start(out=out, in_=in_, allow_slow_non_contiguous=True), r, w,
                       dma=True, isout=isout, cost=c)
        else:
            self.P.add(q, lambda e: e.dma_start(out=out, in_=in_), r, w, dma=True, isout=isout, cost=c)


def bc(ap, shape):
    a = ap
    while len(a.shape) < len(shape):
        a = a.unsqueeze(len(a.shape))
    return a.to_broadcast(list(shape))


C_ID = 0
C_BO = 128
C_RST = 256
C_VS = 384
C_ONE = 512
C_M4 = 640
C_SL = 1152
C_IU = 1280
NCST = 1408


def make_consts():
    c = np.zeros((128, NCST), np.float32)
    i = np.arange(128)
    same = (i[:, None] // 32) == (i[None, :] // 32)
    su = (same & (i[:, None] < i[None, :])).astype(np.float32)
    iu = (same & (i[:, None] <= i[None, :])).astype(np.float32)
    sl = (same & (i[:, None] > i[None, :])).astype(np.float32)
    c[:, C_ID:C_ID + 128] = np.eye(128, dtype=np.float32)
    c[:, C_BO:C_BO + 128] = ((i[:, None] // 64) == (i[None, :] // 64)).astype(np.float32)
    c[:, C_RST:C_RST + 128] = (i[None, :] % 32 != 0).astype(np.float32)
    c[:, C_VS:C_VS + 128] = (i[None, :] % 32 < 8).astype(np.float32)
    c[:, C_ONE:C_ONE + 128] = 1.0
    c[:, C_M4:C_M4 + 512] = np.concatenate([su, iu, su, iu], axis=1)
    c[:, C_SL:C_SL + 128] = sl
    c[:, C_IU:C_IU + 128] = iu
    return c


PR_G = 0
PR_MU = 8
PR_W0 = 22
PR_A0 = 26
PR_KK = 30
PR_KA = 34
PR_RK = 38
PR_LW = 42
PR_LB = 46
PR_BG = 50
PR_NW = 52
PR_OMKA = 53
NPRM = 64


def build(nc, dbg=None, phases="abc", blocks=None):
    BLKS = list(range(NBLK)) if blocks is None else list(blocks)
    gs = contextlib.ExitStack()

    def din(name, shape, dt=F32):
        return nc.dram_tensor(name, list(shape), dt, kind="ExternalInput").ap()

    def dout(name, shape):
        return nc.dram_tensor(name, list(shape), F32, kind="ExternalOutput").ap()

    def dscr(name, shape, dt):
        return nc.dram_tensor(name, list(shape), dt, kind="Internal").ap()

    xp = din("xp", [2048, D])
    xsm = din("xs", [16, 8, D])
    st_shift = din("st_shift", [16, SHIFT])
    st_wkv = din("st_wkv", [16, 8, 64, 64])
    st_gla = din("st_gla", [16, 4, 64, 128])
    st_conv = din("st_conv", [16, 2, F2])
    cst_d = din("cst", [128, NCST])
    norm_mix = din("norm_mix", [D])
    w_in = din("w_in", [D, INC])
    mu_shift = din("mu_shift", [SHIFT])
    rwkv_w0 = din("rwkv_w0", [512])
    rwkv_w2 = din("rwkv_w2", [64, 512])
    rwkv_a0 = din("rwkv_a0", [512])
    rwkv_a2 = din("rwkv_a2", [64, 512])
    rwkv_g2 = din("rwkv_g2", [128, 512])
    rwkv_k_k = din("rwkv_k_k", [512])
    rwkv_k_a = din("rwkv_k_a", [512])
    rwkv_r_k = din("rwkv_r_k", [512])
    rwkv_ln_w = din("rwkv_ln_w", [512])
    rwkv_ln_b = din("rwkv_ln_b", [512])
    gla_wg2 = din("gla_wg2", [16, 256])
    gla_bg = din("gla_bg", [256])
    gla_norm_w = din("gla_norm_w", [128])
    w_out_a = din("w_out_a", [512, D])
    w_out_b = din("w_out_b", [512, D])
    w_o = din("w_o", [D, D])
    norm_ffn = din("norm_ffn", [D])
    ffn_w_up = din("ffn_w_up", [D, F2])
    ffn_conv_w = din("ffn_conv_w", [3, F2])
    ffn_conv_b = din("ffn_conv_b", [F2])
    ffn_w_down = din("ffn_w_down", [FH, D])
    norm_final = din("norm_final", [D])

    yp = dout("yp", [2048, D])
    ysm = dout("ys", [16, 8, D])
    o_shift_p = dout("o_shift_p", [1, SHIFT])
    o_wkv_p = dout("o_wkv_p", [1, 8, 64, 64])
    o_gla_p = dout("o_gla_p", [1, 4, 64, 128])
    o_conv_p = dout("o_conv_p", [1, 2, F2])
    o_shift_s = dout("o_shift_s", [16, SHIFT])
    o_wkv_s = dout("o_wkv_s", [16, 8, 64, 64])
    o_gla_s = dout("o_gla_s", [16, 4, 64, 128])
    o_conv_s = dout("o_conv_s", [16, 2, F2])

    og_s = dscr("og_s", [NBLK, 128, 512], BF16)
    ob_s = dscr("ob_s", [NBLK, 128, 512], BF16)
    x1_s = dscr("x1_s", [NBLK * 128, D], F32)

    dbg_out = None
    if dbg is not None:
        dbg_out = dout("dbg", dbg["shape"])

    def geom(b):
        return (b >= NPB, 4, 32) if b >= NPB else (False, 1, 128)

    def load_consts(h, sb, q="sp"):
        cst = sb("cst", [128, NCST], F32)
        h.dma(q, cst[:], cst_d, [], ["cst"])
        idb = sb("idb", [128, 128], BF16)
        h.cp("dve", idb[:], cst[:, C_ID:C_ID + 128], ["cst"], ["idb"])
        return cst, idb

    def load_x_block(h, b, xt, xtn):
        samp, nseq, L = geom(b)
        if not samp:
            h.dma("sp", xt[:], xp[b * 128:(b + 1) * 128, :], [], [xtn])
        else:
            j = b - NPB
            h.memset("pool", xt[:], 0.0, [xtn])
            for q in range(4):
                h.dma("sp", xt[32 * q:32 * q + 8, :], xsm[4 * j + q], [], [(xtn, q)])

    def rms_to_fm(h, T, xt, xtn, gcol, par=0):
        xp_ = par if len(T["xn"]) > 1 else 0
        xn, ss, rstd, hT = T["xn"][xp_], T["ss"][par], T["rstd"][par], T["hT"][par]
        xnk, ssk, rsk, hTk = "xn%d" % xp_, "ss%d" % par, "rstd%d" % par, "hT%d" % par
        psT, idb, prm = T["psT"], T["idb"], T["prm"]
        h.act(xn[:], xt[:], AF.Square, [xtn], [xnk, ssk], accum_out=ss[:])
        h.ts("dve", rstd[:], ss[:], 1.0 / D, 1e-6, ALU.mult, ALU.add, [ssk], [rsk])
        h.act(rstd[:], rstd[:], AF.Sqrt, [rsk], [rsk])
        h.recip(rstd[:], rstd[:], [rsk], [rsk])
        h.act(xn[:], xt[:], AF.Copy, [xtn, rsk], [xnk], scale=rstd[:, 0:1])
        for c in range(8):
            h.tr(psT[:, c, :], xn[:, c * 128:(c + 1) * 128], idb[:], [xnk, "idb"], [("psT", c)])
        h.tt("dve", hT[:], psT[:], bc(prm[:, gcol:gcol + 8], [128, 8, 128]), ALU.mult,
             ["psT", "prm"], [hTk])
        return hT, hTk

    def alloc_rms(T, sb, nxn=2):
        T["xn"] = [sb("xn%d" % i, [128, D], BF16) for i in range(nxn)]
        T["ss"] = [sb("ss%d" % i, [128, 1], F32) for i in range(2)]
        T["rstd"] = [sb("rstd%d" % i, [128, 1], F32) for i in range(2)]
        T["hT"] = [sb("hT%d" % i, [128, 8, 128], BF16) for i in range(2)]

    def load_param_cols(h, prm, col, src, n):
        h.dma("sp", prm[:, col:col + n], src.rearrange("(c p) -> p c", p=128), [], [("prm", col)], slow=True)

    with contextlib.ExitStack() as ph:
      if "a" in phases:
        P = Prog(nc, gs, "a")
        h = H(P)

        def sb(name, shape, dt):
            return ph.enter_context(nc.sbuf_tensor("a_" + name, list(shape), dt))

        def psb(name, shape, dt):
            return ph.enter_context(nc.psum_tensor("a_" + name, list(shape), dt))

        T = {}
        cst, idb = load_consts(h, sb)
        T["idb"] = idb
        prm = sb("prm", [128, NPRM], F32)
        T["prm"] = prm
        load_param_cols(h, prm, PR_G, norm_mix, 8)
        load_param_cols(h, prm, PR_MU, mu_shift, 14)
        load_param_cols(h, prm, PR_W0, rwkv_w0, 4)
        load_param_cols(h, prm, PR_A0, rwkv_a0, 4)
        load_param_cols(h, prm, PR_KK, rwkv_k_k, 4)
        load_param_cols(h, prm, PR_KA, rwkv_k_a, 4)
        load_param_cols(h, prm, PR_RK, rwkv_r_k, 4)
        load_param_cols(h, prm, PR_LW, rwkv_ln_w, 4)
        load_param_cols(h, prm, PR_LB, rwkv_ln_b, 4)
        load_param_cols(h, prm, PR_BG, gla_bg, 2)
        load_param_cols(h, prm, PR_NW, gla_norm_w, 1)
        h.ts("dve", prm[:, PR_BG:PR_BG + 2], prm[:, PR_BG:PR_BG + 2], -1.0, None, ALU.mult, None,
             [("prm", PR_BG)], [("prm", PR_BG)])
        h.ts("dve", prm[:, PR_OMKA:PR_OMKA + 4], prm[:, PR_KA:PR_KA + 4], -1.0, 1.0, ALU.mult, ALU.add,
             [("prm", PR_KA)], [("prm", PR_OMKA)])

        NA1 = GATE0
        win = sb("win", [128, 8, NA1], BF16)
        w_in_v = w_in.rearrange("(c p) n -> p c n", p=128)
        for kc in range(8):
            h.dma("pool", win[:, kc, :], w_in_v[:, kc, 0:NA1], [], [("win", kc)])
        w2a2 = sb("w2a2", [128, 512], BF16)
        h.dma("pool", w2a2[0:64, :], rwkv_w2, [], [("w2a2", 0)])
        h.dma("pool", w2a2[64:128, :], rwkv_a2, [], [("w2a2", 1)])
        g2 = sb("g2", [128, 512], BF16)
        h.dma("pool", g2[:], rwkv_g2, [], ["g2"])
        wg2 = sb("wg2", [16, 256], BF16)
        h.dma("pool", wg2[:], gla_wg2, [], ["wg2"])

        xt = sb("xt", [128, D], F32)
        alloc_rms(T, sb, nxn=1)
        prw = sb("prw", [128, 14, 132], F32)
        lastc = sb("lastc", [128, 14, 1], F32)
        xs = sb("xs", [128, 14, 128], F32)
        Fm = [sb("F%d" % i, [128, 4, 128], F32) for i in range(10)]
        gTs = [sb("gT%d" % i, [128, 4, 128], F32) for i in range(2)]
        bons = [sb("bon%d" % i, [128, 4, 128], F32) for i in range(2)]
        ARs = [sb("AR%d" % i, [128, 4, 2, 128], BF16) for i in range(2)]
        Hm = [sb("Hb%d" % i, [128, 4, 128], BF16) for i in range(8)]
        tok = [sb("tok%d" % i, [128, 512], BF16) for i in range(4)]
        SCH = sb("SCH", [128, 8, 4, 128], BF16)
        INV = [sb("INV%d" % i, [128, 8, 128], BF16) for i in range(8)]
        lora_in = sb("lora_in", [128, 128], BF16)
        slg = sb("slg", [128, 128], BF16)
        Zbf = sb("Zbf", [128, 512], BF16)
        Yf = sb("Yf", [128, 512], F32)
        WTbf = sb("WTbf", [128, 4, 128], BF16)
        Ubf = sb("Ubf", [128, 512], BF16)
        Pst = sb("Pst", [128, 4, 64], F32)
        Pbf = sb("Pbf", [128, 4, 64], BF16)
        o1 = sb("o1", [128, 512], F32)
        oo = sb("oo", [128, 512], F32)
        stat = sb("stat", [128, 64], F32)
        ogbf = sb("ogbf", [128, 4, 128], BF16)
        obbf = sb("obbf", [128, 4, 128], BF16)
        Sin = sb("Sin", [64, 8, 64], F32)
        shs = [sb("shs%d" % i, [4, 512], F32) for i in range(2)]
        shf = sb("shf", [128, 14, 4], F32)
        sho = [sb("sho%d" % i, [4, 512], F32) for i in range(2)]
        lga = sb("lga", [16, 128], BF16)
        Sg = sb("Sg", [128, 2, 128], F32)
        Sgbf = sb("Sgbf", [128, 4, 2, 128], BF16)
        SgIn = sb("SgIn", [128, 4, 2, 128], F32)
        xg = sb("xg", [128, 12, 128], F32)
        Gf = [sb("G%d" % i, [128, 2, 128], F32) for i in range(6)]
        silu = sb("Gs", [128, 4, 128], F32)
        qdT, kiT, keT = [sb("Gh%d" % i, [128, 2, 128], BF16) for i in range(3)]
        vgbf = sb("Gv", [128, 4, 128], BF16)
        GS = sb("Ggs", [128, 4, 128], BF16)
        Vgtok = sb("Gt0", [128, 512], BF16)
        Ketok = sb("Gt1", [128, 256], BF16)
        o1g = sb("o1g", [128, 512], F32)
        oog = sb("oog", [128, 512], F32)
        statg = sb("statg", [128, 16], F32)

        T["psT"] = psb("psT", [128, 8, 128], BF16)
        psT = T["psT"]
        PS = [psb("ps%d" % i, [128, 512], F32) for i in range(7)]

        if VERBOSE:
            print("[A1] sbuf remaining after alloc:", nc.sbuf_bytes_remaining, flush=True)
        idf = cst[:, C_ID:C_ID + 128]
        bones = cst[:, C_BO:C_BO + 128]
        rstm = cst[:, C_RST:C_RST + 128]
        m4 = cst[:, C_M4:C_M4 + 512]
        msl = cst[:, C_SL:C_SL + 128]
        miu = cst[:, C_IU:C_IU + 128]

        h.memset("pool", Pst[:], 0.0, ["Pst"])
        h.memset("pool", Pbf[:], 0.0, ["Pbf"])
        h.memset("pool", Sg[:], 0.0, ["Sg"])
        h.memset("pool", lastc[:], 0.0, ["lastc"])

        def v4(ap, nseq, L):
            return ap.rearrange("p m (s l) -> p m s l", s=nseq)

        for b in BLKS:
            samp, nseq, L = geom(b)
            j = b - NPB
            valid = cst[:, (C_VS if samp else C_ONE):(C_VS if samp else C_ONE) + 128]
            load_x_block(h, b, xt, "xt")
            hT, hTk = rms_to_fm(h, T, xt, "xt", PR_G, b % 2)

            W = nseq * (L + 1)
            pv = prw[:, :, 0:W].rearrange("p m (s l) -> p m s l", s=nseq)
            for gi in range(4):
                ms = list(range(4 * gi, min(4 * gi + 4, 14)))
                pt = PS[5 + gi % 2]
                pn = "ps%d" % (5 + gi % 2)
                for mi, m in enumerate(ms):
                    for kc in range(8):
                        h.mm(pt[:, mi * 128:(mi + 1) * 128], win[:, kc, m * 128:(m + 1) * 128], hT[:, kc, :],
                             kc == 0, kc == 7, [("win", kc), hTk], [pn])
                nm = len(ms)
                h.cp("act", pv[:, ms[0]:ms[0] + nm, :, 1:L + 1],
                     pt[:, 0:nm * 128].rearrange("p (m s l) -> p m s l", m=nm, s=nseq),
                     [pn], ["prw"])
            gcols = [GLA0 + 128 * i for i in range(8)] + [GLA0 + 1040 + 128 * i for i in range(4)]
            for gi in range(3):
                pt, pn = PS[5 + gi % 2], "ps%d" % (5 + gi % 2)
                for mi in range(4):
                    c0 = gcols[4 * gi + mi]
                    for kc in range(8):
                        h.mm(pt[:, mi * 128:(mi + 1) * 128], win[:, kc, c0:c0 + 128], hT[:, kc, :],
                             kc == 0, kc == 7, [("win", kc), hTk], [pn])
                h.cp("act", xg[:, 4 * gi:4 * gi + 4, :], pt[:].rearrange("p (c l) -> p c l", c=4), [pn], [("xg", gi)])
            for kc in range(8):
                h.mm(PS[6][0:16, 0:128], win[:, kc, GLA0 + 1024:GLA0 + 1040], hT[:, kc, :], kc == 0, kc == 7,
                     [("win", kc), hTk], ["ps6"])
            h.cp("act", lga[:], PS[6][0:16, 0:128], ["ps6"], ["lga"])
            if not samp:
                h.cp("pool", pv[:, :, 0, 0:1], lastc[:], ["lastc"], ["prw"])
            else:
                for g4 in range(4):
                    ms = list(range(4 * g4, min(4 * g4 + 4, 14)))
                    nm = len(ms)
                    sh_, shk = shs[g4 % 2], "shs%d" % (g4 % 2)
                    h.dma("sp", sh_[:, 0:nm * 128], st_shift[4 * j:4 * j + 4, ms[0] * 128:(ms[0] + nm) * 128], [], [shk])
                    for mi, m in enumerate(ms):
                        h.tr(PS[5][:, mi * 4:(mi + 1) * 4], sh_[0:4, mi * 128:(mi + 1) * 128], idf[0:4, 0:4],
                             [shk, "cst"], ["ps5"])
                    h.cp("act", pv[:, ms[0]:ms[0] + nm, :, 0],
                         PS[5][:, 0:nm * 4].rearrange("p (m s) -> p m s", m=nm), ["ps5"], ["prw"])
            last_prompt = (b == NPB - 1)
            if samp or last_prompt:
                ncol = 4 if samp else 1
                if samp:
                    h.cp("pool", shf[:, :, 0:4], pv[:, :, :, 8], ["prw"], ["shf"])
                else:
                    h.cp("pool", shf[:, :, 0:1], pv[:, :, 0, 128:129], ["prw"], ["shf"])
                for g4 in range(4):
                    ms = list(range(4 * g4, min(4 * g4 + 4, 14)))
                    for mi, m in enumerate(ms):
                        h.tr(PS[6][0:ncol, mi * 128:(mi + 1) * 128], shf[:, m, 0:ncol], idf,
                             ["shf", "cst"], ["ps6"])
                    nm = len(ms)
                    so_, sok = sho[g4 % 2], "sho%d" % (g4 % 2)
                    h.cp("act", so_[0:ncol, 0:nm * 128], PS[6][0:ncol, 0:nm * 128], ["ps6"], [sok])
                    if samp:
                        h.dma("sp", o_shift_s[4 * j:4 * j + 4, ms[0] * 128:(ms[0] + nm) * 128], so_[0:4, 0:nm * 128],
                              [sok], [], isout=True)
                    else:
                        h.dma("sp", o_shift_p[0:1, ms[0] * 128:(ms[0] + nm) * 128], so_[0:1, 0:nm * 128],
                              [sok], [], isout=True)
            if not samp:
                h.cp("pool", lastc[:], pv[:, :, 0, 128:129], ["prw"], ["lastc"])
            xs4 = xs[:].rearrange("p m (s l) -> p m s l", s=nseq)
            cur = pv[:, :, :, 1:L + 1]
            prv = pv[:, :, :, 0:L]
            h.tt("dve", xs4[:, 0:9], prv[:, 0:9], cur[:, 0:9], ALU.subtract, ["prw"], [("xs", 0)])
            h.tt("pool", xs4[:, 9:14], prv[:, 9:14], cur[:, 9:14], ALU.subtract, ["prw"], [("xs", 1)])
            for m in range(14):
                h.stt("dve", xs4[:, m], xs4[:, m], prm[:, PR_MU + m:PR_MU + m + 1], cur[:, m], ALU.mult, ALU.add,
                      [("xs", 0 if m < 9 else 1), "prm", "prw"], [("xs", 2 + m)])
            rT = xs[:, 0:4, :]
            kT = xs[:, 4:8, :]
            vT = xs[:, 8:12, :]

            sw, aa, cum, E, Einv, Eprev, Eend, kk, kh, tmp = Fm
            n = lambda i: "F%d" % i
            N_SW, N_AA, N_CUM, N_E, N_EINV, N_EPREV, N_EEND, N_KK, N_KH, N_TMP = [n(i) for i in range(10)]
            gT, N_GT = gTs[b % 2], "gT%d" % (b % 2)
            bon, N_BON = bons[b % 2], "bon%d" % (b % 2)
            AR, N_AR = ARs[b % 2], "AR%d" % (b % 2)
            bT, kTb, BpT, KpT, vbf = Hm[0], Hm[1], Hm[2], Hm[3], Hm[4]
            h.act(lora_in[0:64, :], xs[0:64, 12, :], AF.Tanh, ["xs"], [("lora_in", 0)])
            h.cp("act", lora_in[64:128, :], xs[64:128, 12, :], ["xs"], [("lora_in", 1)])
            h.act(slg[:], xs[:, 13, :], AF.Sigmoid, ["xs"], ["slg"])
            for hg in range(4):
                h.mm(PS[5][:, hg * 128:(hg + 1) * 128], w2a2[0:64, hg * 128:(hg + 1) * 128], lora_in[0:64, :],
                     True, True, [("w2a2", 0), ("lora_in", 0)], ["ps5"])
            for hg in range(4):
                h.mm(PS[6][:, hg * 128:(hg + 1) * 128], w2a2[64:128, hg * 128:(hg + 1) * 128], lora_in[64:128, :],
                     True, True, [("w2a2", 1), ("lora_in", 1)], ["ps6"])
            for hg in range(4):
                h.act(sw[:, hg, :], PS[5][:, hg * 128:(hg + 1) * 128], AF.Sigmoid, ["ps5", "prm"], [(N_SW, hg)],
                      bias=prm[:, PR_W0 + hg:PR_W0 + hg + 1])
                h.act(aa[:, hg, :], PS[6][:, hg * 128:(hg + 1) * 128], AF.Sigmoid, ["ps6", "prm"], [(N_AA, hg)],
                      bias=prm[:, PR_A0 + hg:PR_A0 + hg + 1])
            for hg in range(4):
                h.mm(PS[5][:, hg * 128:(hg + 1) * 128], g2[:, hg * 128:(hg + 1) * 128], slg[:],
                     True, True, ["g2", "slg"], ["ps5"])
            h.cp("act", gT[:], PS[5][:].rearrange("p (c l) -> p c l", c=4), ["ps5"], [N_GT])
            h.stt("dve", sw[:], sw[:], C0, valid.unsqueeze(1).to_broadcast([128, 4, 128]),
                  ALU.mult, ALU.mult, [N_SW, "cst"], [N_SW])
            for hg in range(4):
                h.scan(cum[:, hg, :], rstm, sw[:, hg, :], [N_SW, "cst"], [(N_CUM, hg)])
            h.act(E[:], cum[:], AF.Exp, [N_CUM], [N_E])
            h.act(Einv[:], cum[:], AF.Exp, [N_CUM], [N_EINV], scale=-1.0)
            h.tt("pool", tmp[:], cum[:], sw[:], ALU.subtract, [N_CUM, N_SW], [N_TMP])
            h.act(Eprev[:], tmp[:], AF.Exp, [N_TMP], [N_EPREV])
            cum4 = cum[:].rearrange("p g (c l) -> p g c l", c=4)
            h.tt("pool", tmp[:].rearrange("p g (c l) -> p g c l", c=4),
                 cum4[:, :, :, 31:32].to_broadcast([128, 4, 4, 32]), cum4, ALU.subtract, [N_CUM], [N_TMP])
            h.act(Eend[:], tmp[:], AF.Exp, [N_TMP], [N_EEND])
            if samp:
                h.tt("pool", Eend[:], Eend[:], valid.unsqueeze(1).to_broadcast([128, 4, 128]), ALU.mult,
                     [N_EEND, "cst"], [N_EEND])
            h.tt("dve", kk[:], kT, bc(prm[:, PR_KK:PR_KK + 4], [128, 4, 128]), ALU.mult, ["xs", "prm"], [N_KK])
            h.act(tmp[:], kk[:], AF.Square, [N_KK], [N_TMP])
            for hg in range(4):
                h.mm(PS[6][:, hg * 128:(hg + 1) * 128], bones, tmp[:, hg, :], True, True, ["cst", N_TMP], ["ps6"])
            h.act(tmp[:], PS[6][:].rearrange("p (c l) -> p c l", c=4), AF.Sqrt, ["ps6"], [N_TMP])
            h.ts("dve", tmp[:], tmp[:], 1e-12, None, ALU.max, None, [N_TMP], [N_TMP])
            h.recip(tmp[:], tmp[:], [N_TMP], [N_TMP])
            h.tt("dve", kk[:], kk[:], tmp[:], ALU.mult, [N_KK, N_TMP], [N_KK])
            h.tt("pool", kh[:], aa[:], bc(prm[:, PR_KA:PR_KA + 4], [128, 4, 128]), ALU.mult, [N_AA, "prm"], [N_KH])
            h.tt("pool", kh[:], kh[:], bc(prm[:, PR_OMKA:PR_OMKA + 4], [128, 4, 128]), ALU.add, [N_KH, "prm"], [N_KH])
            h.tt("pool", kh[:], kh[:], kT, ALU.mult, [N_KH, "xs"], [N_KH])
            h.tt("dve", tmp[:], rT, bc(prm[:, PR_RK:PR_RK + 4], [128, 4, 128]), ALU.mult, ["xs", "prm"], [N_TMP])
            h.tt("dve", tmp[:], tmp[:], kh[:], ALU.mult, [N_TMP, N_KH], [N_TMP])
            for hg in range(4):
                h.mm(PS[5][:, hg * 128:(hg + 1) * 128], bones, tmp[:, hg, :], True, True, ["cst", N_TMP], ["ps5"])
            h.tt("dve", bon[:], PS[5][:].rearrange("p (c l) -> p c l", c=4), vT, ALU.mult, ["ps5", "xs"], [N_BON])
            h.tt("pool", aa[:], aa[:], kk[:], ALU.mult, [N_AA, N_KK], [N_AA])
            h.tt("dve", AR[:, :, 1, :], rT, E[:], ALU.mult, ["xs", N_E], [(N_AR, 1)])
            h.stt("dve", AR[:, :, 0, :], kk[:], -1.0, Eprev[:], ALU.mult, ALU.mult, [N_KK, N_EPREV], [(N_AR, 0)])
            h.tt("dve", bT[:], aa[:], Einv[:], ALU.mult, [N_AA, N_EINV], ["Hb0"])
            h.tt("dve", kTb[:], kh[:], Einv[:], ALU.mult, [N_KH, N_EINV], ["Hb1"])
            h.tt("pool", BpT[:], aa[:], Eend[:], ALU.mult, [N_AA, N_EEND], ["Hb2"])
            h.tt("pool", KpT[:], kh[:], Eend[:], ALU.mult, [N_KH, N_EEND], ["Hb3"])
            h.cp("pool", vbf[:], vT, ["xs"], ["Hb4"])
            h.cp("pool", stat[:, 0:16].rearrange("p (g c) -> p g c", g=4),
                 E[:].rearrange("p g (c l) -> p g c l", c=4)[:, :, :, 31], [N_E], [("stat", 0)])
            gam = stat[:, 0:16].rearrange("p (g c) -> p g c", g=4)

            for hd in range(8):
                hg, pb = hd // 2, 64 * (hd % 2)
                px = PS[hd % 2]
                pxn = "ps%d" % (hd % 2)
                h.mm(px[:, 0:256], bT[pb:pb + 64, hg, :], AR[pb:pb + 64, hg, :, :].rearrange("p a l -> p (a l)"),
                     True, True, ["Hb0", N_AR], [pxn])
                h.mm(px[:, 256:512], kTb[pb:pb + 64, hg, :], AR[pb:pb + 64, hg, :, :].rearrange("p a l -> p (a l)"),
                     True, True, ["Hb1", N_AR], [pxn])
                h.tt("dve", SCH[:, hd, :, :].rearrange("p a l -> p (a l)"), px[:], m4, ALU.mult,
                     [pxn, "cst"], [("SCH", hd)])
            Ac, Nn, An, Xc, Xtc, Xn, Xtn, Nc2 = INV
            NI = ["INV%d" % i for i in range(8)]
            Ac4 = Ac[:].rearrange("p (g a) l -> p g a l", a=2)
            for h2 in range(2):
                pt = PS[2 + h2]
                ptn = "ps%d" % (2 + h2)
                pb = 64 * h2
                for hg in range(4):
                    h.mm(pt[:, hg * 128:(hg + 1) * 128], AR[pb:pb + 64, hg, 0, :], bT[pb:pb + 64, hg, :],
                         True, True, [N_AR, "Hb0"], [ptn])
                h.tt("dve", Ac4[:, :, h2, :], pt[:].rearrange("p (q l) -> p q l", q=4),
                     msl.unsqueeze(1).to_broadcast([128, 4, 128]), ALU.mult, [ptn, "cst"], [NI[0]])
            h.tt("pool", Xc[:], SCH[:, :, 0, :], idb[:].unsqueeze(1).to_broadcast([128, 8, 128]), ALU.add,
                 ["SCH", "idb"], [NI[3]])
            h.tt("pool", Xtc[:], Ac[:], idb[:].unsqueeze(1).to_broadcast([128, 8, 128]), ALU.add,
                 [NI[0], "idb"], [NI[4]])

            Ncur_ap = lambda hd: SCH[:, hd, 0, :]
            Ncur_key = "SCH"
            Acur, Acur_key = Ac, NI[0]
            Xcur, Xcur_key, Xtcur, Xtcur_key = Xc, NI[3], Xtc, NI[4]
            Nnext = [(Nn, NI[1]), (Nc2, NI[7])]
            Anext = [(An, NI[2]), (Ac, NI[0])]
            Xnext = [(Xn, NI[5]), (Xc, NI[3])]
            Xtnext = [(Xtn, NI[6]), (Xtc, NI[4])]
            for lvl in range(4):
                last = (lvl == 3)
                Nx, Nxk = Nnext[lvl % 2]
                Ax, Axk = Anext[lvl % 2]
                Xx, Xxk = Xnext[lvl % 2]
                Xtx, Xtxk = Xtnext[lvl % 2]
                for g2i in range(2):
                    pa, pan = PS[2 * g2i], "ps%d" % (2 * g2i)
                    pbk, pbn = PS[2 * g2i + 1], "ps%d" % (2 * g2i + 1)
                    for q in range(4):
                        hd = 4 * g2i + q
                        h.mm(pa[:, q * 128:(q + 1) * 128], Acur[:, hd, :], Ncur_ap(hd), True, True,
                             [Acur_key, Ncur_key], [pan])
                    h.cp("act", Nx[:, 4 * g2i:4 * g2i + 4, :], pa[:].rearrange("p (q l) -> p q l", q=4),
                         [pan], [(Nxk, g2i)])
                    if not last:
                        for q in range(4):
                            hd = 4 * g2i + q
                            h.mm(pbk[:, q * 128:(q + 1) * 128], Ncur_ap(hd), Acur[:, hd, :], True, True,
                                 [Acur_key, Ncur_key], [pbn])
                        h.cp("act", Ax[:, 4 * g2i:4 * g2i + 4, :], pbk[:].rearrange("p (q l) -> p q l", q=4),
                             [pbn], [(Axk, g2i)])
                for g2i in range(2):
                    pa, pan = PS[4], "ps4"
                    for q in range(4):
                        hd = 4 * g2i + q
                        h.mm(pa[:, q * 128:(q + 1) * 128], Xtcur[:, hd, :], Nx[:, hd, :], True, True,
                             [Xtcur_key, (Nxk, g2i)], [pan])
                    h.tt("dve", Xx[:, 4 * g2i:4 * g2i + 4, :], pa[:].rearrange("p (q l) -> p q l", q=4),
                         Xcur[:, 4 * g2i:4 * g2i + 4, :], ALU.add, [pan, Xcur_key], [(Xxk, g2i)])
                if not last:
                    for g2i in range(2):
                        pa, pan = PS[2 * g2i], "ps%d" % (2 * g2i)
                        for q in range(4):
                            hd = 4 * g2i + q
                            h.mm(pa[:, q * 128:(q + 1) * 128], Nx[:, hd, :], Xtcur[:, hd, :], True, True,
                                 [Xtcur_key, (Nxk, g2i)], [pan])
                        h.tt("dve", Xtx[:, 4 * g2i:4 * g2i + 4, :], pa[:].rearrange("p (q l) -> p q l", q=4),
                             Xtcur[:, 4 * g2i:4 * g2i + 4, :], ALU.add, [pan, Xtcur_key], [(Xtxk, g2i)])
                Ncur_ap = (lambda t: (lambda hd: t[:, hd, :]))(Nx)
                Ncur_key = Nxk
                Acur, Acur_key = Ax, Axk
                Xcur, Xcur_key = Xx, Xxk
                Xtcur, Xtcur_key = Xtx, Xtxk
            X4, X4k = Xcur, Xcur_key

            Atok, Bptok, Kptok, Vtok = tok
            srcs = [(AR[:, :, 0, :], N_AR, Atok, "tok0"), (BpT[:], "Hb2", Bptok, "tok1"),
                    (KpT[:], "Hb3", Kptok, "tok2"), (vbf[:], "Hb4", Vtok, "tok3")]
            for si in range(0, 4, 2):
                for u in range(2):
                    src, srck, dst, dstk = srcs[si + u]
                    for hg in range(4):
                        h.tr(psT[:, u * 4 + hg, :], src[:, hg, :], idb[:], [srck, "idb"], [("psT", u * 4 + hg)])
                for u in range(2):
                    src, srck, dst, dstk = srcs[si + u]
                    h.cp("act", dst[:], psT[:, u * 4:(u + 1) * 4, :].rearrange("p c l -> p (c l)"),
                         ["psT"], [dstk])

            for hd in range(8):
                h.mm(PS[0][:, hd * 64:(hd + 1) * 64], SCH[:, hd, 2, :], Vtok[:, hd * 64:(hd + 1) * 64],
                     True, True, ["SCH", "tok3"], ["ps0"])
            h.cp("act", Zbf[:], PS[0][:], ["ps0"], ["Zbf"])
            for hd in range(8):
                h.mm(PS[1][:, hd * 64:(hd + 1) * 64], X4[:, hd, :], Zbf[:, hd * 64:(hd + 1) * 64],
                     True, True, [X4k, "Zbf"], ["ps1"])
            h.cp("act", Yf[:], PS[1][:], ["ps1"], ["Yf"])
            for hd in range(8):
                hg, pb = hd // 2, 64 * (hd % 2)
                h.mm(PS[2][pb:pb + 64, hg * 128:(hg + 1) * 128], Atok[:, hd * 64:(hd + 1) * 64], X4[:, hd, :],
                     True, True, ["tok0", X4k], ["ps2"])
            h.cp("act", WTbf[:], PS[2][:].rearrange("p (c l) -> p c l", c=4), ["ps2"], ["WTbf"])

            psUs, psUn = (PS[0], PS[1]), ("ps0", "ps1")
            psOs, psOn = (PS[2], PS[3]), ("ps2", "ps3")
            psPn = PS[4]
            Ubf4 = Ubf[:].rearrange("p (g a v) -> p g a v", g=4, a=2)
            Yf4 = Yf[:].rearrange("p (g a v) -> p g a v", g=4, a=2)
            for c in range(4):
                s0 = 32 * c
                if samp:
                    seq = 4 * j + c
                    h.dma("sp", Sin[:], st_wkv[seq].rearrange("h v k -> v h k"), [], ["Sin"])
                    for hg in range(4):
                        h.tr(PS[4][:, hg * 64:(hg + 1) * 64],
                             Sin[:, 2 * hg:2 * hg + 2, :].rearrange("p a k -> p (a k)"), idf[0:64, 0:64],
                             ["Sin", "cst"], ["ps4"])
                    h.cp("act", Pst[:], PS[4][:, 0:256].rearrange("p (c v) -> p c v", c=4), ["ps4"], ["Pst"])
                    h.cp("dve", Pbf[:], Pst[:], ["Pst"], ["Pbf"])
                for hd in range(8):
                    hg, h2 = hd // 2, hd % 2
                    pb = 64 * h2
                    h.mm(psUs[h2][s0:s0 + 32, hg * 64:(hg + 1) * 64], WTbf[pb:pb + 64, hg, s0:s0 + 32],
                         Pbf[pb:pb + 64, hg, :], True, True, ["WTbf", "Pbf"], [psUn[h2]], tp=(pb, s0))
                for hd in range(8):
                    hg, h2 = hd // 2, hd % 2
                    pb = 64 * h2
                    h.mm(psOs[h2][s0:s0 + 32, hg * 64:(hg + 1) * 64], AR[pb:pb + 64, hg, 1, s0:s0 + 32],
                         Pbf[pb:pb + 64, hg, :], True, True, [N_AR, "Pbf"], [psOn[h2]], tp=(pb, s0))
                for h2 in range(2):
                    h.tt("dve", Ubf4[s0:s0 + 32, :, h2, :],
                         psUs[h2][s0:s0 + 32, 0:256].rearrange("p (g v) -> p g v", g=4),
                         Yf4[s0:s0 + 32, :, h2, :], ALU.add, [psUn[h2], "Yf"], [("Ubf", c)])
                for hd in range(8):
                    hg, pb = hd // 2, 64 * (hd % 2)
                    h.mm(psPn[pb:pb + 64, hg * 64:(hg + 1) * 64], Bptok[s0:s0 + 32, hd * 64:(hd + 1) * 64],
                         Ubf[s0:s0 + 32, hd * 64:(hd + 1) * 64], True, False, ["tok1", ("Ubf", c)], ["ps4"],
                         tp=(s0, pb))
                    h.mm(psPn[pb:pb + 64, hg * 64:(hg + 1) * 64], Kptok[s0:s0 + 32, hd * 64:(hd + 1) * 64],
                         Vtok[s0:s0 + 32, hd * 64:(hd + 1) * 64], False, True, ["tok2", "tok3"], ["ps4"],
                         tp=(s0, pb))
                for hg in range(4):
                    h.stt("dve", Pst[:, hg, :], Pst[:, hg, :], gam[:, hg, c:c + 1], psPn[:, hg * 64:(hg + 1) * 64],
                          ALU.mult, ALU.add, ["Pst", ("stat", 0), "ps4"], ["Pst"])
                if not samp and not (b == NPB - 1 and c == 3):
                    h.cp("act", Pbf[:], Pst[:], ["Pst"], ["Pbf"])
                if samp or (b == NPB - 1 and c == 3):
                    for hg in range(4):
                        h.tr(PS[4][0:64, hg * 128:(hg + 1) * 128], Pst[:, hg, :], idf, ["Pst", "cst"], ["ps4"])
                    h.cp("act", Sin[:].rearrange("p h k -> p (h k)"), PS[4][0:64, :], ["ps4"], ["Sin"])
                    dst = o_wkv_s[4 * j + c] if samp else o_wkv_p[0]
                    h.dma("sp", dst.rearrange("h v k -> v h k"), Sin[:], ["Sin"], [], isout=True)
            for hd in range(8):
                h.mm(PS[0][:, hd * 64:(hd + 1) * 64], SCH[:, hd, 1, :], Ubf[:, hd * 64:(hd + 1) * 64],
                     True, False, ["SCH", "Ubf"], ["ps0"])
                h.mm(PS[0][:, hd * 64:(hd + 1) * 64], SCH[:, hd, 3, :], Vtok[:, hd * 64:(hd + 1) * 64],
                     False, True, ["SCH", "tok3"], ["ps0"])
            o14 = o1[:].rearrange("p (g a v) -> p g a v", g=4, a=2)
            for h2 in range(2):
                h.cp("act", o14[:, :, h2, :], psOs[h2][:, 0:256].rearrange("p (g v) -> p g v", g=4),
                     [psOn[h2]], ["o1"])
            h.tt("dve", oo[:], o1[:], PS[0][:], ALU.add, ["o1", "ps0"], ["oo"])

            oo3 = oo[:].rearrange("p (h v) -> p h v", h=8)
            osq = o1
            h.act(osq[:], oo[:], AF.Square, ["oo"], ["o1"])
            s1, s2, mean, msq, rs, nb = (stat[:, 16:24], stat[:, 24:32], stat[:, 32:40], stat[:, 40:48],
                                         stat[:, 48:56], stat[:, 56:64])
            h.rsum(s1, oo3, ["oo"], [("stat", 1)])
            h.rsum(s2, osq[:].rearrange("p (h v) -> p h v", h=8), ["o1"], [("stat", 2)])
            h.ts("dve", mean, s1, 1.0 / 64, None, ALU.mult, None, [("stat", 1)], [("stat", 3)])
            h.tt("dve", msq, mean, mean, ALU.mult, [("stat", 3)], [("stat", 4)])
            h.stt("dve", rs, s2, 1.0 / 64, msq, ALU.mult, ALU.subtract, [("stat", 2), ("stat", 4)], [("stat", 5)])
            h.ts("dve", rs, rs, 64e-5, None, ALU.add, None, [("stat", 5)], [("stat", 5)])
            h.act(rs, rs, AF.Sqrt, [("stat", 5)], [("stat", 5)])
            h.recip(rs, rs, [("stat", 5)], [("stat", 5)])
            h.tt("dve", oo3, oo3, mean.unsqueeze(2).to_broadcast([128, 8, 64]), ALU.subtract,
                 ["oo", ("stat", 3)], ["oo"])
            h.tt("dve", oo3, oo3, rs.unsqueeze(2).to_broadcast([128, 8, 64]), ALU.mult,
                 ["oo", ("stat", 5)], ["oo"])
            for hg in range(4):
                h.tr(PS[0][:, hg * 128:(hg + 1) * 128], oo[:, hg * 128:(hg + 1) * 128], idf, ["oo", "cst"], ["ps0"])
            ps0v = PS[0][:].rearrange("p (c l) -> p c l", c=4)
            t2 = o1[:].rearrange("p (c l) -> p c l", c=4)
            h.tt("dve", t2, ps0v, bc(prm[:, PR_LW:PR_LW + 4], [128, 4, 128]), ALU.mult, ["ps0", "prm"], ["o1"])
            h.tt("pool", t2, t2, bc(prm[:, PR_LB:PR_LB + 4], [128, 4, 128]), ALU.add, ["o1", "prm"], ["o1"])
            h.tt("pool", t2, t2, bon[:], ALU.add, ["o1", N_BON], ["o1"])
            h.tt("dve", ogbf[:], t2, gT[:], ALU.mult, ["o1", N_GT], ["ogbf"])
            h.dma("sp", og_s[b], ogbf[:].rearrange("p c l -> p (c l)"), ["ogbf"], [])

            qT, kgT, vgT, ogT = xg[:, 0:2, :], xg[:, 2:4, :], xg[:, 4:8, :], xg[:, 8:12, :]
            for c2 in range(2):
                h.mm(PS[5][:, c2 * 128:(c2 + 1) * 128], wg2[0:16, c2 * 128:(c2 + 1) * 128], lga[:], True, True,
                     ["wg2", "lga"], ["ps5"])
            la_, cg, Eg, Eginv, Egend, gtmp = Gf
            for c2 in range(2):
                h.act(la_[:, c2, :], PS[5][:, c2 * 128:(c2 + 1) * 128], AF.Exp, ["ps5", "prm"], [("G0", c2)],
                      scale=-1.0, bias=prm[:, PR_BG + c2:PR_BG + c2 + 1])
            h.ts("dve", la_[:], la_[:], 1.0, None, ALU.add, None, ["G0"], ["G0"])
            h.act(la_[:], la_[:], AF.Ln, ["G0"], ["G0"])
            h.stt("dve", la_[:], la_[:], -1.0 / 16.0,
                  valid.unsqueeze(1).to_broadcast([128, 2, 128]), ALU.mult, ALU.mult, ["G0", "cst"], ["G0"])
            for c2 in range(2):
                h.scan(cg[:, c2, :], rstm, la_[:, c2, :], ["G0", "cst"], [("G1", c2)])
            h.act(Eg[:], cg[:], AF.Exp, ["G1"], ["G2"])
            h.act(Eginv[:], cg[:], AF.Exp, ["G1"], ["G3"], scale=-1.0)
            cg4 = cg[:].rearrange("p g (c l) -> p g c l", c=4)
            h.tt("pool", gtmp[:].rearrange("p g (c l) -> p g c l", c=4),
                 cg4[:, :, :, 31:32].to_broadcast([128, 2, 4, 32]), cg4, ALU.subtract, ["G1"], ["G5"])
            h.act(Egend[:], gtmp[:], AF.Exp, ["G5"], ["G4"])
            if samp:
                h.tt("pool", Egend[:], Egend[:], valid.unsqueeze(1).to_broadcast([128, 2, 128]),
                     ALU.mult, ["G4", "cst"], ["G4"])
            h.cp("pool", statg[:, 0:8].rearrange("p (g c) -> p g c", g=2),
                 Eg[:].rearrange("p g (c l) -> p g c l", c=4)[:, :, :, 31], ["G2"], [("statg", 0)])
            gamg = statg[:, 0:8].rearrange("p (g c) -> p g c", g=2)
            h.stt("dve", qdT[:], qT, 0.125, Eg[:], ALU.mult, ALU.mult, [("xg", 0), "G2"], ["Gh0"])
            h.tt("pool", kiT[:], kgT, Eginv[:], ALU.mult, [("xg", 0), "G3"], ["Gh1"])
            h.tt("pool", keT[:], kgT, Egend[:], ALU.mult, [("xg", 0), "G4"], ["Gh2"])
            h.cp("pool", vgbf[:], vgT, [("xg", 1)], ["Gv"])
            h.act(silu[:], ogT, AF.Silu, [("xg", 2)], ["Gs"])
            GS4 = GS[:].rearrange("p (g a) l -> p g a l", a=2)
            for h2 in range(2):
                pb = 64 * h2
                pt, ptn = PS[5 + h2], "ps%d" % (5 + h2)
                for c2 in range(2):
                    h.mm(pt[:, c2 * 128:(c2 + 1) * 128], kiT[pb:pb + 64, c2, :], qdT[pb:pb + 64, c2, :], True, True,
                         ["Gh1", "Gh0"], [ptn])
                h.tt("dve", GS4[:, :, h2, :], pt[:, 0:256].rearrange("p (q l) -> p q l", q=2),
                     miu.unsqueeze(1).to_broadcast([128, 2, 128]), ALU.mult, [ptn, "cst"], ["Ggs"])
            for hg in range(4):
                h.tr(psT[:, hg, :], vgbf[:, hg, :], idb[:], ["Gv", "idb"], [("psT", hg)])
            for c2 in range(2):
                h.tr(psT[:, 4 + c2, :], keT[:, c2, :], idb[:], ["Gh2", "idb"], [("psT", 4 + c2)])
            h.cp("act", Vgtok[:], psT[:, 0:4, :].rearrange("p c l -> p (c l)"), ["psT"], ["Gt0"])
            h.cp("act", Ketok[:], psT[:, 4:6, :].rearrange("p c l -> p (c l)"), ["psT"], ["Gt1"])
            if samp:
                h.dma("sp", SgIn[:], st_gla[4 * j:4 * j + 4].rearrange("q (c2 h2) k v -> (h2 k) q c2 v", h2=2),
                      [], ["SgIn"])
                h.cp("act", Sgbf[:], SgIn[:], ["SgIn"], ["Sgbf"])
            else:
                h.cp("act", Sgbf[:, 0, :, :], Sg[:], ["Sg"], [("Sgbf", 0)])
            for c in range(4):
                s0 = 32 * c
                pt, pn = PS[5 + c % 2], "ps%d" % (5 + c % 2)
                for hd in range(4):
                    c2, pb = hd // 2, 64 * (hd % 2)
                    off = c2 * 128
                    h.mm(pt[pb:pb + 64, off:off + 128], Ketok[s0:s0 + 32, hd * 64:(hd + 1) * 64],
                         Vgtok[s0:s0 + 32, hd * 128:(hd + 1) * 128], True, True, ["Gt1", "Gt0"], [pn],
                         tp=(s0, pb))
                dv = pt[:, 0:256].rearrange("p (g v) -> p g v", g=2)
                gb = gamg[:, :, c:c + 1].to_broadcast([128, 2, 128])
                for c2 in range(2):
                    if samp:
                        h.stt("dve", SgIn[:, c, c2, :], SgIn[:, c, c2, :], gamg[:, c2, c:c + 1], dv[:, c2, :],
                              ALU.mult, ALU.add, [("SgIn", c), ("statg", 0), pn], [("SgIn", c)])
                    else:
                        h.stt("dve", Sg[:, c2, :], Sg[:, c2, :], gamg[:, c2, c:c + 1], dv[:, c2, :],
                              ALU.mult, ALU.add, ["Sg", ("statg", 0), pn], ["Sg"])
                if (not samp) and c < 3:
                    h.cp("act", Sgbf[:, c + 1, :, :], Sg[:], ["Sg"], [("Sgbf", c + 1)])
            if samp:
                h.dma("sp", o_gla_s[4 * j:4 * j + 4].rearrange("q (c2 h2) k v -> (h2 k) q c2 v", h2=2),
                      SgIn[:], ["SgIn"], [], isout=True)
            elif b == NPB - 1:
                h.dma("sp", o_gla_p[0].rearrange("(c2 h2) k v -> (h2 k) c2 v", h2=2), Sg[:], ["Sg"], [],
                      isout=True)
            for c in range(4):
                s0 = 32 * c
                for hd in range(4):
                    c2, h2 = hd // 2, hd % 2
                    pb = 64 * h2
                    h.mm(PS[5 + h2][s0:s0 + 32, c2 * 128:(c2 + 1) * 128], qdT[pb:pb + 64, c2, s0:s0 + 32],
                         Sgbf[pb:pb + 64, c, c2, :], True, True, ["Gh0", "Sgbf"], ["ps%d" % (5 + h2)], tp=(pb, s0))
            o1gv = o1g[:].rearrange("p (g a v) -> p g a v", g=2, a=2)
            for h2 in range(2):
                h.cp("act", o1gv[:, :, h2, :], PS[5 + h2][:, 0:256].rearrange("p (g v) -> p g v", g=2),
                     ["ps%d" % (5 + h2)], ["o1g"])
            for hd in range(4):
                h.mm(PS[5][:, hd * 128:(hd + 1) * 128], GS[:, hd, :], Vgtok[:, hd * 128:(hd + 1) * 128], True, True,
                     ["Ggs", "Gt0"], ["ps5"])
            h.tt("dve", oog[:], o1g[:], PS[5][:], ALU.add, ["o1g", "ps5"], ["oog"])
            oo4 = oog[:].rearrange("p (h v) -> p h v", h=4)
            h.act(o1g[:], oog[:], AF.Square, ["oog"], ["o1g"])
            gs2, grs = statg[:, 8:12], statg[:, 12:16]
            h.rsum(gs2, o1g[:].rearrange("p (h v) -> p h v", h=4), ["o1g"], [("statg", 1)])
            h.ts("dve", grs, gs2, 1.0 / 128, 1e-6, ALU.mult, ALU.add, [("statg", 1)], [("statg", 2)])
            h.act(grs, grs, AF.Sqrt, [("statg", 2)], [("statg", 2)])
            h.recip(grs, grs, [("statg", 2)], [("statg", 2)])
            h.tt("dve", oo4, oo4, grs.unsqueeze(2).to_broadcast([128, 4, 128]), ALU.mult, ["oog", ("statg", 2)], ["oog"])
            for hd in range(4):
                h.tr(PS[6][:, hd * 128:(hd + 1) * 128], oog[:, hd * 128:(hd + 1) * 128], idf, ["oog", "cst"], ["ps6"])
            h.stt("dve", obbf[:], PS[6][:].rearrange("p (c l) -> p c l", c=4), prm[:, PR_NW:PR_NW + 1], silu[:],
                  ALU.mult, ALU.mult, ["ps6", "prm", "Gs"], ["obbf"])
            h.dma("sp", ob_s[b], obbf[:].rearrange("p c l -> p (c l)"), ["obbf"], [])

            if dbg is not None and dbg.get("blk") == b and dbg.get("phase") == "a1":
                src, keys = dbg["fn"](dict(locals()))
                h.dma("sp", dbg_out, src, keys, [], isout=True)
        P.emit()

    gx = contextlib.ExitStack()
    wdn = gx.enter_context(nc.sbuf_tensor("g_wdn", [128, 22, D], BF16))
    w_dn_v = ffn_w_down.rearrange("(c p) n -> p c n", p=128)
    wdn_loaded = False
    with contextlib.ExitStack() as ph:
      if "b" in phases:
        P = Prog(nc, gs, "b")
        h = H(P)

        def sb(name, shape, dt):
            return ph.enter_context(nc.sbuf_tensor("b_" + name, list(shape), dt))

        def psb(name, shape, dt):
            return ph.enter_context(nc.psum_tensor("b_" + name, list(shape), dt))

        T = {}
        cst, idb = load_consts(h, sb)
        T["idb"] = idb
        prm = sb("prm", [128, NPRM], F32)
        T["prm"] = prm
        load_param_cols(h, prm, PR_G, norm_mix, 8)
        wing = sb("wing", [128, 8, 2048], BF16)
        w_in_v = w_in.rearrange("(c p) n -> p c n", p=128)
        for kc in range(8):
            h.dma("pool", wing[:, kc, :], w_in_v[:, kc, GATE0:INC], [], [("wing", kc)])
        woa = sb("woa", [128, 4, D], BF16)
        wob = sb("wob", [128, 4, D], BF16)
        wo = sb("wo", [128, 8, D], BF16)
        h.dma("pool", woa[:], w_out_a.rearrange("(c p) n -> p c n", p=128), [], ["woa"])
        h.dma("pool", wob[:], w_out_b.rearrange("(c p) n -> p c n", p=128), [], ["wob"])
        for kc in range(8):
            h.dma("pool", wo[:, kc, :], w_o.rearrange("(c p) n -> p c n", p=128)[:, kc, :], [], [("wo", kc)])
        xts = [sb("xt%d" % i, [128, D], F32) for i in range(4)]
        alloc_rms(T, sb)
        ogbs = [sb("ogb%d" % i, [128, 4, 128], BF16) for i in range(2)]
        obbs = [sb("obb%d" % i, [128, 4, 128], BF16) for i in range(2)]
        sgas = [sb("sga%d" % i, [128, 8, 128], F32) for i in range(2)]
        sgbs = [sb("sgb%d" % i, [128, 8, 128], F32) for i in range(2)]
        tas = [sb("ta%d" % i, [128, 8, 128], F32) for i in range(2)]
        mgs = [sb("mg%d" % i, [128, 8, 128], BF16) for i in range(2)]
        x1ts = [sb("x1t%d" % i, [128, D], F32) for i in range(2)]
        T["psT"] = psb("psT", [128, 8, 128], BF16)
        PS = [psb("ps%d" % i, [128, 512], F32) for i in range(7)]
        if FILLERS:
            _pf, _id = PS[6], idb
            P.filler = ((lambda e: e.matmul(_pf[:, 0:128], lhsT=_id[:], rhs=_id[:], start=True, stop=True)), 70.0)
        for b in BLKS:
            xt, xtn = xts[b % 4], "xt%d" % (b % 4)
            load_x_block(h, b, xt, xtn)
            p2 = b % 2
            ogb, obb, sga, sgb, ta, mg, x1t = ogbs[p2], obbs[p2], sgas[p2], sgbs[p2], tas[p2], mgs[p2], x1ts[p2]
            K_ogb, K_obb, K_sga, K_sgb, K_ta, K_mg, K_x1t = ["%s%d" % (nm, p2) for nm in
                                                            ("ogb", "obb", "sga", "sgb", "ta", "mg", "x1t")]
            h.dma("sp", ogb[:].rearrange("p c l -> p (c l)"), og_s[b], [], [K_ogb])
            h.dma("sp", obb[:].rearrange("p c l -> p (c l)"), ob_s[b], [], [K_obb])
            hT, hTk = rms_to_fm(h, T, xt, xtn, PR_G, b % 2)
            for half, dst, dk in ((0, sga, K_sga), (1, sgb, K_sgb)):
                for gi in range(2):
                    pt, pn = PS[gi], "ps%d" % gi
                    for mi in range(4):
                        c0 = half * 1024 + (4 * gi + mi) * 128
                        for kc in range(8):
                            h.mm(pt[:, mi * 128:(mi + 1) * 128], wing[:, kc, c0:c0 + 128], hT[:, kc, :],
                                 kc == 0, kc == 7, [("wing", kc), hTk], [pn])
                    h.act(dst[:, 4 * gi:4 * gi + 4, :], pt[:].rearrange("p (c l) -> p c l", c=4), AF.Sigmoid,
                          [pn], [(dk, gi)])
            for gi in range(2):
                pt, pn = PS[2 + gi], "ps%d" % (2 + gi)
                for mi in range(4):
                    m = 4 * gi + mi
                    for kc in range(4):
                        h.mm(pt[:, mi * 128:(mi + 1) * 128], woa[:, kc, m * 128:(m + 1) * 128], ogb[:, kc, :],
                             kc == 0, kc == 3, ["woa", K_ogb], [pn])
                h.tt("dve", ta[:, 4 * gi:4 * gi + 4, :], pt[:].rearrange("p (c l) -> p c l", c=4),
                     sga[:, 4 * gi:4 * gi + 4, :], ALU.mult, [pn, (K_sga, gi)], [(K_ta, gi)])
            for gi in range(2):
                pt, pn = PS[4 + gi], "ps%d" % (4 + gi)
                for mi in range(4):
                    m = 4 * gi + mi
                    for kc in range(4):
                        h.mm(pt[:, mi * 128:(mi + 1) * 128], wob[:, kc, m * 128:(m + 1) * 128], obb[:, kc, :],
                             kc == 0, kc == 3, ["wob", K_obb], [pn])
                h.tt("dve", sgb[:, 4 * gi:4 * gi + 4, :], pt[:].rearrange("p (c l) -> p c l", c=4),
                     sgb[:, 4 * gi:4 * gi + 4, :], ALU.mult, [pn, (K_sgb, gi)], [(K_sgb, gi)])
                h.tt("pool", mg[:, 4 * gi:4 * gi + 4, :], ta[:, 4 * gi:4 * gi + 4, :],
                     sgb[:, 4 * gi:4 * gi + 4, :], ALU.add, [(K_ta, gi), (K_sgb, gi)], [(K_mg, gi)])
            for nh in range(2):
                pt, pn = PS[2 + nh], "ps%d" % (2 + nh)
                for kc in range(8):
                    h.mm(pt[:], mg[:, kc, :], wo[:, kc, nh * 512:(nh + 1) * 512], kc == 0, kc == 7,
                         [K_mg, ("wo", kc)], [pn])
                h.tt("dve", x1t[:, nh * 512:(nh + 1) * 512], pt[:], xt[:, nh * 512:(nh + 1) * 512], ALU.add,
                     [pn, xtn], [(K_x1t, nh)])
            h.dma("sp", x1_s[b * 128:(b + 1) * 128, :], x1t[:], [K_x1t], [])
        P.emit()

    with contextlib.ExitStack() as ph:
      if "c" in phases:
        P = Prog(nc, gs, "c")
        h = H(P)

        def sb(name, shape, dt):
            return ph.enter_context(nc.sbuf_tensor("c_" + name, list(shape), dt))

        def psb(name, shape, dt):
            return ph.enter_context(nc.psum_tensor("c_" + name, list(shape), dt))

        T = {}
        cst, idb = load_consts(h, sb)
        T["idb"] = idb
        idf = cst[:, C_ID:C_ID + 128]
        prm = sb("prm", [128, 8], F32)
        T["prm"] = prm
        load_param_cols(h, prm, 0, norm_ffn, 8)
        cw = sb("cw", [128, 4, 44], F32)
        for jx in range(3):
            h.dma("sp", cw[:, jx, :], ffn_conv_w[jx].rearrange("(c p) -> p c", p=128), [], [("cw", jx)], slow=True)
        h.dma("sp", cw[:, 3, :], ffn_conv_b.rearrange("(c p) -> p c", p=128), [], [("cw", 3)], slow=True)
        nfb = sb("nfb", [128, D], F32)
        h.dma("sp", nfb[:], norm_final.partition_broadcast(128), [], ["nfb"])
        wup = sb("wup", [128, 8, F2], BF16)
        w_up_v = ffn_w_up.rearrange("(c p) n -> p c n", p=128)
        for kc in range(8):
            h.dma("pool", wup[:, kc, :], w_up_v[:, kc, :], [], [("wup", kc)])
        if not wdn_loaded:
            for kc in range(22):
                h.dma("pool", wdn[:, kc, :], w_dn_v[:, kc, :], [], [("wdn", kc)])
        xts = [sb("xt0", [128, D], F32), sb("xt1", [128, D], F32)]
        alloc_rms(T, sb)
        ub = [sb("ub0", [128, 4, 136], F32), sb("ub1", [128, 4, 136], F32)]
        ucar = sb("ucar", [128, 44, 2], F32)
        cin = sb("cin", [128, 44, 4, 2], F32)
        cout = sb("cout", [128, 44, 8], F32)
        cstg = [sb("cstg%d" % i, [8, 512], F32) for i in range(2)]
        csto = [sb("csto%d" % i, [8, 512], F32) for i in range(2)]
        fss = [sb("fss%d" % i, [128, 1], F32) for i in range(2)]
        frs = [sb("frs%d" % i, [128, 1], F32) for i in range(2)]
        cc = [sb("cc0", [128, 4, 128], F32), sb("cc1", [128, 4, 128], F32)]
        g1 = [sb("g10", [128, 2, 128], F32), sb("g11", [128, 2, 128], F32)]
        g2t = [sb("g20", [128, 2, 128], F32), sb("g21", [128, 2, 128], F32)]
        actTs = [sb("actT%d" % i, [128, 22, 128], BF16) for i in range(2)]
        x2s = [sb("x2%d" % i, [128, D], F32) for i in range(2)]
        yts = [sb("yt0", [128, D], F32)] * 2
        T["psT"] = psb("psT", [128, 8, 128], BF16)
        PS = [psb("ps%d" % i, [128, 512], F32) for i in range(7)]
        h.memset("pool", ucar[:], 0.0, ["ucar"])
        if FILLERS:
            _pf2, _id2 = PS[6], idb
            P.filler = ((lambda e: e.matmul(_pf2[:, 0:128], lhsT=_id2[:], rhs=_id2[:], start=True, stop=True)), 70.0)
        for b in BLKS:
            samp, nseq, L = geom(b)
            j = b - NPB
            xt, xtn = xts[b % 2], "xt%d" % (b % 2)
            actT, actk = actTs[b % 2], "actT%d" % (b % 2)
            x2, x2k = x2s[b % 2], "x2%d" % (b % 2)
            yt, ytk = yts[0], "yt0"
            h.dma("sp", xt[:], x1_s[b * 128:(b + 1) * 128, :], [], [xtn])
            hT, hTk = rms_to_fm(h, T, xt, xtn, 0, b % 2)
            W = nseq * (L + 2)
            if samp:
                stc_v = st_conv[4 * j:4 * j + 4].rearrange("q t f -> (q t) f")
                for g4 in range(11):
                    cg_, cgk = cstg[g4 % 2], "cstg%d" % (g4 % 2)
                    h.dma("sp", cg_[:], stc_v[:, g4 * 512:(g4 + 1) * 512], [], [cgk])
                    for mi in range(4):
                        m = 4 * g4 + mi
                        h.tr(PS[4][:, mi * 8:(mi + 1) * 8], cg_[0:8, mi * 128:(mi + 1) * 128], idf[0:8, 0:8],
                             [cgk, "cst"], ["ps4"])
                    h.cp("act", cin[:, 4 * g4:4 * g4 + 4, :, :].rearrange("p m q t -> p m (q t)"),
                         PS[4][:, 0:32].rearrange("p (m x) -> p m x", m=4), ["ps4"], [("cin", g4)])
            last_prompt = (b == NPB - 1)
            for gi in range(11):
                u, un = ub[gi % 2], "ub%d" % (gi % 2)
                uv = u[:, :, 0:W].rearrange("p m (s l) -> p m s l", s=nseq)
                pt, pn = PS[gi % 2], "ps%d" % (gi % 2)
                chunks = [2 * gi, 2 * gi + 1, 22 + 2 * gi, 23 + 2 * gi]
                for mi, m in enumerate(chunks):
                    for kc in range(8):
                        h.mm(pt[:, mi * 128:(mi + 1) * 128], wup[:, kc, m * 128:(m + 1) * 128], hT[:, kc, :],
                             kc == 0, kc == 7, [("wup", kc), hTk], [pn])
                h.cp("act", uv[:, :, :, 2:L + 2], pt[:].rearrange("p (m s l) -> p m s l", m=4, s=nseq),
                     [pn], [un])
                for half in range(2):
                    m0 = chunks[2 * half]
                    if samp:
                        h.cp("pool", uv[:, 2 * half:2 * half + 2, :, 0:2], cin[:, m0:m0 + 2, :, :],
                             [("cin", m0 // 4)], [un])
                    else:
                        h.cp("pool", uv[:, 2 * half:2 * half + 2, 0, 0:2], ucar[:, m0:m0 + 2, :],
                             [("ucar", gi)], [un])
                for half in range(2):
                    m0 = chunks[2 * half]
                    if samp:
                        h.cp("pool", cout[:, m0:m0 + 2, :].rearrange("p m (q t) -> p m q t", q=4),
                             uv[:, 2 * half:2 * half + 2, :, 8:10], [un], [("cout", gi)])
                    else:
                        h.cp("pool", ucar[:, m0:m0 + 2, :], uv[:, 2 * half:2 * half + 2, 0, 128:130],
                             [un], [("ucar", gi)])
                        if last_prompt:
                            h.cp("pool", cout[:, m0:m0 + 2, 0:2], uv[:, 2 * half:2 * half + 2, 0, 128:130],
                                 [un], [("cout", gi)])
                c_, cn = cc[gi % 2], "cc%d" % (gi % 2)
                for mi, m in enumerate(chunks):
                    eng = "dve"
                    c4 = c_[:, mi, :].rearrange("p (s l) -> p s l", s=nseq)
                    h.act(c4, uv[:, mi, :, 2:L + 2], AF.Identity, [un, "cw"], [(cn, mi)],
                          scale=cw[:, 2, m:m + 1], bias=cw[:, 3, m:m + 1])
                    h.stt(eng, c4, uv[:, mi, :, 1:L + 1], cw[:, 1, m:m + 1], c4, ALU.mult, ALU.add,
                          [un, "cw", (cn, mi)], [(cn, mi)])
                    h.stt(eng, c4, uv[:, mi, :, 0:L], cw[:, 0, m:m + 1], c4, ALU.mult, ALU.add,
                          [un, "cw", (cn, mi)], [(cn, mi)])
                ga, gan = g1[gi % 2], "g1%d" % (gi % 2)
                gb_, gbn = g2t[gi % 2], "g2%d" % (gi % 2)
                gate = c_[:, 2:4, :]
                val = c_[:, 0:2, :]
                h.act(ga[:], gate, AF.Square, [(cn, 2), (cn, 3)], [gan])
                h.ts("dve", ga[:], ga[:], 0.044715, 1.0, ALU.mult, ALU.add, [gan], [gan])
                h.tt("pool", ga[:], ga[:], gate, ALU.mult, [gan, (cn, 2), (cn, 3)], [gan])
                h.act(gb_[:], ga[:], AF.Sigmoid, [gan], [gbn], scale=GELU_S)
                h.tt("pool", gb_[:], gb_[:], gate, ALU.mult, [gbn, (cn, 2), (cn, 3)], [gbn])
                h.tt("dve", actT[:, 2 * gi:2 * gi + 2, :], gb_[:], val, ALU.mult, [gbn, (cn, 0), (cn, 1)],
                     [(actk, gi)])
            if samp or last_prompt:
                ncol = 8 if samp else 2
                for g4 in range(11):
                    for mi in range(4):
                        m = 4 * g4 + mi
                        h.tr(PS[4][0:ncol, mi * 128:(mi + 1) * 128], cout[:, m, 0:ncol], idf, ["cout", "cst"],
                             ["ps4"])
                    co_, cok = csto[g4 % 2], "csto%d" % (g4 % 2)
                    h.cp("act", co_[0:ncol, :], PS[4][0:ncol, :], ["ps4"], [cok])
                    if samp:
                        h.dma("sp", o_conv_s[4 * j:4 * j + 4].rearrange("q t f -> (q t) f")[:, g4 * 512:(g4 + 1) * 512],
                              co_[:], [cok], [], isout=True)
                    else:
                        h.dma("sp", o_conv_p[0][:, g4 * 512:(g4 + 1) * 512], co_[0:2, :], [cok], [], isout=True)
            for nh in range(2):
                pt, pn = PS[2 + nh], "ps%d" % (2 + nh)
                for kc in range(22):
                    h.mm(pt[:], actT[:, kc, :], wdn[:, kc, nh * 512:(nh + 1) * 512], kc == 0, kc == 21,
                         [(actk, kc // 2), ("wdn", kc)], [pn])
                h.tt("dve", x2[:, nh * 512:(nh + 1) * 512], pt[:], xt[:, nh * 512:(nh + 1) * 512], ALU.add,
                     [pn, xtn], [(x2k, nh)])
            ss2, rs2 = fss[b % 2], frs[b % 2]
            ssk, rsk = "fss%d" % (b % 2), "frs%d" % (b % 2)
            h.act(yt[:], x2[:], AF.Square, [x2k], [ytk, ssk], accum_out=ss2[:])
            h.ts("dve", rs2[:], ss2[:], 1.0 / D, 1e-6, ALU.mult, ALU.add, [ssk], [rsk])
            h.act(rs2[:], rs2[:], AF.Sqrt, [rsk], [rsk])
            h.recip(rs2[:], rs2[:], [rsk], [rsk])
            h.stt("dve", yt[:], x2[:], rs2[:, 0:1], nfb[:], ALU.mult, ALU.mult, [x2k, rsk, "nfb"], [ytk])
            if samp:
                for q in range(4):
                    h.dma("sp", ysm[4 * j + q], yt[32 * q:32 * q + 8, :], [ytk], [], isout=True)
            else:
                h.dma("sp", yp[b * 128:(b + 1) * 128, :], yt[:], [ytk], [], isout=True)
        P.emit()
    gx.close()
    gs.close()
    return nc


_CACHE = {}


def kernel(**inputs):
    f32 = lambda a: np.ascontiguousarray(np.asarray(a), dtype=np.float32)
    if "nc" not in _CACHE:
        nc = bass.Bass("TRN2", target_bir_lowering=False)
        build(nc)
        _CACHE["nc"] = nc
    nc = _CACHE["nc"]
    cst = make_consts()
    shared = {
        "cst": cst,
        "norm_mix": f32(inputs["norm_mix"][0]), "w_in": f32(inputs["w_in"][0]),
        "mu_shift": f32(inputs["mu_shift"][0]), "rwkv_w0": f32(inputs["rwkv_w0"][0]),
        "rwkv_w2": f32(inputs["rwkv_w2"][0]), "rwkv_a0": f32(inputs["rwkv_a0"][0]),
        "rwkv_a2": f32(inputs["rwkv_a2"][0]), "rwkv_g2": f32(inputs["rwkv_g2"][0]),
        "rwkv_k_k": f32(inputs["rwkv_k_k"][0]), "rwkv_k_a": f32(inputs["rwkv_k_a"][0]),
        "rwkv_r_k": f32(inputs["rwkv_r_k"][0]).reshape(512), "rwkv_ln_w": f32(inputs["rwkv_ln_w"][0]),
        "rwkv_ln_b": f32(inputs["rwkv_ln_b"][0]), "gla_wg2": f32(inputs["gla_wg2"][0]),
        "gla_bg": f32(inputs["gla_bg"][0]), "gla_norm_w": f32(inputs["gla_norm_w"][0]),
        "w_out_a": f32(inputs["w_out_a"][0]), "w_out_b": f32(inputs["w_out_b"][0]),
        "w_o": f32(inputs["w_o"][0]), "norm_ffn": f32(inputs["norm_ffn"][0]),
        "ffn_w_up": f32(inputs["ffn_w_up"][0]), "ffn_conv_w": f32(inputs["ffn_conv_w"][0]),
        "ffn_conv_b": f32(inputs["ffn_conv_b"][0]), "ffn_w_down": f32(inputs["ffn_w_down"][0]),
        "norm_final": f32(inputs["norm_final"]),
    }
    in_maps = []
    for c in range(NCORES):
        m = dict(shared)
        sl = slice(16 * c, 16 * c + 16)
        m["xp"] = f32(inputs["x_prompt"][c])
        m["xs"] = f32(inputs["x_sample"][sl])
        m["st_shift"] = f32(inputs["state_rwkv_shift"][0, sl])
        m["st_wkv"] = f32(inputs["state_rwkv_wkv"][0, sl])
        m["st_gla"] = f32(inputs["state_gla"][0, sl])
        m["st_conv"] = f32(inputs["state_ffn_conv"][0, sl])
        in_maps.append(m)
    res = run_bass_kernel_spmd(nc, in_maps, core_ids=list(range(NCORES)))
    R = res.results
    cat = lambda k: np.concatenate([np.asarray(r[k]) for r in R], axis=0)
    y_p = np.stack([np.asarray(r["yp"]) for r in R], axis=0)
    y_s = cat("ys")
    outs = (
        y_p, y_s,
        cat("o_shift_p")[None], cat("o_wkv_p")[None], cat("o_gla_p")[None], cat("o_conv_p")[None],
        cat("o_shift_s")[None], cat("o_wkv_s")[None], cat("o_gla_s")[None], cat("o_conv_s")[None],
    )
    return tuple(np.ascontiguousarray(o, dtype=np.float32) for o in outs)
```

```python
import contextlib
import numpy as np
import concourse.bass as bass
import concourse.mybir as mybir
from concourse.bass_utils import run_bass_kernel_spmd

F32 = mybir.dt.float32
BF16 = mybir.dt.bfloat16
AF = mybir.ActivationFunctionType
ALU = mybir.AluOpType
AX = mybir.AxisListType

NCORES = 8
D = 1024
NPB = 16
NSB = 4
NBLK = NPB + NSB
SHIFT = 1792
INC = 5392
FH = 2816
F2 = 5632
GLA0 = 1792
GATE0 = 3344
C0 = -0.6065306597126334
GELU_S = 1.5957691216057308

ENGS = ("pe", "act", "dve", "pool", "sp")
MAXOPS = None
SCHED = True
VERBOSE = False
PROGS = []
WINDOW = 300
SCHED_LAT = 300.0
ODEP_LAT = 0.0
SCHED_BIAS = 1.0
PE_SCALE = 1.0
DVE_SCALE = 1.0
FILL_MIN = 250.0
FILL_MARGIN = 80.0
FILL_MAX = 24
FILLERS = False
LINES = []
TAGS = []


class Op:
    __slots__ = ("eng", "fn", "reads", "writes", "dma", "deps", "sig", "idx",
                 "dsem", "dval", "dprev", "isout", "mm", "odeps", "cost", "start", "fin")

    def __init__(self, eng, fn, reads, writes, dma, isout, mm, cost=300.0):
        self.odeps = []
        self.cost = cost
        self.eng = eng
        self.fn = fn
        self.reads = reads
        self.writes = writes
        self.dma = dma
        self.deps = []
        self.sig = None
        self.dsem = None
        self.dval = None
        self.dprev = None
        self.isout = isout
        self.mm = mm


def _norm(k):
    return k if isinstance(k, tuple) else (k, None)


class Prog:
    def __init__(self, nc, semstack, tag, n_dma_sems=6):
        self.nc = nc
        self.ops = []
        self.n_dma_sems = n_dma_sems
        self.st = {}
        self.semstack = semstack
        self.tag = tag
        self.filler = None

    @staticmethod
    def _conf(a, b):
        return a is None or b is None or a == b

    def add(self, eng, fn, reads=(), writes=(), dma=False, isout=False, mm=False, cost=300.0):
        op = Op(eng, fn, [_norm(k) for k in reads], [_norm(k) for k in writes], dma, isout, mm, cost)
        odeps = {}
        op.idx = len(self.ops)
        if MAXOPS is not None:
            import sys as _s
            f = _s._getframe(1)
            while f is not None and f.f_code.co_name != "build":
                f = f.f_back
            LINES.append(f.f_lineno if f is not None else -1)
            TAGS.append(f.f_locals.get("b", -1) if f is not None else -1)
        deps = {}
        for (name, sub) in op.reads:
            s = self.st.setdefault(name, {"w": {}, "r": {}})
            for ws, wop in s["w"].items():
                if self._conf(ws, sub):
                    deps[wop.idx] = wop
        for (name, sub) in op.writes:
            s = self.st.setdefault(name, {"w": {}, "r": {}})
            for ws, wop in s["w"].items():
                if self._conf(ws, sub):
                    if not (op.mm and wop.mm):
                        deps[wop.idx] = wop
                    else:
                        odeps[wop.idx] = wop
            for rs, rops in s["r"].items():
                if self._conf(rs, sub):
                    for rop in rops:
                        deps[rop.idx] = rop
        for (name, sub) in op.reads:
            self.st[name]["r"].setdefault(sub, []).append(op)
        for (name, sub) in op.writes:
            s = self.st[name]
            if sub is None:
                s["w"] = {None: op}
                s["r"] = {}
            else:
                s["w"][sub] = op
                s["r"][sub] = []
        deps.pop(op.idx, None)
        op.deps = list(deps.values())
        op.odeps = [o for k, o in odeps.items() if k not in deps]
        self.ops.append(op)
        return op

    def schedule(self, window=None):
        window = window or WINDOW
        ops = self.ops
        n = len(ops)
        ndep = [0] * n
        users = [[] for _ in range(n)]
        truedep = set()
        for op in ops:
            for d in op.deps:
                truedep.add((op.idx, d.idx))
            ds = {d.idx for d in op.deps} | {d.idx for d in op.odeps}
            ndep[op.idx] = len(ds)
            for d in ds:
                users[d].append(op.idx)
        ready_t = [0.0] * n
        per = {e: [op.idx for op in ops if op.eng == e] for e in ENGS}
        head = {e: 0 for e in ENGS}
        done = [False] * n
        t_e = {e: 0.0 for e in ENGS}
        order = []
        remaining = n
        LAT = SCHED_LAT
        while remaining:
            best = None
            for e in ENGS:
                lst = per[e]
                hp = head[e]
                while hp < len(lst) and done[lst[hp]]:
                    hp += 1
                head[e] = hp
                if hp >= len(lst):
                    continue
                cnt = 0
                k = hp
                cand = None
                rdy = []
                while k < len(lst) and cnt < window:
                    i = lst[k]
                    if not done[i]:
                        cnt += 1
                        if ndep[i] == 0:
                            st = max(t_e[e], ready_t[i])
                            key = (st + SCHED_BIAS * (cnt - 1), i)
                            rdy.append((st, i))
                            if cand is None or key < cand[0]:
                                cand = (key, i, st)
                    k += 1
                if cand is not None:
                    bst = cand[2]
                    for (st, i) in rdy:
                        if i != cand[1] and st + (60.0 if ops[i].dma else ops[i].cost) <= bst:
                            cand = ((st, i), i, st)
                            break
                    if best is None or cand[0] < best[0]:
                        best = (cand[0], cand[1], cand[2], e)
            assert best is not None, "scheduler deadlock"
            _, i, st, e = best
            op = ops[i]
            op.start = st
            if op.dma:
                t_e[e] = st + 60.0
            else:
                t_e[e] = st + op.cost
            op.fin = st + op.cost
            done[i] = True
            remaining -= 1
            order.append(op)
            for u in users[i]:
                ndep[u] -= 1
                lat_ = LAT if ((u, i) in truedep) else ODEP_LAT
                if ready_t[u] < op.fin + lat_:
                    ready_t[u] = op.fin + lat_
        if self.filler is not None:
            fn, fcost = self.filler
            out = []
            pe_end = None
            nf = 0
            for op in order:
                if op.eng == "pe":
                    if pe_end is not None:
                        gap = op.start - pe_end
                        if gap > FILL_MIN:
                            k = min(FILL_MAX, int((gap - FILL_MARGIN) / fcost))
                            for _ in range(max(0, k)):
                                f = Op("pe", fn, [], [], False, False, True, fcost)
                                f.idx = -1
                                f.start = pe_end
                                f.fin = pe_end + fcost
                                out.append(f)
                                nf += 1
                    pe_end = op.start + op.cost
                out.append(op)
            order = out
            if VERBOSE:
                print("[sched %s] fillers inserted: %d" % (self.tag, nf), flush=True)
        self.ops = order
        self.est = max(op.fin for op in order) if order else 0.0
        if VERBOSE:
            PROGS.append(self)
            busy = {e: sum(o.cost for o in order if o.eng == e and not o.dma) for e in ENGS}
            print("[sched %s] n=%d est=%.1f us busy(us): %s" % (
                self.tag, n, self.est / 1e3, " ".join("%s=%.0f" % (e, busy[e] / 1e3) for e in ENGS)), flush=True)

    def emit(self):
        nc = self.nc
        if MAXOPS is not None:
            self.ops = self.ops[:MAXOPS]
        if SCHED:
            self.schedule()
        ops = self.ops
        needed = set()
        for op in ops:
            for d in op.deps:
                needed.add(d.idx)
        cnt = {e: 0 for e in ENGS}
        for op in ops:
            if not op.dma and op.idx in needed:
                cnt[op.eng] += 1
                op.sig = cnt[op.eng]
        dcount = {e: 0 for e in ENGS}
        last_on_slot = {}
        for op in ops:
            if not op.dma:
                continue
            j = dcount[op.eng]
            dcount[op.eng] += 1
            slot = j % self.n_dma_sems
            op.dsem = (op.eng, slot)
            op.dval = 16 * (j // self.n_dma_sems + 1)
            op.dprev = last_on_slot.get(op.dsem)
            last_on_slot[op.dsem] = op
        out_ops = [op for op in ops if op.dma and op.isout]
        per_eng = {e: [op for op in ops if op.eng == e] for e in ENGS}
        es = self.semstack
        csem = {e: es.enter_context(nc.semaphore("cs%s_%s" % (self.tag, e)))
                for e in ENGS if e != "sp"}
        dsem = {}
        for e in ENGS:
            for s in range(min(self.n_dma_sems, dcount[e])):
                dsem[(e, s)] = es.enter_context(nc.semaphore("ds%s_%s%d" % (self.tag, e, s)))

        def run_engine(e, eng):
            known = {}

            def wait(key, sem, val):
                if known.get(key, 0) >= val:
                    return
                known[key] = val
                eng.wait_ge(sem, val)

            for op in per_eng[e]:
                for d in op.deps:
                    if d.dma:
                        wait(d.dsem, dsem[d.dsem], d.dval)
                    else:
                        wait(d.eng, csem[d.eng], d.sig)
                if op.dma and op.dprev is not None:
                    wait(op.dsem, dsem[op.dsem], op.dprev.dval)
                ins = op.fn(eng)
                if op.dma:
                    ins.then_inc(dsem[op.dsem], 16)
                elif op.sig is not None:
                    ins.then_inc(csem[e], 1)
            if e == "sp":
                for op in out_ops:
                    wait(op.dsem, dsem[op.dsem], op.dval)
                for key, op in last_on_slot.items():
                    wait(op.dsem, dsem[op.dsem], op.dval)

        with nc.Block() as block:
            @block.sync
            def _(eng):
                run_engine("sp", eng)

            @block.tensor
            def _(eng):
                run_engine("pe", eng)

            @block.scalar
            def _(eng):
                run_engine("act", eng)

            @block.vector
            def _(eng):
                run_engine("dve", eng)

            @block.gpsimd
            def _(eng):
                run_engine("pool", eng)


def _fsz(ap):
    n = 1
    for d in ap.shape[1:]:
        n *= int(d)
    return n


class H:
    def __init__(self, P):
        self.P = P

    def act(self, out, in_, func, r, w, **kw):
        self.P.add("act", lambda e: e.activation(out=out, in_=in_, func=func, **kw), r, w,
                   cost=220.0 + 1.05 * _fsz(out))

    def tt(self, eng, out, in0, in1, op, r, w):
        self.P.add(eng, lambda e: e.tensor_tensor(out=out, in0=in0, in1=in1, op=op), r, w,
                   cost=(100.0 + 1.05 * _fsz(out)) if eng == "dve" else (160.0 + 2.1 * _fsz(out)))

    def ts(self, eng, out, in0, s1, s2, op0, op1, r, w):
        if op1 is None:
            self.P.add(eng, lambda e: e.tensor_scalar(out=out, in0=in0, scalar1=s1, scalar2=None, op0=op0), r, w,
                       cost=100.0 + 1.05 * _fsz(out))
        else:
            self.P.add(eng, lambda e: e.tensor_scalar(out=out, in0=in0, scalar1=s1, scalar2=s2, op0=op0, op1=op1), r, w,
                       cost=100.0 + 1.05 * _fsz(out))

    def stt(self, eng, out, in0, scalar, in1, op0, op1, r, w):
        self.P.add(eng, lambda e: e.scalar_tensor_tensor(out=out, in0=in0, scalar=scalar, in1=in1, op0=op0, op1=op1), r, w,
                   cost=100.0 + 1.05 * _fsz(out))

    def cp(self, eng, out, in_, r, w):
        if eng == "act":
            self.P.add("act", lambda e: e.activation(out=out, in_=in_, func=AF.Copy), r, w,
                       cost=220.0 + 1.05 * _fsz(out))
        else:
            self.P.add(eng, lambda e: e.tensor_copy(out=out, in_=in_), r, w,
                       cost=(100.0 + 1.05 * _fsz(out)) if eng == "dve" else (160.0 + 2.1 * _fsz(out)))

    def memset(self, eng, ap, val, w):
        self.P.add(eng, lambda e: e.memset(ap, val), [], w, cost=160.0 + 1.0 * _fsz(ap))

    def recip(self, out, in_, r, w):
        self.P.add("dve", lambda e: e.reciprocal(out=out, in_=in_), r, w, cost=100.0 + 1.05 * _fsz(out))

    def scan(self, out, d0, d1, r, w):
        self.P.add("dve", lambda e: e.tensor_tensor_scan(out=out, data0=d0, data1=d1, initial=0.0,
                                                         op0=ALU.mult, op1=ALU.add), r, w,
                   cost=100.0 + 2.1 * _fsz(out))

    def rsum(self, out, in_, r, w):
        self.P.add("dve", lambda e: e.tensor_reduce(out=out, in_=in_, axis=AX.X, op=ALU.add), r, w,
                   cost=100.0 + 1.05 * _fsz(in_))

    def mm(self, out, lhsT, rhs, start, stop, r, w, tp=None):
        c = (max(64.0, float(_fsz(rhs))) / 2.0 + 16.0) * PE_SCALE
        if lhsT.dtype == F32:
            c *= 4.0
        if tp is None:
            self.P.add("pe", lambda e: e.matmul(out, lhsT=lhsT, rhs=rhs, start=start, stop=stop), r, w, mm=True,
                       cost=c)
        else:
            self.P.add("pe", lambda e: e.matmul(out, lhsT=lhsT, rhs=rhs, start=start, stop=stop,
                                                tile_position=tp), r, w, mm=True, cost=c)

    def tr(self, out, in_, ident, r, w):
        self.P.add("pe", lambda e: e.transpose(out=out, in_=in_, identity=ident), r, w, mm=True, cost=110.0)

    def dma(self, q, out, in_, r, w, isout=False, slow=False):
        nbytes = 1
        for d in out.shape:
            nbytes *= int(d)
        c = 2500.0 + 4.0 * nbytes / 150.0
        if slow:
            self.P.add(q, lambda e: e.dma_start(out=out, in_=in_, allow_slow_non_contiguous=True), r, w,
                       dma=True, isout=isout, cost=c)
        else:
            self.P.add(q, lambda e: e.dma_start(out=out, in_=in_), r, w, dma=True, isout=isout, cost=c)


def bc(ap, shape):
    a = ap
    while len(a.shape) < len(shape):
        a = a.unsqueeze(len(a.shape))
    return a.to_broadcast(list(shape))


C_ID = 0
C_BO = 128
C_RST = 256
C_VS = 384
C_ONE = 512
C_M4 = 640
C_SL = 1152
C_IU = 1280
NCST = 1408


def make_consts():
    c = np.zeros((128, NCST), np.float32)
    i = np.arange(128)
    same = (i[:, None] // 32) == (i[None, :] // 32)
    su = (same & (i[:, None] < i[None, :])).astype(np.float32)
    iu = (same & (i[:, None] <= i[None, :])).astype(np.float32)
    sl = (same & (i[:, None] > i[None, :])).astype(np.float32)
    c[:, C_ID:C_ID + 128] = np.eye(128, dtype=np.float32)
    c[:, C_BO:C_BO + 128] = ((i[:, None] // 64) == (i[None, :] // 64)).astype(np.float32)
    c[:, C_RST:C_RST + 128] = (i[None, :] % 32 != 0).astype(np.float32)
    c[:, C_VS:C_VS + 128] = (i[None, :] % 32 < 8).astype(np.float32)
    c[:, C_ONE:C_ONE + 128] = 1.0
    c[:, C_M4:C_M4 + 512] = np.concatenate([su, iu, su, iu], axis=1)
    c[:, C_SL:C_SL + 128] = sl
    c[:, C_IU:C_IU + 128] = iu
    return c


PR_G = 0
PR_MU = 8
PR_W0 = 22
PR_A0 = 26
PR_KK = 30
PR_KA = 34
PR_RK = 38
PR_LW = 42
PR_LB = 46
PR_BG = 50
PR_NW = 52
PR_OMKA = 53
NPRM = 64


def build(nc, dbg=None, phases="abc", blocks=None):
    BLKS = list(range(NBLK)) if blocks is None else list(blocks)
    gs = contextlib.ExitStack()

    def din(name, shape, dt=F32):
        return nc.dram_tensor(name, list(shape), dt, kind="ExternalInput").ap()

    def dout(name, shape):
        return nc.dram_tensor(name, list(shape), F32, kind="ExternalOutput").ap()

    def dscr(name, shape, dt):
        return nc.dram_tensor(name, list(shape), dt, kind="Internal").ap()

    xp = din("xp", [2048, D])
    xsm = din("xs", [16, 8, D])
    st_shift = din("st_shift", [16, SHIFT])
    st_wkv = din("st_wkv", [16, 8, 64, 64])
    st_gla = din("st_gla", [16, 4, 64, 128])
    st_conv = din("st_conv", [16, 2, F2])
    cst_d = din("cst", [128, NCST])
    norm_mix = din("norm_mix", [D])
    w_in = din("w_in", [D, INC])
    mu_shift = din("mu_shift", [SHIFT])
    rwkv_w0 = din("rwkv_w0", [512])
    rwkv_w2 = din("rwkv_w2", [64, 512])
    rwkv_a0 = din("rwkv_a0", [512])
    rwkv_a2 = din("rwkv_a2", [64, 512])
    rwkv_g2 = din("rwkv_g2", [128, 512])
    rwkv_k_k = din("rwkv_k_k", [512])
    rwkv_k_a = din("rwkv_k_a", [512])
    rwkv_r_k = din("rwkv_r_k", [512])
    rwkv_ln_w = din("rwkv_ln_w", [512])
    rwkv_ln_b = din("rwkv_ln_b", [512])
    gla_wg2 = din("gla_wg2", [16, 256])
    gla_bg = din("gla_bg", [256])
    gla_norm_w = din("gla_norm_w", [128])
    w_out_a = din("w_out_a", [512, D])
    w_out_b = din("w_out_b", [512, D])
    w_o = din("w_o", [D, D])
    norm_ffn = din("norm_ffn", [D])
    ffn_w_up = din("ffn_w_up", [D, F2])
    ffn_conv_w = din("ffn_conv_w", [3, F2])
    ffn_conv_b = din("ffn_conv_b", [F2])
    ffn_w_down = din("ffn_w_down", [FH, D])
    norm_final = din("norm_final", [D])

    yp = dout("yp", [2048, D])
    ysm = dout("ys", [16, 8, D])
    o_shift_p = dout("o_shift_p", [1, SHIFT])
    o_wkv_p = dout("o_wkv_p", [1, 8, 64, 64])
    o_gla_p = dout("o_gla_p", [1, 4, 64, 128])
    o_conv_p = dout("o_conv_p", [1, 2, F2])
    o_shift_s = dout("o_shift_s", [16, SHIFT])
    o_wkv_s = dout("o_wkv_s", [16, 8, 64, 64])
    o_gla_s = dout("o_gla_s", [16, 4, 64, 128])
    o_conv_s = dout("o_conv_s", [16, 2, F2])

    og_s = dscr("og_s", [NBLK, 128, 512], BF16)
    ob_s = dscr("ob_s", [NBLK, 128, 512], BF16)
    x1_s = dscr("x1_s", [NBLK * 128, D], F32)

    dbg_out = None
    if dbg is not None:
        dbg_out = dout("dbg", dbg["shape"])

    def geom(b):
        return (b >= NPB, 4, 32) if b >= NPB else (False, 1, 128)

    def load_consts(h, sb, q="sp"):
        cst = sb("cst", [128, NCST], F32)
        h.dma(q, cst[:], cst_d, [], ["cst"])
        idb = sb("idb", [128, 128], BF16)
        h.cp("dve", idb[:], cst[:, C_ID:C_ID + 128], ["cst"], ["idb"])
        return cst, idb

    def load_x_block(h, b, xt, xtn):
        samp, nseq, L = geom(b)
        if not samp:
            h.dma("sp", xt[:], xp[b * 128:(b + 1) * 128, :], [], [xtn])
        else:
            j = b - NPB
            h.memset("pool", xt[:], 0.0, [xtn])
            for q in range(4):
                h.dma("sp", xt[32 * q:32 * q + 8, :], xsm[4 * j + q], [], [(xtn, q)])

    def rms_to_fm(h, T, xt, xtn, gcol, par=0):
        xp_ = par if len(T["xn"]) > 1 else 0
        xn, ss, rstd, hT = T["xn"][xp_], T["ss"][par], T["rstd"][par], T["hT"][par]
        xnk, ssk, rsk, hTk = "xn%d" % xp_, "ss%d" % par, "rstd%d" % par, "hT%d" % par
        psT, idb, prm = T["psT"], T["idb"], T["prm"]
        h.act(xn[:], xt[:], AF.Square, [xtn], [xnk, ssk], accum_out=ss[:])
        h.ts("dve", rstd[:], ss[:], 1.0 / D, 1e-6, ALU.mult, ALU.add, [ssk], [rsk])
        h.act(rstd[:], rstd[:], AF.Sqrt, [rsk], [rsk])
        h.recip(rstd[:], rstd[:], [rsk], [rsk])
        h.act(xn[:], xt[:], AF.Copy, [xtn, rsk], [xnk], scale=rstd[:, 0:1])
        for c in range(8):
            h.tr(psT[:, c, :], xn[:, c * 128:(c + 1) * 128], idb[:], [xnk, "idb"], [("psT", c)])
        h.tt("dve", hT[:], psT[:], bc(prm[:, gcol:gcol + 8], [128, 8, 128]), ALU.mult,
             ["psT", "prm"], [hTk])
        return hT, hTk

    def alloc_rms(T, sb, nxn=2):
        T["xn"] = [sb("xn%d" % i, [128, D], BF16) for i in range(nxn)]
        T["ss"] = [sb("ss%d" % i, [128, 1], F32) for i in range(2)]
        T["rstd"] = [sb("rstd%d" % i, [128, 1], F32) for i in range(2)]
        T["hT"] = [sb("hT%d" % i, [128, 8, 128], BF16) for i in range(2)]

    def load_param_cols(h, prm, col, src, n):
        h.dma("sp", prm[:, col:col + n], src.rearrange("(c p) -> p c", p=128), [], [("prm", col)], slow=True)

    with contextlib.ExitStack() as ph:
      if "a" in phases:
        P = Prog(nc, gs, "a")
        h = H(P)

        def sb(name, shape, dt):
            return ph.enter_context(nc.sbuf_tensor("a_" + name, list(shape), dt))

        def psb(name, shape, dt):
            return ph.enter_context(nc.psum_tensor("a_" + name, list(shape), dt))

        T = {}
        cst, idb = load_consts(h, sb)
        T["idb"] = idb
        prm = sb("prm", [128, NPRM], F32)
        T["prm"] = prm
        load_param_cols(h, prm, PR_G, norm_mix, 8)
        load_param_cols(h, prm, PR_MU, mu_shift, 14)
        load_param_cols(h, prm, PR_W0, rwkv_w0, 4)
        load_param_cols(h, prm, PR_A0, rwkv_a0, 4)
        load_param_cols(h, prm, PR_KK, rwkv_k_k, 4)
        load_param_cols(h, prm, PR_KA, rwkv_k_a, 4)
        load_param_cols(h, prm, PR_RK, rwkv_r_k, 4)
        load_param_cols(h, prm, PR_LW, rwkv_ln_w, 4)
        load_param_cols(h, prm, PR_LB, rwkv_ln_b, 4)
        load_param_cols(h, prm, PR_BG, gla_bg, 2)
        load_param_cols(h, prm, PR_NW, gla_norm_w, 1)
        h.ts("dve", prm[:, PR_BG:PR_BG + 2], prm[:, PR_BG:PR_BG + 2], -1.0, None, ALU.mult, None,
             [("prm", PR_BG)], [("prm", PR_BG)])
        h.ts("dve", prm[:, PR_OMKA:PR_OMKA + 4], prm[:, PR_KA:PR_KA + 4], -1.0, 1.0, ALU.mult, ALU.add,
             [("prm", PR_KA)], [("prm", PR_OMKA)])

        NA1 = GATE0
        win = sb("win", [128, 8, NA1], BF16)
        w_in_v = w_in.rearrange("(c p) n -> p c n", p=128)
        for kc in range(8):
            h.dma("pool", win[:, kc, :], w_in_v[:, kc, 0:NA1], [], [("win", kc)])
        w2a2 = sb("w2a2", [128, 512], BF16)
        h.dma("pool", w2a2[0:64, :], rwkv_w2, [], [("w2a2", 0)])
        h.dma("pool", w2a2[64:128, :], rwkv_a2, [], [("w2a2", 1)])
        g2 = sb("g2", [128, 512], BF16)
        h.dma("pool", g2[:], rwkv_g2, [], ["g2"])
        wg2 = sb("wg2", [16, 256], BF16)
        h.dma("pool", wg2[:], gla_wg2, [], ["wg2"])

        xt = sb("xt", [128, D], F32)
        alloc_rms(T, sb, nxn=1)
        prw = sb("prw", [128, 14, 132], F32)
        lastc = sb("lastc", [128, 14, 1], F32)
        xs = sb("xs", [128, 14, 128], F32)
        Fm = [sb("F%d" % i, [128, 4, 128], F32) for i in range(10)]
        gTs = [sb("gT%d" % i, [128, 4, 128], F32) for i in range(2)]
        bons = [sb("bon%d" % i, [128, 4, 128], F32) for i in range(2)]
        ARs = [sb("AR%d" % i, [128, 4, 2, 128], BF16) for i in range(2)]
        Hm = [sb("Hb%d" % i, [128, 4, 128], BF16) for i in range(8)]
        tok = [sb("tok%d" % i, [128, 512], BF16) for i in range(4)]
        SCH = sb("SCH", [128, 8, 4, 128], BF16)
        INV = [sb("INV%d" % i, [128, 8, 128], BF16) for i in range(8)]
        lora_in = sb("lora_in", [128, 128], BF16)
        slg = sb("slg", [128, 128], BF16)
        Zbf = sb("Zbf", [128, 512], BF16)
        Yf = sb("Yf", [128, 512], F32)
        WTbf = sb("WTbf", [128, 4, 128], BF16)
        Ubf = sb("Ubf", [128, 512], BF16)
        Pst = sb("Pst", [128, 4, 64], F32)
        Pbf = sb("Pbf", [128, 4, 64], BF16)
        o1 = sb("o1", [128, 512], F32)
        oo = sb("oo", [128, 512], F32)
        stat = sb("stat", [128, 64], F32)
        ogbf = sb("ogbf", [128, 4, 128], BF16)
        obbf = sb("obbf", [128, 4, 128], BF16)
        Sin = sb("Sin", [64, 8, 64], F32)
        shs = [sb("shs%d" % i, [4, 512], F32) for i in range(2)]
        shf = sb("shf", [128, 14, 4], F32)
        sho = [sb("sho%d" % i, [4, 512], F32) for i in range(2)]
        lga = sb("lga", [16, 128], BF16)
        Sg = sb("Sg", [128, 2, 128], F32)
        Sgbf = sb("Sgbf", [128, 4, 2, 128], BF16)
        SgIn = sb("SgIn", [128, 4, 2, 128], F32)
        xg = sb("xg", [128, 12, 128], F32)
        Gf = [sb("G%d" % i, [128, 2, 128], F32) for i in range(6)]
        silu = sb("Gs", [128, 4, 128], F32)
        qdT, kiT, keT = [sb("Gh%d" % i, [128, 2, 128], BF16) for i in range(3)]
        vgbf = sb("Gv", [128, 4, 128], BF16)
        GS = sb("Ggs", [128, 4, 128], BF16)
        Vgtok = sb("Gt0", [128, 512], BF16)
        Ketok = sb("Gt1", [128, 256], BF16)
        o1g = sb("o1g", [128, 512], F32)
        oog = sb("oog", [128, 512], F32)
        statg = sb("statg", [128, 16], F32)

        T["psT"] = psb("psT", [128, 8, 128], BF16)
        psT = T["psT"]
        PS = [psb("ps%d" % i, [128, 512], F32) for i in range(7)]

        if VERBOSE:
            print("[A1] sbuf remaining after alloc:", nc.sbuf_bytes_remaining, flush=True)
        idf = cst[:, C_ID:C_ID + 128]
        bones = cst[:, C_BO:C_BO + 128]
        rstm = cst[:, C_RST:C_RST + 128]
        m4 = cst[:, C_M4:C_M4 + 512]
        msl = cst[:, C_SL:C_SL + 128]
        miu = cst[:, C_IU:C_IU + 128]

        h.memset("pool", Pst[:], 0.0, ["Pst"])
        h.memset("pool", Pbf[:], 0.0, ["Pbf"])
        h.memset("pool", Sg[:], 0.0, ["Sg"])
        h.memset("pool", lastc[:], 0.0, ["lastc"])

        def v4(ap, nseq, L):
            return ap.rearrange("p m (s l) -> p m s l", s=nseq)

        for b in BLKS:
            samp, nseq, L = geom(b)
            j = b - NPB
            valid = cst[:, (C_VS if samp else C_ONE):(C_VS if samp else C_ONE) + 128]
            load_x_block(h, b, xt, "xt")
            hT, hTk = rms_to_fm(h, T, xt, "xt", PR_G, b % 2)

            W = nseq * (L + 1)
            pv = prw[:, :, 0:W].rearrange("p m (s l) -> p m s l", s=nseq)
            for gi in range(4):
                ms = list(range(4 * gi, min(4 * gi + 4, 14)))
                pt = PS[5 + gi % 2]
                pn = "ps%d" % (5 + gi % 2)
                for mi, m in enumerate(ms):
                    for kc in range(8):
                        h.mm(pt[:, mi * 128:(mi + 1) * 128], win[:, kc, m * 128:(m + 1) * 128], hT[:, kc, :],
                             kc == 0, kc == 7, [("win", kc), hTk], [pn])
                nm = len(ms)
                h.cp("act", pv[:, ms[0]:ms[0] + nm, :, 1:L + 1],
                     pt[:, 0:nm * 128].rearrange("p (m s l) -> p m s l", m=nm, s=nseq),
                     [pn], ["prw"])
            gcols = [GLA0 + 128 * i for i in range(8)] + [GLA0 + 1040 + 128 * i for i in range(4)]
            for gi in range(3):
                pt, pn = PS[5 + gi % 2], "ps%d" % (5 + gi % 2)
                for mi in range(4):
                    c0 = gcols[4 * gi + mi]
                    for kc in range(8):
                        h.mm(pt[:, mi * 128:(mi + 1) * 128], win[:, kc, c0:c0 + 128], hT[:, kc, :],
                             kc == 0, kc == 7, [("win", kc), hTk], [pn])
                h.cp("act", xg[:, 4 * gi:4 * gi + 4, :], pt[:].rearrange("p (c l) -> p c l", c=4), [pn], [("xg", gi)])
            for kc in range(8):
                h.mm(PS[6][0:16, 0:128], win[:, kc, GLA0 + 1024:GLA0 + 1040], hT[:, kc, :], kc == 0, kc == 7,
                     [("win", kc), hTk], ["ps6"])
            h.cp("act", lga[:], PS[6][0:16, 0:128], ["ps6"], ["lga"])
            if not samp:
                h.cp("pool", pv[:, :, 0, 0:1], lastc[:], ["lastc"], ["prw"])
            else:
                for g4 in range(4):
                    ms = list(range(4 * g4, min(4 * g4 + 4, 14)))
                    nm = len(ms)
                    sh_, shk = shs[g4 % 2], "shs%d" % (g4 % 2)
                    h.dma("sp", sh_[:, 0:nm * 128], st_shift[4 * j:4 * j + 4, ms[0] * 128:(ms[0] + nm) * 128], [], [shk])
                    for mi, m in enumerate(ms):
                        h.tr(PS[5][:, mi * 4:(mi + 1) * 4], sh_[0:4, mi * 128:(mi + 1) * 128], idf[0:4, 0:4],
                             [shk, "cst"], ["ps5"])
                    h.cp("act", pv[:, ms[0]:ms[0] + nm, :, 0],
                         PS[5][:, 0:nm * 4].rearrange("p (m s) -> p m s", m=nm), ["ps5"], ["prw"])
            last_prompt = (b == NPB - 1)
            if samp or last_prompt:
                ncol = 4 if samp else 1
                if samp:
                    h.cp("pool", shf[:, :, 0:4], pv[:, :, :, 8], ["prw"], ["shf"])
                else:
                    h.cp("pool", shf[:, :, 0:1], pv[:, :, 0, 128:129], ["prw"], ["shf"])
                for g4 in range(4):
                    ms = list(range(4 * g4, min(4 * g4 + 4, 14)))
                    for mi, m in enumerate(ms):
                        h.tr(PS[6][0:ncol, mi * 128:(mi + 1) * 128], shf[:, m, 0:ncol], idf,
                             ["shf", "cst"], ["ps6"])
                    nm = len(ms)
                    so_, sok = sho[g4 % 2], "sho%d" % (g4 % 2)
                    h.cp("act", so_[0:ncol, 0:nm * 128], PS[6][0:ncol, 0:nm * 128], ["ps6"], [sok])
                    if samp:
                        h.dma("sp", o_shift_s[4 * j:4 * j + 4, ms[0] * 128:(ms[0] + nm) * 128], so_[0:4, 0:nm * 128],
                              [sok], [], isout=True)
                    else:
                        h.dma("sp", o_shift_p[0:1, ms[0] * 128:(ms[0] + nm) * 128], so_[0:1, 0:nm * 128],
                              [sok], [], isout=True)
            if not samp:
                h.cp("pool", lastc[:], pv[:, :, 0, 128:129], ["prw"], ["lastc"])
            xs4 = xs[:].rearrange("p m (s l) -> p m s l", s=nseq)
            cur = pv[:, :, :, 1:L + 1]
            prv = pv[:, :, :, 0:L]
            h.tt("dve", xs4[:, 0:9], prv[:, 0:9], cur[:, 0:9], ALU.subtract, ["prw"], [("xs", 0)])
            h.tt("pool", xs4[:, 9:14], prv[:, 9:14], cur[:, 9:14], ALU.subtract, ["prw"], [("xs", 1)])
            for m in range(14):
                h.stt("dve", xs4[:, m], xs4[:, m], prm[:, PR_MU + m:PR_MU + m + 1], cur[:, m], ALU.mult, ALU.add,
                      [("xs", 0 if m < 9 else 1), "prm", "prw"], [("xs", 2 + m)])
            rT = xs[:, 0:4, :]
            kT = xs[:, 4:8, :]
            vT = xs[:, 8:12, :]

            sw, aa, cum, E, Einv, Eprev, Eend, kk, kh, tmp = Fm
            n = lambda i: "F%d" % i
            N_SW, N_AA, N_CUM, N_E, N_EINV, N_EPREV, N_EEND, N_KK, N_KH, N_TMP = [n(i) for i in range(10)]
            gT, N_GT = gTs[b % 2], "gT%d" % (b % 2)
            bon, N_BON = bons[b % 2], "bon%d" % (b % 2)
            AR, N_AR = ARs[b % 2], "AR%d" % (b % 2)
            bT, kTb, BpT, KpT, vbf = Hm[0], Hm[1], Hm[2], Hm[3], Hm[4]
            h.act(lora_in[0:64, :], xs[0:64, 12, :], AF.Tanh, ["xs"], [("lora_in", 0)])
            h.cp("act", lora_in[64:128, :], xs[64:128, 12, :], ["xs"], [("lora_in", 1)])
            h.act(slg[:], xs[:, 13, :], AF.Sigmoid, ["xs"], ["slg"])
            for hg in range(4):
                h.mm(PS[5][:, hg * 128:(hg + 1) * 128], w2a2[0:64, hg * 128:(hg + 1) * 128], lora_in[0:64, :],
                     True, True, [("w2a2", 0), ("lora_in", 0)], ["ps5"])
            for hg in range(4):
                h.mm(PS[6][:, hg * 128:(hg + 1) * 128], w2a2[64:128, hg * 128:(hg + 1) * 128], lora_in[64:128, :],
                     True, True, [("w2a2", 1), ("lora_in", 1)], ["ps6"])
            for hg in range(4):
                h.act(sw[:, hg, :], PS[5][:, hg * 128:(hg + 1) * 128], AF.Sigmoid, ["ps5", "prm"], [(N_SW, hg)],
                      bias=prm[:, PR_W0 + hg:PR_W0 + hg + 1])
                h.act(aa[:, hg, :], PS[6][:, hg * 128:(hg + 1) * 128], AF.Sigmoid, ["ps6", "prm"], [(N_AA, hg)],
                      bias=prm[:, PR_A0 + hg:PR_A0 + hg + 1])
            for hg in range(4):
                h.mm(PS[5][:, hg * 128:(hg + 1) * 128], g2[:, hg * 128:(hg + 1) * 128], slg[:],
                     True, True, ["g2", "slg"], ["ps5"])
            h.cp("act", gT[:], PS[5][:].rearrange("p (c l) -> p c l", c=4), ["ps5"], [N_GT])
            h.stt("dve", sw[:], sw[:], C0, valid.unsqueeze(1).to_broadcast([128, 4, 128]),
                  ALU.mult, ALU.mult, [N_SW, "cst"], [N_SW])
            for hg in range(4):
                h.scan(cum[:, hg, :], rstm, sw[:, hg, :], [N_SW, "cst"], [(N_CUM, hg)])
            h.act(E[:], cum[:], AF.Exp, [N_CUM], [N_E])
            h.act(Einv[:], cum[:], AF.Exp, [N_CUM], [N_EINV], scale=-1.0)
            h.tt("pool", tmp[:], cum[:], sw[:], ALU.subtract, [N_CUM, N_SW], [N_TMP])
            h.act(Eprev[:], tmp[:], AF.Exp, [N_TMP], [N_EPREV])
            cum4 = cum[:].rearrange("p g (c l) -> p g c l", c=4)
            h.tt("pool", tmp[:].rearrange("p g (c l) -> p g c l", c=4),
                 cum4[:, :, :, 31:32].to_broadcast([128, 4, 4, 32]), cum4, ALU.subtract, [N_CUM], [N_TMP])
            h.act(Eend[:], tmp[:], AF.Exp, [N_TMP], [N_EEND])
            if samp:
                h.tt("pool", Eend[:], Eend[:], valid.unsqueeze(1).to_broadcast([128, 4, 128]), ALU.mult,
                     [N_EEND, "cst"], [N_EEND])
            h.tt("dve", kk[:], kT, bc(prm[:, PR_KK:PR_KK + 4], [128, 4, 128]), ALU.mult, ["xs", "prm"], [N_KK])
            h.act(tmp[:], kk[:], AF.Square, [N_KK], [N_TMP])
            for hg in range(4):
                h.mm(PS[6][:, hg * 128:(hg + 1) * 128], bones, tmp[:, hg, :], True, True, ["cst", N_TMP], ["ps6"])
            h.act(tmp[:], PS[6][:].rearrange("p (c l) -> p c l", c=4), AF.Sqrt, ["ps6"], [N_TMP])
            h.ts("dve", tmp[:], tmp[:], 1e-12, None, ALU.max, None, [N_TMP], [N_TMP])
            h.recip(tmp[:], tmp[:], [N_TMP], [N_TMP])
            h.tt("dve", kk[:], kk[:], tmp[:], ALU.mult, [N_KK, N_TMP], [N_KK])
            h.tt("pool", kh[:], aa[:], bc(prm[:, PR_KA:PR_KA + 4], [128, 4, 128]), ALU.mult, [N_AA, "prm"], [N_KH])
            h.tt("pool", kh[:], kh[:], bc(prm[:, PR_OMKA:PR_OMKA + 4], [128, 4, 128]), ALU.add, [N_KH, "prm"], [N_KH])
            h.tt("pool", kh[:], kh[:], kT, ALU.mult, [N_KH, "xs"], [N_KH])
            h.tt("dve", tmp[:], rT, bc(prm[:, PR_RK:PR_RK + 4], [128, 4, 128]), ALU.mult, ["xs", "prm"], [N_TMP])
            h.tt("dve", tmp[:], tmp[:], kh[:], ALU.mult, [N_TMP, N_KH], [N_TMP])
            for hg in range(4):
                h.mm(PS[5][:, hg * 128:(hg + 1) * 128], bones, tmp[:, hg, :], True, True, ["cst", N_TMP], ["ps5"])
            h.tt("dve", bon[:], PS[5][:].rearrange("p (c l) -> p c l", c=4), vT, ALU.mult, ["ps5", "xs"], [N_BON])
            h.tt("pool", aa[:], aa[:], kk[:], ALU.mult, [N_AA, N_KK], [N_AA])
            h.tt("dve", AR[:, :, 1, :], rT, E[:], ALU.mult, ["xs", N_E], [(N_AR, 1)])
            h.stt("dve", AR[:, :, 0, :], kk[:], -1.0, Eprev[:], ALU.mult, ALU.mult, [N_KK, N_EPREV], [(N_AR, 0)])
            h.tt("dve", bT[:], aa[:], Einv[:], ALU.mult, [N_AA, N_EINV], ["Hb0"])
            h.tt("dve", kTb[:], kh[:], Einv[:], ALU.mult, [N_KH, N_EINV], ["Hb1"])
            h.tt("pool", BpT[:], aa[:], Eend[:], ALU.mult, [N_AA, N_EEND], ["Hb2"])
            h.tt("pool", KpT[:], kh[:], Eend[:], ALU.mult, [N_KH, N_EEND], ["Hb3"])
            h.cp("pool", vbf[:], vT, ["xs"], ["Hb4"])
            h.cp("pool", stat[:, 0:16].rearrange("p (g c) -> p g c", g=4),
                 E[:].rearrange("p g (c l) -> p g c l", c=4)[:, :, :, 31], [N_E], [("stat", 0)])
            gam = stat[:, 0:16].rearrange("p (g c) -> p g c", g=4)

            for hd in range(8):
                hg, pb = hd // 2, 64 * (hd % 2)
                px = PS[hd % 2]
                pxn = "ps%d" % (hd % 2)
                h.mm(px[:, 0:256], bT[pb:pb + 64, hg, :], AR[pb:pb + 64, hg, :, :].rearrange("p a l -> p (a l)"),
                     True, True, ["Hb0", N_AR], [pxn])
                h.mm(px[:, 256:512], kTb[pb:pb + 64, hg, :], AR[pb:pb + 64, hg, :, :].rearrange("p a l -> p (a l)"),
                     True, True, ["Hb1", N_AR], [pxn])
                h.tt("dve", SCH[:, hd, :, :].rearrange("p a l -> p (a l)"), px[:], m4, ALU.mult,
                     [pxn, "cst"], [("SCH", hd)])
            Ac, Nn, An, Xc, Xtc, Xn, Xtn, Nc2 = INV
            NI = ["INV%d" % i for i in range(8)]
            Ac4 = Ac[:].rearrange("p (g a) l -> p g a l", a=2)
            for h2 in range(2):
                pt = PS[2 + h2]
                ptn = "ps%d" % (2 + h2)
                pb = 64 * h2
                for hg in range(4):
                    h.mm(pt[:, hg * 128:(hg + 1) * 128], AR[pb:pb + 64, hg, 0, :], bT[pb:pb + 64, hg, :],
                         True, True, [N_AR, "Hb0"], [ptn])
                h.tt("dve", Ac4[:, :, h2, :], pt[:].rearrange("p (q l) -> p q l", q=4),
                     msl.unsqueeze(1).to_broadcast([128, 4, 128]), ALU.mult, [ptn, "cst"], [NI[0]])
            h.tt("pool", Xc[:], SCH[:, :, 0, :], idb[:].unsqueeze(1).to_broadcast([128, 8, 128]), ALU.add,
                 ["SCH", "idb"], [NI[3]])
            h.tt("pool", Xtc[:], Ac[:], idb[:].unsqueeze(1).to_broadcast([128, 8, 128]), ALU.add,
                 [NI[0], "idb"], [NI[4]])

            Ncur_ap = lambda hd: SCH[:, hd, 0, :]
            Ncur_key = "SCH"
            Acur, Acur_key = Ac, NI[0]
            Xcur, Xcur_key, Xtcur, Xtcur_key = Xc, NI[3], Xtc, NI[4]
            Nnext = [(Nn, NI[1]), (Nc2, NI[7])]
            Anext = [(An, NI[2]), (Ac, NI[0])]
            Xnext = [(Xn, NI[5]), (Xc, NI[3])]
            Xtnext = [(Xtn, NI[6]), (Xtc, NI[4])]
            for lvl in range(4):
                last = (lvl == 3)
                Nx, Nxk = Nnext[lvl % 2]
                Ax, Axk = Anext[lvl % 2]
                Xx, Xxk = Xnext[lvl % 2]
                Xtx, Xtxk = Xtnext[lvl % 2]
                for g2i in range(2):
                    pa, pan = PS[2 * g2i], "ps%d" % (2 * g2i)
                    pbk, pbn = PS[2 * g2i + 1], "ps%d" % (2 * g2i + 1)
                    for q in range(4):
                        hd = 4 * g2i + q
                        h.mm(pa[:, q * 128:(q + 1) * 128], Acur[:, hd, :], Ncur_ap(hd), True, True,
                             [Acur_key, Ncur_key], [pan])
                    h.cp("act", Nx[:, 4 * g2i:4 * g2i + 4, :], pa[:].rearrange("p (q l) -> p q l", q=4),
                         [pan], [(Nxk, g2i)])
                    if not last:
                        for q in range(4):
                            hd = 4 * g2i + q
                            h.mm(pbk[:, q * 128:(q + 1) * 128], Ncur_ap(hd), Acur[:, hd, :], True, True,
                                 [Acur_key, Ncur_key], [pbn])
                        h.cp("act", Ax[:, 4 * g2i:4 * g2i + 4, :], pbk[:].rearrange("p (q l) -> p q l", q=4),
                             [pbn], [(Axk, g2i)])
                for g2i in range(2):
                    pa, pan = PS[4], "ps4"
                    for q in range(4):
                        hd = 4 * g2i + q
                        h.mm(pa[:, q * 128:(q + 1) * 128], Xtcur[:, hd, :], Nx[:, hd, :], True, True,
                             [Xtcur_key, (Nxk, g2i)], [pan])
                    h.tt("dve", Xx[:, 4 * g2i:4 * g2i + 4, :], pa[:].rearrange("p (q l) -> p q l", q=4),
                         Xcur[:, 4 * g2i:4 * g2i + 4, :], ALU.add, [pan, Xcur_key], [(Xxk, g2i)])
                if not last:
                    for g2i in range(2):
                        pa, pan = PS[2 * g2i], "ps%d" % (2 * g2i)
                        for q in range(4):
                            hd = 4 * g2i + q
                            h.mm(pa[:, q * 128:(q + 1) * 128], Nx[:, hd, :], Xtcur[:, hd, :], True, True,
                                 [Xtcur_key, (Nxk, g2i)], [pan])
                        h.tt("dve", Xtx[:, 4 * g2i:4 * g2i + 4, :], pa[:].rearrange("p (q l) -> p q l", q=4),
                             Xtcur[:, 4 * g2i:4 * g2i + 4, :], ALU.add, [pan, Xtcur_key], [(Xtxk, g2i)])
                Ncur_ap = (lambda t: (lambda hd: t[:, hd, :]))(Nx)
                Ncur_key = Nxk
                Acur, Acur_key = Ax, Axk
                Xcur, Xcur_key = Xx, Xxk
                Xtcur, Xtcur_key = Xtx, Xtxk
            X4, X4k = Xcur, Xcur_key

            Atok, Bptok, Kptok, Vtok = tok
            srcs = [(AR[:, :, 0, :], N_AR, Atok, "tok0"), (BpT[:], "Hb2", Bptok, "tok1"),
                    (KpT[:], "Hb3", Kptok, "tok2"), (vbf[:], "Hb4", Vtok, "tok3")]
            for si in range(0, 4, 2):
                for u in range(2):
                    src, srck, dst, dstk = srcs[si + u]
                    for hg in range(4):
                        h.tr(psT[:, u * 4 + hg, :], src[:, hg, :], idb[:], [srck, "idb"], [("psT", u * 4 + hg)])
                for u in range(2):
                    src, srck, dst, dstk = srcs[si + u]
                    h.cp("act", dst[:], psT[:, u * 4:(u + 1) * 4, :].rearrange("p c l -> p (c l)"),
                         ["psT"], [dstk])

            for hd in range(8):
                h.mm(PS[0][:, hd * 64:(hd + 1) * 64], SCH[:, hd, 2, :], Vtok[:, hd * 64:(hd + 1) * 64],
                     True, True, ["SCH", "tok3"], ["ps0"])
            h.cp("act", Zbf[:], PS[0][:], ["ps0"], ["Zbf"])
            for hd in range(8):
                h.mm(PS[1][:, hd * 64:(hd + 1) * 64], X4[:, hd, :], Zbf[:, hd * 64:(hd + 1) * 64],
                     True, True, [X4k, "Zbf"], ["ps1"])
            h.cp("act", Yf[:], PS[1][:], ["ps1"], ["Yf"])
            for hd in range(8):
                hg, pb = hd // 2, 64 * (hd % 2)
                h.mm(PS[2][pb:pb + 64, hg * 128:(hg + 1) * 128], Atok[:, hd * 64:(hd + 1) * 64], X4[:, hd, :],
                     True, True, ["tok0", X4k], ["ps2"])
            h.cp("act", WTbf[:], PS[2][:].rearrange("p (c l) -> p c l", c=4), ["ps2"], ["WTbf"])

            psUs, psUn = (PS[0], PS[1]), ("ps0", "ps1")
            psOs, psOn = (PS[2], PS[3]), ("ps2", "ps3")
            psPn = PS[4]
            Ubf4 = Ubf[:].rearrange("p (g a v) -> p g a v", g=4, a=2)
            Yf4 = Yf[:].rearrange("p (g a v) -> p g a v", g=4, a=2)
            for c in range(4):
                s0 = 32 * c
                if samp:
                    seq = 4 * j + c
                    h.dma("sp", Sin[:], st_wkv[seq].rearrange("h v k -> v h k"), [], ["Sin"])
                    for hg in range(4):
                        h.tr(PS[4][:, hg * 64:(hg + 1) * 64],
                             Sin[:, 2 * hg:2 * hg + 2, :].rearrange("p a k -> p (a k)"), idf[0:64, 0:64],
                             ["Sin", "cst"], ["ps4"])
                    h.cp("act", Pst[:], PS[4][:, 0:256].rearrange("p (c v) -> p c v", c=4), ["ps4"], ["Pst"])
                    h.cp("dve", Pbf[:], Pst[:], ["Pst"], ["Pbf"])
                for hd in range(8):
                    hg, h2 = hd // 2, hd % 2
                    pb = 64 * h2
                    h.mm(psUs[h2][s0:s0 + 32, hg * 64:(hg + 1) * 64], WTbf[pb:pb + 64, hg, s0:s0 + 32],
                         Pbf[pb:pb + 64, hg, :], True, True, ["WTbf", "Pbf"], [psUn[h2]], tp=(pb, s0))
                for hd in range(8):
                    hg, h2 = hd // 2, hd % 2
                    pb = 64 * h2
                    h.mm(psOs[h2][s0:s0 + 32, hg * 64:(hg + 1) * 64], AR[pb:pb + 64, hg, 1, s0:s0 + 32],
                         Pbf[pb:pb + 64, hg, :], True, True, [N_AR, "Pbf"], [psOn[h2]], tp=(pb, s0))
                for h2 in range(2):
                    h.tt("dve", Ubf4[s0:s0 + 32, :, h2, :],
                         psUs[h2][s0:s0 + 32, 0:256].rearrange("p (g v) -> p g v", g=4),
                         Yf4[s0:s0 + 32, :, h2, :], ALU.add, [psUn[h2], "Yf"], [("Ubf", c)])
                for hd in range(8):
                    hg, pb = hd // 2, 64 * (hd % 2)
                    h.mm(psPn[pb:pb + 64, hg * 64:(hg + 1) * 64], Bptok[s0:s0 + 32, hd * 64:(hd + 1) * 64],
                         Ubf[s0:s0 + 32, hd * 64:(hd + 1) * 64], True, False, ["tok1", ("Ubf", c)], ["ps4"],
                         tp=(s0, pb))
                    h.mm(psPn[pb:pb + 64, hg * 64:(hg + 1) * 64], Kptok[s0:s0 + 32, hd * 64:(hd + 1) * 64],
                         Vtok[s0:s0 + 32, hd * 64:(hd + 1) * 64], False, True, ["tok2", "tok3"], ["ps4"],
                         tp=(s0, pb))
                for hg in range(4):
                    h.stt("dve", Pst[:, hg, :], Pst[:, hg, :], gam[:, hg, c:c + 1], psPn[:, hg * 64:(hg + 1) * 64],
                          ALU.mult, ALU.add, ["Pst", ("stat", 0), "ps4"], ["Pst"])
                if not samp and not (b == NPB - 1 and c == 3):
                    h.cp("act", Pbf[:], Pst[:], ["Pst"], ["Pbf"])
                if samp or (b == NPB - 1 and c == 3):
                    for hg in range(4):
                        h.tr(PS[4][0:64, hg * 128:(hg + 1) * 128], Pst[:, hg, :], idf, ["Pst", "cst"], ["ps4"])
                    h.cp("act", Sin[:].rearrange("p h k -> p (h k)"), PS[4][0:64, :], ["ps4"], ["Sin"])
                    dst = o_wkv_s[4 * j + c] if samp else o_wkv_p[0]
                    h.dma("sp", dst.rearrange("h v k -> v h k"), Sin[:], ["Sin"], [], isout=True)
            for hd in range(8):
                h.mm(PS[0][:, hd * 64:(hd + 1) * 64], SCH[:, hd, 1, :], Ubf[:, hd * 64:(hd + 1) * 64],
                     True, False, ["SCH", "Ubf"], ["ps0"])
                h.mm(PS[0][:, hd * 64:(hd + 1) * 64], SCH[:, hd, 3, :], Vtok[:, hd * 64:(hd + 1) * 64],
                     False, True, ["SCH", "tok3"], ["ps0"])
            o14 = o1[:].rearrange("p (g a v) -> p g a v", g=4, a=2)
            for h2 in range(2):
                h.cp("act", o14[:, :, h2, :], psOs[h2][:, 0:256].rearrange("p (g v) -> p g v", g=4),
                     [psOn[h2]], ["o1"])
            h.tt("dve", oo[:], o1[:], PS[0][:], ALU.add, ["o1", "ps0"], ["oo"])

            oo3 = oo[:].rearrange("p (h v) -> p h v", h=8)
            osq = o1
            h.act(osq[:], oo[:], AF.Square, ["oo"], ["o1"])
            s1, s2, mean, msq, rs, nb = (stat[:, 16:24], stat[:, 24:32], stat[:, 32:40], stat[:, 40:48],
                                         stat[:, 48:56], stat[:, 56:64])
            h.rsum(s1, oo3, ["oo"], [("stat", 1)])
            h.rsum(s2, osq[:].rearrange("p (h v) -> p h v", h=8), ["o1"], [("stat", 2)])
            h.ts("dve", mean, s1, 1.0 / 64, None, ALU.mult, None, [("stat", 1)], [("stat", 3)])
            h.tt("dve", msq, mean, mean, ALU.mult, [("stat", 3)], [("stat", 4)])
            h.stt("dve", rs, s2, 1.0 / 64, msq, ALU.mult, ALU.subtract, [("stat", 2), ("stat", 4)], [("stat", 5)])
            h.ts("dve", rs, rs, 64e-5, None, ALU.add, None, [("stat", 5)], [("stat", 5)])
            h.act(rs, rs, AF.Sqrt, [("stat", 5)], [("stat", 5)])
            h.recip(rs, rs, [("stat", 5)], [("stat", 5)])
            h.tt("dve", oo3, oo3, mean.unsqueeze(2).to_broadcast([128, 8, 64]), ALU.subtract,
                 ["oo", ("stat", 3)], ["oo"])
            h.tt("dve", oo3, oo3, rs.unsqueeze(2).to_broadcast([128, 8, 64]), ALU.mult,
                 ["oo", ("stat", 5)], ["oo"])
            for hg in range(4):
                h.tr(PS[0][:, hg * 128:(hg + 1) * 128], oo[:, hg * 128:(hg + 1) * 128], idf, ["oo", "cst"], ["ps0"])
            ps0v = PS[0][:].rearrange("p (c l) -> p c l", c=4)
            t2 = o1[:].rearrange("p (c l) -> p c l", c=4)
            h.tt("dve", t2, ps0v, bc(prm[:, PR_LW:PR_LW + 4], [128, 4, 128]), ALU.mult, ["ps0", "prm"], ["o1"])
            h.tt("pool", t2, t2, bc(prm[:, PR_LB:PR_LB + 4], [128, 4, 128]), ALU.add, ["o1", "prm"], ["o1"])
            h.tt("pool", t2, t2, bon[:], ALU.add, ["o1", N_BON], ["o1"])
            h.tt("dve", ogbf[:], t2, gT[:], ALU.mult, ["o1", N_GT], ["ogbf"])
            h.dma("sp", og_s[b], ogbf[:].rearrange("p c l -> p (c l)"), ["ogbf"], [])

            qT, kgT, vgT, ogT = xg[:, 0:2, :], xg[:, 2:4, :], xg[:, 4:8, :], xg[:, 8:12, :]
            for c2 in range(2):
                h.mm(PS[5][:, c2 * 128:(c2 + 1) * 128], wg2[0:16, c2 * 128:(c2 + 1) * 128], lga[:], True, True,
                     ["wg2", "lga"], ["ps5"])
            la_, cg, Eg, Eginv, Egend, gtmp = Gf
            for c2 in range(2):
                h.act(la_[:, c2, :], PS[5][:, c2 * 128:(c2 + 1) * 128], AF.Exp, ["ps5", "prm"], [("G0", c2)],
                      scale=-1.0, bias=prm[:, PR_BG + c2:PR_BG + c2 + 1])
            h.ts("dve", la_[:], la_[:], 1.0, None, ALU.add, None, ["G0"], ["G0"])
            h.act(la_[:], la_[:], AF.Ln, ["G0"], ["G0"])
            h.stt("dve", la_[:], la_[:], -1.0 / 16.0,
                  valid.unsqueeze(1).to_broadcast([128, 2, 128]), ALU.mult, ALU.mult, ["G0", "cst"], ["G0"])
            for c2 in range(2):
                h.scan(cg[:, c2, :], rstm, la_[:, c2, :], ["G0", "cst"], [("G1", c2)])
            h.act(Eg[:], cg[:], AF.Exp, ["G1"], ["G2"])
            h.act(Eginv[:], cg[:], AF.Exp, ["G1"], ["G3"], scale=-1.0)
            cg4 = cg[:].rearrange("p g (c l) -> p g c l", c=4)
            h.tt("pool", gtmp[:].rearrange("p g (c l) -> p g c l", c=4),
                 cg4[:, :, :, 31:32].to_broadcast([128, 2, 4, 32]), cg4, ALU.subtract, ["G1"], ["G5"])
            h.act(Egend[:], gtmp[:], AF.Exp, ["G5"], ["G4"])
            if samp:
                h.tt("pool", Egend[:], Egend[:], valid.unsqueeze(1).to_broadcast([128, 2, 128]),
                     ALU.mult, ["G4", "cst"], ["G4"])
            h.cp("pool", statg[:, 0:8].rearrange("p (g c) -> p g c", g=2),
                 Eg[:].rearrange("p g (c l) -> p g c l", c=4)[:, :, :, 31], ["G2"], [("statg", 0)])
            gamg = statg[:, 0:8].rearrange("p (g c) -> p g c", g=2)
            h.stt("dve", qdT[:], qT, 0.125, Eg[:], ALU.mult, ALU.mult, [("xg", 0), "G2"], ["Gh0"])
            h.tt("pool", kiT[:], kgT, Eginv[:], ALU.mult, [("xg", 0), "G3"], ["Gh1"])
            h.tt("pool", keT[:], kgT, Egend[:], ALU.mult, [("xg", 0), "G4"], ["Gh2"])
            h.cp("pool", vgbf[:], vgT, [("xg", 1)], ["Gv"])
            h.act(silu[:], ogT, AF.Silu, [("xg", 2)], ["Gs"])
            GS4 = GS[:].rearrange("p (g a) l -> p g a l", a=2)
            for h2 in range(2):
                pb = 64 * h2
                pt, ptn = PS[5 + h2], "ps%d" % (5 + h2)
                for c2 in range(2):
                    h.mm(pt[:, c2 * 128:(c2 + 1) * 128], kiT[pb:pb + 64, c2, :], qdT[pb:pb + 64, c2, :], True, True,
                         ["Gh1", "Gh0"], [ptn])
                h.tt("dve", GS4[:, :, h2, :], pt[:, 0:256].rearrange("p (q l) -> p q l", q=2),
                     miu.unsqueeze(1).to_broadcast([128, 2, 128]), ALU.mult, [ptn, "cst"], ["Ggs"])
            for hg in range(4):
                h.tr(psT[:, hg, :], vgbf[:, hg, :], idb[:], ["Gv", "idb"], [("psT", hg)])
            for c2 in range(2):
                h.tr(psT[:, 4 + c2, :], keT[:, c2, :], idb[:], ["Gh2", "idb"], [("psT", 4 + c2)])
            h.cp("act", Vgtok[:], psT[:, 0:4, :].rearrange("p c l -> p (c l)"), ["psT"], ["Gt0"])
            h.cp("act", Ketok[:], psT[:, 4:6, :].rearrange("p c l -> p (c l)"), ["psT"], ["Gt1"])
            if samp:
                h.dma("sp", SgIn[:], st_gla[4 * j:4 * j + 4].rearrange("q (c2 h2) k v -> (h2 k) q c2 v", h2=2),
                      [], ["SgIn"])
                h.cp("act", Sgbf[:], SgIn[:], ["SgIn"], ["Sgbf"])
            else:
                h.cp("act", Sgbf[:, 0, :, :], Sg[:], ["Sg"], [("Sgbf", 0)])
            for c in range(4):
                s0 = 32 * c
                pt, pn = PS[5 + c % 2], "ps%d" % (5 + c % 2)
                for hd in range(4):
                    c2, pb = hd // 2, 64 * (hd % 2)
                    off = c2 * 128
                    h.mm(pt[pb:pb + 64, off:off + 128], Ketok[s0:s0 + 32, hd * 64:(hd + 1) * 64],
                         Vgtok[s0:s0 + 32, hd * 128:(hd + 1) * 128], True, True, ["Gt1", "Gt0"], [pn],
                         tp=(s0, pb))
                dv = pt[:, 0:256].rearrange("p (g v) -> p g v", g=2)
                gb = gamg[:, :, c:c + 1].to_broadcast([128, 2, 128])
                for c2 in range(2):
                    if samp:
                        h.stt("dve", SgIn[:, c, c2, :], SgIn[:, c, c2, :], gamg[:, c2, c:c + 1], dv[:, c2, :],
                              ALU.mult, ALU.add, [("SgIn", c), ("statg", 0), pn], [("SgIn", c)])
                    else:
                        h.stt("dve", Sg[:, c2, :], Sg[:, c2, :], gamg[:, c2, c:c + 1], dv[:, c2, :],
                              ALU.mult, ALU.add, ["Sg", ("statg", 0), pn], ["Sg"])
                if (not samp) and c < 3:
                    h.cp("act", Sgbf[:, c + 1, :, :], Sg[:], ["Sg"], [("Sgbf", c + 1)])
            if samp:
                h.dma("sp", o_gla_s[4 * j:4 * j + 4].rearrange("q (c2 h2) k v -> (h2 k) q c2 v", h2=2),
                      SgIn[:], ["SgIn"], [], isout=True)
            elif b == NPB - 1:
                h.dma("sp", o_gla_p[0].rearrange("(c2 h2) k v -> (h2 k) c2 v", h2=2), Sg[:], ["Sg"], [],
                      isout=True)
            for c in range(4):
                s0 = 32 * c
                for hd in range(4):
                    c2, h2 = hd // 2, hd % 2
                    pb = 64 * h2
                    h.mm(PS[5 + h2][s0:s0 + 32, c2 * 128:(c2 + 1) * 128], qdT[pb:pb + 64, c2, s0:s0 + 32],
                         Sgbf[pb:pb + 64, c, c2, :], True, True, ["Gh0", "Sgbf"], ["ps%d" % (5 + h2)], tp=(pb, s0))
            o1gv = o1g[:].rearrange("p (g a v) -> p g a v", g=2, a=2)
            for h2 in range(2):
                h.cp("act", o1gv[:, :, h2, :], PS[5 + h2][:, 0:256].rearrange("p (g v) -> p g v", g=2),
                     ["ps%d" % (5 + h2)], ["o1g"])
            for hd in range(4):
                h.mm(PS[5][:, hd * 128:(hd + 1) * 128], GS[:, hd, :], Vgtok[:, hd * 128:(hd + 1) * 128], True, True,
                     ["Ggs", "Gt0"], ["ps5"])
            h.tt("dve", oog[:], o1g[:], PS[5][:], ALU.add, ["o1g", "ps5"], ["oog"])
            oo4 = oog[:].rearrange("p (h v) -> p h v", h=4)
            h.act(o1g[:], oog[:], AF.Square, ["oog"], ["o1g"])
            gs2, grs = statg[:, 8:12], statg[:, 12:16]
            h.rsum(gs2, o1g[:].rearrange("p (h v) -> p h v", h=4), ["o1g"], [("statg", 1)])
            h.ts("dve", grs, gs2, 1.0 / 128, 1e-6, ALU.mult, ALU.add, [("statg", 1)], [("statg", 2)])
            h.act(grs, grs, AF.Sqrt, [("statg", 2)], [("statg", 2)])
            h.recip(grs, grs, [("statg", 2)], [("statg", 2)])
            h.tt("dve", oo4, oo4, grs.unsqueeze(2).to_broadcast([128, 4, 128]), ALU.mult, ["oog", ("statg", 2)], ["oog"])
            for hd in range(4):
                h.tr(PS[6][:, hd * 128:(hd + 1) * 128], oog[:, hd * 128:(hd + 1) * 128], idf, ["oog", "cst"], ["ps6"])
            h.stt("dve", obbf[:], PS[6][:].rearrange("p (c l) -> p c l", c=4), prm[:, PR_NW:PR_NW + 1], silu[:],
                  ALU.mult, ALU.mult, ["ps6", "prm", "Gs"], ["obbf"])
            h.dma("sp", ob_s[b], obbf[:].rearrange("p c l -> p (c l)"), ["obbf"], [])

            if dbg is not None and dbg.get("blk") == b and dbg.get("phase") == "a1":
                src, keys = dbg["fn"](dict(locals()))
                h.dma("sp", dbg_out, src, keys, [], isout=True)
        P.emit()

    gx = contextlib.ExitStack()
    wdn = gx.enter_context(nc.sbuf_tensor("g_wdn", [128, 22, D], BF16))
    w_dn_v = ffn_w_down.rearrange("(c p) n -> p c n", p=128)
    wdn_loaded = False
    with contextlib.ExitStack() as ph:
      if "b" in phases:
        P = Prog(nc, gs, "b")
        h = H(P)

        def sb(name, shape, dt):
            return ph.enter_context(nc.sbuf_tensor("b_" + name, list(shape), dt))

        def psb(name, shape, dt):
            return ph.enter_context(nc.psum_tensor("b_" + name, list(shape), dt))

        T = {}
        cst, idb = load_consts(h, sb)
        T["idb"] = idb
        prm = sb("prm", [128, NPRM], F32)
        T["prm"] = prm
        load_param_cols(h, prm, PR_G, norm_mix, 8)
        wing = sb("wing", [128, 8, 2048], BF16)
        w_in_v = w_in.rearrange("(c p) n -> p c n", p=128)
        for kc in range(8):
            h.dma("pool", wing[:, kc, :], w_in_v[:, kc, GATE0:INC], [], [("wing", kc)])
        woa = sb("woa", [128, 4, D], BF16)
        wob = sb("wob", [128, 4, D], BF16)
        wo = sb("wo", [128, 8, D], BF16)
        h.dma("pool", woa[:], w_out_a.rearrange("(c p) n -> p c n", p=128), [], ["woa"])
        h.dma("pool", wob[:], w_out_b.rearrange("(c p) n -> p c n", p=128), [], ["wob"])
        for kc in range(8):
            h.dma("pool", wo[:, kc, :], w_o.rearrange("(c p) n -> p c n", p=128)[:, kc, :], [], [("wo", kc)])
        xts = [sb("xt%d" % i, [128, D], F32) for i in range(4)]
        alloc_rms(T, sb)
        ogbs = [sb("ogb%d" % i, [128, 4, 128], BF16) for i in range(2)]
        obbs = [sb("obb%d" % i, [128, 4, 128], BF16) for i in range(2)]
        sgas = [sb("sga%d" % i, [128, 8, 128], F32) for i in range(2)]
        sgbs = [sb("sgb%d" % i, [128, 8, 128], F32) for i in range(2)]
        tas = [sb("ta%d" % i, [128, 8, 128], F32) for i in range(2)]
        mgs = [sb("mg%d" % i, [128, 8, 128], BF16) for i in range(2)]
        x1ts = [sb("x1t%d" % i, [128, D], F32) for i in range(2)]
        T["psT"] = psb("psT", [128, 8, 128], BF16)
        PS = [psb("ps%d" % i, [128, 512], F32) for i in range(7)]
        if FILLERS:
            _pf, _id = PS[6], idb
            P.filler = ((lambda e: e.matmul(_pf[:, 0:128], lhsT=_id[:], rhs=_id[:], start=True, stop=True)), 70.0)
        for b in BLKS:
            xt, xtn = xts[b % 4], "xt%d" % (b % 4)
            load_x_block(h, b, xt, xtn)
            p2 = b % 2
            ogb, obb, sga, sgb, ta, mg, x1t = ogbs[p2], obbs[p2], sgas[p2], sgbs[p2], tas[p2], mgs[p2], x1ts[p2]
            K_ogb, K_obb, K_sga, K_sgb, K_ta, K_mg, K_x1t = ["%s%d" % (nm, p2) for nm in
                                                            ("ogb", "obb", "sga", "sgb", "ta", "mg", "x1t")]
            h.dma("sp", ogb[:].rearrange("p c l -> p (c l)"), og_s[b], [], [K_ogb])
            h.dma("sp", obb[:].rearrange("p c l -> p (c l)"), ob_s[b], [], [K_obb])
            hT, hTk = rms_to_fm(h, T, xt, xtn, PR_G, b % 2)
            for half, dst, dk in ((0, sga, K_sga), (1, sgb, K_sgb)):
                for gi in range(2):
                    pt, pn = PS[gi], "ps%d" % gi
                    for mi in range(4):
                        c0 = half * 1024 + (4 * gi + mi) * 128
                        for kc in range(8):
                            h.mm(pt[:, mi * 128:(mi + 1) * 128], wing[:, kc, c0:c0 + 128], hT[:, kc, :],
                                 kc == 0, kc == 7, [("wing", kc), hTk], [pn])
                    h.act(dst[:, 4 * gi:4 * gi + 4, :], pt[:].rearrange("p (c l) -> p c l", c=4), AF.Sigmoid,
                          [pn], [(dk, gi)])
            for gi in range(2):
                pt, pn = PS[2 + gi], "ps%d" % (2 + gi)
                for mi in range(4):
                    m = 4 * gi + mi
                    for kc in range(4):
                        h.mm(pt[:, mi * 128:(mi + 1) * 128], woa[:, kc, m * 128:(m + 1) * 128], ogb[:, kc, :],
                             kc == 0, kc == 3, ["woa", K_ogb], [pn])
                h.tt("dve", ta[:, 4 * gi:4 * gi + 4, :], pt[:].rearrange("p (c l) -> p c l", c=4),
                     sga[:, 4 * gi:4 * gi + 4, :], ALU.mult, [pn, (K_sga, gi)], [(K_ta, gi)])
            for gi in range(2):
                pt, pn = PS[4 + gi], "ps%d" % (4 + gi)
                for mi in range(4):
                    m = 4 * gi + mi
                    for kc in range(4):
                        h.mm(pt[:, mi * 128:(mi + 1) * 128], wob[:, kc, m * 128:(m + 1) * 128], obb[:, kc, :],
                             kc == 0, kc == 3, ["wob", K_obb], [pn])
                h.tt("dve", sgb[:, 4 * gi:4 * gi + 4, :], pt[:].rearrange("p (c l) -> p c l", c=4),
                     sgb[:, 4 * gi:4 * gi + 4, :], ALU.mult, [pn, (K_sgb, gi)], [(K_sgb, gi)])
                h.tt("pool", mg[:, 4 * gi:4 * gi + 4, :], ta[:, 4 * gi:4 * gi + 4, :],
                     sgb[:, 4 * gi:4 * gi + 4, :], ALU.add, [(K_ta, gi), (K_sgb, gi)], [(K_mg, gi)])
            for nh in range(2):
                pt, pn = PS[2 + nh], "ps%d" % (2 + nh)
                for kc in range(8):
                    h.mm(pt[:], mg[:, kc, :], wo[:, kc, nh * 512:(nh + 1) * 512], kc == 0, kc == 7,
                         [K_mg, ("wo", kc)], [pn])
                h.tt("dve", x1t[:, nh * 512:(nh + 1) * 512], pt[:], xt[:, nh * 512:(nh + 1) * 512], ALU.add,
                     [pn, xtn], [(K_x1t, nh)])
            h.dma("sp", x1_s[b * 128:(b + 1) * 128, :], x1t[:], [K_x1t], [])
        P.emit()

    with contextlib.ExitStack() as ph:
      if "c" in phases:
        P = Prog(nc, gs, "c")
        h = H(P)

        def sb(name, shape, dt):
            return ph.enter_context(nc.sbuf_tensor("c_" + name, list(shape), dt))

        def psb(name, shape, dt):
            return ph.enter_context(nc.psum_tensor("c_" + name, list(shape), dt))

        T = {}
        cst, idb = load_consts(h, sb)
        T["idb"] = idb
        idf = cst[:, C_ID:C_ID + 128]
        prm = sb("prm", [128, 8], F32)
        T["prm"] = prm
        load_param_cols(h, prm, 0, norm_ffn, 8)
        cw = sb("cw", [128, 4, 44], F32)
        for jx in range(3):
            h.dma("sp", cw[:, jx, :], ffn_conv_w[jx].rearrange("(c p) -> p c", p=128), [], [("cw", jx)], slow=True)
        h.dma("sp", cw[:, 3, :], ffn_conv_b.rearrange("(c p) -> p c", p=128), [], [("cw", 3)], slow=True)
        nfb = sb("nfb", [128, D], F32)
        h.dma("sp", nfb[:], norm_final.partition_broadcast(128), [], ["nfb"])
        wup = sb("wup", [128, 8, F2], BF16)
        w_up_v = ffn_w_up.rearrange("(c p) n -> p c n", p=128)
        for kc in range(8):
            h.dma("pool", wup[:, kc, :], w_up_v[:, kc, :], [], [("wup", kc)])
        if not wdn_loaded:
            for kc in range(22):
                h.dma("pool", wdn[:, kc, :], w_dn_v[:, kc, :], [], [("wdn", kc)])
        xts = [sb("xt0", [128, D], F32), sb("xt1", [128, D], F32)]
        alloc_rms(T, sb)
        ub = [sb("ub0", [128, 4, 136], F32), sb("ub1", [128, 4, 136], F32)]
        ucar = sb("ucar", [128, 44, 2], F32)
        cin = sb("cin", [128, 44, 4, 2], F32)
        cout = sb("cout", [128, 44, 8], F32)
        cstg = [sb("cstg%d" % i, [8, 512], F32) for i in range(2)]
        csto = [sb("csto%d" % i, [8, 512], F32) for i in range(2)]
        fss = [sb("fss%d" % i, [128, 1], F32) for i in range(2)]
        frs = [sb("frs%d" % i, [128, 1], F32) for i in range(2)]
        cc = [sb("cc0", [128, 4, 128], F32), sb("cc1", [128, 4, 128], F32)]
        g1 = [sb("g10", [128, 2, 128], F32), sb("g11", [128, 2, 128], F32)]
        g2t = [sb("g20", [128, 2, 128], F32), sb("g21", [128, 2, 128], F32)]
        actTs = [sb("actT%d" % i, [128, 22, 128], BF16) for i in range(2)]
        x2s = [sb("x2%d" % i, [128, D], F32) for i in range(2)]
        yts = [sb("yt0", [128, D], F32)] * 2
        T["psT"] = psb("psT", [128, 8, 128], BF16)
        PS = [psb("ps%d" % i, [128, 512], F32) for i in range(7)]
        h.memset("pool", ucar[:], 0.0, ["ucar"])
        if FILLERS:
            _pf2, _id2 = PS[6], idb
            P.filler = ((lambda e: e.matmul(_pf2[:, 0:128], lhsT=_id2[:], rhs=_id2[:], start=True, stop=True)), 70.0)
        for b in BLKS:
            samp, nseq, L = geom(b)
            j = b - NPB
            xt, xtn = xts[b % 2], "xt%d" % (b % 2)
            actT, actk = actTs[b % 2], "actT%d" % (b % 2)
            x2, x2k = x2s[b % 2], "x2%d" % (b % 2)
            yt, ytk = yts[0], "yt0"
            h.dma("sp", xt[:], x1_s[b * 128:(b + 1) * 128, :], [], [xtn])
            hT, hTk = rms_to_fm(h, T, xt, xtn, 0, b % 2)
            W = nseq * (L + 2)
            if samp:
                stc_v = st_conv[4 * j:4 * j + 4].rearrange("q t f -> (q t) f")
                for g4 in range(11):
                    cg_, cgk = cstg[g4 % 2], "cstg%d" % (g4 % 2)
                    h.dma("sp", cg_[:], stc_v[:, g4 * 512:(g4 + 1) * 512], [], [cgk])
                    for mi in range(4):
                        m = 4 * g4 + mi
                        h.tr(PS[4][:, mi * 8:(mi + 1) * 8], cg_[0:8, mi * 128:(mi + 1) * 128], idf[0:8, 0:8],
                             [cgk, "cst"], ["ps4"])
                    h.cp("act", cin[:, 4 * g4:4 * g4 + 4, :, :].rearrange("p m q t -> p m (q t)"),
                         PS[4][:, 0:32].rearrange("p (m x) -> p m x", m=4), ["ps4"], [("cin", g4)])
            last_prompt = (b == NPB - 1)
            for gi in range(11):
                u, un = ub[gi % 2], "ub%d" % (gi % 2)
                uv = u[:, :, 0:W].rearrange("p m (s l) -> p m s l", s=nseq)
                pt, pn = PS[gi % 2], "ps%d" % (gi % 2)
                chunks = [2 * gi, 2 * gi + 1, 22 + 2 * gi, 23 + 2 * gi]
                for mi, m in enumerate(chunks):
                    for kc in range(8):
                        h.mm(pt[:, mi * 128:(mi + 1) * 128], wup[:, kc, m * 128:(m + 1) * 128], hT[:, kc, :],
                             kc == 0, kc == 7, [("wup", kc), hTk], [pn])
                h.cp("act", uv[:, :, :, 2:L + 2], pt[:].rearrange("p (m s l) -> p m s l", m=4, s=nseq),
                     [pn], [un])
                for half in range(2):
                    m0 = chunks[2 * half]
                    if samp:
                        h.cp("pool", uv[:, 2 * half:2 * half + 2, :, 0:2], cin[:, m0:m0 + 2, :, :],
                             [("cin", m0 // 4)], [un])
                    else:
                        h.cp("pool", uv[:, 2 * half:2 * half + 2, 0, 0:2], ucar[:, m0:m0 + 2, :],
                             [("ucar", gi)], [un])
                for half in range(2):
                    m0 = chunks[2 * half]
                    if samp:
                        h.cp("pool", cout[:, m0:m0 + 2, :].rearrange("p m (q t) -> p m q t", q=4),
                             uv[:, 2 * half:2 * half + 2, :, 8:10], [un], [("cout", gi)])
                    else:
                        h.cp("pool", ucar[:, m0:m0 + 2, :], uv[:, 2 * half:2 * half + 2, 0, 128:130],
                             [un], [("ucar", gi)])
                        if last_prompt:
                            h.cp("pool", cout[:, m0:m0 + 2, 0:2], uv[:, 2 * half:2 * half + 2, 0, 128:130],
                                 [un], [("cout", gi)])
                c_, cn = cc[gi % 2], "cc%d" % (gi % 2)
                for mi, m in enumerate(chunks):
                    eng = "dve"
                    c4 = c_[:, mi, :].rearrange("p (s l) -> p s l", s=nseq)
                    h.act(c4, uv[:, mi, :, 2:L + 2], AF.Identity, [un, "cw"], [(cn, mi)],
                          scale=cw[:, 2, m:m + 1], bias=cw[:, 3, m:m + 1])
                    h.stt(eng, c4, uv[:, mi, :, 1:L + 1], cw[:, 1, m:m + 1], c4, ALU.mult, ALU.add,
                          [un, "cw", (cn, mi)], [(cn, mi)])
                    h.stt(eng, c4, uv[:, mi, :, 0:L], cw[:, 0, m:m + 1], c4, ALU.mult, ALU.add,
                          [un, "cw", (cn, mi)], [(cn, mi)])
                ga, gan = g1[gi % 2], "g1%d" % (gi % 2)
                gb_, gbn = g2t[gi % 2], "g2%d" % (gi % 2)
                gate = c_[:, 2:4, :]
                val = c_[:, 0:2, :]
                h.act(ga[:], gate, AF.Square, [(cn, 2), (cn, 3)], [gan])
                h.ts("dve", ga[:], ga[:], 0.044715, 1.0, ALU.mult, ALU.add, [gan], [gan])
                h.tt("pool", ga[:], ga[:], gate, ALU.mult, [gan, (cn, 2), (cn, 3)], [gan])
                h.act(gb_[:], ga[:], AF.Sigmoid, [gan], [gbn], scale=GELU_S)
                h.tt("pool", gb_[:], gb_[:], gate, ALU.mult, [gbn, (cn, 2), (cn, 3)], [gbn])
                h.tt("dve", actT[:, 2 * gi:2 * gi + 2, :], gb_[:], val, ALU.mult, [gbn, (cn, 0), (cn, 1)],
                     [(actk, gi)])
            if samp or last_prompt:
                ncol = 8 if samp else 2
                for g4 in range(11):
                    for mi in range(4):
                        m = 4 * g4 + mi
                        h.tr(PS[4][0:ncol, mi * 128:(mi + 1) * 128], cout[:, m, 0:ncol], idf, ["cout", "cst"],
                             ["ps4"])
                    co_, cok = csto[g4 % 2], "csto%d" % (g4 % 2)
                    h.cp("act", co_[0:ncol, :], PS[4][0:ncol, :], ["ps4"], [cok])
                    if samp:
                        h.dma("sp", o_conv_s[4 * j:4 * j + 4].rearrange("q t f -> (q t) f")[:, g4 * 512:(g4 + 1) * 512],
                              co_[:], [cok], [], isout=True)
                    else:
                        h.dma("sp", o_conv_p[0][:, g4 * 512:(g4 + 1) * 512], co_[0:2, :], [cok], [], isout=True)
            for nh in range(2):
                pt, pn = PS[2 + nh], "ps%d" % (2 + nh)
                for kc in range(22):
                    h.mm(pt[:], actT[:, kc, :], wdn[:, kc, nh * 512:(nh + 1) * 512], kc == 0, kc == 21,
                         [(actk, kc // 2), ("wdn", kc)], [pn])
                h.tt("dve", x2[:, nh * 512:(nh + 1) * 512], pt[:], xt[:, nh * 512:(nh + 1) * 512], ALU.add,
                     [pn, xtn], [(x2k, nh)])
            ss2, rs2 = fss[b % 2], frs[b % 2]
            ssk, rsk = "fss%d" % (b % 2), "frs%d" % (b % 2)
            h.act(yt[:], x2[:], AF.Square, [x2k], [ytk, ssk], accum_out=ss2[:])
            h.ts("dve", rs2[:], ss2[:], 1.0 / D, 1e-6, ALU.mult, ALU.add, [ssk], [rsk])
            h.act(rs2[:], rs2[:], AF.Sqrt, [rsk], [rsk])
            h.recip(rs2[:], rs2[:], [rsk], [rsk])
            h.stt("dve", yt[:], x2[:], rs2[:, 0:1], nfb[:], ALU.mult, ALU.mult, [x2k, rsk, "nfb"], [ytk])
            if samp:
                for q in range(4):
                    h.dma("sp", ysm[4 * j + q], yt[32 * q:32 * q + 8, :], [ytk], [], isout=True)
            else:
                h.dma("sp", yp[b * 128:(b + 1) * 128, :], yt[:], [ytk], [], isout=True)
        P.emit()
    gx.close()
    gs.close()
    return nc


_CACHE = {}


def kernel(**inputs):
    f32 = lambda a: np.ascontiguousarray(np.asarray(a), dtype=np.float32)
    if "nc" not in _CACHE:
        nc = bass.Bass("TRN2", target_bir_lowering=False)
        build(nc)
        _CACHE["nc"] = nc
    nc = _CACHE["nc"]
    cst = make_consts()
    shared = {
        "cst": cst,
        "norm_mix": f32(inputs["norm_mix"][0]), "w_in": f32(inputs["w_in"][0]),
        "mu_shift": f32(inputs["mu_shift"][0]), "rwkv_w0": f32(inputs["rwkv_w0"][0]),
        "rwkv_w2": f32(inputs["rwkv_w2"][0]), "rwkv_a0": f32(inputs["rwkv_a0"][0]),
        "rwkv_a2": f32(inputs["rwkv_a2"][0]), "rwkv_g2": f32(inputs["rwkv_g2"][0]),
        "rwkv_k_k": f32(inputs["rwkv_k_k"][0]), "rwkv_k_a": f32(inputs["rwkv_k_a"][0]),
        "rwkv_r_k": f32(inputs["rwkv_r_k"][0]).reshape(512), "rwkv_ln_w": f32(inputs["rwkv_ln_w"][0]),
        "rwkv_ln_b": f32(inputs["rwkv_ln_b"][0]), "gla_wg2": f32(inputs["gla_wg2"][0]),
        "gla_bg": f32(inputs["gla_bg"][0]), "gla_norm_w": f32(inputs["gla_norm_w"][0]),
        "w_out_a": f32(inputs["w_out_a"][0]), "w_out_b": f32(inputs["w_out_b"][0]),
        "w_o": f32(inputs["w_o"][0]), "norm_ffn": f32(inputs["norm_ffn"][0]),
        "ffn_w_up": f32(inputs["ffn_w_up"][0]), "ffn_conv_w": f32(inputs["ffn_conv_w"][0]),
        "ffn_conv_b": f32(inputs["ffn_conv_b"][0]), "ffn_w_down": f32(inputs["ffn_w_down"][0]),
        "norm_final": f32(inputs["norm_final"]),
    }
    in_maps = []
    for c in range(NCORES):
        m = dict(shared)
        sl = slice(16 * c, 16 * c + 16)
        m["xp"] = f32(inputs["x_prompt"][c])
        m["xs"] = f32(inputs["x_sample"][sl])
        m["st_shift"] = f32(inputs["state_rwkv_shift"][0, sl])
        m["st_wkv"] = f32(inputs["state_rwkv_wkv"][0, sl])
        m["st_gla"] = f32(inputs["state_gla"][0, sl])
        m["st_conv"] = f32(inputs["state_ffn_conv"][0, sl])
        in_maps.append(m)
    res = run_bass_kernel_spmd(nc, in_maps, core_ids=list(range(NCORES)))
    R = res.results
    cat = lambda k: np.concatenate([np.asarray(r[k]) for r in R], axis=0)
    y_p = np.stack([np.asarray(r["yp"]) for r in R], axis=0)
    y_s = cat("ys")
    outs = (
        y_p, y_s,
        cat("o_shift_p")[None], cat("o_wkv_p")[None], cat("o_gla_p")[None], cat("o_conv_p")[None],
        cat("o_shift_s")[None], cat("o_wkv_s")[None], cat("o_gla_s")[None], cat("o_conv_s")[None],
    )
    return tuple(np.ascontiguousarray(o, dtype=np.float32) for o in outs)
```

```python
import contextlib
import numpy as np
import concourse.bass as bass
import concourse.mybir as mybir
from concourse.bass_utils import run_bass_kernel_spmd

F32 = mybir.dt.float32
BF16 = mybir.dt.bfloat16
AF = mybir.ActivationFunctionType
ALU = mybir.AluOpType
AX = mybir.AxisListType

NCORES = 8
D = 1024
NPB = 16
NSB = 4
NBLK = NPB + NSB
SHIFT = 1792
INC = 5392
FH = 2816
F2 = 5632
GLA0 = 1792
GATE0 = 3344
C0 = -0.6065306597126334
GELU_S = 1.5957691216057308

ENGS = ("pe", "act", "dve", "pool", "sp")
MAXOPS = None
SCHED = True
VERBOSE = False
PROGS = []
WINDOW = 300
SCHED_LAT = 300.0
ODEP_LAT = 0.0
SCHED_BIAS = 1.0
PE_SCALE = 1.0
DVE_SCALE = 1.0
ENG_SCALE = {}
FILL_MIN = 250.0
FILL_MARGIN = 80.0
FILL_MAX = 24
FILLERS = False
LINES = []
TAGS = []


class Op:
    __slots__ = ("eng", "fn", "reads", "writes", "dma", "deps", "sig", "idx",
                 "dsem", "dval", "dprev", "isout", "mm", "odeps", "cost", "start", "fin")

    def __init__(self, eng, fn, reads, writes, dma, isout, mm, cost=300.0):
        self.odeps = []
        self.cost = cost
        self.eng = eng
        self.fn = fn
        self.reads = reads
        self.writes = writes
        self.dma = dma
        self.deps = []
        self.sig = None
        self.dsem = None
        self.dval = None
        self.dprev = None
        self.isout = isout
        self.mm = mm


def _norm(k):
    return k if isinstance(k, tuple) else (k, None)


class Prog:
    def __init__(self, nc, semstack, tag, n_dma_sems=6):
        self.nc = nc
        self.ops = []
        self.n_dma_sems = n_dma_sems
        self.st = {}
        self.semstack = semstack
        self.tag = tag
        self.filler = None

    @staticmethod
    def _conf(a, b):
        return a is None or b is None or a == b

    def add(self, eng, fn, reads=(), writes=(), dma=False, isout=False, mm=False, cost=300.0):
        op = Op(eng, fn, [_norm(k) for k in reads], [_norm(k) for k in writes], dma, isout, mm, cost)
        odeps = {}
        op.idx = len(self.ops)
        if MAXOPS is not None:
            import sys as _s
            f = _s._getframe(1)
            while f is not None and f.f_code.co_name != "build":
                f = f.f_back
            LINES.append(f.f_lineno if f is not None else -1)
            TAGS.append(f.f_locals.get("b", -1) if f is not None else -1)
        deps = {}
        for (name, sub) in op.reads:
            s = self.st.setdefault(name, {"w": {}, "r": {}})
            for ws, wop in s["w"].items():
                if self._conf(ws, sub):
                    deps[wop.idx] = wop
        for (name, sub) in op.writes:
            s = self.st.setdefault(name, {"w": {}, "r": {}})
            for ws, wop in s["w"].items():
                if self._conf(ws, sub):
                    if not (op.mm and wop.mm):
                        deps[wop.idx] = wop
                    else:
                        odeps[wop.idx] = wop
            for rs, rops in s["r"].items():
                if self._conf(rs, sub):
                    for rop in rops:
                        deps[rop.idx] = rop
        for (name, sub) in op.reads:
            self.st[name]["r"].setdefault(sub, []).append(op)
        for (name, sub) in op.writes:
            s = self.st[name]
            if sub is None:
                s["w"] = {None: op}
                s["r"] = {}
            else:
                s["w"][sub] = op
                s["r"][sub] = []
        deps.pop(op.idx, None)
        op.deps = list(deps.values())
        op.odeps = [o for k, o in odeps.items() if k not in deps]
        self.ops.append(op)
        return op

    def schedule(self, window=None):
        window = window or WINDOW
        ops = self.ops
        n = len(ops)
        for op in ops:
            if not op.dma:
                op.cost = op.cost * ENG_SCALE.get(op.eng, 1.0)
        ndep = [0] * n
        users = [[] for _ in range(n)]
        truedep = set()
        for op in ops:
            for d in op.deps:
                truedep.add((op.idx, d.idx))
            ds = {d.idx for d in op.deps} | {d.idx for d in op.odeps}
            ndep[op.idx] = len(ds)
            for d in ds:
                users[d].append(op.idx)
        ready_t = [0.0] * n
        per = {e: [op.idx for op in ops if op.eng == e] for e in ENGS}
        head = {e: 0 for e in ENGS}
        done = [False] * n
        t_e = {e: 0.0 for e in ENGS}
        order = []
        remaining = n
        LAT = SCHED_LAT
        while remaining:
            best = None
            for e in ENGS:
                lst = per[e]
                hp = head[e]
                while hp < len(lst) and done[lst[hp]]:
                    hp += 1
                head[e] = hp
                if hp >= len(lst):
                    continue
                cnt = 0
                k = hp
                cand = None
                rdy = []
                while k < len(lst) and cnt < window:
                    i = lst[k]
                    if not done[i]:
                        cnt += 1
                        if ndep[i] == 0:
                            st = max(t_e[e], ready_t[i])
                            key = (st + SCHED_BIAS * (cnt - 1), i)
                            rdy.append((st, i))
                            if cand is None or key < cand[0]:
                                cand = (key, i, st)
                    k += 1
                if cand is not None:
                    bst = cand[2]
                    for (st, i) in rdy:
                        if i != cand[1] and st + (60.0 if ops[i].dma else ops[i].cost) <= bst:
                            cand = ((st, i), i, st)
                            break
                    if best is None or cand[0] < best[0]:
                        best = (cand[0], cand[1], cand[2], e)
            assert best is not None, "scheduler deadlock"
            _, i, st, e = best
            op = ops[i]
            op.start = st
            if op.dma:
                t_e[e] = st + 60.0
            else:
                t_e[e] = st + op.cost
            op.fin = st + op.cost
            done[i] = True
            remaining -= 1
            order.append(op)
            for u in users[i]:
                ndep[u] -= 1
                lat_ = LAT if ((u, i) in truedep) else ODEP_LAT
                if ready_t[u] < op.fin + lat_:
                    ready_t[u] = op.fin + lat_
        if self.filler is not None:
            fn, fcost = self.filler
            out = []
            pe_end = None
            nf = 0
            for op in order:
                if op.eng == "pe":
                    if pe_end is not None:
                        gap = op.start - pe_end
                        if gap > FILL_MIN:
                            k = min(FILL_MAX, int((gap - FILL_MARGIN) / fcost))
                            for _ in range(max(0, k)):
                                f = Op("pe", fn, [], [], False, False, True, fcost)
                                f.idx = -1
                                f.start = pe_end
                                f.fin = pe_end + fcost
                                out.append(f)
                                nf += 1
                    pe_end = op.start + op.cost
                out.append(op)
            order = out
            if VERBOSE:
                print("[sched %s] fillers inserted: %d" % (self.tag, nf), flush=True)
        self.ops = order
        self.est = max(op.fin for op in order) if order else 0.0
        if VERBOSE:
            PROGS.append(self)
            busy = {e: sum(o.cost for o in order if o.eng == e and not o.dma) for e in ENGS}
            print("[sched %s] n=%d est=%.1f us busy(us): %s" % (
                self.tag, n, self.est / 1e3, " ".join("%s=%.0f" % (e, busy[e] / 1e3) for e in ENGS)), flush=True)

    def emit(self):
        nc = self.nc
        if MAXOPS is not None:
            self.ops = self.ops[:MAXOPS]
        if SCHED:
            self.schedule()
        ops = self.ops
        needed = set()
        for op in ops:
            for d in op.deps:
                needed.add(d.idx)
        cnt = {e: 0 for e in ENGS}
        for op in ops:
            if not op.dma and op.idx in needed:
                cnt[op.eng] += 1
                op.sig = cnt[op.eng]
        dcount = {e: 0 for e in ENGS}
        last_on_slot = {}
        for op in ops:
            if not op.dma:
                continue
            j = dcount[op.eng]
            dcount[op.eng] += 1
            slot = j % self.n_dma_sems
            op.dsem = (op.eng, slot)
            op.dval = 16 * (j // self.n_dma_sems + 1)
            op.dprev = last_on_slot.get(op.dsem)
            last_on_slot[op.dsem] = op
        out_ops = [op for op in ops if op.dma and op.isout]
        per_eng = {e: [op for op in ops if op.eng == e] for e in ENGS}
        es = self.semstack
        csem = {e: es.enter_context(nc.semaphore("cs%s_%s" % (self.tag, e)))
                for e in ENGS if e != "sp"}
        dsem = {}
        for e in ENGS:
            for s in range(min(self.n_dma_sems, dcount[e])):
                dsem[(e, s)] = es.enter_context(nc.semaphore("ds%s_%s%d" % (self.tag, e, s)))

        def run_engine(e, eng):
            known = {}

            def wait(key, sem, val):
                if known.get(key, 0) >= val:
                    return
                known[key] = val
                eng.wait_ge(sem, val)

            for op in per_eng[e]:
                for d in op.deps:
                    if d.dma:
                        wait(d.dsem, dsem[d.dsem], d.dval)
                    else:
                        wait(d.eng, csem[d.eng], d.sig)
                if op.dma and op.dprev is not None:
                    wait(op.dsem, dsem[op.dsem], op.dprev.dval)
                ins = op.fn(eng)
                if op.dma:
                    ins.then_inc(dsem[op.dsem], 16)
                elif op.sig is not None:
                    ins.then_inc(csem[e], 1)
            if e == "sp":
                for op in out_ops:
                    wait(op.dsem, dsem[op.dsem], op.dval)
                for key, op in last_on_slot.items():
                    wait(op.dsem, dsem[op.dsem], op.dval)

        with nc.Block() as block:
            @block.sync
            def _(eng):
                run_engine("sp", eng)

            @block.tensor
            def _(eng):
                run_engine("pe", eng)

            @block.scalar
            def _(eng):
                run_engine("act", eng)

            @block.vector
            def _(eng):
                run_engine("dve", eng)

            @block.gpsimd
            def _(eng):
                run_engine("pool", eng)


def _fsz(ap):
    n = 1
    for d in ap.shape[1:]:
        n *= int(d)
    return n


class H:
    def __init__(self, P):
        self.P = P

    def act(self, out, in_, func, r, w, **kw):
        self.P.add("act", lambda e: e.activation(out=out, in_=in_, func=func, **kw), r, w,
                   cost=220.0 + 1.05 * _fsz(out))

    def tt(self, eng, out, in0, in1, op, r, w):
        self.P.add(eng, lambda e: e.tensor_tensor(out=out, in0=in0, in1=in1, op=op), r, w,
                   cost=(100.0 + 1.05 * _fsz(out)) if eng == "dve" else (160.0 + 2.1 * _fsz(out)))

    def ts(self, eng, out, in0, s1, s2, op0, op1, r, w):
        if op1 is None:
            self.P.add(eng, lambda e: e.tensor_scalar(out=out, in0=in0, scalar1=s1, scalar2=None, op0=op0), r, w,
                       cost=100.0 + 1.05 * _fsz(out))
        else:
            self.P.add(eng, lambda e: e.tensor_scalar(out=out, in0=in0, scalar1=s1, scalar2=s2, op0=op0, op1=op1), r, w,
                       cost=100.0 + 1.05 * _fsz(out))

    def stt(self, eng, out, in0, scalar, in1, op0, op1, r, w):
        self.P.add(eng, lambda e: e.scalar_tensor_tensor(out=out, in0=in0, scalar=scalar, in1=in1, op0=op0, op1=op1), r, w,
                   cost=100.0 + 1.05 * _fsz(out))

    def cp(self, eng, out, in_, r, w):
        if eng == "act":
            self.P.add("act", lambda e: e.activation(out=out, in_=in_, func=AF.Copy), r, w,
                       cost=220.0 + 1.05 * _fsz(out))
        else:
            self.P.add(eng, lambda e: e.tensor_copy(out=out, in_=in_), r, w,
                       cost=(100.0 + 1.05 * _fsz(out)) if eng == "dve" else (160.0 + 2.1 * _fsz(out)))

    def memset(self, eng, ap, val, w):
        self.P.add(eng, lambda e: e.memset(ap, val), [], w, cost=160.0 + 1.0 * _fsz(ap))

    def recip(self, out, in_, r, w):
        self.P.add("dve", lambda e: e.reciprocal(out=out, in_=in_), r, w, cost=100.0 + 1.05 * _fsz(out))

    def scan(self, out, d0, d1, r, w):
        self.P.add("dve", lambda e: e.tensor_tensor_scan(out=out, data0=d0, data1=d1, initial=0.0,
                                                         op0=ALU.mult, op1=ALU.add), r, w,
                   cost=100.0 + 2.1 * _fsz(out))

    def rsum(self, out, in_, r, w):
        self.P.add("dve", lambda e: e.tensor_reduce(out=out, in_=in_, axis=AX.X, op=ALU.add), r, w,
                   cost=100.0 + 1.05 * _fsz(in_))

    def mm(self, out, lhsT, rhs, start, stop, r, w, tp=None):
        c = (max(64.0, float(_fsz(rhs))) / 2.0 + 16.0) * PE_SCALE
        if lhsT.dtype == F32:
            c *= 4.0
        if tp is None:
            self.P.add("pe", lambda e: e.matmul(out, lhsT=lhsT, rhs=rhs, start=start, stop=stop), r, w, mm=True,
                       cost=c)
        else:
            self.P.add("pe", lambda e: e.matmul(out, lhsT=lhsT, rhs=rhs, start=start, stop=stop,
                                                tile_position=tp), r, w, mm=True, cost=c)

    def tr(self, out, in_, ident, r, w):
        self.P.add("pe", lambda e: e.transpose(out=out, in_=in_, identity=ident), r, w, mm=True, cost=110.0)

    def dma(self, q, out, in_, r, w, isout=False, slow=False):
        nbytes = 1
        for d in out.shape:
            nbytes *= int(d)
        c = 2500.0 + 4.0 * nbytes / 150.0
        if slow:
            self.P.add(q, lambda e: e.dma_start(out=out, in_=in_, allow_slow_non_contiguous=True), r, w,
                       dma=True, isout=isout, cost=c)
        else:
            self.P.add(q, lambda e: e.dma_start(out=out, in_=in_), r, w, dma=True, isout=isout, cost=c)


def bc(ap, shape):
    a = ap
    while len(a.shape) < len(shape):
        a = a.unsqueeze(len(a.shape))
    return a.to_broadcast(list(shape))


C_ID = 0
C_BO = 128
C_RST = 256
C_VS = 384
C_ONE = 512
C_M4 = 640
C_SL = 1152
C_IU = 1280
NCST = 1408


def make_consts():
    c = np.zeros((128, NCST), np.float32)
    i = np.arange(128)
    same = (i[:, None] // 32) == (i[None, :] // 32)
    su = (same & (i[:, None] < i[None, :])).astype(np.float32)
    iu = (same & (i[:, None] <= i[None, :])).astype(np.float32)
    sl = (same & (i[:, None] > i[None, :])).astype(np.float32)
    c[:, C_ID:C_ID + 128] = np.eye(128, dtype=np.float32)
    c[:, C_BO:C_BO + 128] = ((i[:, None] // 64) == (i[None, :] // 64)).astype(np.float32)
    c[:, C_RST:C_RST + 128] = (i[None, :] % 32 != 0).astype(np.float32)
    c[:, C_VS:C_VS + 128] = (i[None, :] % 32 < 8).astype(np.float32)
    c[:, C_ONE:C_ONE + 128] = 1.0
    c[:, C_M4:C_M4 + 512] = np.concatenate([su, iu, su, iu], axis=1)
    c[:, C_SL:C_SL + 128] = sl
    c[:, C_IU:C_IU + 128] = iu
    return c


PR_G = 0
PR_MU = 8
PR_W0 = 22
PR_A0 = 26
PR_KK = 30
PR_KA = 34
PR_RK = 38
PR_LW = 42
PR_LB = 46
PR_BG = 50
PR_NW = 52
PR_OMKA = 53
NPRM = 64


def build(nc, dbg=None, phases="abc", blocks=None):
    BLKS = list(range(NBLK)) if blocks is None else list(blocks)
    gs = contextlib.ExitStack()

    def din(name, shape, dt=F32):
        return nc.dram_tensor(name, list(shape), dt, kind="ExternalInput").ap()

    def dout(name, shape):
        return nc.dram_tensor(name, list(shape), F32, kind="ExternalOutput").ap()

    def dscr(name, shape, dt):
        return nc.dram_tensor(name, list(shape), dt, kind="Internal").ap()

    xp = din("xp", [2048, D])
    xsm = din("xs", [16, 8, D])
    st_shift = din("st_shift", [16, SHIFT])
    st_wkv = din("st_wkv", [16, 8, 64, 64])
    st_gla = din("st_gla", [16, 4, 64, 128])
    st_conv = din("st_conv", [16, 2, F2])
    cst_d = din("cst", [128, NCST])
    norm_mix = din("norm_mix", [D])
    w_in = din("w_in", [D, INC])
    mu_shift = din("mu_shift", [SHIFT])
    rwkv_w0 = din("rwkv_w0", [512])
    rwkv_w2 = din("rwkv_w2", [64, 512])
    rwkv_a0 = din("rwkv_a0", [512])
    rwkv_a2 = din("rwkv_a2", [64, 512])
    rwkv_g2 = din("rwkv_g2", [128, 512])
    rwkv_k_k = din("rwkv_k_k", [512])
    rwkv_k_a = din("rwkv_k_a", [512])
    rwkv_r_k = din("rwkv_r_k", [512])
    rwkv_ln_w = din("rwkv_ln_w", [512])
    rwkv_ln_b = din("rwkv_ln_b", [512])
    gla_wg2 = din("gla_wg2", [16, 256])
    gla_bg = din("gla_bg", [256])
    gla_norm_w = din("gla_norm_w", [128])
    w_out_a = din("w_out_a", [512, D])
    w_out_b = din("w_out_b", [512, D])
    w_o = din("w_o", [D, D])
    norm_ffn = din("norm_ffn", [D])
    ffn_w_up = din("ffn_w_up", [D, F2])
    ffn_conv_w = din("ffn_conv_w", [3, F2])
    ffn_conv_b = din("ffn_conv_b", [F2])
    ffn_w_down = din("ffn_w_down", [FH, D])
    norm_final = din("norm_final", [D])

    yp = dout("yp", [2048, D])
    ysm = dout("ys", [16, 8, D])
    o_shift_p = dout("o_shift_p", [1, SHIFT])
    o_wkv_p = dout("o_wkv_p", [1, 8, 64, 64])
    o_gla_p = dout("o_gla_p", [1, 4, 64, 128])
    o_conv_p = dout("o_conv_p", [1, 2, F2])
    o_shift_s = dout("o_shift_s", [16, SHIFT])
    o_wkv_s = dout("o_wkv_s", [16, 8, 64, 64])
    o_gla_s = dout("o_gla_s", [16, 4, 64, 128])
    o_conv_s = dout("o_conv_s", [16, 2, F2])

    og_s = dscr("og_s", [NBLK, 128, 512], BF16)
    ob_s = dscr("ob_s", [NBLK, 128, 512], BF16)
    x1_s = dscr("x1_s", [NBLK * 128, D], F32)

    dbg_out = None
    if dbg is not None:
        dbg_out = dout("dbg", dbg["shape"])

    def geom(b):
        return (b >= NPB, 4, 32) if b >= NPB else (False, 1, 128)

    def load_consts(h, sb, q="sp"):
        cst = sb("cst", [128, NCST], F32)
        h.dma(q, cst[:], cst_d, [], ["cst"])
        idb = sb("idb", [128, 128], BF16)
        h.cp("dve", idb[:], cst[:, C_ID:C_ID + 128], ["cst"], ["idb"])
        return cst, idb

    def load_x_block(h, b, xt, xtn):
        samp, nseq, L = geom(b)
        if not samp:
            h.dma("sp", xt[:], xp[b * 128:(b + 1) * 128, :], [], [xtn])
        else:
            j = b - NPB
            h.memset("pool", xt[:], 0.0, [xtn])
            for q in range(4):
                h.dma("sp", xt[32 * q:32 * q + 8, :], xsm[4 * j + q], [], [(xtn, q)])

    def rms_to_fm(h, T, xt, xtn, gcol, par=0):
        xp_ = par if len(T["xn"]) > 1 else 0
        xn, ss, rstd, hT = T["xn"][xp_], T["ss"][par], T["rstd"][par], T["hT"][par]
        xnk, ssk, rsk, hTk = "xn%d" % xp_, "ss%d" % par, "rstd%d" % par, "hT%d" % par
        psT, idb, prm = T["psT"], T["idb"], T["prm"]
        h.act(xn[:], xt[:], AF.Square, [xtn], [xnk, ssk], accum_out=ss[:])
        h.ts("dve", rstd[:], ss[:], 1.0 / D, 1e-6, ALU.mult, ALU.add, [ssk], [rsk])
        h.act(rstd[:], rstd[:], AF.Sqrt, [rsk], [rsk])
        h.recip(rstd[:], rstd[:], [rsk], [rsk])
        h.act(xn[:], xt[:], AF.Copy, [xtn, rsk], [xnk], scale=rstd[:, 0:1])
        for c in range(8):
            h.tr(psT[:, c, :], xn[:, c * 128:(c + 1) * 128], idb[:], [xnk, "idb"], [("psT", c)])
        h.tt("dve", hT[:], psT[:], bc(prm[:, gcol:gcol + 8], [128, 8, 128]), ALU.mult,
             ["psT", "prm"], [hTk])
        return hT, hTk

    def alloc_rms(T, sb, nxn=2):
        T["xn"] = [sb("xn%d" % i, [128, D], BF16) for i in range(nxn)]
        T["ss"] = [sb("ss%d" % i, [128, 1], F32) for i in range(2)]
        T["rstd"] = [sb("rstd%d" % i, [128, 1], F32) for i in range(2)]
        T["hT"] = [sb("hT%d" % i, [128, 8, 128], BF16) for i in range(2)]

    def load_param_cols(h, prm, col, src, n):
        h.dma("sp", prm[:, col:col + n], src.rearrange("(c p) -> p c", p=128), [], [("prm", col)], slow=True)

    with contextlib.ExitStack() as ph:
      if "a" in phases:
        P = Prog(nc, gs, "a")
        h = H(P)

        def sb(name, shape, dt):
            return ph.enter_context(nc.sbuf_tensor("a_" + name, list(shape), dt))

        def psb(name, shape, dt):
            return ph.enter_context(nc.psum_tensor("a_" + name, list(shape), dt))

        T = {}
        cst, idb = load_consts(h, sb)
        T["idb"] = idb
        prm = sb("prm", [128, NPRM], F32)
        T["prm"] = prm
        load_param_cols(h, prm, PR_G, norm_mix, 8)
        load_param_cols(h, prm, PR_MU, mu_shift, 14)
        load_param_cols(h, prm, PR_W0, rwkv_w0, 4)
        load_param_cols(h, prm, PR_A0, rwkv_a0, 4)
        load_param_cols(h, prm, PR_KK, rwkv_k_k, 4)
        load_param_cols(h, prm, PR_KA, rwkv_k_a, 4)
        load_param_cols(h, prm, PR_RK, rwkv_r_k, 4)
        load_param_cols(h, prm, PR_LW, rwkv_ln_w, 4)
        load_param_cols(h, prm, PR_LB, rwkv_ln_b, 4)
        load_param_cols(h, prm, PR_BG, gla_bg, 2)
        load_param_cols(h, prm, PR_NW, gla_norm_w, 1)
        h.ts("dve", prm[:, PR_BG:PR_BG + 2], prm[:, PR_BG:PR_BG + 2], -1.0, None, ALU.mult, None,
             [("prm", PR_BG)], [("prm", PR_BG)])
        h.ts("dve", prm[:, PR_OMKA:PR_OMKA + 4], prm[:, PR_KA:PR_KA + 4], -1.0, 1.0, ALU.mult, ALU.add,
             [("prm", PR_KA)], [("prm", PR_OMKA)])

        NA1 = GATE0
        win = sb("win", [128, 8, NA1], BF16)
        w_in_v = w_in.rearrange("(c p) n -> p c n", p=128)
        WG = [(0, 512), (512, 1152), (1152, 1792), (1792, 2560), (2560, NA1)]
        for gi_, (lo, hi) in enumerate(WG):
            h.dma("pool", win[:, :, lo:hi], w_in_v[:, :, lo:hi], [], [("win", gi_)])

        def wkey(c0, c1):
            ks = [("win", gi_) for gi_, (lo, hi) in enumerate(WG) if lo < c1 and c0 < hi]
            return ks
        w2a2 = sb("w2a2", [128, 512], BF16)
        h.dma("pool", w2a2[0:64, :], rwkv_w2, [], [("w2a2", 0)])
        h.dma("pool", w2a2[64:128, :], rwkv_a2, [], [("w2a2", 1)])
        g2 = sb("g2", [128, 512], BF16)
        h.dma("pool", g2[:], rwkv_g2, [], ["g2"])
        wg2 = sb("wg2", [16, 256], BF16)
        h.dma("pool", wg2[:], gla_wg2, [], ["wg2"])

        xt = sb("xt", [128, D], F32)
        alloc_rms(T, sb, nxn=1)
        prw = sb("prw", [128, 14, 132], F32)
        lastc = sb("lastc", [128, 14, 1], F32)
        xs = sb("xs", [128, 14, 128], F32)
        Fm = [sb("F%d" % i, [128, 4, 128], F32) for i in range(10)]
        gTs = [sb("gT%d" % i, [128, 4, 128], F32) for i in range(2)]
        bons = [sb("bon%d" % i, [128, 4, 128], F32) for i in range(2)]
        ARs = [sb("AR%d" % i, [128, 4, 2, 128], BF16) for i in range(2)]
        Hm = [sb("Hb%d" % i, [128, 4, 128], BF16) for i in range(8)]
        tok = [sb("tok%d" % i, [128, 512], BF16) for i in range(4)]
        SCH = sb("SCH", [128, 8, 4, 128], BF16)
        INV = [sb("INV%d" % i, [128, 8, 128], BF16) for i in range(8)]
        lora_in = sb("lora_in", [128, 128], BF16)
        slg = sb("slg", [128, 128], BF16)
        Zbf = sb("Zbf", [128, 512], BF16)
        Yf = sb("Yf", [128, 512], F32)
        WTbf = sb("WTbf", [128, 4, 128], BF16)
        Ubf = sb("Ubf", [128, 512], BF16)
        Pst = sb("Pst", [128, 4, 64], F32)
        Pbf = sb("Pbf", [128, 4, 64], BF16)
        o1 = sb("o1", [128, 512], F32)
        oo = sb("oo", [128, 512], F32)
        stat = sb("stat", [128, 64], F32)
        ogbf = sb("ogbf", [128, 4, 128], BF16)
        obbf = sb("obbf", [128, 4, 128], BF16)
        Sin = sb("Sin", [64, 8, 64], F32)
        shs = [sb("shs%d" % i, [4, 512], F32) for i in range(2)]
        shf = sb("shf", [128, 14, 4], F32)
        sho = [sb("sho%d" % i, [4, 512], F32) for i in range(2)]
        lga = sb("lga", [16, 128], BF16)
        Sg = sb("Sg", [128, 2, 128], F32)
        Sgbf = sb("Sgbf", [128, 4, 2, 128], BF16)
        SgIn = sb("SgIn", [128, 4, 2, 128], F32)
        xg = sb("xg", [128, 12, 128], F32)
        Gf = [sb("G%d" % i, [128, 2, 128], F32) for i in range(6)]
        silu = sb("Gs", [128, 4, 128], F32)
        qdT, kiT, keT = [sb("Gh%d" % i, [128, 2, 128], BF16) for i in range(3)]
        vgbf = sb("Gv", [128, 4, 128], BF16)
        GS = sb("Ggs", [128, 4, 128], BF16)
        Vgtok = sb("Gt0", [128, 512], BF16)
        Ketok = sb("Gt1", [128, 256], BF16)
        o1g = sb("o1g", [128, 512], F32)
        oog = sb("oog", [128, 512], F32)
        statg = sb("statg", [128, 16], F32)

        T["psT"] = psb("psT", [128, 8, 128], BF16)
        psT = T["psT"]
        PS = [psb("ps%d" % i, [128, 512], F32) for i in range(7)]

        if VERBOSE:
            print("[A1] sbuf remaining after alloc:", nc.sbuf_bytes_remaining, flush=True)
        idf = cst[:, C_ID:C_ID + 128]
        bones = cst[:, C_BO:C_BO + 128]
        rstm = cst[:, C_RST:C_RST + 128]
        m4 = cst[:, C_M4:C_M4 + 512]
        msl = cst[:, C_SL:C_SL + 128]
        miu = cst[:, C_IU:C_IU + 128]

        h.memset("pool", Pst[:], 0.0, ["Pst"])
        h.memset("pool", Pbf[:], 0.0, ["Pbf"])
        h.memset("pool", Sg[:], 0.0, ["Sg"])
        h.memset("pool", lastc[:], 0.0, ["lastc"])

        def v4(ap, nseq, L):
            return ap.rearrange("p m (s l) -> p m s l", s=nseq)

        for b in BLKS:
            samp, nseq, L = geom(b)
            j = b - NPB
            valid = cst[:, (C_VS if samp else C_ONE):(C_VS if samp else C_ONE) + 128]
            load_x_block(h, b, xt, "xt")
            hT, hTk = rms_to_fm(h, T, xt, "xt", PR_G, b % 2)

            W = nseq * (L + 1)
            pv = prw[:, :, 0:W].rearrange("p m (s l) -> p m s l", s=nseq)
            for gi in range(4):
                ms = list(range(4 * gi, min(4 * gi + 4, 14)))
                pt = PS[5 + gi % 2]
                pn = "ps%d" % (5 + gi % 2)
                for mi, m in enumerate(ms):
                    for kc in range(8):
                        h.mm(pt[:, mi * 128:(mi + 1) * 128], win[:, kc, m * 128:(m + 1) * 128], hT[:, kc, :],
                             kc == 0, kc == 7, wkey(m * 128, (m + 1) * 128) + [hTk], [pn])
                nm = len(ms)
                h.cp("act", pv[:, ms[0]:ms[0] + nm, :, 1:L + 1],
                     pt[:, 0:nm * 128].rearrange("p (m s l) -> p m s l", m=nm, s=nseq),
                     [pn], ["prw"])
            gcols = [GLA0 + 128 * i for i in range(8)] + [GLA0 + 1040 + 128 * i for i in range(4)]
            for gi in range(3):
                pt, pn = PS[5 + gi % 2], "ps%d" % (5 + gi % 2)
                for mi in range(4):
                    c0 = gcols[4 * gi + mi]
                    for kc in range(8):
                        h.mm(pt[:, mi * 128:(mi + 1) * 128], win[:, kc, c0:c0 + 128], hT[:, kc, :],
                             kc == 0, kc == 7, wkey(c0, c0 + 128) + [hTk], [pn])
                h.cp("act", xg[:, 4 * gi:4 * gi + 4, :], pt[:].rearrange("p (c l) -> p c l", c=4), [pn], [("xg", gi)])
            for kc in range(8):
                h.mm(PS[6][0:16, 0:128], win[:, kc, GLA0 + 1024:GLA0 + 1040], hT[:, kc, :], kc == 0, kc == 7,
                     wkey(GLA0 + 1024, GLA0 + 1040) + [hTk], ["ps6"])
            h.cp("act", lga[:], PS[6][0:16, 0:128], ["ps6"], ["lga"])
            if not samp:
                h.cp("pool", pv[:, :, 0, 0:1], lastc[:], ["lastc"], ["prw"])
            else:
                for g4 in range(4):
                    ms = list(range(4 * g4, min(4 * g4 + 4, 14)))
                    nm = len(ms)
                    sh_, shk = shs[g4 % 2], "shs%d" % (g4 % 2)
                    h.dma("sp", sh_[:, 0:nm * 128], st_shift[4 * j:4 * j + 4, ms[0] * 128:(ms[0] + nm) * 128], [], [shk])
                    for mi, m in enumerate(ms):
                        h.tr(PS[5][:, mi * 4:(mi + 1) * 4], sh_[0:4, mi * 128:(mi + 1) * 128], idf[0:4, 0:4],
                             [shk, "cst"], ["ps5"])
                    h.cp("act", pv[:, ms[0]:ms[0] + nm, :, 0],
                         PS[5][:, 0:nm * 4].rearrange("p (m s) -> p m s", m=nm), ["ps5"], ["prw"])
            last_prompt = (b == NPB - 1)
            if samp or last_prompt:
                ncol = 4 if samp else 1
                if samp:
                    h.cp("pool", shf[:, :, 0:4], pv[:, :, :, 8], ["prw"], ["shf"])
                else:
                    h.cp("pool", shf[:, :, 0:1], pv[:, :, 0, 128:129], ["prw"], ["shf"])
                for g4 in range(4):
                    ms = list(range(4 * g4, min(4 * g4 + 4, 14)))
                    for mi, m in enumerate(ms):
                        h.tr(PS[6][0:ncol, mi * 128:(mi + 1) * 128], shf[:, m, 0:ncol], idf,
                             ["shf", "cst"], ["ps6"])
                    nm = len(ms)
                    so_, sok = sho[g4 % 2], "sho%d" % (g4 % 2)
                    h.cp("act", so_[0:ncol, 0:nm * 128], PS[6][0:ncol, 0:nm * 128], ["ps6"], [sok])
                    if samp:
                        h.dma("sp", o_shift_s[4 * j:4 * j + 4, ms[0] * 128:(ms[0] + nm) * 128], so_[0:4, 0:nm * 128],
                              [sok], [], isout=True)
                    else:
                        h.dma("sp", o_shift_p[0:1, ms[0] * 128:(ms[0] + nm) * 128], so_[0:1, 0:nm * 128],
                              [sok], [], isout=True)
            if not samp:
                h.cp("pool", lastc[:], pv[:, :, 0, 128:129], ["prw"], ["lastc"])
            xs4 = xs[:].rearrange("p m (s l) -> p m s l", s=nseq)
            cur = pv[:, :, :, 1:L + 1]
            prv = pv[:, :, :, 0:L]
            h.tt("dve", xs4[:, 0:9], prv[:, 0:9], cur[:, 0:9], ALU.subtract, ["prw"], [("xs", 0)])
            h.tt("pool", xs4[:, 9:14], prv[:, 9:14], cur[:, 9:14], ALU.subtract, ["prw"], [("xs", 1)])
            for m in range(14):
                h.stt("dve", xs4[:, m], xs4[:, m], prm[:, PR_MU + m:PR_MU + m + 1], cur[:, m], ALU.mult, ALU.add,
                      [("xs", 0 if m < 9 else 1), "prm", "prw"], [("xs", 2 + m)])
            rT = xs[:, 0:4, :]
            kT = xs[:, 4:8, :]
            vT = xs[:, 8:12, :]

            sw, aa, cum, E, Einv, Eprev, Eend, kk, kh, tmp = Fm
            n = lambda i: "F%d" % i
            N_SW, N_AA, N_CUM, N_E, N_EINV, N_EPREV, N_EEND, N_KK, N_KH, N_TMP = [n(i) for i in range(10)]
            gT, N_GT = gTs[b % 2], "gT%d" % (b % 2)
            bon, N_BON = bons[b % 2], "bon%d" % (b % 2)
            AR, N_AR = ARs[b % 2], "AR%d" % (b % 2)
            bT, kTb, BpT, KpT, vbf = Hm[0], Hm[1], Hm[2], Hm[3], Hm[4]
            h.act(lora_in[0:64, :], xs[0:64, 12, :], AF.Tanh, ["xs"], [("lora_in", 0)])
            h.cp("act", lora_in[64:128, :], xs[64:128, 12, :], ["xs"], [("lora_in", 1)])
            h.act(slg[:], xs[:, 13, :], AF.Sigmoid, ["xs"], ["slg"])
            for hg in range(4):
                h.mm(PS[5][:, hg * 128:(hg + 1) * 128], w2a2[0:64, hg * 128:(hg + 1) * 128], lora_in[0:64, :],
                     True, True, [("w2a2", 0), ("lora_in", 0)], ["ps5"])
            for hg in range(4):
                h.mm(PS[6][:, hg * 128:(hg + 1) * 128], w2a2[64:128, hg * 128:(hg + 1) * 128], lora_in[64:128, :],
                     True, True, [("w2a2", 1), ("lora_in", 1)], ["ps6"])
            for hg in range(4):
                h.act(sw[:, hg, :], PS[5][:, hg * 128:(hg + 1) * 128], AF.Sigmoid, ["ps5", "prm"], [(N_SW, hg)],
                      bias=prm[:, PR_W0 + hg:PR_W0 + hg + 1])
                h.act(aa[:, hg, :], PS[6][:, hg * 128:(hg + 1) * 128], AF.Sigmoid, ["ps6", "prm"], [(N_AA, hg)],
                      bias=prm[:, PR_A0 + hg:PR_A0 + hg + 1])
            for hg in range(4):
                h.mm(PS[5][:, hg * 128:(hg + 1) * 128], g2[:, hg * 128:(hg + 1) * 128], slg[:],
                     True, True, ["g2", "slg"], ["ps5"])
            h.cp("act", gT[:], PS[5][:].rearrange("p (c l) -> p c l", c=4), ["ps5"], [N_GT])
            h.stt("dve", sw[:], sw[:], C0, valid.unsqueeze(1).to_broadcast([128, 4, 128]),
                  ALU.mult, ALU.mult, [N_SW, "cst"], [N_SW])
            for hg in range(4):
                h.scan(cum[:, hg, :], rstm, sw[:, hg, :], [N_SW, "cst"], [(N_CUM, hg)])
            h.act(E[:], cum[:], AF.Exp, [N_CUM], [N_E])
            h.act(Einv[:], cum[:], AF.Exp, [N_CUM], [N_EINV], scale=-1.0)
            h.tt("pool", tmp[:], cum[:], sw[:], ALU.subtract, [N_CUM, N_SW], [N_TMP])
            h.act(Eprev[:], tmp[:], AF.Exp, [N_TMP], [N_EPREV])
            cum4 = cum[:].rearrange("p g (c l) -> p g c l", c=4)
            h.tt("pool", tmp[:].rearrange("p g (c l) -> p g c l", c=4),
                 cum4[:, :, :, 31:32].to_broadcast([128, 4, 4, 32]), cum4, ALU.subtract, [N_CUM], [N_TMP])
            h.act(Eend[:], tmp[:], AF.Exp, [N_TMP], [N_EEND])
            if samp:
                h.tt("pool", Eend[:], Eend[:], valid.unsqueeze(1).to_broadcast([128, 4, 128]), ALU.mult,
                     [N_EEND, "cst"], [N_EEND])
            h.tt("dve", kk[:], kT, bc(prm[:, PR_KK:PR_KK + 4], [128, 4, 128]), ALU.mult, ["xs", "prm"], [N_KK])
            h.act(tmp[:], kk[:], AF.Square, [N_KK], [N_TMP])
            for hg in range(4):
                h.mm(PS[6][:, hg * 128:(hg + 1) * 128], bones, tmp[:, hg, :], True, True, ["cst", N_TMP], ["ps6"])
            h.act(tmp[:], PS[6][:].rearrange("p (c l) -> p c l", c=4), AF.Sqrt, ["ps6"], [N_TMP])
            h.ts("dve", tmp[:], tmp[:], 1e-12, None, ALU.max, None, [N_TMP], [N_TMP])
            h.recip(tmp[:], tmp[:], [N_TMP], [N_TMP])
            h.tt("dve", kk[:], kk[:], tmp[:], ALU.mult, [N_KK, N_TMP], [N_KK])
            h.tt("pool", kh[:], aa[:], bc(prm[:, PR_KA:PR_KA + 4], [128, 4, 128]), ALU.mult, [N_AA, "prm"], [N_KH])
            h.tt("pool", kh[:], kh[:], bc(prm[:, PR_OMKA:PR_OMKA + 4], [128, 4, 128]), ALU.add, [N_KH, "prm"], [N_KH])
            h.tt("pool", kh[:], kh[:], kT, ALU.mult, [N_KH, "xs"], [N_KH])
            h.tt("dve", tmp[:], rT, bc(prm[:, PR_RK:PR_RK + 4], [128, 4, 128]), ALU.mult, ["xs", "prm"], [N_TMP])
            h.tt("dve", tmp[:], tmp[:], kh[:], ALU.mult, [N_TMP, N_KH], [N_TMP])
            for hg in range(4):
                h.mm(PS[5][:, hg * 128:(hg + 1) * 128], bones, tmp[:, hg, :], True, True, ["cst", N_TMP], ["ps5"])
            h.tt("dve", bon[:], PS[5][:].rearrange("p (c l) -> p c l", c=4), vT, ALU.mult, ["ps5", "xs"], [N_BON])
            h.tt("pool", aa[:], aa[:], kk[:], ALU.mult, [N_AA, N_KK], [N_AA])
            h.tt("dve", AR[:, :, 1, :], rT, E[:], ALU.mult, ["xs", N_E], [(N_AR, 1)])
            h.stt("dve", AR[:, :, 0, :], kk[:], -1.0, Eprev[:], ALU.mult, ALU.mult, [N_KK, N_EPREV], [(N_AR, 0)])
            h.tt("dve", bT[:], aa[:], Einv[:], ALU.mult, [N_AA, N_EINV], ["Hb0"])
            h.tt("dve", kTb[:], kh[:], Einv[:], ALU.mult, [N_KH, N_EINV], ["Hb1"])
            h.tt("pool", BpT[:], aa[:], Eend[:], ALU.mult, [N_AA, N_EEND], ["Hb2"])
            h.tt("pool", KpT[:], kh[:], Eend[:], ALU.mult, [N_KH, N_EEND], ["Hb3"])
            h.cp("pool", vbf[:], vT, ["xs"], ["Hb4"])
            h.cp("pool", stat[:, 0:16].rearrange("p (g c) -> p g c", g=4),
                 E[:].rearrange("p g (c l) -> p g c l", c=4)[:, :, :, 31], [N_E], [("stat", 0)])
            gam = stat[:, 0:16].rearrange("p (g c) -> p g c", g=4)

            for hd in range(8):
                hg, pb = hd // 2, 64 * (hd % 2)
                px = PS[hd % 2]
                pxn = "ps%d" % (hd % 2)
                h.mm(px[:, 0:256], bT[pb:pb + 64, hg, :], AR[pb:pb + 64, hg, :, :].rearrange("p a l -> p (a l)"),
                     True, True, ["Hb0", N_AR], [pxn])
                h.mm(px[:, 256:512], kTb[pb:pb + 64, hg, :], AR[pb:pb + 64, hg, :, :].rearrange("p a l -> p (a l)"),
                     True, True, ["Hb1", N_AR], [pxn])
                h.tt("dve", SCH[:, hd, :, :].rearrange("p a l -> p (a l)"), px[:], m4, ALU.mult,
                     [pxn, "cst"], [("SCH", hd)])
            Ac, Nn, An, Xc, Xtc, Xn, Xtn, Nc2 = INV
            NI = ["INV%d" % i for i in range(8)]
            Ac4 = Ac[:].rearrange("p (g a) l -> p g a l", a=2)
            for h2 in range(2):
                pt = PS[2 + h2]
                ptn = "ps%d" % (2 + h2)
                pb = 64 * h2
                for hg in range(4):
                    h.mm(pt[:, hg * 128:(hg + 1) * 128], AR[pb:pb + 64, hg, 0, :], bT[pb:pb + 64, hg, :],
                         True, True, [N_AR, "Hb0"], [ptn])
                h.tt("dve", Ac4[:, :, h2, :], pt[:].rearrange("p (q l) -> p q l", q=4),
                     msl.unsqueeze(1).to_broadcast([128, 4, 128]), ALU.mult, [ptn, "cst"], [NI[0]])
            h.tt("pool", Xc[:], SCH[:, :, 0, :], idb[:].unsqueeze(1).to_broadcast([128, 8, 128]), ALU.add,
                 ["SCH", "idb"], [NI[3]])
            h.tt("pool", Xtc[:], Ac[:], idb[:].unsqueeze(1).to_broadcast([128, 8, 128]), ALU.add,
                 [NI[0], "idb"], [NI[4]])

            Ncur_ap = lambda hd: SCH[:, hd, 0, :]
            Ncur_key = "SCH"
            Acur, Acur_key = Ac, NI[0]
            Xcur, Xcur_key, Xtcur, Xtcur_key = Xc, NI[3], Xtc, NI[4]
            Nnext = [(Nn, NI[1]), (Nc2, NI[7])]
            Anext = [(An, NI[2]), (Ac, NI[0])]
            Xnext = [(Xn, NI[5]), (Xc, NI[3])]
            Xtnext = [(Xtn, NI[6]), (Xtc, NI[4])]
            for lvl in range(4):
                last = (lvl == 3)
                Nx, Nxk = Nnext[lvl % 2]
                Ax, Axk = Anext[lvl % 2]
                Xx, Xxk = Xnext[lvl % 2]
                Xtx, Xtxk = Xtnext[lvl % 2]
                for g2i in range(2):
                    pa, pan = PS[2 * g2i], "ps%d" % (2 * g2i)
                    pbk, pbn = PS[2 * g2i + 1], "ps%d" % (2 * g2i + 1)
                    for q in range(4):
                        hd = 4 * g2i + q
                        h.mm(pa[:, q * 128:(q + 1) * 128], Acur[:, hd, :], Ncur_ap(hd), True, True,
                             [Acur_key, Ncur_key], [pan])
                    h.cp("act", Nx[:, 4 * g2i:4 * g2i + 4, :], pa[:].rearrange("p (q l) -> p q l", q=4),
                         [pan], [(Nxk, g2i)])
                    if not last:
                        for q in range(4):
                            hd = 4 * g2i + q
                            h.mm(pbk[:, q * 128:(q + 1) * 128], Ncur_ap(hd), Acur[:, hd, :], True, True,
                                 [Acur_key, Ncur_key], [pbn])
                        h.cp("act", Ax[:, 4 * g2i:4 * g2i + 4, :], pbk[:].rearrange("p (q l) -> p q l", q=4),
                             [pbn], [(Axk, g2i)])
                for g2i in range(2):
                    pa, pan = PS[4], "ps4"
                    for q in range(4):
                        hd = 4 * g2i + q
                        h.mm(pa[:, q * 128:(q + 1) * 128], Xtcur[:, hd, :], Nx[:, hd, :], True, True,
                             [Xtcur_key, (Nxk, g2i)], [pan])
                    h.tt("dve", Xx[:, 4 * g2i:4 * g2i + 4, :], pa[:].rearrange("p (q l) -> p q l", q=4),
                         Xcur[:, 4 * g2i:4 * g2i + 4, :], ALU.add, [pan, Xcur_key], [(Xxk, g2i)])
                if not last:
                    for g2i in range(2):
                        pa, pan = PS[2 * g2i], "ps%d" % (2 * g2i)
                        for q in range(4):
                            hd = 4 * g2i + q
                            h.mm(pa[:, q * 128:(q + 1) * 128], Nx[:, hd, :], Xtcur[:, hd, :], True, True,
                                 [Xtcur_key, (Nxk, g2i)], [pan])
                        h.tt("dve", Xtx[:, 4 * g2i:4 * g2i + 4, :], pa[:].rearrange("p (q l) -> p q l", q=4),
                             Xtcur[:, 4 * g2i:4 * g2i + 4, :], ALU.add, [pan, Xtcur_key], [(Xtxk, g2i)])
                Ncur_ap = (lambda t: (lambda hd: t[:, hd, :]))(Nx)
                Ncur_key = Nxk
                Acur, Acur_key = Ax, Axk
                Xcur, Xcur_key = Xx, Xxk
                Xtcur, Xtcur_key = Xtx, Xtxk
            X4, X4k = Xcur, Xcur_key

            Atok, Bptok, Kptok, Vtok = tok
            srcs = [(AR[:, :, 0, :], N_AR, Atok, "tok0"), (BpT[:], "Hb2", Bptok, "tok1"),
                    (KpT[:], "Hb3", Kptok, "tok2"), (vbf[:], "Hb4", Vtok, "tok3")]
            for si in range(0, 4, 2):
                for u in range(2):
                    src, srck, dst, dstk = srcs[si + u]
                    for hg in range(4):
                        h.tr(psT[:, u * 4 + hg, :], src[:, hg, :], idb[:], [srck, "idb"], [("psT", u * 4 + hg)])
                for u in range(2):
                    src, srck, dst, dstk = srcs[si + u]
                    h.cp("act", dst[:], psT[:, u * 4:(u + 1) * 4, :].rearrange("p c l -> p (c l)"),
                         ["psT"], [dstk])

            for hd in range(8):
                h.mm(PS[0][:, hd * 64:(hd + 1) * 64], SCH[:, hd, 2, :], Vtok[:, hd * 64:(hd + 1) * 64],
                     True, True, ["SCH", "tok3"], ["ps0"])
            h.cp("act", Zbf[:], PS[0][:], ["ps0"], ["Zbf"])
            for hd in range(8):
                h.mm(PS[1][:, hd * 64:(hd + 1) * 64], X4[:, hd, :], Zbf[:, hd * 64:(hd + 1) * 64],
                     True, True, [X4k, "Zbf"], ["ps1"])
            h.cp("act", Yf[:], PS[1][:], ["ps1"], ["Yf"])
            for hd in range(8):
                hg, pb = hd // 2, 64 * (hd % 2)
                h.mm(PS[2][pb:pb + 64, hg * 128:(hg + 1) * 128], Atok[:, hd * 64:(hd + 1) * 64], X4[:, hd, :],
                     True, True, ["tok0", X4k], ["ps2"])
            h.cp("act", WTbf[:], PS[2][:].rearrange("p (c l) -> p c l", c=4), ["ps2"], ["WTbf"])

            psUs, psUn = (PS[0], PS[1]), ("ps0", "ps1")
            psOs, psOn = (PS[2], PS[3]), ("ps2", "ps3")
            psPn = PS[4]
            Ubf4 = Ubf[:].rearrange("p (g a v) -> p g a v", g=4, a=2)
            Yf4 = Yf[:].rearrange("p (g a v) -> p g a v", g=4, a=2)
            for c in range(4):
                s0 = 32 * c
                if samp:
                    seq = 4 * j + c
                    h.dma("sp", Sin[:], st_wkv[seq].rearrange("h v k -> v h k"), [], ["Sin"])
                    for hg in range(4):
                        h.tr(PS[4][:, hg * 64:(hg + 1) * 64],
                             Sin[:, 2 * hg:2 * hg + 2, :].rearrange("p a k -> p (a k)"), idf[0:64, 0:64],
                             ["Sin", "cst"], ["ps4"])
                    h.cp("act", Pst[:], PS[4][:, 0:256].rearrange("p (c v) -> p c v", c=4), ["ps4"], ["Pst"])
                    h.cp("dve", Pbf[:], Pst[:], ["Pst"], ["Pbf"])
                for hd in range(8):
                    hg, h2 = hd // 2, hd % 2
                    pb = 64 * h2
                    h.mm(psUs[h2][s0:s0 + 32, hg * 64:(hg + 1) * 64], WTbf[pb:pb + 64, hg, s0:s0 + 32],
                         Pbf[pb:pb + 64, hg, :], True, True, ["WTbf", "Pbf"], [psUn[h2]], tp=(pb, s0))
                for hd in range(8):
                    hg, h2 = hd // 2, hd % 2
                    pb = 64 * h2
                    h.mm(psOs[h2][s0:s0 + 32, hg * 64:(hg + 1) * 64], AR[pb:pb + 64, hg, 1, s0:s0 + 32],
                         Pbf[pb:pb + 64, hg, :], True, True, [N_AR, "Pbf"], [psOn[h2]], tp=(pb, s0))
                for h2 in range(2):
                    h.tt("dve", Ubf4[s0:s0 + 32, :, h2, :],
                         psUs[h2][s0:s0 + 32, 0:256].rearrange("p (g v) -> p g v", g=4),
                         Yf4[s0:s0 + 32, :, h2, :], ALU.add, [psUn[h2], "Yf"], [("Ubf", c)])
                for hd in range(8):
                    hg, pb = hd // 2, 64 * (hd % 2)
                    h.mm(psPn[pb:pb + 64, hg * 64:(hg + 1) * 64], Bptok[s0:s0 + 32, hd * 64:(hd + 1) * 64],
                         Ubf[s0:s0 + 32, hd * 64:(hd + 1) * 64], True, False, ["tok1", ("Ubf", c)], ["ps4"],
                         tp=(s0, pb))
                    h.mm(psPn[pb:pb + 64, hg * 64:(hg + 1) * 64], Kptok[s0:s0 + 32, hd * 64:(hd + 1) * 64],
                         Vtok[s0:s0 + 32, hd * 64:(hd + 1) * 64], False, True, ["tok2", "tok3"], ["ps4"],
                         tp=(s0, pb))
                for hg in range(4):
                    h.stt("dve", Pst[:, hg, :], Pst[:, hg, :], gam[:, hg, c:c + 1], psPn[:, hg * 64:(hg + 1) * 64],
                          ALU.mult, ALU.add, ["Pst", ("stat", 0), "ps4"], ["Pst"])
                if not samp and not (b == NPB - 1 and c == 3):
                    h.cp("act", Pbf[:], Pst[:], ["Pst"], ["Pbf"])
                if samp or (b == NPB - 1 and c == 3):
                    for hg in range(4):
                        h.tr(PS[4][0:64, hg * 128:(hg + 1) * 128], Pst[:, hg, :], idf, ["Pst", "cst"], ["ps4"])
                    h.cp("act", Sin[:].rearrange("p h k -> p (h k)"), PS[4][0:64, :], ["ps4"], ["Sin"])
                    dst = o_wkv_s[4 * j + c] if samp else o_wkv_p[0]
                    h.dma("sp", dst.rearrange("h v k -> v h k"), Sin[:], ["Sin"], [], isout=True)
            for hd in range(8):
                h.mm(PS[0][:, hd * 64:(hd + 1) * 64], SCH[:, hd, 1, :], Ubf[:, hd * 64:(hd + 1) * 64],
                     True, False, ["SCH", "Ubf"], ["ps0"])
                h.mm(PS[0][:, hd * 64:(hd + 1) * 64], SCH[:, hd, 3, :], Vtok[:, hd * 64:(hd + 1) * 64],
                     False, True, ["SCH", "tok3"], ["ps0"])
            o14 = o1[:].rearrange("p (g a v) -> p g a v", g=4, a=2)
            for h2 in range(2):
                h.cp("act", o14[:, :, h2, :], psOs[h2][:, 0:256].rearrange("p (g v) -> p g v", g=4),
                     [psOn[h2]], ["o1"])
            h.tt("dve", oo[:], o1[:], PS[0][:], ALU.add, ["o1", "ps0"], ["oo"])

            oo3 = oo[:].rearrange("p (h v) -> p h v", h=8)
            osq = o1
            h.act(osq[:], oo[:], AF.Square, ["oo"], ["o1"])
            s1, s2, mean, msq, rs, nb = (stat[:, 16:24], stat[:, 24:32], stat[:, 32:40], stat[:, 40:48],
                                         stat[:, 48:56], stat[:, 56:64])
            h.rsum(s1, oo3, ["oo"], [("stat", 1)])
            h.rsum(s2, osq[:].rearrange("p (h v) -> p h v", h=8), ["o1"], [("stat", 2)])
            h.ts("dve", mean, s1, 1.0 / 64, None, ALU.mult, None, [("stat", 1)], [("stat", 3)])
            h.tt("dve", msq, mean, mean, ALU.mult, [("stat", 3)], [("stat", 4)])
            h.stt("dve", rs, s2, 1.0 / 64, msq, ALU.mult, ALU.subtract, [("stat", 2), ("stat", 4)], [("stat", 5)])
            h.ts("dve", rs, rs, 64e-5, None, ALU.add, None, [("stat", 5)], [("stat", 5)])
            h.act(rs, rs, AF.Sqrt, [("stat", 5)], [("stat", 5)])
            h.recip(rs, rs, [("stat", 5)], [("stat", 5)])
            h.tt("dve", oo3, oo3, mean.unsqueeze(2).to_broadcast([128, 8, 64]), ALU.subtract,
                 ["oo", ("stat", 3)], ["oo"])
            h.tt("dve", oo3, oo3, rs.unsqueeze(2).to_broadcast([128, 8, 64]), ALU.mult,
                 ["oo", ("stat", 5)], ["oo"])
            for hg in range(4):
                h.tr(PS[0][:, hg * 128:(hg + 1) * 128], oo[:, hg * 128:(hg + 1) * 128], idf, ["oo", "cst"], ["ps0"])
            ps0v = PS[0][:].rearrange("p (c l) -> p c l", c=4)
            t2 = o1[:].rearrange("p (c l) -> p c l", c=4)
            h.tt("dve", t2, ps0v, bc(prm[:, PR_LW:PR_LW + 4], [128, 4, 128]), ALU.mult, ["ps0", "prm"], ["o1"])
            h.tt("pool", t2, t2, bc(prm[:, PR_LB:PR_LB + 4], [128, 4, 128]), ALU.add, ["o1", "prm"], ["o1"])
            h.tt("pool", t2, t2, bon[:], ALU.add, ["o1", N_BON], ["o1"])
            h.tt("dve", ogbf[:], t2, gT[:], ALU.mult, ["o1", N_GT], ["ogbf"])
            h.dma("sp", og_s[b], ogbf[:].rearrange("p c l -> p (c l)"), ["ogbf"], [])

            qT, kgT, vgT, ogT = xg[:, 0:2, :], xg[:, 2:4, :], xg[:, 4:8, :], xg[:, 8:12, :]
            for c2 in range(2):
                h.mm(PS[5][:, c2 * 128:(c2 + 1) * 128], wg2[0:16, c2 * 128:(c2 + 1) * 128], lga[:], True, True,
                     ["wg2", "lga"], ["ps5"])
            la_, cg, Eg, Eginv, Egend, gtmp = Gf
            for c2 in range(2):
                h.act(la_[:, c2, :], PS[5][:, c2 * 128:(c2 + 1) * 128], AF.Exp, ["ps5", "prm"], [("G0", c2)],
                      scale=-1.0, bias=prm[:, PR_BG + c2:PR_BG + c2 + 1])
            h.ts("dve", la_[:], la_[:], 1.0, None, ALU.add, None, ["G0"], ["G0"])
            h.act(la_[:], la_[:], AF.Ln, ["G0"], ["G0"])
            h.stt("dve", la_[:], la_[:], -1.0 / 16.0,
                  valid.unsqueeze(1).to_broadcast([128, 2, 128]), ALU.mult, ALU.mult, ["G0", "cst"], ["G0"])
            for c2 in range(2):
                h.scan(cg[:, c2, :], rstm, la_[:, c2, :], ["G0", "cst"], [("G1", c2)])
            h.act(Eg[:], cg[:], AF.Exp, ["G1"], ["G2"])
            h.act(Eginv[:], cg[:], AF.Exp, ["G1"], ["G3"], scale=-1.0)
            cg4 = cg[:].rearrange("p g (c l) -> p g c l", c=4)
            h.tt("pool", gtmp[:].rearrange("p g (c l) -> p g c l", c=4),
                 cg4[:, :, :, 31:32].to_broadcast([128, 2, 4, 32]), cg4, ALU.subtract, ["G1"], ["G5"])
            h.act(Egend[:], gtmp[:], AF.Exp, ["G5"], ["G4"])
            if samp:
                h.tt("pool", Egend[:], Egend[:], valid.unsqueeze(1).to_broadcast([128, 2, 128]),
                     ALU.mult, ["G4", "cst"], ["G4"])
            h.cp("pool", statg[:, 0:8].rearrange("p (g c) -> p g c", g=2),
                 Eg[:].rearrange("p g (c l) -> p g c l", c=4)[:, :, :, 31], ["G2"], [("statg", 0)])
            gamg = statg[:, 0:8].rearrange("p (g c) -> p g c", g=2)
            h.stt("dve", qdT[:], qT, 0.125, Eg[:], ALU.mult, ALU.mult, [("xg", 0), "G2"], ["Gh0"])
            h.tt("pool", kiT[:], kgT, Eginv[:], ALU.mult, [("xg", 0), "G3"], ["Gh1"])
            h.tt("pool", keT[:], kgT, Egend[:], ALU.mult, [("xg", 0), "G4"], ["Gh2"])
            h.cp("pool", vgbf[:], vgT, [("xg", 1)], ["Gv"])
            h.act(silu[:], ogT, AF.Silu, [("xg", 2)], ["Gs"])
            GS4 = GS[:].rearrange("p (g a) l -> p g a l", a=2)
            for h2 in range(2):
                pb = 64 * h2
                pt, ptn = PS[5 + h2], "ps%d" % (5 + h2)
                for c2 in range(2):
                    h.mm(pt[:, c2 * 128:(c2 + 1) * 128], kiT[pb:pb + 64, c2, :], qdT[pb:pb + 64, c2, :], True, True,
                         ["Gh1", "Gh0"], [ptn])
                h.tt("dve", GS4[:, :, h2, :], pt[:, 0:256].rearrange("p (q l) -> p q l", q=2),
                     miu.unsqueeze(1).to_broadcast([128, 2, 128]), ALU.mult, [ptn, "cst"], ["Ggs"])
            for hg in range(4):
                h.tr(psT[:, hg, :], vgbf[:, hg, :], idb[:], ["Gv", "idb"], [("psT", hg)])
            for c2 in range(2):
                h.tr(psT[:, 4 + c2, :], keT[:, c2, :], idb[:], ["Gh2", "idb"], [("psT", 4 + c2)])
            h.cp("act", Vgtok[:], psT[:, 0:4, :].rearrange("p c l -> p (c l)"), ["psT"], ["Gt0"])
            h.cp("act", Ketok[:], psT[:, 4:6, :].rearrange("p c l -> p (c l)"), ["psT"], ["Gt1"])
            if samp:
                h.dma("sp", SgIn[:], st_gla[4 * j:4 * j + 4].rearrange("q (c2 h2) k v -> (h2 k) q c2 v", h2=2),
                      [], ["SgIn"])
                h.cp("act", Sgbf[:], SgIn[:], ["SgIn"], ["Sgbf"])
            else:
                h.cp("act", Sgbf[:, 0, :, :], Sg[:], ["Sg"], [("Sgbf", 0)])
            for c in range(4):
                s0 = 32 * c
                pt, pn = PS[5 + c % 2], "ps%d" % (5 + c % 2)
                for hd in range(4):
                    c2, pb = hd // 2, 64 * (hd % 2)
                    off = c2 * 128
                    h.mm(pt[pb:pb + 64, off:off + 128], Ketok[s0:s0 + 32, hd * 64:(hd + 1) * 64],
                         Vgtok[s0:s0 + 32, hd * 128:(hd + 1) * 128], True, True, ["Gt1", "Gt0"], [pn],
                         tp=(s0, pb))
                dv = pt[:, 0:256].rearrange("p (g v) -> p g v", g=2)
                gb = gamg[:, :, c:c + 1].to_broadcast([128, 2, 128])
                for c2 in range(2):
                    if samp:
                        h.stt("dve", SgIn[:, c, c2, :], SgIn[:, c, c2, :], gamg[:, c2, c:c + 1], dv[:, c2, :],
                              ALU.mult, ALU.add, [("SgIn", c), ("statg", 0), pn], [("SgIn", c)])
                    else:
                        h.stt("dve", Sg[:, c2, :], Sg[:, c2, :], gamg[:, c2, c:c + 1], dv[:, c2, :],
                              ALU.mult, ALU.add, ["Sg", ("statg", 0), pn], ["Sg"])
                if (not samp) and c < 3:
                    h.cp("act", Sgbf[:, c + 1, :, :], Sg[:], ["Sg"], [("Sgbf", c + 1)])
            if samp:
                h.dma("sp", o_gla_s[4 * j:4 * j + 4].rearrange("q (c2 h2) k v -> (h2 k) q c2 v", h2=2),
                      SgIn[:], ["SgIn"], [], isout=True)
            elif b == NPB - 1:
                h.dma("sp", o_gla_p[0].rearrange("(c2 h2) k v -> (h2 k) c2 v", h2=2), Sg[:], ["Sg"], [],
                      isout=True)
            for c in range(4):
                s0 = 32 * c
                for hd in range(4):
                    c2, h2 = hd // 2, hd % 2
                    pb = 64 * h2
                    h.mm(PS[5 + h2][s0:s0 + 32, c2 * 128:(c2 + 1) * 128], qdT[pb:pb + 64, c2, s0:s0 + 32],
                         Sgbf[pb:pb + 64, c, c2, :], True, True, ["Gh0", "Sgbf"], ["ps%d" % (5 + h2)], tp=(pb, s0))
            o1gv = o1g[:].rearrange("p (g a v) -> p g a v", g=2, a=2)
            for h2 in range(2):
                h.cp("act", o1gv[:, :, h2, :], PS[5 + h2][:, 0:256].rearrange("p (g v) -> p g v", g=2),
                     ["ps%d" % (5 + h2)], ["o1g"])
            for hd in range(4):
                h.mm(PS[5][:, hd * 128:(hd + 1) * 128], GS[:, hd, :], Vgtok[:, hd * 128:(hd + 1) * 128], True, True,
                     ["Ggs", "Gt0"], ["ps5"])
            h.tt("dve", oog[:], o1g[:], PS[5][:], ALU.add, ["o1g", "ps5"], ["oog"])
            oo4 = oog[:].rearrange("p (h v) -> p h v", h=4)
            h.act(o1g[:], oog[:], AF.Square, ["oog"], ["o1g"])
            gs2, grs = statg[:, 8:12], statg[:, 12:16]
            h.rsum(gs2, o1g[:].rearrange("p (h v) -> p h v", h=4), ["o1g"], [("statg", 1)])
            h.ts("dve", grs, gs2, 1.0 / 128, 1e-6, ALU.mult, ALU.add, [("statg", 1)], [("statg", 2)])
            h.act(grs, grs, AF.Sqrt, [("statg", 2)], [("statg", 2)])
            h.recip(grs, grs, [("statg", 2)], [("statg", 2)])
            h.tt("dve", oo4, oo4, grs.unsqueeze(2).to_broadcast([128, 4, 128]), ALU.mult, ["oog", ("statg", 2)], ["oog"])
            for hd in range(4):
                h.tr(PS[6][:, hd * 128:(hd + 1) * 128], oog[:, hd * 128:(hd + 1) * 128], idf, ["oog", "cst"], ["ps6"])
            h.stt("dve", obbf[:], PS[6][:].rearrange("p (c l) -> p c l", c=4), prm[:, PR_NW:PR_NW + 1], silu[:],
                  ALU.mult, ALU.mult, ["ps6", "prm", "Gs"], ["obbf"])
            h.dma("sp", ob_s[b], obbf[:].rearrange("p c l -> p (c l)"), ["obbf"], [])

            if dbg is not None and dbg.get("blk") == b and dbg.get("phase") == "a1":
                src, keys = dbg["fn"](dict(locals()))
                h.dma("sp", dbg_out, src, keys, [], isout=True)
        P.emit()

    gx = contextlib.ExitStack()
    wdn = gx.enter_context(nc.sbuf_tensor("g_wdn", [128, 22, D], BF16))
    w_dn_v = ffn_w_down.rearrange("(c p) n -> p c n", p=128)
    wdn_loaded = False
    with contextlib.ExitStack() as ph:
      if "b" in phases:
        P = Prog(nc, gs, "b")
        h = H(P)

        def sb(name, shape, dt):
            return ph.enter_context(nc.sbuf_tensor("b_" + name, list(shape), dt))

        def psb(name, shape, dt):
            return ph.enter_context(nc.psum_tensor("b_" + name, list(shape), dt))

        T = {}
        cst, idb = load_consts(h, sb)
        T["idb"] = idb
        prm = sb("prm", [128, NPRM], F32)
        T["prm"] = prm
        load_param_cols(h, prm, PR_G, norm_mix, 8)
        wing = sb("wing", [128, 8, 2048], BF16)
        w_in_v = w_in.rearrange("(c p) n -> p c n", p=128)
        for q4 in range(4):
            h.dma("pool", wing[:, :, q4 * 512:(q4 + 1) * 512], w_in_v[:, :, GATE0 + q4 * 512:GATE0 + (q4 + 1) * 512],
                  [], [("wing", q4)])
        woa = sb("woa", [128, 4, D], BF16)
        wob = sb("wob", [128, 4, D], BF16)
        wo = sb("wo", [128, 8, D], BF16)
        h.dma("pool", woa[:], w_out_a.rearrange("(c p) n -> p c n", p=128), [], ["woa"])
        h.dma("pool", wob[:], w_out_b.rearrange("(c p) n -> p c n", p=128), [], ["wob"])
        for kc in range(8):
            h.dma("pool", wo[:, kc, :], w_o.rearrange("(c p) n -> p c n", p=128)[:, kc, :], [], [("wo", kc)])
        for kc in range(22):
            h.dma("pool", wdn[:, kc, :], w_dn_v[:, kc, :], [], [("wdn", kc)])
        wdn_loaded = True
        xts = [sb("xt%d" % i, [128, D], F32) for i in range(4)]
        alloc_rms(T, sb)
        ogbs = [sb("ogb%d" % i, [128, 4, 128], BF16) for i in range(2)]
        obbs = [sb("obb%d" % i, [128, 4, 128], BF16) for i in range(2)]
        sgas = [sb("sga%d" % i, [128, 8, 128], F32) for i in range(2)]
        sgbs = [sb("sgb%d" % i, [128, 8, 128], F32) for i in range(2)]
        tas = [sb("ta%d" % i, [128, 8, 128], F32) for i in range(2)]
        mgs = [sb("mg%d" % i, [128, 8, 128], BF16) for i in range(2)]
        x1ts = [sb("x1t%d" % i, [128, D], F32) for i in range(2)]
        T["psT"] = psb("psT", [128, 8, 128], BF16)
        PS = [psb("ps%d" % i, [128, 512], F32) for i in range(7)]
        if FILLERS:
            _pf, _id = PS[6], idb
            P.filler = ((lambda e: e.matmul(_pf[:, 0:128], lhsT=_id[:], rhs=_id[:], start=True, stop=True)), 70.0)
        for b in BLKS:
            xt, xtn = xts[b % 4], "xt%d" % (b % 4)
            load_x_block(h, b, xt, xtn)
            p2 = b % 2
            ogb, obb, sga, sgb, ta, mg, x1t = ogbs[p2], obbs[p2], sgas[p2], sgbs[p2], tas[p2], mgs[p2], x1ts[p2]
            K_ogb, K_obb, K_sga, K_sgb, K_ta, K_mg, K_x1t = ["%s%d" % (nm, p2) for nm in
                                                            ("ogb", "obb", "sga", "sgb", "ta", "mg", "x1t")]
            h.dma("sp", ogb[:].rearrange("p c l -> p (c l)"), og_s[b], [], [K_ogb])
            h.dma("sp", obb[:].rearrange("p c l -> p (c l)"), ob_s[b], [], [K_obb])
            hT, hTk = rms_to_fm(h, T, xt, xtn, PR_G, b % 2)
            for half, dst, dk in ((0, sga, K_sga), (1, sgb, K_sgb)):
                for gi in range(2):
                    pt, pn = PS[gi], "ps%d" % gi
                    for mi in range(4):
                        c0 = half * 1024 + (4 * gi + mi) * 128
                        for kc in range(8):
                            h.mm(pt[:, mi * 128:(mi + 1) * 128], wing[:, kc, c0:c0 + 128], hT[:, kc, :],
                                 kc == 0, kc == 7, [("wing", c0 // 512), hTk], [pn])
                    h.act(dst[:, 4 * gi:4 * gi + 4, :], pt[:].rearrange("p (c l) -> p c l", c=4), AF.Sigmoid,
                          [pn], [(dk, gi)])
            for gi in range(2):
                pt, pn = PS[2 + gi], "ps%d" % (2 + gi)
                for mi in range(4):
                    m = 4 * gi + mi
                    for kc in range(4):
                        h.mm(pt[:, mi * 128:(mi + 1) * 128], woa[:, kc, m * 128:(m + 1) * 128], ogb[:, kc, :],
                             kc == 0, kc == 3, ["woa", K_ogb], [pn])
                h.tt("dve", ta[:, 4 * gi:4 * gi + 4, :], pt[:].rearrange("p (c l) -> p c l", c=4),
                     sga[:, 4 * gi:4 * gi + 4, :], ALU.mult, [pn, (K_sga, gi)], [(K_ta, gi)])
            for gi in range(2):
                pt, pn = PS[4 + gi], "ps%d" % (4 + gi)
                for mi in range(4):
                    m = 4 * gi + mi
                    for kc in range(4):
                        h.mm(pt[:, mi * 128:(mi + 1) * 128], wob[:, kc, m * 128:(m + 1) * 128], obb[:, kc, :],
                             kc == 0, kc == 3, ["wob", K_obb], [pn])
                h.tt("dve", sgb[:, 4 * gi:4 * gi + 4, :], pt[:].rearrange("p (c l) -> p c l", c=4),
                     sgb[:, 4 * gi:4 * gi + 4, :], ALU.mult, [pn, (K_sgb, gi)], [(K_sgb, gi)])
                h.tt("pool", mg[:, 4 * gi:4 * gi + 4, :], ta[:, 4 * gi:4 * gi + 4, :],
                     sgb[:, 4 * gi:4 * gi + 4, :], ALU.add, [(K_ta, gi), (K_sgb, gi)], [(K_mg, gi)])
            for nh in range(2):
                pt, pn = PS[2 + nh], "ps%d" % (2 + nh)
                for kc in range(8):
                    h.mm(pt[:], mg[:, kc, :], wo[:, kc, nh * 512:(nh + 1) * 512], kc == 0, kc == 7,
                         [K_mg, ("wo", kc)], [pn])
                h.tt("dve", x1t[:, nh * 512:(nh + 1) * 512], pt[:], xt[:, nh * 512:(nh + 1) * 512], ALU.add,
                     [pn, xtn], [(K_x1t, nh)])
            h.dma("sp", x1_s[b * 128:(b + 1) * 128, :], x1t[:], [K_x1t], [])
        P.emit()

    with contextlib.ExitStack() as ph:
      if "c" in phases:
        P = Prog(nc, gs, "c")
        h = H(P)

        def sb(name, shape, dt):
            return ph.enter_context(nc.sbuf_tensor("c_" + name, list(shape), dt))

        def psb(name, shape, dt):
            return ph.enter_context(nc.psum_tensor("c_" + name, list(shape), dt))

        T = {}
        cst, idb = load_consts(h, sb)
        T["idb"] = idb
        idf = cst[:, C_ID:C_ID + 128]
        prm = sb("prm", [128, 8], F32)
        T["prm"] = prm
        load_param_cols(h, prm, 0, norm_ffn, 8)
        cw = sb("cw", [128, 4, 44], F32)
        nfb = sb("nfb", [128, D], F32)
        h.dma("sp", nfb[:], norm_final.partition_broadcast(128), [], ["nfb"])
        wup = sb("wup", [128, 8, F2], BF16)
        w_up_v = ffn_w_up.rearrange("(c p) n -> p c n", p=128)
        WQ = [(0, 4), (4, 10), (10, 16), (16, 22)]
        for qi, (c0_, c1_) in enumerate(WQ):
            for half in range(2):
                lo, hi = (half * 22 + c0_) * 128, (half * 22 + c1_) * 128
                h.dma("pool", wup[:, :, lo:hi], w_up_v[:, :, lo:hi], [], [("wup", qi)])
        if not wdn_loaded:
            for kc in range(22):
                h.dma("pool", wdn[:, kc, :], w_dn_v[:, kc, :], [], [("wdn", kc)])
        xts = [sb("xt0", [128, D], F32), sb("xt1", [128, D], F32)]
        alloc_rms(T, sb)
        ub = [sb("ub0", [128, 4, 136], F32), sb("ub1", [128, 4, 136], F32)]
        ucar = sb("ucar", [128, 44, 2], F32)
        cin = sb("cin", [128, 44, 4, 2], F32)
        cout = sb("cout", [128, 44, 8], F32)
        cstg = [sb("cstg%d" % i, [8, 512], F32) for i in range(2)]
        csto = [sb("csto%d" % i, [8, 512], F32) for i in range(2)]
        fss = [sb("fss%d" % i, [128, 1], F32) for i in range(2)]
        frs = [sb("frs%d" % i, [128, 1], F32) for i in range(2)]
        cc = [sb("cc0", [128, 4, 128], F32), sb("cc1", [128, 4, 128], F32)]
        g1 = [sb("g10", [128, 2, 128], F32), sb("g11", [128, 2, 128], F32)]
        g2t = [sb("g20", [128, 2, 128], F32), sb("g21", [128, 2, 128], F32)]
        actTs = [sb("actT%d" % i, [128, 22, 128], BF16) for i in range(2)]
        x2s = [sb("x2%d" % i, [128, D], F32) for i in range(2)]
        yts = [sb("yt0", [128, D], F32)] * 2
        T["psT"] = psb("psT", [128, 8, 128], BF16)
        PS = [psb("ps%d" % i, [128, 512], F32) for i in range(7)]
        h.memset("pool", ucar[:], 0.0, ["ucar"])
        cb_v = ffn_conv_b.rearrange("(o f) -> o f", o=1)
        for g4 in range(11):
            cg_, cgk = cstg[g4 % 2], "cstg%d" % (g4 % 2)
            h.dma("sp", cg_[0:3, :], ffn_conv_w[:, g4 * 512:(g4 + 1) * 512], [], [(cgk, 0)])
            h.dma("sp", cg_[3:4, :], cb_v[:, g4 * 512:(g4 + 1) * 512], [], [(cgk, 1)])
            for mi in range(4):
                h.tr(PS[4][:, mi * 4:(mi + 1) * 4], cg_[0:4, mi * 128:(mi + 1) * 128], idf[0:4, 0:4],
                     [cgk, "cst"], ["ps4"])
            h.cp("act", cw[:, :, 4 * g4:4 * g4 + 4], PS[4][:, 0:16].rearrange("p (m j) -> p j m", j=4),
                 ["ps4"], [("cw", g4)])
        if FILLERS:
            _pf2, _id2 = PS[6], idb
            P.filler = ((lambda e: e.matmul(_pf2[:, 0:128], lhsT=_id2[:], rhs=_id2[:], start=True, stop=True)), 70.0)
        for b in BLKS:
            samp, nseq, L = geom(b)
            j = b - NPB
            xt, xtn = xts[b % 2], "xt%d" % (b % 2)
            actT, actk = actTs[b % 2], "actT%d" % (b % 2)
            x2, x2k = x2s[b % 2], "x2%d" % (b % 2)
            yt, ytk = yts[0], "yt0"
            h.dma("sp", xt[:], x1_s[b * 128:(b + 1) * 128, :], [], [xtn])
            hT, hTk = rms_to_fm(h, T, xt, xtn, 0, b % 2)
            W = nseq * (L + 2)
            if samp:
                stc_v = st_conv[4 * j:4 * j + 4].rearrange("q t f -> (q t) f")
                for g4 in range(11):
                    cg_, cgk = cstg[g4 % 2], "cstg%d" % (g4 % 2)
                    h.dma("sp", cg_[:], stc_v[:, g4 * 512:(g4 + 1) * 512], [], [cgk])
                    for mi in range(4):
                        m = 4 * g4 + mi
                        h.tr(PS[4][:, mi * 8:(mi + 1) * 8], cg_[0:8, mi * 128:(mi + 1) * 128], idf[0:8, 0:8],
                             [cgk, "cst"], ["ps4"])
                    h.cp("act", cin[:, 4 * g4:4 * g4 + 4, :, :].rearrange("p m q t -> p m (q t)"),
                         PS[4][:, 0:32].rearrange("p (m x) -> p m x", m=4), ["ps4"], [("cin", g4)])
            last_prompt = (b == NPB - 1)
            for gi in range(11):
                u, un = ub[gi % 2], "ub%d" % (gi % 2)
                uv = u[:, :, 0:W].rearrange("p m (s l) -> p m s l", s=nseq)
                pt, pn = PS[gi % 2], "ps%d" % (gi % 2)
                chunks = [2 * gi, 2 * gi + 1, 22 + 2 * gi, 23 + 2 * gi]
                for mi, m in enumerate(chunks):
                    for kc in range(8):
                        h.mm(pt[:, mi * 128:(mi + 1) * 128], wup[:, kc, m * 128:(m + 1) * 128], hT[:, kc, :],
                             kc == 0, kc == 7, [("wup", [q_ for q_, (a_, b_) in enumerate(WQ) if a_ <= (m % 22) < b_][0]), hTk], [pn])
                h.cp("act", uv[:, :, :, 2:L + 2], pt[:].rearrange("p (m s l) -> p m s l", m=4, s=nseq),
                     [pn], [un])
                for half in range(2):
                    m0 = chunks[2 * half]
                    if samp:
                        h.cp("pool", uv[:, 2 * half:2 * half + 2, :, 0:2], cin[:, m0:m0 + 2, :, :],
                             [("cin", m0 // 4)], [un])
                    else:
                        h.cp("pool", uv[:, 2 * half:2 * half + 2, 0, 0:2], ucar[:, m0:m0 + 2, :],
                             [("ucar", gi)], [un])
                for half in range(2):
                    m0 = chunks[2 * half]
                    if samp:
                        h.cp("pool", cout[:, m0:m0 + 2, :].rearrange("p m (q t) -> p m q t", q=4),
                             uv[:, 2 * half:2 * half + 2, :, 8:10], [un], [("cout", gi)])
                    else:
                        h.cp("pool", ucar[:, m0:m0 + 2, :], uv[:, 2 * half:2 * half + 2, 0, 128:130],
                             [un], [("ucar", gi)])
                        if last_prompt:
                            h.cp("pool", cout[:, m0:m0 + 2, 0:2], uv[:, 2 * half:2 * half + 2, 0, 128:130],
                                 [un], [("cout", gi)])
                c_, cn = cc[gi % 2], "cc%d" % (gi % 2)
                for mi, m in enumerate(chunks):
                    eng = "dve"
                    c4 = c_[:, mi, :].rearrange("p (s l) -> p s l", s=nseq)
                    h.act(c4, uv[:, mi, :, 2:L + 2], AF.Identity, [un, "cw"], [(cn, mi)],
                          scale=cw[:, 2, m:m + 1], bias=cw[:, 3, m:m + 1])
                    h.stt(eng, c4, uv[:, mi, :, 1:L + 1], cw[:, 1, m:m + 1], c4, ALU.mult, ALU.add,
                          [un, "cw", (cn, mi)], [(cn, mi)])
                    h.stt(eng, c4, uv[:, mi, :, 0:L], cw[:, 0, m:m + 1], c4, ALU.mult, ALU.add,
                          [un, "cw", (cn, mi)], [(cn, mi)])
                ga, gan = g1[gi % 2], "g1%d" % (gi % 2)
                gb_, gbn = g2t[gi % 2], "g2%d" % (gi % 2)
                gate = c_[:, 2:4, :]
                val = c_[:, 0:2, :]
                h.act(ga[:], gate, AF.Square, [(cn, 2), (cn, 3)], [gan])
                h.ts("dve", ga[:], ga[:], 0.044715, 1.0, ALU.mult, ALU.add, [gan], [gan])
                h.tt("pool", ga[:], ga[:], gate, ALU.mult, [gan, (cn, 2), (cn, 3)], [gan])
                h.act(gb_[:], ga[:], AF.Sigmoid, [gan], [gbn], scale=GELU_S)
                h.tt("pool", gb_[:], gb_[:], gate, ALU.mult, [gbn, (cn, 2), (cn, 3)], [gbn])
                h.tt("dve", actT[:, 2 * gi:2 * gi + 2, :], gb_[:], val, ALU.mult, [gbn, (cn, 0), (cn, 1)],
                     [(actk, gi)])
            if samp or last_prompt:
                ncol = 8 if samp else 2
                for g4 in range(11):
                    for mi in range(4):
                        m = 4 * g4 + mi
                        h.tr(PS[4][0:ncol, mi * 128:(mi + 1) * 128], cout[:, m, 0:ncol], idf, ["cout", "cst"],
                             ["ps4"])
                    co_, cok = csto[g4 % 2], "csto%d" % (g4 % 2)
                    h.cp("act", co_[0:ncol, :], PS[4][0:ncol, :], ["ps4"], [cok])
                    if samp:
                        h.dma("sp", o_conv_s[4 * j:4 * j + 4].rearrange("q t f -> (q t) f")[:, g4 * 512:(g4 + 1) * 512],
                              co_[:], [cok], [], isout=True)
                    else:
                        h.dma("sp", o_conv_p[0][:, g4 * 512:(g4 + 1) * 512], co_[0:2, :], [cok], [], isout=True)
            for nh in range(2):
                pt, pn = PS[2 + nh], "ps%d" % (2 + nh)
                for kc in range(22):
                    h.mm(pt[:], actT[:, kc, :], wdn[:, kc, nh * 512:(nh + 1) * 512], kc == 0, kc == 21,
                         [(actk, kc // 2), ("wdn", kc)], [pn])
                h.tt("dve", x2[:, nh * 512:(nh + 1) * 512], pt[:], xt[:, nh * 512:(nh + 1) * 512], ALU.add,
                     [pn, xtn], [(x2k, nh)])
            ss2, rs2 = fss[b % 2], frs[b % 2]
            ssk, rsk = "fss%d" % (b % 2), "frs%d" % (b % 2)
            h.act(yt[:], x2[:], AF.Square, [x2k], [ytk, ssk], accum_out=ss2[:])
            h.ts("dve", rs2[:], ss2[:], 1.0 / D, 1e-6, ALU.mult, ALU.add, [ssk], [rsk])
            h.act(rs2[:], rs2[:], AF.Sqrt, [rsk], [rsk])
            h.recip(rs2[:], rs2[:], [rsk], [rsk])
            h.stt("dve", yt[:], x2[:], rs2[:, 0:1], nfb[:], ALU.mult, ALU.mult, [x2k, rsk, "nfb"], [ytk])
            if samp:
                for q in range(4):
                    h.dma("sp", ysm[4 * j + q], yt[32 * q:32 * q + 8, :], [ytk], [], isout=True)
            else:
                h.dma("sp", yp[b * 128:(b + 1) * 128, :], yt[:], [ytk], [], isout=True)
        P.emit()
    gx.close()
    gs.close()
    return nc


_CACHE = {}


def kernel(**inputs):
    f32 = lambda a: np.ascontiguousarray(np.asarray(a), dtype=np.float32)
    if "nc" not in _CACHE:
        nc = bass.Bass("TRN2", target_bir_lowering=False)
        build(nc)
        _CACHE["nc"] = nc
    nc = _CACHE["nc"]
    cst = make_consts()
    shared = {
        "cst": cst,
        "norm_mix": f32(inputs["norm_mix"][0]), "w_in": f32(inputs["w_in"][0]),
        "mu_shift": f32(inputs["mu_shift"][0]), "rwkv_w0": f32(inputs["rwkv_w0"][0]),
        "rwkv_w2": f32(inputs["rwkv_w2"][0]), "rwkv_a0": f32(inputs["rwkv_a0"][0]),
        "rwkv_a2": f32(inputs["rwkv_a2"][0]), "rwkv_g2": f32(inputs["rwkv_g2"][0]),
        "rwkv_k_k": f32(inputs["rwkv_k_k"][0]), "rwkv_k_a": f32(inputs["rwkv_k_a"][0]),
        "rwkv_r_k": f32(inputs["rwkv_r_k"][0]).reshape(512), "rwkv_ln_w": f32(inputs["rwkv_ln_w"][0]),
        "rwkv_ln_b": f32(inputs["rwkv_ln_b"][0]), "gla_wg2": f32(inputs["gla_wg2"][0]),
        "gla_bg": f32(inputs["gla_bg"][0]), "gla_norm_w": f32(inputs["gla_norm_w"][0]),
        "w_out_a": f32(inputs["w_out_a"][0]), "w_out_b": f32(inputs["w_out_b"][0]),
        "w_o": f32(inputs["w_o"][0]), "norm_ffn": f32(inputs["norm_ffn"][0]),
        "ffn_w_up": f32(inputs["ffn_w_up"][0]), "ffn_conv_w": f32(inputs["ffn_conv_w"][0]),
        "ffn_conv_b": f32(inputs["ffn_conv_b"][0]), "ffn_w_down": f32(inputs["ffn_w_down"][0]),
        "norm_final": f32(inputs["norm_final"]),
    }
    in_maps = []
    for c in range(NCORES):
        m = dict(shared)
        sl = slice(16 * c, 16 * c + 16)
        m["xp"] = f32(inputs["x_prompt"][c])
        m["xs"] = f32(inputs["x_sample"][sl])
        m["st_shift"] = f32(inputs["state_rwkv_shift"][0, sl])
        m["st_wkv"] = f32(inputs["state_rwkv_wkv"][0, sl])
        m["st_gla"] = f32(inputs["state_gla"][0, sl])
        m["st_conv"] = f32(inputs["state_ffn_conv"][0, sl])
        in_maps.append(m)
    res = run_bass_kernel_spmd(nc, in_maps, core_ids=list(range(NCORES)))
    R = res.results
    cat = lambda k: np.concatenate([np.asarray(r[k]) for r in R], axis=0)
    y_p = np.stack([np.asarray(r["yp"]) for r in R], axis=0)
    y_s = cat("ys")
    outs = (
        y_p, y_s,
        cat("o_shift_p")[None], cat("o_wkv_p")[None], cat("o_gla_p")[None], cat("o_conv_p")[None],
        cat("o_shift_s")[None], cat("o_wkv_s")[None], cat("o_gla_s")[None], cat("o_conv_s")[None],
    )
    return tuple(np.ascontiguousarray(o, dtype=np.float32) for o in outs)
```

```python
import contextlib
import numpy as np
import concourse.bass as bass
import concourse.mybir as mybir
from concourse.bass_utils import run_bass_kernel_spmd

F32 = mybir.dt.float32
BF16 = mybir.dt.bfloat16
AF = mybir.ActivationFunctionType
ALU = mybir.AluOpType
AX = mybir.AxisListType

NCORES = 8
D = 1024
NPB = 16
NSB = 4
NBLK = NPB + NSB
SHIFT = 1792
INC = 5392
FH = 2816
F2 = 5632
GLA0 = 1792
GATE0 = 3344
C0 = -0.6065306597126334
GELU_S = 1.5957691216057308

ENGS = ("pe", "act", "dve", "pool", "sp")
MAXOPS = None
SCHED = True
VERBOSE = False
PROGS = []
WINDOW = 300
SCHED_LAT = 300.0
ODEP_LAT = 0.0
SCHED_BIAS = 1.0
PE_SCALE = 1.0
DVE_SCALE = 1.0
ENG_SCALE = {}
FILL_MIN = 250.0
FILL_MARGIN = 80.0
FILL_MAX = 24
FILLERS = False
LINES = []
TAGS = []


class Op:
    __slots__ = ("eng", "fn", "reads", "writes", "dma", "deps", "sig", "idx",
                 "dsem", "dval", "dprev", "isout", "mm", "odeps", "cost", "start", "fin")

    def __init__(self, eng, fn, reads, writes, dma, isout, mm, cost=300.0):
        self.odeps = []
        self.cost = cost
        self.eng = eng
        self.fn = fn
        self.reads = reads
        self.writes = writes
        self.dma = dma
        self.deps = []
        self.sig = None
        self.dsem = None
        self.dval = None
        self.dprev = None
        self.isout = isout
        self.mm = mm


def _norm(k):
    return k if isinstance(k, tuple) else (k, None)


class Prog:
    def __init__(self, nc, semstack, tag, n_dma_sems=6):
        self.nc = nc
        self.ops = []
        self.n_dma_sems = n_dma_sems
        self.st = {}
        self.semstack = semstack
        self.tag = tag
        self.filler = None

    @staticmethod
    def _conf(a, b):
        return a is None or b is None or a == b

    def add(self, eng, fn, reads=(), writes=(), dma=False, isout=False, mm=False, cost=300.0):
        op = Op(eng, fn, [_norm(k) for k in reads], [_norm(k) for k in writes], dma, isout, mm, cost)
        odeps = {}
        op.idx = len(self.ops)
        if MAXOPS is not None:
            import sys as _s
            f = _s._getframe(1)
            while f is not None and f.f_code.co_name != "build":
                f = f.f_back
            LINES.append(f.f_lineno if f is not None else -1)
            TAGS.append(f.f_locals.get("b", -1) if f is not None else -1)
        deps = {}
        for (name, sub) in op.reads:
            s = self.st.setdefault(name, {"w": {}, "r": {}})
            for ws, wop in s["w"].items():
                if self._conf(ws, sub):
                    deps[wop.idx] = wop
        for (name, sub) in op.writes:
            s = self.st.setdefault(name, {"w": {}, "r": {}})
            for ws, wop in s["w"].items():
                if self._conf(ws, sub):
                    if not (op.mm and wop.mm):
                        deps[wop.idx] = wop
                    else:
                        odeps[wop.idx] = wop
            for rs, rops in s["r"].items():
                if self._conf(rs, sub):
                    for rop in rops:
                        deps[rop.idx] = rop
        for (name, sub) in op.reads:
            self.st[name]["r"].setdefault(sub, []).append(op)
        for (name, sub) in op.writes:
            s = self.st[name]
            if sub is None:
                s["w"] = {None: op}
                s["r"] = {}
            else:
                s["w"][sub] = op
                s["r"][sub] = []
        deps.pop(op.idx, None)
        op.deps = list(deps.values())
        op.odeps = [o for k, o in odeps.items() if k not in deps]
        self.ops.append(op)
        return op

    def schedule(self, window=None):
        window = window or WINDOW
        ops = self.ops
        n = len(ops)
        for op in ops:
            if not op.dma:
                op.cost = op.cost * ENG_SCALE.get(op.eng, 1.0)
        ndep = [0] * n
        users = [[] for _ in range(n)]
        truedep = set()
        for op in ops:
            for d in op.deps:
                truedep.add((op.idx, d.idx))
            ds = {d.idx for d in op.deps} | {d.idx for d in op.odeps}
            ndep[op.idx] = len(ds)
            for d in ds:
                users[d].append(op.idx)
        ready_t = [0.0] * n
        per = {e: [op.idx for op in ops if op.eng == e] for e in ENGS}
        head = {e: 0 for e in ENGS}
        done = [False] * n
        t_e = {e: 0.0 for e in ENGS}
        order = []
        remaining = n
        LAT = SCHED_LAT
        while remaining:
            best = None
            for e in ENGS:
                lst = per[e]
                hp = head[e]
                while hp < len(lst) and done[lst[hp]]:
                    hp += 1
                head[e] = hp
                if hp >= len(lst):
                    continue
                cnt = 0
                k = hp
                cand = None
                rdy = []
                while k < len(lst) and cnt < window:
                    i = lst[k]
                    if not done[i]:
                        cnt += 1
                        if ndep[i] == 0:
                            st = max(t_e[e], ready_t[i])
                            key = (st + SCHED_BIAS * (cnt - 1), i)
                            rdy.append((st, i))
                            if cand is None or key < cand[0]:
                                cand = (key, i, st)
                    k += 1
                if cand is not None:
                    bst = cand[2]
                    for (st, i) in rdy:
                        if i != cand[1] and st + (60.0 if ops[i].dma else ops[i].cost) <= bst:
                            cand = ((st, i), i, st)
                            break
                    if best is None or cand[0] < best[0]:
                        best = (cand[0], cand[1], cand[2], e)
            assert best is not None, "scheduler deadlock"
            _, i, st, e = best
            op = ops[i]
            op.start = st
            if op.dma:
                t_e[e] = st + 60.0
            else:
                t_e[e] = st + op.cost
            op.fin = st + op.cost
            done[i] = True
            remaining -= 1
            order.append(op)
            for u in users[i]:
                ndep[u] -= 1
                lat_ = LAT if ((u, i) in truedep) else ODEP_LAT
                if ready_t[u] < op.fin + lat_:
                    ready_t[u] = op.fin + lat_
        if self.filler is not None:
            fn, fcost = self.filler
            out = []
            pe_end = None
            nf = 0
            for op in order:
                if op.eng == "pe":
                    if pe_end is not None:
                        gap = op.start - pe_end
                        if gap > FILL_MIN:
                            k = min(FILL_MAX, int((gap - FILL_MARGIN) / fcost))
                            for _ in range(max(0, k)):
                                f = Op("pe", fn, [], [], False, False, True, fcost)
                                f.idx = -1
                                f.start = pe_end
                                f.fin = pe_end + fcost
                                out.append(f)
                                nf += 1
                    pe_end = op.start + op.cost
                out.append(op)
            order = out
            if VERBOSE:
                print("[sched %s] fillers inserted: %d" % (self.tag, nf), flush=True)
        self.ops = order
        self.est = max(op.fin for op in order) if order else 0.0
        if VERBOSE:
            PROGS.append(self)
            busy = {e: sum(o.cost for o in order if o.eng == e and not o.dma) for e in ENGS}
            print("[sched %s] n=%d est=%.1f us busy(us): %s" % (
                self.tag, n, self.est / 1e3, " ".join("%s=%.0f" % (e, busy[e] / 1e3) for e in ENGS)), flush=True)

    def emit(self):
        nc = self.nc
        if MAXOPS is not None:
            self.ops = self.ops[:MAXOPS]
        if SCHED:
            self.schedule()
        ops = self.ops
        needed = set()
        for op in ops:
            for d in op.deps:
                needed.add(d.idx)
        cnt = {e: 0 for e in ENGS}
        for op in ops:
            if not op.dma and op.idx in needed:
                cnt[op.eng] += 1
                op.sig = cnt[op.eng]
        dcount = {e: 0 for e in ENGS}
        last_on_slot = {}
        for op in ops:
            if not op.dma:
                continue
            j = dcount[op.eng]
            dcount[op.eng] += 1
            slot = j % self.n_dma_sems
            op.dsem = (op.eng, slot)
            op.dval = 16 * (j // self.n_dma_sems + 1)
            op.dprev = last_on_slot.get(op.dsem)
            last_on_slot[op.dsem] = op
        out_ops = [op for op in ops if op.dma and op.isout]
        per_eng = {e: [op for op in ops if op.eng == e] for e in ENGS}
        es = self.semstack
        csem = {e: es.enter_context(nc.semaphore("cs%s_%s" % (self.tag, e)))
                for e in ENGS if e != "sp"}
        dsem = {}
        for e in ENGS:
            for s in range(min(self.n_dma_sems, dcount[e])):
                dsem[(e, s)] = es.enter_context(nc.semaphore("ds%s_%s%d" % (self.tag, e, s)))

        def run_engine(e, eng):
            known = {}

            def wait(key, sem, val):
                if known.get(key, 0) >= val:
                    return
                known[key] = val
                eng.wait_ge(sem, val)

            for op in per_eng[e]:
                for d in op.deps:
                    if d.dma:
                        wait(d.dsem, dsem[d.dsem], d.dval)
                    else:
                        wait(d.eng, csem[d.eng], d.sig)
                if op.dma and op.dprev is not None:
                    wait(op.dsem, dsem[op.dsem], op.dprev.dval)
                ins = op.fn(eng)
                if op.dma:
                    ins.then_inc(dsem[op.dsem], 16)
                elif op.sig is not None:
                    ins.then_inc(csem[e], 1)
            if e == "sp":
                for op in out_ops:
                    wait(op.dsem, dsem[op.dsem], op.dval)
                for key, op in last_on_slot.items():
                    wait(op.dsem, dsem[op.dsem], op.dval)

        with nc.Block() as block:
            @block.sync
            def _(eng):
                run_engine("sp", eng)

            @block.tensor
            def _(eng):
                run_engine("pe", eng)

            @block.scalar
            def _(eng):
                run_engine("act", eng)

            @block.vector
            def _(eng):
                run_engine("dve", eng)

            @block.gpsimd
            def _(eng):
                run_engine("pool", eng)


def _fsz(ap):
    n = 1
    for d in ap.shape[1:]:
        n *= int(d)
    return n


class H:
    def __init__(self, P):
        self.P = P

    def act(self, out, in_, func, r, w, **kw):
        self.P.add("act", lambda e: e.activation(out=out, in_=in_, func=func, **kw), r, w,
                   cost=220.0 + 1.05 * _fsz(out))

    def tt(self, eng, out, in0, in1, op, r, w):
        self.P.add(eng, lambda e: e.tensor_tensor(out=out, in0=in0, in1=in1, op=op), r, w,
                   cost=(100.0 + 1.05 * _fsz(out)) if eng == "dve" else (160.0 + 2.1 * _fsz(out)))

    def ts(self, eng, out, in0, s1, s2, op0, op1, r, w):
        if op1 is None:
            self.P.add(eng, lambda e: e.tensor_scalar(out=out, in0=in0, scalar1=s1, scalar2=None, op0=op0), r, w,
                       cost=100.0 + 1.05 * _fsz(out))
        else:
            self.P.add(eng, lambda e: e.tensor_scalar(out=out, in0=in0, scalar1=s1, scalar2=s2, op0=op0, op1=op1), r, w,
                       cost=100.0 + 1.05 * _fsz(out))

    def stt(self, eng, out, in0, scalar, in1, op0, op1, r, w):
        self.P.add(eng, lambda e: e.scalar_tensor_tensor(out=out, in0=in0, scalar=scalar, in1=in1, op0=op0, op1=op1), r, w,
                   cost=100.0 + 1.05 * _fsz(out))

    def cp(self, eng, out, in_, r, w):
        if eng == "act":
            self.P.add("act", lambda e: e.activation(out=out, in_=in_, func=AF.Copy), r, w,
                       cost=220.0 + 1.05 * _fsz(out))
        else:
            self.P.add(eng, lambda e: e.tensor_copy(out=out, in_=in_), r, w,
                       cost=(100.0 + 1.05 * _fsz(out)) if eng == "dve" else (160.0 + 2.1 * _fsz(out)))

    def memset(self, eng, ap, val, w):
        self.P.add(eng, lambda e: e.memset(ap, val), [], w, cost=160.0 + 1.0 * _fsz(ap))

    def recip(self, out, in_, r, w):
        self.P.add("dve", lambda e: e.reciprocal(out=out, in_=in_), r, w, cost=100.0 + 1.05 * _fsz(out))

    def scan(self, out, d0, d1, r, w):
        self.P.add("dve", lambda e: e.tensor_tensor_scan(out=out, data0=d0, data1=d1, initial=0.0,
                                                         op0=ALU.mult, op1=ALU.add), r, w,
                   cost=100.0 + 2.1 * _fsz(out))

    def rsum(self, out, in_, r, w):
        self.P.add("dve", lambda e: e.tensor_reduce(out=out, in_=in_, axis=AX.X, op=ALU.add), r, w,
                   cost=100.0 + 1.05 * _fsz(in_))

    def mm(self, out, lhsT, rhs, start, stop, r, w, tp=None):
        c = (max(64.0, float(_fsz(rhs))) / 2.0 + 16.0) * PE_SCALE
        if lhsT.dtype == F32:
            c *= 4.0
        if tp is None:
            self.P.add("pe", lambda e: e.matmul(out, lhsT=lhsT, rhs=rhs, start=start, stop=stop), r, w, mm=True,
                       cost=c)
        else:
            self.P.add("pe", lambda e: e.matmul(out, lhsT=lhsT, rhs=rhs, start=start, stop=stop,
                                                tile_position=tp), r, w, mm=True, cost=c)

    def tr(self, out, in_, ident, r, w):
        self.P.add("pe", lambda e: e.transpose(out=out, in_=in_, identity=ident), r, w, mm=True, cost=110.0)

    def dma(self, q, out, in_, r, w, isout=False, slow=False):
        nbytes = 1
        for d in out.shape:
            nbytes *= int(d)
        c = 2500.0 + 4.0 * nbytes / 150.0
        if slow:
            self.P.add(q, lambda e: e.dma_start(out=out, in_=in_, allow_slow_non_contiguous=True), r, w,
                       dma=True, isout=isout, cost=c)
        else:
            self.P.add(q, lambda e: e.dma_start(out=out, in_=in_), r, w, dma=True, isout=isout, cost=c)


def bc(ap, shape):
    a = ap
    while len(a.shape) < len(shape):
        a = a.unsqueeze(len(a.shape))
    return a.to_broadcast(list(shape))


C_ID = 0
C_BO = 128
C_RST = 256
C_VS = 384
C_ONE = 512
C_M4 = 640
C_SL = 1152
C_IU = 1280
NCST = 1408


def make_consts():
    c = np.zeros((128, NCST), np.float32)
    i = np.arange(128)
    same = (i[:, None] // 32) == (i[None, :] // 32)
    su = (same & (i[:, None] < i[None, :])).astype(np.float32)
    iu = (same & (i[:, None] <= i[None, :])).astype(np.float32)
    sl = (same & (i[:, None] > i[None, :])).astype(np.float32)
    c[:, C_ID:C_ID + 128] = np.eye(128, dtype=np.float32)
    c[:, C_BO:C_BO + 128] = ((i[:, None] // 64) == (i[None, :] // 64)).astype(np.float32)
    c[:, C_RST:C_RST + 128] = (i[None, :] % 32 != 0).astype(np.float32)
    c[:, C_VS:C_VS + 128] = (i[None, :] % 32 < 8).astype(np.float32)
    c[:, C_ONE:C_ONE + 128] = 1.0
    c[:, C_M4:C_M4 + 512] = np.concatenate([su, iu, su, iu], axis=1)
    c[:, C_SL:C_SL + 128] = sl
    c[:, C_IU:C_IU + 128] = iu
    return c


PR_G = 0
PR_MU = 8
PR_W0 = 22
PR_A0 = 26
PR_KK = 30
PR_KA = 34
PR_RK = 38
PR_LW = 42
PR_LB = 46
PR_BG = 50
PR_NW = 52
PR_OMKA = 53
NPRM = 64


def build(nc, dbg=None, phases="abc", blocks=None):
    BLKS = list(range(NBLK)) if blocks is None else list(blocks)
    BLKS2 = [b for b in BLKS if b <= NPB]
    gs = contextlib.ExitStack()

    def din(name, shape, dt=F32):
        return nc.dram_tensor(name, list(shape), dt, kind="ExternalInput").ap()

    def dout(name, shape):
        return nc.dram_tensor(name, list(shape), F32, kind="ExternalOutput").ap()

    def dscr(name, shape, dt):
        return nc.dram_tensor(name, list(shape), dt, kind="Internal").ap()

    xp = din("xp", [2048, D])
    xsm = din("xs", [16, 8, D])
    st_shift = din("st_shift", [16, SHIFT])
    st_wkv = din("st_wkv", [16, 8, 64, 64])
    st_gla = din("st_gla", [16, 4, 64, 128])
    st_conv = din("st_conv", [16, 2, F2])
    cst_d = din("cst", [128, NCST])
    norm_mix = din("norm_mix", [D])
    w_in = din("w_in", [D, INC])
    mu_shift = din("mu_shift", [SHIFT])
    rwkv_w0 = din("rwkv_w0", [512])
    rwkv_w2 = din("rwkv_w2", [64, 512])
    rwkv_a0 = din("rwkv_a0", [512])
    rwkv_a2 = din("rwkv_a2", [64, 512])
    rwkv_g2 = din("rwkv_g2", [128, 512])
    rwkv_k_k = din("rwkv_k_k", [512])
    rwkv_k_a = din("rwkv_k_a", [512])
    rwkv_r_k = din("rwkv_r_k", [512])
    rwkv_ln_w = din("rwkv_ln_w", [512])
    rwkv_ln_b = din("rwkv_ln_b", [512])
    gla_wg2 = din("gla_wg2", [16, 256])
    gla_bg = din("gla_bg", [256])
    gla_norm_w = din("gla_norm_w", [128])
    w_out_a = din("w_out_a", [512, D])
    w_out_b = din("w_out_b", [512, D])
    w_o = din("w_o", [D, D])
    norm_ffn = din("norm_ffn", [D])
    ffn_w_up = din("ffn_w_up", [D, F2])
    ffn_conv_w = din("ffn_conv_w", [3, F2])
    ffn_conv_b = din("ffn_conv_b", [F2])
    ffn_w_down = din("ffn_w_down", [FH, D])
    norm_final = din("norm_final", [D])

    yp = dout("yp", [2048, D])
    ysm = dout("ys", [16, 8, D])
    o_shift_p = dout("o_shift_p", [1, SHIFT])
    o_wkv_p = dout("o_wkv_p", [1, 8, 64, 64])
    o_gla_p = dout("o_gla_p", [1, 4, 64, 128])
    o_conv_p = dout("o_conv_p", [1, 2, F2])
    o_shift_s = dout("o_shift_s", [16, SHIFT])
    o_wkv_s = dout("o_wkv_s", [16, 8, 64, 64])
    o_gla_s = dout("o_gla_s", [16, 4, 64, 128])
    o_conv_s = dout("o_conv_s", [16, 2, F2])

    og_s = dscr("og_s", [NBLK, 128, 512], BF16)
    ob_s = dscr("ob_s", [NBLK, 128, 512], BF16)
    x1_s = dscr("x1_s", [NBLK * 128, D], F32)

    dbg_out = None
    if dbg is not None:
        dbg_out = dout("dbg", dbg["shape"])

    def geom(b):
        return (b >= NPB, 4, 32) if b >= NPB else (False, 1, 128)

    def load_consts(h, sb, q="sp"):
        cst = sb("cst", [128, NCST], F32)
        h.dma(q, cst[:], cst_d, [], ["cst"])
        idb = sb("idb", [128, 128], BF16)
        h.cp("dve", idb[:], cst[:, C_ID:C_ID + 128], ["cst"], ["idb"])
        return cst, idb

    def load_x_block(h, b, xt, xtn, packed=False):
        samp, nseq, L = geom(b)
        if packed and samp:
            h.dma("sp", xt[:], xsm.rearrange("s t d -> (s t) d"), [], [xtn])
        elif not samp:
            h.dma("sp", xt[:], xp[b * 128:(b + 1) * 128, :], [], [xtn])
        else:
            j = b - NPB
            h.memset("pool", xt[:], 0.0, [xtn])
            for q in range(4):
                h.dma("sp", xt[32 * q:32 * q + 8, :], xsm[4 * j + q], [], [(xtn, q)])

    def rms_to_fm(h, T, xt, xtn, gcol, par=0):
        xp_ = par if len(T["xn"]) > 1 else 0
        xn, ss, rstd, hT = T["xn"][xp_], T["ss"][par], T["rstd"][par], T["hT"][par]
        xnk, ssk, rsk, hTk = "xn%d" % xp_, "ss%d" % par, "rstd%d" % par, "hT%d" % par
        psT, idb, prm = T["psT"], T["idb"], T["prm"]
        h.act(xn[:], xt[:], AF.Square, [xtn], [xnk, ssk], accum_out=ss[:])
        h.ts("dve", rstd[:], ss[:], 1.0 / D, 1e-6, ALU.mult, ALU.add, [ssk], [rsk])
        h.act(rstd[:], rstd[:], AF.Sqrt, [rsk], [rsk])
        h.recip(rstd[:], rstd[:], [rsk], [rsk])
        h.act(xn[:], xt[:], AF.Copy, [xtn, rsk], [xnk], scale=rstd[:, 0:1])
        for c in range(8):
            h.tr(psT[:, c, :], xn[:, c * 128:(c + 1) * 128], idb[:], [xnk, "idb"], [("psT", c)])
        h.tt("dve", hT[:], psT[:], bc(prm[:, gcol:gcol + 8], [128, 8, 128]), ALU.mult,
             ["psT", "prm"], [hTk])
        return hT, hTk

    def alloc_rms(T, sb, nxn=2):
        T["xn"] = [sb("xn%d" % i, [128, D], BF16) for i in range(nxn)]
        T["ss"] = [sb("ss%d" % i, [128, 1], F32) for i in range(2)]
        T["rstd"] = [sb("rstd%d" % i, [128, 1], F32) for i in range(2)]
        T["hT"] = [sb("hT%d" % i, [128, 8, 128], BF16) for i in range(2)]

    def load_param_cols(h, prm, col, src, n):
        h.dma("sp", prm[:, col:col + n], src.rearrange("(c p) -> p c", p=128), [], [("prm", col)], slow=True)

    with contextlib.ExitStack() as ph:
      if "a" in phases:
        P = Prog(nc, gs, "a")
        h = H(P)

        def sb(name, shape, dt):
            return ph.enter_context(nc.sbuf_tensor("a_" + name, list(shape), dt))

        def psb(name, shape, dt):
            return ph.enter_context(nc.psum_tensor("a_" + name, list(shape), dt))

        T = {}
        cst, idb = load_consts(h, sb)
        T["idb"] = idb
        prm = sb("prm", [128, NPRM], F32)
        T["prm"] = prm
        load_param_cols(h, prm, PR_G, norm_mix, 8)
        load_param_cols(h, prm, PR_MU, mu_shift, 14)
        load_param_cols(h, prm, PR_W0, rwkv_w0, 4)
        load_param_cols(h, prm, PR_A0, rwkv_a0, 4)
        load_param_cols(h, prm, PR_KK, rwkv_k_k, 4)
        load_param_cols(h, prm, PR_KA, rwkv_k_a, 4)
        load_param_cols(h, prm, PR_RK, rwkv_r_k, 4)
        load_param_cols(h, prm, PR_LW, rwkv_ln_w, 4)
        load_param_cols(h, prm, PR_LB, rwkv_ln_b, 4)
        load_param_cols(h, prm, PR_BG, gla_bg, 2)
        load_param_cols(h, prm, PR_NW, gla_norm_w, 1)
        h.ts("dve", prm[:, PR_BG:PR_BG + 2], prm[:, PR_BG:PR_BG + 2], -1.0, None, ALU.mult, None,
             [("prm", PR_BG)], [("prm", PR_BG)])
        h.ts("dve", prm[:, PR_OMKA:PR_OMKA + 4], prm[:, PR_KA:PR_KA + 4], -1.0, 1.0, ALU.mult, ALU.add,
             [("prm", PR_KA)], [("prm", PR_OMKA)])

        NA1 = GATE0
        win = sb("win", [128, 8, NA1], BF16)
        w_in_v = w_in.rearrange("(c p) n -> p c n", p=128)
        WG = [(0, 512), (512, 1152), (1152, 1792), (1792, 2560), (2560, NA1)]
        for gi_, (lo, hi) in enumerate(WG):
            h.dma("pool", win[:, :, lo:hi], w_in_v[:, :, lo:hi], [], [("win", gi_)])

        def wkey(c0, c1):
            ks = [("win", gi_) for gi_, (lo, hi) in enumerate(WG) if lo < c1 and c0 < hi]
            return ks
        w2a2 = sb("w2a2", [128, 512], BF16)
        h.dma("pool", w2a2[0:64, :], rwkv_w2, [], [("w2a2", 0)])
        h.dma("pool", w2a2[64:128, :], rwkv_a2, [], [("w2a2", 1)])
        g2 = sb("g2", [128, 512], BF16)
        h.dma("pool", g2[:], rwkv_g2, [], ["g2"])
        wg2 = sb("wg2", [16, 256], BF16)
        h.dma("pool", wg2[:], gla_wg2, [], ["wg2"])

        xt = sb("xt", [128, D], F32)
        alloc_rms(T, sb, nxn=1)
        prw = sb("prw", [128, 14, 132], F32)
        lastc = sb("lastc", [128, 14, 1], F32)
        xs = sb("xs", [128, 14, 128], F32)
        Fm = [sb("F%d" % i, [128, 4, 128], F32) for i in range(10)]
        gTs = [sb("gT%d" % i, [128, 4, 128], F32) for i in range(2)]
        bons = [sb("bon%d" % i, [128, 4, 128], F32) for i in range(2)]
        ARs = [sb("AR%d" % i, [128, 4, 2, 128], BF16) for i in range(2)]
        Hm = [sb("Hb%d" % i, [128, 4, 128], BF16) for i in range(8)]
        tok = [sb("tok%d" % i, [128, 512], BF16) for i in range(4)]
        SCH = sb("SCH", [128, 8, 4, 128], BF16)
        INV = [sb("INV%d" % i, [128, 8, 128], BF16) for i in range(8)]
        lora_in = sb("lora_in", [128, 128], BF16)
        slg = sb("slg", [128, 128], BF16)
        Zbf = sb("Zbf", [128, 512], BF16)
        Yf = sb("Yf", [128, 512], F32)
        WTbf = sb("WTbf", [128, 4, 128], BF16)
        Ubf = sb("Ubf", [128, 512], BF16)
        Pst = sb("Pst", [128, 4, 64], F32)
        Pbf = sb("Pbf", [128, 4, 64], BF16)
        o1 = sb("o1", [128, 512], F32)
        oo = sb("oo", [128, 512], F32)
        stat = sb("stat", [128, 64], F32)
        ogbf = sb("ogbf", [128, 4, 128], BF16)
        obbf = sb("obbf", [128, 4, 128], BF16)
        Sin = sb("Sin", [64, 8, 64], F32)
        shs = [sb("shs%d" % i, [4, 512], F32) for i in range(2)]
        shf = sb("shf", [128, 14, 4], F32)
        sho = [sb("sho%d" % i, [4, 512], F32) for i in range(2)]
        lga = sb("lga", [16, 128], BF16)
        Sg = sb("Sg", [128, 2, 128], F32)
        Sgbf = sb("Sgbf", [128, 4, 2, 128], BF16)
        SgIn = sb("SgIn", [128, 4, 2, 128], F32)
        xg = sb("xg", [128, 12, 128], F32)
        Gf = [sb("G%d" % i, [128, 2, 128], F32) for i in range(6)]
        silu = sb("Gs", [128, 4, 128], F32)
        qdT, kiT, keT = [sb("Gh%d" % i, [128, 2, 128], BF16) for i in range(3)]
        vgbf = sb("Gv", [128, 4, 128], BF16)
        GS = sb("Ggs", [128, 4, 128], BF16)
        Vgtok = sb("Gt0", [128, 512], BF16)
        Ketok = sb("Gt1", [128, 256], BF16)
        o1g = sb("o1g", [128, 512], F32)
        oog = sb("oog", [128, 512], F32)
        statg = sb("statg", [128, 16], F32)

        T["psT"] = psb("psT", [128, 8, 128], BF16)
        psT = T["psT"]
        PS = [psb("ps%d" % i, [128, 512], F32) for i in range(7)]

        if VERBOSE:
            print("[A1] sbuf remaining after alloc:", nc.sbuf_bytes_remaining, flush=True)
        idf = cst[:, C_ID:C_ID + 128]
        bones = cst[:, C_BO:C_BO + 128]
        rstm = cst[:, C_RST:C_RST + 128]
        m4 = cst[:, C_M4:C_M4 + 512]
        msl = cst[:, C_SL:C_SL + 128]
        miu = cst[:, C_IU:C_IU + 128]

        h.memset("pool", Pst[:], 0.0, ["Pst"])
        h.memset("pool", Pbf[:], 0.0, ["Pbf"])
        h.memset("pool", Sg[:], 0.0, ["Sg"])
        h.memset("pool", lastc[:], 0.0, ["lastc"])

        def v4(ap, nseq, L):
            return ap.rearrange("p m (s l) -> p m s l", s=nseq)

        for b in BLKS:
            samp, nseq, L = geom(b)
            j = b - NPB
            valid = cst[:, (C_VS if samp else C_ONE):(C_VS if samp else C_ONE) + 128]
            load_x_block(h, b, xt, "xt")
            hT, hTk = rms_to_fm(h, T, xt, "xt", PR_G, b % 2)

            W = nseq * (L + 1)
            pv = prw[:, :, 0:W].rearrange("p m (s l) -> p m s l", s=nseq)
            for gi in range(4):
                ms = list(range(4 * gi, min(4 * gi + 4, 14)))
                pt = PS[5 + gi % 2]
                pn = "ps%d" % (5 + gi % 2)
                for mi, m in enumerate(ms):
                    for kc in range(8):
                        h.mm(pt[:, mi * 128:(mi + 1) * 128], win[:, kc, m * 128:(m + 1) * 128], hT[:, kc, :],
                             kc == 0, kc == 7, wkey(m * 128, (m + 1) * 128) + [hTk], [pn])
                nm = len(ms)
                h.cp("act", pv[:, ms[0]:ms[0] + nm, :, 1:L + 1],
                     pt[:, 0:nm * 128].rearrange("p (m s l) -> p m s l", m=nm, s=nseq),
                     [pn], ["prw"])
            gcols = [GLA0 + 128 * i for i in range(8)] + [GLA0 + 1040 + 128 * i for i in range(4)]
            for gi in range(3):
                pt, pn = PS[5 + gi % 2], "ps%d" % (5 + gi % 2)
                for mi in range(4):
                    c0 = gcols[4 * gi + mi]
                    for kc in range(8):
                        h.mm(pt[:, mi * 128:(mi + 1) * 128], win[:, kc, c0:c0 + 128], hT[:, kc, :],
                             kc == 0, kc == 7, wkey(c0, c0 + 128) + [hTk], [pn])
                h.cp("act", xg[:, 4 * gi:4 * gi + 4, :], pt[:].rearrange("p (c l) -> p c l", c=4), [pn], [("xg", gi)])
            for kc in range(8):
                h.mm(PS[6][0:16, 0:128], win[:, kc, GLA0 + 1024:GLA0 + 1040], hT[:, kc, :], kc == 0, kc == 7,
                     wkey(GLA0 + 1024, GLA0 + 1040) + [hTk], ["ps6"])
            h.cp("act", lga[:], PS[6][0:16, 0:128], ["ps6"], ["lga"])
            if not samp:
                h.cp("pool", pv[:, :, 0, 0:1], lastc[:], ["lastc"], ["prw"])
            else:
                for g4 in range(4):
                    ms = list(range(4 * g4, min(4 * g4 + 4, 14)))
                    nm = len(ms)
                    sh_, shk = shs[g4 % 2], "shs%d" % (g4 % 2)
                    h.dma("sp", sh_[:, 0:nm * 128], st_shift[4 * j:4 * j + 4, ms[0] * 128:(ms[0] + nm) * 128], [], [shk])
                    for mi, m in enumerate(ms):
                        h.tr(PS[5][:, mi * 4:(mi + 1) * 4], sh_[0:4, mi * 128:(mi + 1) * 128], idf[0:4, 0:4],
                             [shk, "cst"], ["ps5"])
                    h.cp("act", pv[:, ms[0]:ms[0] + nm, :, 0],
                         PS[5][:, 0:nm * 4].rearrange("p (m s) -> p m s", m=nm), ["ps5"], ["prw"])
            last_prompt = (b == NPB - 1)
            if samp or last_prompt:
                ncol = 4 if samp else 1
                if samp:
                    h.cp("pool", shf[:, :, 0:4], pv[:, :, :, 8], ["prw"], ["shf"])
                else:
                    h.cp("pool", shf[:, :, 0:1], pv[:, :, 0, 128:129], ["prw"], ["shf"])
                for g4 in range(4):
                    ms = list(range(4 * g4, min(4 * g4 + 4, 14)))
                    for mi, m in enumerate(ms):
                        h.tr(PS[6][0:ncol, mi * 128:(mi + 1) * 128], shf[:, m, 0:ncol], idf,
                             ["shf", "cst"], ["ps6"])
                    nm = len(ms)
                    so_, sok = sho[g4 % 2], "sho%d" % (g4 % 2)
                    h.cp("act", so_[0:ncol, 0:nm * 128], PS[6][0:ncol, 0:nm * 128], ["ps6"], [sok])
                    if samp:
                        h.dma("sp", o_shift_s[4 * j:4 * j + 4, ms[0] * 128:(ms[0] + nm) * 128], so_[0:4, 0:nm * 128],
                              [sok], [], isout=True)
                    else:
                        h.dma("sp", o_shift_p[0:1, ms[0] * 128:(ms[0] + nm) * 128], so_[0:1, 0:nm * 128],
                              [sok], [], isout=True)
            if not samp:
                h.cp("pool", lastc[:], pv[:, :, 0, 128:129], ["prw"], ["lastc"])
            xs4 = xs[:].rearrange("p m (s l) -> p m s l", s=nseq)
            cur = pv[:, :, :, 1:L + 1]
            prv = pv[:, :, :, 0:L]
            h.tt("dve", xs4[:, 0:9], prv[:, 0:9], cur[:, 0:9], ALU.subtract, ["prw"], [("xs", 0)])
            h.tt("pool", xs4[:, 9:14], prv[:, 9:14], cur[:, 9:14], ALU.subtract, ["prw"], [("xs", 1)])
            for m in range(14):
                h.stt("dve", xs4[:, m], xs4[:, m], prm[:, PR_MU + m:PR_MU + m + 1], cur[:, m], ALU.mult, ALU.add,
                      [("xs", 0 if m < 9 else 1), "prm", "prw"], [("xs", 2 + m)])
            rT = xs[:, 0:4, :]
            kT = xs[:, 4:8, :]
            vT = xs[:, 8:12, :]

            sw, aa, cum, E, Einv, Eprev, Eend, kk, kh, tmp = Fm
            n = lambda i: "F%d" % i
            N_SW, N_AA, N_CUM, N_E, N_EINV, N_EPREV, N_EEND, N_KK, N_KH, N_TMP = [n(i) for i in range(10)]
            gT, N_GT = gTs[b % 2], "gT%d" % (b % 2)
            bon, N_BON = bons[b % 2], "bon%d" % (b % 2)
            AR, N_AR = ARs[b % 2], "AR%d" % (b % 2)
            bT, kTb, BpT, KpT, vbf = Hm[0], Hm[1], Hm[2], Hm[3], Hm[4]
            h.act(lora_in[0:64, :], xs[0:64, 12, :], AF.Tanh, ["xs"], [("lora_in", 0)])
            h.cp("act", lora_in[64:128, :], xs[64:128, 12, :], ["xs"], [("lora_in", 1)])
            h.act(slg[:], xs[:, 13, :], AF.Sigmoid, ["xs"], ["slg"])
            for hg in range(4):
                h.mm(PS[5][:, hg * 128:(hg + 1) * 128], w2a2[0:64, hg * 128:(hg + 1) * 128], lora_in[0:64, :],
                     True, True, [("w2a2", 0), ("lora_in", 0)], ["ps5"])
            for hg in range(4):
                h.mm(PS[6][:, hg * 128:(hg + 1) * 128], w2a2[64:128, hg * 128:(hg + 1) * 128], lora_in[64:128, :],
                     True, True, [("w2a2", 1), ("lora_in", 1)], ["ps6"])
            for hg in range(4):
                h.act(sw[:, hg, :], PS[5][:, hg * 128:(hg + 1) * 128], AF.Sigmoid, ["ps5", "prm"], [(N_SW, hg)],
                      bias=prm[:, PR_W0 + hg:PR_W0 + hg + 1])
                h.act(aa[:, hg, :], PS[6][:, hg * 128:(hg + 1) * 128], AF.Sigmoid, ["ps6", "prm"], [(N_AA, hg)],
                      bias=prm[:, PR_A0 + hg:PR_A0 + hg + 1])
            for hg in range(4):
                h.mm(PS[5][:, hg * 128:(hg + 1) * 128], g2[:, hg * 128:(hg + 1) * 128], slg[:],
                     True, True, ["g2", "slg"], ["ps5"])
            h.cp("act", gT[:], PS[5][:].rearrange("p (c l) -> p c l", c=4), ["ps5"], [N_GT])
            h.stt("dve", sw[:], sw[:], C0, valid.unsqueeze(1).to_broadcast([128, 4, 128]),
                  ALU.mult, ALU.mult, [N_SW, "cst"], [N_SW])
            for hg in range(4):
                h.scan(cum[:, hg, :], rstm, sw[:, hg, :], [N_SW, "cst"], [(N_CUM, hg)])
            h.act(E[:], cum[:], AF.Exp, [N_CUM], [N_E])
            h.act(Einv[:], cum[:], AF.Exp, [N_CUM], [N_EINV], scale=-1.0)
            h.tt("pool", tmp[:], cum[:], sw[:], ALU.subtract, [N_CUM, N_SW], [N_TMP])
            h.act(Eprev[:], tmp[:], AF.Exp, [N_TMP], [N_EPREV])
            cum4 = cum[:].rearrange("p g (c l) -> p g c l", c=4)
            h.tt("pool", tmp[:].rearrange("p g (c l) -> p g c l", c=4),
                 cum4[:, :, :, 31:32].to_broadcast([128, 4, 4, 32]), cum4, ALU.subtract, [N_CUM], [N_TMP])
            h.act(Eend[:], tmp[:], AF.Exp, [N_TMP], [N_EEND])
            if samp:
                h.tt("pool", Eend[:], Eend[:], valid.unsqueeze(1).to_broadcast([128, 4, 128]), ALU.mult,
                     [N_EEND, "cst"], [N_EEND])
            h.tt("dve", kk[:], kT, bc(prm[:, PR_KK:PR_KK + 4], [128, 4, 128]), ALU.mult, ["xs", "prm"], [N_KK])
            h.act(tmp[:], kk[:], AF.Square, [N_KK], [N_TMP])
            for hg in range(4):
                h.mm(PS[6][:, hg * 128:(hg + 1) * 128], bones, tmp[:, hg, :], True, True, ["cst", N_TMP], ["ps6"])
            h.act(tmp[:], PS[6][:].rearrange("p (c l) -> p c l", c=4), AF.Sqrt, ["ps6"], [N_TMP])
            h.ts("dve", tmp[:], tmp[:], 1e-12, None, ALU.max, None, [N_TMP], [N_TMP])
            h.recip(tmp[:], tmp[:], [N_TMP], [N_TMP])
            h.tt("dve", kk[:], kk[:], tmp[:], ALU.mult, [N_KK, N_TMP], [N_KK])
            h.tt("pool", kh[:], aa[:], bc(prm[:, PR_KA:PR_KA + 4], [128, 4, 128]), ALU.mult, [N_AA, "prm"], [N_KH])
            h.tt("pool", kh[:], kh[:], bc(prm[:, PR_OMKA:PR_OMKA + 4], [128, 4, 128]), ALU.add, [N_KH, "prm"], [N_KH])
            h.tt("pool", kh[:], kh[:], kT, ALU.mult, [N_KH, "xs"], [N_KH])
            h.tt("dve", tmp[:], rT, bc(prm[:, PR_RK:PR_RK + 4], [128, 4, 128]), ALU.mult, ["xs", "prm"], [N_TMP])
            h.tt("dve", tmp[:], tmp[:], kh[:], ALU.mult, [N_TMP, N_KH], [N_TMP])
            for hg in range(4):
                h.mm(PS[5][:, hg * 128:(hg + 1) * 128], bones, tmp[:, hg, :], True, True, ["cst", N_TMP], ["ps5"])
            h.tt("dve", bon[:], PS[5][:].rearrange("p (c l) -> p c l", c=4), vT, ALU.mult, ["ps5", "xs"], [N_BON])
            h.tt("pool", aa[:], aa[:], kk[:], ALU.mult, [N_AA, N_KK], [N_AA])
            h.tt("dve", AR[:, :, 1, :], rT, E[:], ALU.mult, ["xs", N_E], [(N_AR, 1)])
            h.stt("dve", AR[:, :, 0, :], kk[:], -1.0, Eprev[:], ALU.mult, ALU.mult, [N_KK, N_EPREV], [(N_AR, 0)])
            h.tt("dve", bT[:], aa[:], Einv[:], ALU.mult, [N_AA, N_EINV], ["Hb0"])
            h.tt("dve", kTb[:], kh[:], Einv[:], ALU.mult, [N_KH, N_EINV], ["Hb1"])
            h.tt("pool", BpT[:], aa[:], Eend[:], ALU.mult, [N_AA, N_EEND], ["Hb2"])
            h.tt("pool", KpT[:], kh[:], Eend[:], ALU.mult, [N_KH, N_EEND], ["Hb3"])
            h.cp("pool", vbf[:], vT, ["xs"], ["Hb4"])
            h.cp("pool", stat[:, 0:16].rearrange("p (g c) -> p g c", g=4),
                 E[:].rearrange("p g (c l) -> p g c l", c=4)[:, :, :, 31], [N_E], [("stat", 0)])
            gam = stat[:, 0:16].rearrange("p (g c) -> p g c", g=4)

            for hd in range(8):
                hg, pb = hd // 2, 64 * (hd % 2)
                px = PS[hd % 2]
                pxn = "ps%d" % (hd % 2)
                h.mm(px[:, 0:256], bT[pb:pb + 64, hg, :], AR[pb:pb + 64, hg, :, :].rearrange("p a l -> p (a l)"),
                     True, True, ["Hb0", N_AR], [pxn])
                h.mm(px[:, 256:512], kTb[pb:pb + 64, hg, :], AR[pb:pb + 64, hg, :, :].rearrange("p a l -> p (a l)"),
                     True, True, ["Hb1", N_AR], [pxn])
                h.tt("dve", SCH[:, hd, :, :].rearrange("p a l -> p (a l)"), px[:], m4, ALU.mult,
                     [pxn, "cst"], [("SCH", hd)])
            Ac, Nn, An, Xc, Xtc, Xn, Xtn, Nc2 = INV
            NI = ["INV%d" % i for i in range(8)]
            Ac4 = Ac[:].rearrange("p (g a) l -> p g a l", a=2)
            for h2 in range(2):
                pt = PS[2 + h2]
                ptn = "ps%d" % (2 + h2)
                pb = 64 * h2
                for hg in range(4):
                    h.mm(pt[:, hg * 128:(hg + 1) * 128], AR[pb:pb + 64, hg, 0, :], bT[pb:pb + 64, hg, :],
                         True, True, [N_AR, "Hb0"], [ptn])
                h.tt("dve", Ac4[:, :, h2, :], pt[:].rearrange("p (q l) -> p q l", q=4),
                     msl.unsqueeze(1).to_broadcast([128, 4, 128]), ALU.mult, [ptn, "cst"], [NI[0]])
            h.tt("pool", Xc[:], SCH[:, :, 0, :], idb[:].unsqueeze(1).to_broadcast([128, 8, 128]), ALU.add,
                 ["SCH", "idb"], [NI[3]])
            h.tt("pool", Xtc[:], Ac[:], idb[:].unsqueeze(1).to_broadcast([128, 8, 128]), ALU.add,
                 [NI[0], "idb"], [NI[4]])

            Ncur_ap = lambda hd: SCH[:, hd, 0, :]
            Ncur_key = "SCH"
            Acur, Acur_key = Ac, NI[0]
            Xcur, Xcur_key, Xtcur, Xtcur_key = Xc, NI[3], Xtc, NI[4]
            Nnext = [(Nn, NI[1]), (Nc2, NI[7])]
            Anext = [(An, NI[2]), (Ac, NI[0])]
            Xnext = [(Xn, NI[5]), (Xc, NI[3])]
            Xtnext = [(Xtn, NI[6]), (Xtc, NI[4])]
            for lvl in range(4):
                last = (lvl == 3)
                Nx, Nxk = Nnext[lvl % 2]
                Ax, Axk = Anext[lvl % 2]
                Xx, Xxk = Xnext[lvl % 2]
                Xtx, Xtxk = Xtnext[lvl % 2]
                for g2i in range(2):
                    pa, pan = PS[2 * g2i], "ps%d" % (2 * g2i)
                    pbk, pbn = PS[2 * g2i + 1], "ps%d" % (2 * g2i + 1)
                    for q in range(4):
                        hd = 4 * g2i + q
                        h.mm(pa[:, q * 128:(q + 1) * 128], Acur[:, hd, :], Ncur_ap(hd), True, True,
                             [Acur_key, Ncur_key], [pan])
                    h.cp("act", Nx[:, 4 * g2i:4 * g2i + 4, :], pa[:].rearrange("p (q l) -> p q l", q=4),
                         [pan], [(Nxk, g2i)])
                    if not last:
                        for q in range(4):
                            hd = 4 * g2i + q
                            h.mm(pbk[:, q * 128:(q + 1) * 128], Ncur_ap(hd), Acur[:, hd, :], True, True,
                                 [Acur_key, Ncur_key], [pbn])
                        h.cp("act", Ax[:, 4 * g2i:4 * g2i + 4, :], pbk[:].rearrange("p (q l) -> p q l", q=4),
                             [pbn], [(Axk, g2i)])
                for g2i in range(2):
                    pa, pan = PS[4], "ps4"
                    for q in range(4):
                        hd = 4 * g2i + q
                        h.mm(pa[:, q * 128:(q + 1) * 128], Xtcur[:, hd, :], Nx[:, hd, :], True, True,
                             [Xtcur_key, (Nxk, g2i)], [pan])
                    h.tt("dve", Xx[:, 4 * g2i:4 * g2i + 4, :], pa[:].rearrange("p (q l) -> p q l", q=4),
                         Xcur[:, 4 * g2i:4 * g2i + 4, :], ALU.add, [pan, Xcur_key], [(Xxk, g2i)])
                if not last:
                    for g2i in range(2):
                        pa, pan = PS[2 * g2i], "ps%d" % (2 * g2i)
                        for q in range(4):
                            hd = 4 * g2i + q
                            h.mm(pa[:, q * 128:(q + 1) * 128], Nx[:, hd, :], Xtcur[:, hd, :], True, True,
                                 [Xtcur_key, (Nxk, g2i)], [pan])
                        h.tt("dve", Xtx[:, 4 * g2i:4 * g2i + 4, :], pa[:].rearrange("p (q l) -> p q l", q=4),
                             Xtcur[:, 4 * g2i:4 * g2i + 4, :], ALU.add, [pan, Xtcur_key], [(Xtxk, g2i)])
                Ncur_ap = (lambda t: (lambda hd: t[:, hd, :]))(Nx)
                Ncur_key = Nxk
                Acur, Acur_key = Ax, Axk
                Xcur, Xcur_key = Xx, Xxk
                Xtcur, Xtcur_key = Xtx, Xtxk
            X4, X4k = Xcur, Xcur_key

            Atok, Bptok, Kptok, Vtok = tok
            srcs = [(AR[:, :, 0, :], N_AR, Atok, "tok0"), (BpT[:], "Hb2", Bptok, "tok1"),
                    (KpT[:], "Hb3", Kptok, "tok2"), (vbf[:], "Hb4", Vtok, "tok3")]
            for si in range(0, 4, 2):
                for u in range(2):
                    src, srck, dst, dstk = srcs[si + u]
                    for hg in range(4):
                        h.tr(psT[:, u * 4 + hg, :], src[:, hg, :], idb[:], [srck, "idb"], [("psT", u * 4 + hg)])
                for u in range(2):
                    src, srck, dst, dstk = srcs[si + u]
                    h.cp("act", dst[:], psT[:, u * 4:(u + 1) * 4, :].rearrange("p c l -> p (c l)"),
                         ["psT"], [dstk])

            for hd in range(8):
                h.mm(PS[0][:, hd * 64:(hd + 1) * 64], SCH[:, hd, 2, :], Vtok[:, hd * 64:(hd + 1) * 64],
                     True, True, ["SCH", "tok3"], ["ps0"])
            h.cp("act", Zbf[:], PS[0][:], ["ps0"], ["Zbf"])
            for hd in range(8):
                h.mm(PS[1][:, hd * 64:(hd + 1) * 64], X4[:, hd, :], Zbf[:, hd * 64:(hd + 1) * 64],
                     True, True, [X4k, "Zbf"], ["ps1"])
            h.cp("act", Yf[:], PS[1][:], ["ps1"], ["Yf"])
            for hd in range(8):
                hg, pb = hd // 2, 64 * (hd % 2)
                h.mm(PS[2][pb:pb + 64, hg * 128:(hg + 1) * 128], Atok[:, hd * 64:(hd + 1) * 64], X4[:, hd, :],
                     True, True, ["tok0", X4k], ["ps2"])
            h.cp("act", WTbf[:], PS[2][:].rearrange("p (c l) -> p c l", c=4), ["ps2"], ["WTbf"])

            psUs, psUn = (PS[0], PS[1]), ("ps0", "ps1")
            psOs, psOn = (PS[2], PS[3]), ("ps2", "ps3")
            psPn = PS[4]
            Ubf4 = Ubf[:].rearrange("p (g a v) -> p g a v", g=4, a=2)
            Yf4 = Yf[:].rearrange("p (g a v) -> p g a v", g=4, a=2)
            for c in range(4):
                s0 = 32 * c
                if samp:
                    seq = 4 * j + c
                    h.dma("sp", Sin[:], st_wkv[seq].rearrange("h v k -> v h k"), [], ["Sin"])
                    for hg in range(4):
                        h.tr(PS[4][:, hg * 64:(hg + 1) * 64],
                             Sin[:, 2 * hg:2 * hg + 2, :].rearrange("p a k -> p (a k)"), idf[0:64, 0:64],
                             ["Sin", "cst"], ["ps4"])
                    h.cp("act", Pst[:], PS[4][:, 0:256].rearrange("p (c v) -> p c v", c=4), ["ps4"], ["Pst"])
                    h.cp("dve", Pbf[:], Pst[:], ["Pst"], ["Pbf"])
                for hd in range(8):
                    hg, h2 = hd // 2, hd % 2
                    pb = 64 * h2
                    h.mm(psUs[h2][s0:s0 + 32, hg * 64:(hg + 1) * 64], WTbf[pb:pb + 64, hg, s0:s0 + 32],
                         Pbf[pb:pb + 64, hg, :], True, True, ["WTbf", "Pbf"], [psUn[h2]], tp=(pb, s0))
                for hd in range(8):
                    hg, h2 = hd // 2, hd % 2
                    pb = 64 * h2
                    h.mm(psOs[h2][s0:s0 + 32, hg * 64:(hg + 1) * 64], AR[pb:pb + 64, hg, 1, s0:s0 + 32],
                         Pbf[pb:pb + 64, hg, :], True, True, [N_AR, "Pbf"], [psOn[h2]], tp=(pb, s0))
                for h2 in range(2):
                    h.tt("dve", Ubf4[s0:s0 + 32, :, h2, :],
                         psUs[h2][s0:s0 + 32, 0:256].rearrange("p (g v) -> p g v", g=4),
                         Yf4[s0:s0 + 32, :, h2, :], ALU.add, [psUn[h2], "Yf"], [("Ubf", c)])
                for hd in range(8):
                    hg, pb = hd // 2, 64 * (hd % 2)
                    h.mm(psPn[pb:pb + 64, hg * 64:(hg + 1) * 64], Bptok[s0:s0 + 32, hd * 64:(hd + 1) * 64],
                         Ubf[s0:s0 + 32, hd * 64:(hd + 1) * 64], True, False, ["tok1", ("Ubf", c)], ["ps4"],
                         tp=(s0, pb))
                    h.mm(psPn[pb:pb + 64, hg * 64:(hg + 1) * 64], Kptok[s0:s0 + 32, hd * 64:(hd + 1) * 64],
                         Vtok[s0:s0 + 32, hd * 64:(hd + 1) * 64], False, True, ["tok2", "tok3"], ["ps4"],
                         tp=(s0, pb))
                for hg in range(4):
                    h.stt("dve", Pst[:, hg, :], Pst[:, hg, :], gam[:, hg, c:c + 1], psPn[:, hg * 64:(hg + 1) * 64],
                          ALU.mult, ALU.add, ["Pst", ("stat", 0), "ps4"], ["Pst"])
                if not samp and not (b == NPB - 1 and c == 3):
                    h.cp("act", Pbf[:], Pst[:], ["Pst"], ["Pbf"])
                if samp or (b == NPB - 1 and c == 3):
                    for hg in range(4):
                        h.tr(PS[4][0:64, hg * 128:(hg + 1) * 128], Pst[:, hg, :], idf, ["Pst", "cst"], ["ps4"])
                    h.cp("act", Sin[:].rearrange("p h k -> p (h k)"), PS[4][0:64, :], ["ps4"], ["Sin"])
                    dst = o_wkv_s[4 * j + c] if samp else o_wkv_p[0]
                    h.dma("sp", dst.rearrange("h v k -> v h k"), Sin[:], ["Sin"], [], isout=True)
            for hd in range(8):
                h.mm(PS[0][:, hd * 64:(hd + 1) * 64], SCH[:, hd, 1, :], Ubf[:, hd * 64:(hd + 1) * 64],
                     True, False, ["SCH", "Ubf"], ["ps0"])
                h.mm(PS[0][:, hd * 64:(hd + 1) * 64], SCH[:, hd, 3, :], Vtok[:, hd * 64:(hd + 1) * 64],
                     False, True, ["SCH", "tok3"], ["ps0"])
            o14 = o1[:].rearrange("p (g a v) -> p g a v", g=4, a=2)
            for h2 in range(2):
                h.cp("act", o14[:, :, h2, :], psOs[h2][:, 0:256].rearrange("p (g v) -> p g v", g=4),
                     [psOn[h2]], ["o1"])
            h.tt("dve", oo[:], o1[:], PS[0][:], ALU.add, ["o1", "ps0"], ["oo"])

            oo3 = oo[:].rearrange("p (h v) -> p h v", h=8)
            osq = o1
            h.act(osq[:], oo[:], AF.Square, ["oo"], ["o1"])
            s1, s2, mean, msq, rs, nb = (stat[:, 16:24], stat[:, 24:32], stat[:, 32:40], stat[:, 40:48],
                                         stat[:, 48:56], stat[:, 56:64])
            h.rsum(s1, oo3, ["oo"], [("stat", 1)])
            h.rsum(s2, osq[:].rearrange("p (h v) -> p h v", h=8), ["o1"], [("stat", 2)])
            h.ts("dve", mean, s1, 1.0 / 64, None, ALU.mult, None, [("stat", 1)], [("stat", 3)])
            h.tt("dve", msq, mean, mean, ALU.mult, [("stat", 3)], [("stat", 4)])
            h.stt("dve", rs, s2, 1.0 / 64, msq, ALU.mult, ALU.subtract, [("stat", 2), ("stat", 4)], [("stat", 5)])
            h.ts("dve", rs, rs, 64e-5, None, ALU.add, None, [("stat", 5)], [("stat", 5)])
            h.act(rs, rs, AF.Sqrt, [("stat", 5)], [("stat", 5)])
            h.recip(rs, rs, [("stat", 5)], [("stat", 5)])
            h.tt("dve", oo3, oo3, mean.unsqueeze(2).to_broadcast([128, 8, 64]), ALU.subtract,
                 ["oo", ("stat", 3)], ["oo"])
            h.tt("dve", oo3, oo3, rs.unsqueeze(2).to_broadcast([128, 8, 64]), ALU.mult,
                 ["oo", ("stat", 5)], ["oo"])
            for hg in range(4):
                h.tr(PS[0][:, hg * 128:(hg + 1) * 128], oo[:, hg * 128:(hg + 1) * 128], idf, ["oo", "cst"], ["ps0"])
            ps0v = PS[0][:].rearrange("p (c l) -> p c l", c=4)
            t2 = o1[:].rearrange("p (c l) -> p c l", c=4)
            h.tt("dve", t2, ps0v, bc(prm[:, PR_LW:PR_LW + 4], [128, 4, 128]), ALU.mult, ["ps0", "prm"], ["o1"])
            h.tt("pool", t2, t2, bc(prm[:, PR_LB:PR_LB + 4], [128, 4, 128]), ALU.add, ["o1", "prm"], ["o1"])
            h.tt("pool", t2, t2, bon[:], ALU.add, ["o1", N_BON], ["o1"])
            h.tt("dve", ogbf[:], t2, gT[:], ALU.mult, ["o1", N_GT], ["ogbf"])
            if samp:
                h.dma("sp", og_s[NPB].rearrange("p (c j q t) -> p c j q t", c=4, j=4, q=4)[:, :, j],
                      ogbf[:].rearrange("p c (q l) -> p c q l", q=4)[:, :, :, 0:8], ["ogbf"], [])
            else:
                h.dma("sp", og_s[b], ogbf[:].rearrange("p c l -> p (c l)"), ["ogbf"], [])

            qT, kgT, vgT, ogT = xg[:, 0:2, :], xg[:, 2:4, :], xg[:, 4:8, :], xg[:, 8:12, :]
            for c2 in range(2):
                h.mm(PS[5][:, c2 * 128:(c2 + 1) * 128], wg2[0:16, c2 * 128:(c2 + 1) * 128], lga[:], True, True,
                     ["wg2", "lga"], ["ps5"])
            la_, cg, Eg, Eginv, Egend, gtmp = Gf
            for c2 in range(2):
                h.act(la_[:, c2, :], PS[5][:, c2 * 128:(c2 + 1) * 128], AF.Exp, ["ps5", "prm"], [("G0", c2)],
                      scale=-1.0, bias=prm[:, PR_BG + c2:PR_BG + c2 + 1])
            h.ts("dve", la_[:], la_[:], 1.0, None, ALU.add, None, ["G0"], ["G0"])
            h.act(la_[:], la_[:], AF.Ln, ["G0"], ["G0"])
            h.stt("dve", la_[:], la_[:], -1.0 / 16.0,
                  valid.unsqueeze(1).to_broadcast([128, 2, 128]), ALU.mult, ALU.mult, ["G0", "cst"], ["G0"])
            for c2 in range(2):
                h.scan(cg[:, c2, :], rstm, la_[:, c2, :], ["G0", "cst"], [("G1", c2)])
            h.act(Eg[:], cg[:], AF.Exp, ["G1"], ["G2"])
            h.act(Eginv[:], cg[:], AF.Exp, ["G1"], ["G3"], scale=-1.0)
            cg4 = cg[:].rearrange("p g (c l) -> p g c l", c=4)
            h.tt("pool", gtmp[:].rearrange("p g (c l) -> p g c l", c=4),
                 cg4[:, :, :, 31:32].to_broadcast([128, 2, 4, 32]), cg4, ALU.subtract, ["G1"], ["G5"])
            h.act(Egend[:], gtmp[:], AF.Exp, ["G5"], ["G4"])
            if samp:
                h.tt("pool", Egend[:], Egend[:], valid.unsqueeze(1).to_broadcast([128, 2, 128]),
                     ALU.mult, ["G4", "cst"], ["G4"])
            h.cp("pool", statg[:, 0:8].rearrange("p (g c) -> p g c", g=2),
                 Eg[:].rearrange("p g (c l) -> p g c l", c=4)[:, :, :, 31], ["G2"], [("statg", 0)])
            gamg = statg[:, 0:8].rearrange("p (g c) -> p g c", g=2)
            h.stt("dve", qdT[:], qT, 0.125, Eg[:], ALU.mult, ALU.mult, [("xg", 0), "G2"], ["Gh0"])
            h.tt("pool", kiT[:], kgT, Eginv[:], ALU.mult, [("xg", 0), "G3"], ["Gh1"])
            h.tt("pool", keT[:], kgT, Egend[:], ALU.mult, [("xg", 0), "G4"], ["Gh2"])
            h.cp("pool", vgbf[:], vgT, [("xg", 1)], ["Gv"])
            h.act(silu[:], ogT, AF.Silu, [("xg", 2)], ["Gs"])
            GS4 = GS[:].rearrange("p (g a) l -> p g a l", a=2)
            for h2 in range(2):
                pb = 64 * h2
                pt, ptn = PS[5 + h2], "ps%d" % (5 + h2)
                for c2 in range(2):
                    h.mm(pt[:, c2 * 128:(c2 + 1) * 128], kiT[pb:pb + 64, c2, :], qdT[pb:pb + 64, c2, :], True, True,
                         ["Gh1", "Gh0"], [ptn])
                h.tt("dve", GS4[:, :, h2, :], pt[:, 0:256].rearrange("p (q l) -> p q l", q=2),
                     miu.unsqueeze(1).to_broadcast([128, 2, 128]), ALU.mult, [ptn, "cst"], ["Ggs"])
            for hg in range(4):
                h.tr(psT[:, hg, :], vgbf[:, hg, :], idb[:], ["Gv", "idb"], [("psT", hg)])
            for c2 in range(2):
                h.tr(psT[:, 4 + c2, :], keT[:, c2, :], idb[:], ["Gh2", "idb"], [("psT", 4 + c2)])
            h.cp("act", Vgtok[:], psT[:, 0:4, :].rearrange("p c l -> p (c l)"), ["psT"], ["Gt0"])
            h.cp("act", Ketok[:], psT[:, 4:6, :].rearrange("p c l -> p (c l)"), ["psT"], ["Gt1"])
            if samp:
                h.dma("sp", SgIn[:], st_gla[4 * j:4 * j + 4].rearrange("q (c2 h2) k v -> (h2 k) q c2 v", h2=2),
                      [], ["SgIn"])
                h.cp("act", Sgbf[:], SgIn[:], ["SgIn"], ["Sgbf"])
            else:
                h.cp("act", Sgbf[:, 0, :, :], Sg[:], ["Sg"], [("Sgbf", 0)])
            for c in range(4):
                s0 = 32 * c
                pt, pn = PS[5 + c % 2], "ps%d" % (5 + c % 2)
                for hd in range(4):
                    c2, pb = hd // 2, 64 * (hd % 2)
                    off = c2 * 128
                    h.mm(pt[pb:pb + 64, off:off + 128], Ketok[s0:s0 + 32, hd * 64:(hd + 1) * 64],
                         Vgtok[s0:s0 + 32, hd * 128:(hd + 1) * 128], True, True, ["Gt1", "Gt0"], [pn],
                         tp=(s0, pb))
                dv = pt[:, 0:256].rearrange("p (g v) -> p g v", g=2)
                gb = gamg[:, :, c:c + 1].to_broadcast([128, 2, 128])
                for c2 in range(2):
                    if samp:
                        h.stt("dve", SgIn[:, c, c2, :], SgIn[:, c, c2, :], gamg[:, c2, c:c + 1], dv[:, c2, :],
                              ALU.mult, ALU.add, [("SgIn", c), ("statg", 0), pn], [("SgIn", c)])
                    else:
                        h.stt("dve", Sg[:, c2, :], Sg[:, c2, :], gamg[:, c2, c:c + 1], dv[:, c2, :],
                              ALU.mult, ALU.add, ["Sg", ("statg", 0), pn], ["Sg"])
                if (not samp) and c < 3:
                    h.cp("act", Sgbf[:, c + 1, :, :], Sg[:], ["Sg"], [("Sgbf", c + 1)])
            if samp:
                h.dma("sp", o_gla_s[4 * j:4 * j + 4].rearrange("q (c2 h2) k v -> (h2 k) q c2 v", h2=2),
                      SgIn[:], ["SgIn"], [], isout=True)
            elif b == NPB - 1:
                h.dma("sp", o_gla_p[0].rearrange("(c2 h2) k v -> (h2 k) c2 v", h2=2), Sg[:], ["Sg"], [],
                      isout=True)
            for c in range(4):
                s0 = 32 * c
                for hd in range(4):
                    c2, h2 = hd // 2, hd % 2
                    pb = 64 * h2
                    h.mm(PS[5 + h2][s0:s0 + 32, c2 * 128:(c2 + 1) * 128], qdT[pb:pb + 64, c2, s0:s0 + 32],
                         Sgbf[pb:pb + 64, c, c2, :], True, True, ["Gh0", "Sgbf"], ["ps%d" % (5 + h2)], tp=(pb, s0))
            o1gv = o1g[:].rearrange("p (g a v) -> p g a v", g=2, a=2)
            for h2 in range(2):
                h.cp("act", o1gv[:, :, h2, :], PS[5 + h2][:, 0:256].rearrange("p (g v) -> p g v", g=2),
                     ["ps%d" % (5 + h2)], ["o1g"])
            for hd in range(4):
                h.mm(PS[5][:, hd * 128:(hd + 1) * 128], GS[:, hd, :], Vgtok[:, hd * 128:(hd + 1) * 128], True, True,
                     ["Ggs", "Gt0"], ["ps5"])
            h.tt("dve", oog[:], o1g[:], PS[5][:], ALU.add, ["o1g", "ps5"], ["oog"])
            oo4 = oog[:].rearrange("p (h v) -> p h v", h=4)
            h.act(o1g[:], oog[:], AF.Square, ["oog"], ["o1g"])
            gs2, grs = statg[:, 8:12], statg[:, 12:16]
            h.rsum(gs2, o1g[:].rearrange("p (h v) -> p h v", h=4), ["o1g"], [("statg", 1)])
            h.ts("dve", grs, gs2, 1.0 / 128, 1e-6, ALU.mult, ALU.add, [("statg", 1)], [("statg", 2)])
            h.act(grs, grs, AF.Sqrt, [("statg", 2)], [("statg", 2)])
            h.recip(grs, grs, [("statg", 2)], [("statg", 2)])
            h.tt("dve", oo4, oo4, grs.unsqueeze(2).to_broadcast([128, 4, 128]), ALU.mult, ["oog", ("statg", 2)], ["oog"])
            for hd in range(4):
                h.tr(PS[6][:, hd * 128:(hd + 1) * 128], oog[:, hd * 128:(hd + 1) * 128], idf, ["oog", "cst"], ["ps6"])
            h.stt("dve", obbf[:], PS[6][:].rearrange("p (c l) -> p c l", c=4), prm[:, PR_NW:PR_NW + 1], silu[:],
                  ALU.mult, ALU.mult, ["ps6", "prm", "Gs"], ["obbf"])
            if samp:
                h.dma("sp", ob_s[NPB].rearrange("p (c j q t) -> p c j q t", c=4, j=4, q=4)[:, :, j],
                      obbf[:].rearrange("p c (q l) -> p c q l", q=4)[:, :, :, 0:8], ["obbf"], [])
            else:
                h.dma("sp", ob_s[b], obbf[:].rearrange("p c l -> p (c l)"), ["obbf"], [])

            if dbg is not None and dbg.get("blk") == b and dbg.get("phase") == "a1":
                src, keys = dbg["fn"](dict(locals()))
                h.dma("sp", dbg_out, src, keys, [], isout=True)
        P.emit()

    gx = contextlib.ExitStack()
    wdn = gx.enter_context(nc.sbuf_tensor("g_wdn", [128, 22, D], BF16))
    w_dn_v = ffn_w_down.rearrange("(c p) n -> p c n", p=128)
    wdn_loaded = False
    with contextlib.ExitStack() as ph:
      if "b" in phases:
        P = Prog(nc, gs, "b")
        h = H(P)

        def sb(name, shape, dt):
            return ph.enter_context(nc.sbuf_tensor("b_" + name, list(shape), dt))

        def psb(name, shape, dt):
            return ph.enter_context(nc.psum_tensor("b_" + name, list(shape), dt))

        T = {}
        cst, idb = load_consts(h, sb)
        T["idb"] = idb
        prm = sb("prm", [128, NPRM], F32)
        T["prm"] = prm
        load_param_cols(h, prm, PR_G, norm_mix, 8)
        wing = sb("wing", [128, 8, 2048], BF16)
        w_in_v = w_in.rearrange("(c p) n -> p c n", p=128)
        for q4 in range(4):
            h.dma("pool", wing[:, :, q4 * 512:(q4 + 1) * 512], w_in_v[:, :, GATE0 + q4 * 512:GATE0 + (q4 + 1) * 512],
                  [], [("wing", q4)])
        woa = sb("woa", [128, 4, D], BF16)
        wob = sb("wob", [128, 4, D], BF16)
        wo = sb("wo", [128, 8, D], BF16)
        h.dma("pool", woa[:], w_out_a.rearrange("(c p) n -> p c n", p=128), [], ["woa"])
        h.dma("pool", wob[:], w_out_b.rearrange("(c p) n -> p c n", p=128), [], ["wob"])
        for kc in range(8):
            h.dma("pool", wo[:, kc, :], w_o.rearrange("(c p) n -> p c n", p=128)[:, kc, :], [], [("wo", kc)])
        for kc in range(22):
            h.dma("pool", wdn[:, kc, :], w_dn_v[:, kc, :], [], [("wdn", kc)])
        wdn_loaded = True
        xts = [sb("xt%d" % i, [128, D], F32) for i in range(4)]
        alloc_rms(T, sb)
        ogbs = [sb("ogb%d" % i, [128, 4, 128], BF16) for i in range(2)]
        obbs = [sb("obb%d" % i, [128, 4, 128], BF16) for i in range(2)]
        sgas = [sb("sga%d" % i, [128, 8, 128], F32) for i in range(2)]
        sgbs = [sb("sgb%d" % i, [128, 8, 128], F32) for i in range(2)]
        tas = [sb("ta%d" % i, [128, 8, 128], F32) for i in range(2)]
        mgs = [sb("mg%d" % i, [128, 8, 128], BF16) for i in range(2)]
        x1ts = [sb("x1t%d" % i, [128, D], F32) for i in range(2)]
        T["psT"] = psb("psT", [128, 8, 128], BF16)
        PS = [psb("ps%d" % i, [128, 512], F32) for i in range(7)]
        if FILLERS:
            _pf, _id = PS[6], idb
            P.filler = ((lambda e: e.matmul(_pf[:, 0:128], lhsT=_id[:], rhs=_id[:], start=True, stop=True)), 70.0)
        for b in BLKS2:
            xt, xtn = xts[b % 4], "xt%d" % (b % 4)
            load_x_block(h, b, xt, xtn, packed=True)
            p2 = b % 2
            ogb, obb, sga, sgb, ta, mg, x1t = ogbs[p2], obbs[p2], sgas[p2], sgbs[p2], tas[p2], mgs[p2], x1ts[p2]
            K_ogb, K_obb, K_sga, K_sgb, K_ta, K_mg, K_x1t = ["%s%d" % (nm, p2) for nm in
                                                            ("ogb", "obb", "sga", "sgb", "ta", "mg", "x1t")]
            h.dma("sp", ogb[:].rearrange("p c l -> p (c l)"), og_s[b], [], [K_ogb])
            h.dma("sp", obb[:].rearrange("p c l -> p (c l)"), ob_s[b], [], [K_obb])
            hT, hTk = rms_to_fm(h, T, xt, xtn, PR_G, b % 2)
            for half, dst, dk in ((0, sga, K_sga), (1, sgb, K_sgb)):
                for gi in range(2):
                    pt, pn = PS[gi], "ps%d" % gi
                    for mi in range(4):
                        c0 = half * 1024 + (4 * gi + mi) * 128
                        for kc in range(8):
                            h.mm(pt[:, mi * 128:(mi + 1) * 128], wing[:, kc, c0:c0 + 128], hT[:, kc, :],
                                 kc == 0, kc == 7, [("wing", c0 // 512), hTk], [pn])
                    h.act(dst[:, 4 * gi:4 * gi + 4, :], pt[:].rearrange("p (c l) -> p c l", c=4), AF.Sigmoid,
                          [pn], [(dk, gi)])
            for gi in range(2):
                pt, pn = PS[2 + gi], "ps%d" % (2 + gi)
                for mi in range(4):
                    m = 4 * gi + mi
                    for kc in range(4):
                        h.mm(pt[:, mi * 128:(mi + 1) * 128], woa[:, kc, m * 128:(m + 1) * 128], ogb[:, kc, :],
                             kc == 0, kc == 3, ["woa", K_ogb], [pn])
                h.tt("dve", ta[:, 4 * gi:4 * gi + 4, :], pt[:].rearrange("p (c l) -> p c l", c=4),
                     sga[:, 4 * gi:4 * gi + 4, :], ALU.mult, [pn, (K_sga, gi)], [(K_ta, gi)])
            for gi in range(2):
                pt, pn = PS[4 + gi], "ps%d" % (4 + gi)
                for mi in range(4):
                    m = 4 * gi + mi
                    for kc in range(4):
                        h.mm(pt[:, mi * 128:(mi + 1) * 128], wob[:, kc, m * 128:(m + 1) * 128], obb[:, kc, :],
                             kc == 0, kc == 3, ["wob", K_obb], [pn])
                h.tt("dve", sgb[:, 4 * gi:4 * gi + 4, :], pt[:].rearrange("p (c l) -> p c l", c=4),
                     sgb[:, 4 * gi:4 * gi + 4, :], ALU.mult, [pn, (K_sgb, gi)], [(K_sgb, gi)])
                h.tt("pool", mg[:, 4 * gi:4 * gi + 4, :], ta[:, 4 * gi:4 * gi + 4, :],
                     sgb[:, 4 * gi:4 * gi + 4, :], ALU.add, [(K_ta, gi), (K_sgb, gi)], [(K_mg, gi)])
            for nh in range(2):
                pt, pn = PS[2 + nh], "ps%d" % (2 + nh)
                for kc in range(8):
                    h.mm(pt[:], mg[:, kc, :], wo[:, kc, nh * 512:(nh + 1) * 512], kc == 0, kc == 7,
                         [K_mg, ("wo", kc)], [pn])
                h.tt("dve", x1t[:, nh * 512:(nh + 1) * 512], pt[:], xt[:, nh * 512:(nh + 1) * 512], ALU.add,
                     [pn, xtn], [(K_x1t, nh)])
            h.dma("sp", x1_s[b * 128:(b + 1) * 128, :], x1t[:], [K_x1t], [])
        P.emit()

    with contextlib.ExitStack() as ph:
      if "c" in phases:
        P = Prog(nc, gs, "c")
        h = H(P)

        def sb(name, shape, dt):
            return ph.enter_context(nc.sbuf_tensor("c_" + name, list(shape), dt))

        def psb(name, shape, dt):
            return ph.enter_context(nc.psum_tensor("c_" + name, list(shape), dt))

        T = {}
        cst, idb = load_consts(h, sb)
        T["idb"] = idb
        idf = cst[:, C_ID:C_ID + 128]
        prm = sb("prm", [128, 8], F32)
        T["prm"] = prm
        load_param_cols(h, prm, 0, norm_ffn, 8)
        cw = sb("cw", [128, 4, 44], F32)
        nfb = sb("nfb", [128, D], F32)
        h.dma("sp", nfb[:], norm_final.partition_broadcast(128), [], ["nfb"])
        wup = sb("wup", [128, 8, F2], BF16)
        w_up_v = ffn_w_up.rearrange("(c p) n -> p c n", p=128)
        WQ = [(0, 4), (4, 10), (10, 16), (16, 22)]
        for qi, (c0_, c1_) in enumerate(WQ):
            for half in range(2):
                lo, hi = (half * 22 + c0_) * 128, (half * 22 + c1_) * 128
                h.dma("pool", wup[:, :, lo:hi], w_up_v[:, :, lo:hi], [], [("wup", qi)])
        if not wdn_loaded:
            for kc in range(22):
                h.dma("pool", wdn[:, kc, :], w_dn_v[:, kc, :], [], [("wdn", kc)])
        xts = [sb("xt0", [128, D], F32), sb("xt1", [128, D], F32)]
        alloc_rms(T, sb)
        ub = [sb("ub0", [128, 4, 160], F32), sb("ub1", [128, 4, 160], F32)]
        ucar = sb("ucar", [128, 44, 2], F32)
        cin = sb("cin", [128, 44, 16, 2], F32)
        cout = sb("cout", [128, 44, 32], F32)
        cstg = [sb("cstg%d" % i, [32, 512], F32) for i in range(2)]
        csto = [sb("csto%d" % i, [32, 512], F32) for i in range(2)]
        fss = [sb("fss%d" % i, [128, 1], F32) for i in range(2)]
        frs = [sb("frs%d" % i, [128, 1], F32) for i in range(2)]
        cc = [sb("cc0", [128, 4, 128], F32), sb("cc1", [128, 4, 128], F32)]
        g1 = [sb("g10", [128, 2, 128], F32), sb("g11", [128, 2, 128], F32)]
        g2t = [sb("g20", [128, 2, 128], F32), sb("g21", [128, 2, 128], F32)]
        actTs = [sb("actT%d" % i, [128, 22, 128], BF16) for i in range(2)]
        x2s = [sb("x20", [128, D], F32)] * 2
        T["psT"] = psb("psT", [128, 8, 128], BF16)
        PS = [psb("ps%d" % i, [128, 512], F32) for i in range(7)]
        h.memset("pool", ucar[:], 0.0, ["ucar"])
        cb_v = ffn_conv_b.rearrange("(o f) -> o f", o=1)
        for g4 in range(11):
            cg_, cgk = cstg[g4 % 2], "cstg%d" % (g4 % 2)
            h.dma("sp", cg_[0:3, :], ffn_conv_w[:, g4 * 512:(g4 + 1) * 512], [], [(cgk, 0)])
            h.dma("sp", cg_[3:4, :], cb_v[:, g4 * 512:(g4 + 1) * 512], [], [(cgk, 1)])
            for mi in range(4):
                h.tr(PS[4][:, mi * 4:(mi + 1) * 4], cg_[0:4, mi * 128:(mi + 1) * 128], idf[0:4, 0:4],
                     [cgk, "cst"], ["ps4"])
            h.cp("act", cw[:, :, 4 * g4:4 * g4 + 4], PS[4][:, 0:16].rearrange("p (m j) -> p j m", j=4),
                 ["ps4"], [("cw", g4)])
        if FILLERS:
            _pf2, _id2 = PS[6], idb
            P.filler = ((lambda e: e.matmul(_pf2[:, 0:128], lhsT=_id2[:], rhs=_id2[:], start=True, stop=True)), 70.0)
        for b in BLKS2:
            samp, nseq, L = (True, 16, 8) if b >= NPB else (False, 1, 128)
            j = b - NPB
            xt, xtn = xts[b % 2], "xt%d" % (b % 2)
            actT, actk = actTs[b % 2], "actT%d" % (b % 2)
            x2, x2k = x2s[0], "x20"
            yt, ytk = x2, x2k
            h.dma("sp", xt[:], x1_s[b * 128:(b + 1) * 128, :], [], [xtn])
            hT, hTk = rms_to_fm(h, T, xt, xtn, 0, b % 2)
            W = nseq * (L + 2)
            if samp:
                stc_v = st_conv.rearrange("q t f -> (q t) f")
                for g4 in range(11):
                    cg_, cgk = cstg[g4 % 2], "cstg%d" % (g4 % 2)
                    h.dma("sp", cg_[:], stc_v[:, g4 * 512:(g4 + 1) * 512], [], [cgk])
                    for mi in range(4):
                        m = 4 * g4 + mi
                        h.tr(PS[4][:, mi * 32:(mi + 1) * 32], cg_[0:32, mi * 128:(mi + 1) * 128], idf[0:32, 0:32],
                             [cgk, "cst"], ["ps4"])
                    h.cp("act", cin[:, 4 * g4:4 * g4 + 4, :, :].rearrange("p m q t -> p m (q t)"),
                         PS[4][:, 0:128].rearrange("p (m x) -> p m x", m=4), ["ps4"], [("cin", g4)])
            last_prompt = (b == NPB - 1)
            for gi in range(11):
                u, un = ub[gi % 2], "ub%d" % (gi % 2)
                uv = u[:, :, 0:W].rearrange("p m (s l) -> p m s l", s=nseq)
                pt, pn = PS[gi % 2], "ps%d" % (gi % 2)
                chunks = [2 * gi, 2 * gi + 1, 22 + 2 * gi, 23 + 2 * gi]
                for mi, m in enumerate(chunks):
                    for kc in range(8):
                        h.mm(pt[:, mi * 128:(mi + 1) * 128], wup[:, kc, m * 128:(m + 1) * 128], hT[:, kc, :],
                             kc == 0, kc == 7, [("wup", [q_ for q_, (a_, b_) in enumerate(WQ) if a_ <= (m % 22) < b_][0]), hTk], [pn])
                h.cp("act", uv[:, :, :, 2:L + 2], pt[:].rearrange("p (m s l) -> p m s l", m=4, s=nseq),
                     [pn], [un])
                for half in range(2):
                    m0 = chunks[2 * half]
                    if samp:
                        h.cp("pool", uv[:, 2 * half:2 * half + 2, :, 0:2], cin[:, m0:m0 + 2, :, :],
                             [("cin", m0 // 4)], [un])
                    else:
                        h.cp("pool", uv[:, 2 * half:2 * half + 2, 0, 0:2], ucar[:, m0:m0 + 2, :],
                             [("ucar", gi)], [un])
                for half in range(2):
                    m0 = chunks[2 * half]
                    if samp:
                        h.cp("pool", cout[:, m0:m0 + 2, :].rearrange("p m (q t) -> p m q t", q=16),
                             uv[:, 2 * half:2 * half + 2, :, 8:10], [un], [("cout", gi)])
                    else:
                        h.cp("pool", ucar[:, m0:m0 + 2, :], uv[:, 2 * half:2 * half + 2, 0, 128:130],
                             [un], [("ucar", gi)])
                        if last_prompt:
                            h.cp("pool", cout[:, m0:m0 + 2, 0:2], uv[:, 2 * half:2 * half + 2, 0, 128:130],
                                 [un], [("cout", gi)])
                c_, cn = cc[gi % 2], "cc%d" % (gi % 2)
                for mi, m in enumerate(chunks):
                    eng = "dve"
                    c4 = c_[:, mi, :].rearrange("p (s l) -> p s l", s=nseq)
                    h.act(c4, uv[:, mi, :, 2:L + 2], AF.Identity, [un, "cw"], [(cn, mi)],
                          scale=cw[:, 2, m:m + 1], bias=cw[:, 3, m:m + 1])
                    h.stt(eng, c4, uv[:, mi, :, 1:L + 1], cw[:, 1, m:m + 1], c4, ALU.mult, ALU.add,
                          [un, "cw", (cn, mi)], [(cn, mi)])
                    h.stt(eng, c4, uv[:, mi, :, 0:L], cw[:, 0, m:m + 1], c4, ALU.mult, ALU.add,
                          [un, "cw", (cn, mi)], [(cn, mi)])
                ga, gan = g1[gi % 2], "g1%d" % (gi % 2)
                gb_, gbn = g2t[gi % 2], "g2%d" % (gi % 2)
                gate = c_[:, 2:4, :]
                val = c_[:, 0:2, :]
                h.act(ga[:], gate, AF.Square, [(cn, 2), (cn, 3)], [gan])
                h.ts("dve", ga[:], ga[:], 0.044715, 1.0, ALU.mult, ALU.add, [gan], [gan])
                h.tt("pool", ga[:], ga[:], gate, ALU.mult, [gan, (cn, 2), (cn, 3)], [gan])
                h.act(gb_[:], ga[:], AF.Sigmoid, [gan], [gbn], scale=GELU_S)
                h.tt("pool", gb_[:], gb_[:], gate, ALU.mult, [gbn, (cn, 2), (cn, 3)], [gbn])
                h.tt("dve", actT[:, 2 * gi:2 * gi + 2, :], gb_[:], val, ALU.mult, [gbn, (cn, 0), (cn, 1)],
                     [(actk, gi)])
            if samp or last_prompt:
                ncol = 32 if samp else 2
                for g4 in range(11):
                    for mi in range(4):
                        m = 4 * g4 + mi
                        h.tr(PS[4][0:ncol, mi * 128:(mi + 1) * 128], cout[:, m, 0:ncol], idf, ["cout", "cst"],
                             ["ps4"])
                    co_, cok = csto[g4 % 2], "csto%d" % (g4 % 2)
                    h.cp("act", co_[0:ncol, :], PS[4][0:ncol, :], ["ps4"], [cok])
                    if samp:
                        h.dma("sp", o_conv_s.rearrange("q t f -> (q t) f")[:, g4 * 512:(g4 + 1) * 512],
                              co_[:], [cok], [], isout=True)
                    else:
                        h.dma("sp", o_conv_p[0][:, g4 * 512:(g4 + 1) * 512], co_[0:2, :], [cok], [], isout=True)
            for nh in range(2):
                pt, pn = PS[2 + nh], "ps%d" % (2 + nh)
                for kc in range(22):
                    h.mm(pt[:], actT[:, kc, :], wdn[:, kc, nh * 512:(nh + 1) * 512], kc == 0, kc == 21,
                         [(actk, kc // 2), ("wdn", kc)], [pn])
                h.tt("dve", x2[:, nh * 512:(nh + 1) * 512], pt[:], xt[:, nh * 512:(nh + 1) * 512], ALU.add,
                     [pn, xtn], [(x2k, nh)])
            ss2, rs2 = fss[b % 2], frs[b % 2]
            ssk, rsk = "fss%d" % (b % 2), "frs%d" % (b % 2)
            jk, jkk = T["xn"][b % 2], "xn%d" % (b % 2)
            h.act(jk[:], x2[:], AF.Square, [x2k], [jkk, ssk], accum_out=ss2[:])
            h.ts("dve", rs2[:], ss2[:], 1.0 / D, 1e-6, ALU.mult, ALU.add, [ssk], [rsk])
            h.act(rs2[:], rs2[:], AF.Sqrt, [rsk], [rsk])
            h.recip(rs2[:], rs2[:], [rsk], [rsk])
            h.stt("dve", yt[:], x2[:], rs2[:, 0:1], nfb[:], ALU.mult, ALU.mult, [x2k, rsk, "nfb"], [ytk])
            if samp:
                h.dma("sp", ysm.rearrange("s t d -> (s t) d"), yt[:], [ytk], [], isout=True)
            else:
                h.dma("sp", yp[b * 128:(b + 1) * 128, :], yt[:], [ytk], [], isout=True)
        P.emit()
    gx.close()
    gs.close()
    return nc


_CACHE = {}


def kernel(**inputs):
    f32 = lambda a: np.ascontiguousarray(np.asarray(a), dtype=np.float32)
    if "nc" not in _CACHE:
        nc = bass.Bass("TRN2", target_bir_lowering=False)
        build(nc)
        _CACHE["nc"] = nc
    nc = _CACHE["nc"]
    cst = make_consts()
    shared = {
        "cst": cst,
        "norm_mix": f32(inputs["norm_mix"][0]), "w_in": f32(inputs["w_in"][0]),
        "mu_shift": f32(inputs["mu_shift"][0]), "rwkv_w0": f32(inputs["rwkv_w0"][0]),
        "rwkv_w2": f32(inputs["rwkv_w2"][0]), "rwkv_a0": f32(inputs["rwkv_a0"][0]),
        "rwkv_a2": f32(inputs["rwkv_a2"][0]), "rwkv_g2": f32(inputs["rwkv_g2"][0]),
        "rwkv_k_k": f32(inputs["rwkv_k_k"][0]), "rwkv_k_a": f32(inputs["rwkv_k_a"][0]),
        "rwkv_r_k": f32(inputs["rwkv_r_k"][0]).reshape(512), "rwkv_ln_w": f32(inputs["rwkv_ln_w"][0]),
        "rwkv_ln_b": f32(inputs["rwkv_ln_b"][0]), "gla_wg2": f32(inputs["gla_wg2"][0]),
        "gla_bg": f32(inputs["gla_bg"][0]), "gla_norm_w": f32(inputs["gla_norm_w"][0]),
        "w_out_a": f32(inputs["w_out_a"][0]), "w_out_b": f32(inputs["w_out_b"][0]),
        "w_o": f32(inputs["w_o"][0]), "norm_ffn": f32(inputs["norm_ffn"][0]),
        "ffn_w_up": f32(inputs["ffn_w_up"][0]), "ffn_conv_w": f32(inputs["ffn_conv_w"][0]),
        "ffn_conv_b": f32(inputs["ffn_conv_b"][0]), "ffn_w_down": f32(inputs["ffn_w_down"][0]),
        "norm_final": f32(inputs["norm_final"]),
    }
    in_maps = []
    for c in range(NCORES):
        m = dict(shared)
        sl = slice(16 * c, 16 * c + 16)
        m["xp"] = f32(inputs["x_prompt"][c])
        m["xs"] = f32(inputs["x_sample"][sl])
        m["st_shift"] = f32(inputs["state_rwkv_shift"][0, sl])
        m["st_wkv"] = f32(inputs["state_rwkv_wkv"][0, sl])
        m["st_gla"] = f32(inputs["state_gla"][0, sl])
        m["st_conv"] = f32(inputs["state_ffn_conv"][0, sl])
        in_maps.append(m)
    res = run_bass_kernel_spmd(nc, in_maps, core_ids=list(range(NCORES)))
    R = res.results
    cat = lambda k: np.concatenate([np.asarray(r[k]) for r in R], axis=0)
    y_p = np.stack([np.asarray(r["yp"]) for r in R], axis=0)
    y_s = cat("ys")
    outs = (
        y_p, y_s,
        cat("o_shift_p")[None], cat("o_wkv_p")[None], cat("o_gla_p")[None], cat("o_conv_p")[None],
        cat("o_shift_s")[None], cat("o_wkv_s")[None], cat("o_gla_s")[None], cat("o_conv_s")[None],
    )
    return tuple(np.ascontiguousarray(o, dtype=np.float32) for o in outs)
```

```python
import contextlib
import numpy as np
import concourse.bass as bass
import concourse.mybir as mybir
from concourse.bass_utils import run_bass_kernel_spmd

F32 = mybir.dt.float32
BF16 = mybir.dt.bfloat16
AF = mybir.ActivationFunctionType
ALU = mybir.AluOpType
AX = mybir.AxisListType

NCORES = 8
D = 1024
NPB = 16
NSB = 4
NBLK = NPB + NSB
SHIFT = 1792
INC = 5392
FH = 2816
F2 = 5632
GLA0 = 1792
GATE0 = 3344
C0 = -0.6065306597126334
GELU_S = 1.5957691216057308

ENGS = ("pe", "act", "dve", "pool", "sp")
MAXOPS = None
SCHED = True
VERBOSE = False
PROGS = []
WINDOW = 300
SCHED_LAT = 300.0
ODEP_LAT = 0.0
SCHED_BIAS = 1.0
PE_SCALE = 1.0
DVE_SCALE = 1.0
ENG_SCALE = {}
FILL_MIN = 250.0
FILL_MARGIN = 80.0
FILL_MAX = 24
FILLERS = False
LINES = []
TAGS = []


class Op:
    __slots__ = ("eng", "fn", "reads", "writes", "dma", "deps", "sig", "idx",
                 "dsem", "dval", "dprev", "isout", "mm", "odeps", "cost", "start", "fin")

    def __init__(self, eng, fn, reads, writes, dma, isout, mm, cost=300.0):
        self.odeps = []
        self.cost = cost
        self.eng = eng
        self.fn = fn
        self.reads = reads
        self.writes = writes
        self.dma = dma
        self.deps = []
        self.sig = None
        self.dsem = None
        self.dval = None
        self.dprev = None
        self.isout = isout
        self.mm = mm


def _norm(k):
    return k if isinstance(k, tuple) else (k, None)


class Prog:
    def __init__(self, nc, semstack, tag, n_dma_sems=6):
        self.nc = nc
        self.ops = []
        self.n_dma_sems = n_dma_sems
        self.st = {}
        self.semstack = semstack
        self.tag = tag
        self.filler = None

    @staticmethod
    def _conf(a, b):
        return a is None or b is None or a == b

    def add(self, eng, fn, reads=(), writes=(), dma=False, isout=False, mm=False, cost=300.0):
        op = Op(eng, fn, [_norm(k) for k in reads], [_norm(k) for k in writes], dma, isout, mm, cost)
        odeps = {}
        op.idx = len(self.ops)
        if MAXOPS is not None:
            import sys as _s
            f = _s._getframe(1)
            while f is not None and f.f_code.co_name != "build":
                f = f.f_back
            LINES.append(f.f_lineno if f is not None else -1)
            TAGS.append(f.f_locals.get("b", -1) if f is not None else -1)
        deps = {}
        for (name, sub) in op.reads:
            s = self.st.setdefault(name, {"w": {}, "r": {}})
            for ws, wop in s["w"].items():
                if self._conf(ws, sub):
                    deps[wop.idx] = wop
        for (name, sub) in op.writes:
            s = self.st.setdefault(name, {"w": {}, "r": {}})
            for ws, wop in s["w"].items():
                if self._conf(ws, sub):
                    if not (op.mm and wop.mm):
                        deps[wop.idx] = wop
                    else:
                        odeps[wop.idx] = wop
            for rs, rops in s["r"].items():
                if self._conf(rs, sub):
                    for rop in rops:
                        deps[rop.idx] = rop
        for (name, sub) in op.reads:
            self.st[name]["r"].setdefault(sub, []).append(op)
        for (name, sub) in op.writes:
            s = self.st[name]
            if sub is None:
                s["w"] = {None: op}
                s["r"] = {}
            else:
                s["w"][sub] = op
                s["r"][sub] = []
        deps.pop(op.idx, None)
        op.deps = list(deps.values())
        op.odeps = [o for k, o in odeps.items() if k not in deps]
        self.ops.append(op)
        return op

    def schedule(self, window=None):
        window = window or WINDOW
        ops = self.ops
        n = len(ops)
        for op in ops:
            if not op.dma:
                op.cost = op.cost * ENG_SCALE.get(op.eng, 1.0)
        ndep = [0] * n
        users = [[] for _ in range(n)]
        truedep = set()
        for op in ops:
            for d in op.deps:
                truedep.add((op.idx, d.idx))
            ds = {d.idx for d in op.deps} | {d.idx for d in op.odeps}
            ndep[op.idx] = len(ds)
            for d in ds:
                users[d].append(op.idx)
        ready_t = [0.0] * n
        per = {e: [op.idx for op in ops if op.eng == e] for e in ENGS}
        head = {e: 0 for e in ENGS}
        done = [False] * n
        t_e = {e: 0.0 for e in ENGS}
        order = []
        remaining = n
        LAT = SCHED_LAT
        while remaining:
            best = None
            for e in ENGS:
                lst = per[e]
                hp = head[e]
                while hp < len(lst) and done[lst[hp]]:
                    hp += 1
                head[e] = hp
                if hp >= len(lst):
                    continue
                cnt = 0
                k = hp
                cand = None
                rdy = []
                while k < len(lst) and cnt < window:
                    i = lst[k]
                    if not done[i]:
                        cnt += 1
                        if ndep[i] == 0:
                            st = max(t_e[e], ready_t[i])
                            key = (st + SCHED_BIAS * (cnt - 1), i)
                            rdy.append((st, i))
                            if cand is None or key < cand[0]:
                                cand = (key, i, st)
                    k += 1
                if cand is not None:
                    bst = cand[2]
                    for (st, i) in rdy:
                        if i != cand[1] and st + (60.0 if ops[i].dma else ops[i].cost) <= bst:
                            cand = ((st, i), i, st)
                            break
                    if best is None or cand[0] < best[0]:
                        best = (cand[0], cand[1], cand[2], e)
            assert best is not None, "scheduler deadlock"
            _, i, st, e = best
            op = ops[i]
            op.start = st
            if op.dma:
                t_e[e] = st + 60.0
            else:
                t_e[e] = st + op.cost
            op.fin = st + op.cost
            done[i] = True
            remaining -= 1
            order.append(op)
            for u in users[i]:
                ndep[u] -= 1
                lat_ = LAT if ((u, i) in truedep) else ODEP_LAT
                if ready_t[u] < op.fin + lat_:
                    ready_t[u] = op.fin + lat_
        if self.filler is not None:
            fn, fcost = self.filler
            out = []
            pe_end = None
            nf = 0
            for op in order:
                if op.eng == "pe":
                    if pe_end is not None:
                        gap = op.start - pe_end
                        if gap > FILL_MIN:
                            k = min(FILL_MAX, int((gap - FILL_MARGIN) / fcost))
                            for _ in range(max(0, k)):
                                f = Op("pe", fn, [], [], False, False, True, fcost)
                                f.idx = -1
                                f.start = pe_end
                                f.fin = pe_end + fcost
                                out.append(f)
                                nf += 1
                    pe_end = op.start + op.cost
                out.append(op)
            order = out
            if VERBOSE:
                print("[sched %s] fillers inserted: %d" % (self.tag, nf), flush=True)
        self.ops = order
        self.est = max(op.fin for op in order) if order else 0.0
        if VERBOSE:
            PROGS.append(self)
            busy = {e: sum(o.cost for o in order if o.eng == e and not o.dma) for e in ENGS}
            print("[sched %s] n=%d est=%.1f us busy(us): %s" % (
                self.tag, n, self.est / 1e3, " ".join("%s=%.0f" % (e, busy[e] / 1e3) for e in ENGS)), flush=True)

    def emit(self):
        nc = self.nc
        if MAXOPS is not None:
            self.ops = self.ops[:MAXOPS]
        if SCHED:
            self.schedule()
        ops = self.ops
        needed = set()
        for op in ops:
            for d in op.deps:
                needed.add(d.idx)
        cnt = {e: 0 for e in ENGS}
        for op in ops:
            if not op.dma and op.idx in needed:
                cnt[op.eng] += 1
                op.sig = cnt[op.eng]
        dcount = {e: 0 for e in ENGS}
        last_on_slot = {}
        for op in ops:
            if not op.dma:
                continue
            j = dcount[op.eng]
            dcount[op.eng] += 1
            slot = j % self.n_dma_sems
            op.dsem = (op.eng, slot)
            op.dval = 16 * (j // self.n_dma_sems + 1)
            op.dprev = last_on_slot.get(op.dsem)
            last_on_slot[op.dsem] = op
        out_ops = [op for op in ops if op.dma and op.isout]
        per_eng = {e: [op for op in ops if op.eng == e] for e in ENGS}
        es = self.semstack
        csem = {e: es.enter_context(nc.semaphore("cs%s_%s" % (self.tag, e)))
                for e in ENGS if e != "sp"}
        dsem = {}
        for e in ENGS:
            for s in range(min(self.n_dma_sems, dcount[e])):
                dsem[(e, s)] = es.enter_context(nc.semaphore("ds%s_%s%d" % (self.tag, e, s)))

        def run_engine(e, eng):
            known = {}

            def wait(key, sem, val):
                if known.get(key, 0) >= val:
                    return
                known[key] = val
                eng.wait_ge(sem, val)

            for op in per_eng[e]:
                for d in op.deps:
                    if d.dma:
                        wait(d.dsem, dsem[d.dsem], d.dval)
                    else:
                        wait(d.eng, csem[d.eng], d.sig)
                if op.dma and op.dprev is not None:
                    wait(op.dsem, dsem[op.dsem], op.dprev.dval)
                ins = op.fn(eng)
                if op.dma:
                    ins.then_inc(dsem[op.dsem], 16)
                elif op.sig is not None:
                    ins.then_inc(csem[e], 1)
            if e == "sp":
                for op in out_ops:
                    wait(op.dsem, dsem[op.dsem], op.dval)
                for key, op in last_on_slot.items():
                    wait(op.dsem, dsem[op.dsem], op.dval)

        with nc.Block() as block:
            @block.sync
            def _(eng):
                run_engine("sp", eng)

            @block.tensor
            def _(eng):
                run_engine("pe", eng)

            @block.scalar
            def _(eng):
                run_engine("act", eng)

            @block.vector
            def _(eng):
                run_engine("dve", eng)

            @block.gpsimd
            def _(eng):
                run_engine("pool", eng)


def _fsz(ap):
    n = 1
    for d in ap.shape[1:]:
        n *= int(d)
    return n


class H:
    def __init__(self, P):
        self.P = P

    def act(self, out, in_, func, r, w, **kw):
        self.P.add("act", lambda e: e.activation(out=out, in_=in_, func=func, **kw), r, w,
                   cost=220.0 + 1.05 * _fsz(out))

    def tt(self, eng, out, in0, in1, op, r, w):
        self.P.add(eng, lambda e: e.tensor_tensor(out=out, in0=in0, in1=in1, op=op), r, w,
                   cost=(100.0 + 1.05 * _fsz(out)) if eng == "dve" else (160.0 + 2.1 * _fsz(out)))

    def ts(self, eng, out, in0, s1, s2, op0, op1, r, w):
        if op1 is None:
            self.P.add(eng, lambda e: e.tensor_scalar(out=out, in0=in0, scalar1=s1, scalar2=None, op0=op0), r, w,
                       cost=100.0 + 1.05 * _fsz(out))
        else:
            self.P.add(eng, lambda e: e.tensor_scalar(out=out, in0=in0, scalar1=s1, scalar2=s2, op0=op0, op1=op1), r, w,
                       cost=100.0 + 1.05 * _fsz(out))

    def stt(self, eng, out, in0, scalar, in1, op0, op1, r, w):
        self.P.add(eng, lambda e: e.scalar_tensor_tensor(out=out, in0=in0, scalar=scalar, in1=in1, op0=op0, op1=op1), r, w,
                   cost=100.0 + 1.05 * _fsz(out))

    def cp(self, eng, out, in_, r, w):
        if eng == "act":
            self.P.add("act", lambda e: e.activation(out=out, in_=in_, func=AF.Copy), r, w,
                       cost=220.0 + 1.05 * _fsz(out))
        else:
            self.P.add(eng, lambda e: e.tensor_copy(out=out, in_=in_), r, w,
                       cost=(100.0 + 1.05 * _fsz(out)) if eng == "dve" else (160.0 + 2.1 * _fsz(out)))

    def memset(self, eng, ap, val, w):
        self.P.add(eng, lambda e: e.memset(ap, val), [], w, cost=160.0 + 1.0 * _fsz(ap))

    def recip(self, out, in_, r, w):
        self.P.add("dve", lambda e: e.reciprocal(out=out, in_=in_), r, w, cost=100.0 + 1.05 * _fsz(out))

    def scan(self, out, d0, d1, r, w):
        self.P.add("dve", lambda e: e.tensor_tensor_scan(out=out, data0=d0, data1=d1, initial=0.0,
                                                         op0=ALU.mult, op1=ALU.add), r, w,
                   cost=100.0 + 2.1 * _fsz(out))

    def rsum(self, out, in_, r, w):
        self.P.add("dve", lambda e: e.tensor_reduce(out=out, in_=in_, axis=AX.X, op=ALU.add), r, w,
                   cost=100.0 + 1.05 * _fsz(in_))

    def mm(self, out, lhsT, rhs, start, stop, r, w, tp=None):
        c = (max(64.0, float(_fsz(rhs))) / 2.0 + 16.0) * PE_SCALE
        if lhsT.dtype == F32:
            c *= 4.0
        if tp is None:
            self.P.add("pe", lambda e: e.matmul(out, lhsT=lhsT, rhs=rhs, start=start, stop=stop), r, w, mm=True,
                       cost=c)
        else:
            self.P.add("pe", lambda e: e.matmul(out, lhsT=lhsT, rhs=rhs, start=start, stop=stop,
                                                tile_position=tp), r, w, mm=True, cost=c)

    def tr(self, out, in_, ident, r, w):
        self.P.add("pe", lambda e: e.transpose(out=out, in_=in_, identity=ident), r, w, mm=True, cost=110.0)

    def dma(self, q, out, in_, r, w, isout=False, slow=False):
        nbytes = 1
        for d in out.shape:
            nbytes *= int(d)
        c = 2500.0 + 4.0 * nbytes / 150.0
        if slow:
            self.P.add(q, lambda e: e.dma_start(out=out, in_=in_, allow_slow_non_contiguous=True), r, w,
                       dma=True, isout=isout, cost=c)
        else:
            self.P.add(q, lambda e: e.dma_start(out=out, in_=in_), r, w, dma=True, isout=isout, cost=c)


def bc(ap, shape):
    a = ap
    while len(a.shape) < len(shape):
        a = a.unsqueeze(len(a.shape))
    return a.to_broadcast(list(shape))


C_ID = 0
C_BO = 128
C_RST = 256
C_VS = 384
C_ONE = 512
C_M4 = 640
C_SL = 1152
C_IU = 1280
NCST = 1408


def make_consts():
    c = np.zeros((128, NCST), np.float32)
    i = np.arange(128)
    same = (i[:, None] // 32) == (i[None, :] // 32)
    su = (same & (i[:, None] < i[None, :])).astype(np.float32)
    iu = (same & (i[:, None] <= i[None, :])).astype(np.float32)
    sl = (same & (i[:, None] > i[None, :])).astype(np.float32)
    c[:, C_ID:C_ID + 128] = np.eye(128, dtype=np.float32)
    c[:, C_BO:C_BO + 128] = ((i[:, None] // 64) == (i[None, :] // 64)).astype(np.float32)
    c[:, C_RST:C_RST + 128] = (i[None, :] % 32 != 0).astype(np.float32)
    c[:, C_VS:C_VS + 128] = (i[None, :] % 32 < 8).astype(np.float32)
    c[:, C_ONE:C_ONE + 128] = 1.0
    c[:, C_M4:C_M4 + 512] = np.concatenate([su, iu, su, iu], axis=1)
    c[:, C_SL:C_SL + 128] = sl
    c[:, C_IU:C_IU + 128] = iu
    return c


PR_G = 0
PR_MU = 8
PR_W0 = 22
PR_A0 = 26
PR_KK = 30
PR_KA = 34
PR_RK = 38
PR_LW = 42
PR_LB = 46
PR_BG = 50
PR_NW = 52
PR_OMKA = 53
NPRM = 64


def build(nc, dbg=None, phases="abc", blocks=None):
    BLKS = list(range(NBLK)) if blocks is None else list(blocks)
    BLKS2 = [b for b in BLKS if b <= NPB]
    gs = contextlib.ExitStack()

    def din(name, shape, dt=F32):
        return nc.dram_tensor(name, list(shape), dt, kind="ExternalInput").ap()

    def dout(name, shape):
        return nc.dram_tensor(name, list(shape), F32, kind="ExternalOutput").ap()

    def dscr(name, shape, dt):
        return nc.dram_tensor(name, list(shape), dt, kind="Internal").ap()

    xp = din("xp", [2048, D])
    xsm = din("xs", [16, 8, D])
    st_shift = din("st_shift", [16, SHIFT])
    st_wkv = din("st_wkv", [16, 8, 64, 64])
    st_gla = din("st_gla", [16, 4, 64, 128])
    st_conv = din("st_conv", [16, 2, F2])
    cst_d = din("cst", [128, NCST])
    norm_mix = din("norm_mix", [D])
    w_in = din("w_in", [D, INC])
    mu_shift = din("mu_shift", [SHIFT])
    rwkv_w0 = din("rwkv_w0", [512])
    rwkv_w2 = din("rwkv_w2", [64, 512])
    rwkv_a0 = din("rwkv_a0", [512])
    rwkv_a2 = din("rwkv_a2", [64, 512])
    rwkv_g2 = din("rwkv_g2", [128, 512])
    rwkv_k_k = din("rwkv_k_k", [512])
    rwkv_k_a = din("rwkv_k_a", [512])
    rwkv_r_k = din("rwkv_r_k", [512])
    rwkv_ln_w = din("rwkv_ln_w", [512])
    rwkv_ln_b = din("rwkv_ln_b", [512])
    gla_wg2 = din("gla_wg2", [16, 256])
    gla_bg = din("gla_bg", [256])
    gla_norm_w = din("gla_norm_w", [128])
    w_out_a = din("w_out_a", [512, D])
    w_out_b = din("w_out_b", [512, D])
    w_o = din("w_o", [D, D])
    norm_ffn = din("norm_ffn", [D])
    ffn_w_up = din("ffn_w_up", [D, F2])
    ffn_conv_w = din("ffn_conv_w", [3, F2])
    ffn_conv_b = din("ffn_conv_b", [F2])
    ffn_w_down = din("ffn_w_down", [FH, D])
    norm_final = din("norm_final", [D])

    yp = dout("yp", [2048, D])
    ysm = dout("ys", [16, 8, D])
    o_shift_p = dout("o_shift_p", [1, SHIFT])
    o_wkv_p = dout("o_wkv_p", [1, 8, 64, 64])
    o_gla_p = dout("o_gla_p", [1, 4, 64, 128])
    o_conv_p = dout("o_conv_p", [1, 2, F2])
    o_shift_s = dout("o_shift_s", [16, SHIFT])
    o_wkv_s = dout("o_wkv_s", [16, 8, 64, 64])
    o_gla_s = dout("o_gla_s", [16, 4, 64, 128])
    o_conv_s = dout("o_conv_s", [16, 2, F2])

    og_s = dscr("og_s", [NBLK, 128, 512], BF16)
    ob_s = dscr("ob_s", [NBLK, 128, 512], BF16)
    x1_s = dscr("x1_s", [NBLK * 128, D], F32)

    dbg_out = None
    if dbg is not None:
        dbg_out = dout("dbg", dbg["shape"])

    def geom(b):
        return (b >= NPB, 4, 32) if b >= NPB else (False, 1, 128)

    def load_consts(h, sb, q="sp"):
        cst = sb("cst", [128, NCST], F32)
        h.dma(q, cst[:], cst_d, [], ["cst"])
        idb = sb("idb", [128, 128], BF16)
        h.cp("dve", idb[:], cst[:, C_ID:C_ID + 128], ["cst"], ["idb"])
        return cst, idb

    def load_x_block(h, b, xt, xtn, packed=False):
        samp, nseq, L = geom(b)
        if packed and samp:
            h.dma("sp", xt[:], xsm.rearrange("s t d -> (s t) d"), [], [xtn])
        elif not samp:
            h.dma("sp", xt[:], xp[b * 128:(b + 1) * 128, :], [], [xtn])
        else:
            j = b - NPB
            h.memset("pool", xt[:], 0.0, [xtn])
            for q in range(4):
                h.dma("sp", xt[32 * q:32 * q + 8, :], xsm[4 * j + q], [], [(xtn, q)])

    def rms_to_fm(h, T, xt, xtn, gcol, par=0):
        xp_ = par if len(T["xn"]) > 1 else 0
        xn, ss, rstd, hT = T["xn"][xp_], T["ss"][par], T["rstd"][par], T["hT"][par]
        xnk, ssk, rsk, hTk = "xn%d" % xp_, "ss%d" % par, "rstd%d" % par, "hT%d" % par
        psT, idb, prm = T["psT"], T["idb"], T["prm"]
        h.act(xn[:], xt[:], AF.Square, [xtn], [xnk, ssk], accum_out=ss[:])
        h.ts("dve", rstd[:], ss[:], 1.0 / D, 1e-6, ALU.mult, ALU.add, [ssk], [rsk])
        h.act(rstd[:], rstd[:], AF.Sqrt, [rsk], [rsk])
        h.recip(rstd[:], rstd[:], [rsk], [rsk])
        h.act(xn[:], xt[:], AF.Copy, [xtn, rsk], [xnk], scale=rstd[:, 0:1])
        for c in range(8):
            h.tr(psT[:, c, :], xn[:, c * 128:(c + 1) * 128], idb[:], [xnk, "idb"], [("psT", c)])
        h.tt("dve", hT[:], psT[:], bc(prm[:, gcol:gcol + 8], [128, 8, 128]), ALU.mult,
             ["psT", "prm"], [hTk])
        return hT, hTk

    def alloc_rms(T, sb, nxn=2):
        T["xn"] = [sb("xn%d" % i, [128, D], BF16) for i in range(nxn)]
        T["ss"] = [sb("ss%d" % i, [128, 1], F32) for i in range(2)]
        T["rstd"] = [sb("rstd%d" % i, [128, 1], F32) for i in range(2)]
        T["hT"] = [sb("hT%d" % i, [128, 8, 128], BF16) for i in range(2)]

    def load_param_cols(h, prm, col, src, n):
        h.dma("sp", prm[:, col:col + n], src.rearrange("(c p) -> p c", p=128), [], [("prm", col)], slow=True)

    with contextlib.ExitStack() as ph:
      if "a" in phases:
        P = Prog(nc, gs, "a")
        h = H(P)

        def sb(name, shape, dt):
            return ph.enter_context(nc.sbuf_tensor("a_" + name, list(shape), dt))

        def psb(name, shape, dt):
            return ph.enter_context(nc.psum_tensor("a_" + name, list(shape), dt))

        T = {}
        cst, idb = load_consts(h, sb)
        T["idb"] = idb
        prm = sb("prm", [128, NPRM], F32)
        T["prm"] = prm
        load_param_cols(h, prm, PR_G, norm_mix, 8)
        load_param_cols(h, prm, PR_MU, mu_shift, 14)
        load_param_cols(h, prm, PR_W0, rwkv_w0, 4)
        load_param_cols(h, prm, PR_A0, rwkv_a0, 4)
        load_param_cols(h, prm, PR_KK, rwkv_k_k, 4)
        load_param_cols(h, prm, PR_KA, rwkv_k_a, 4)
        load_param_cols(h, prm, PR_RK, rwkv_r_k, 4)
        load_param_cols(h, prm, PR_LW, rwkv_ln_w, 4)
        load_param_cols(h, prm, PR_LB, rwkv_ln_b, 4)
        load_param_cols(h, prm, PR_BG, gla_bg, 2)
        load_param_cols(h, prm, PR_NW, gla_norm_w, 1)
        h.ts("dve", prm[:, PR_BG:PR_BG + 2], prm[:, PR_BG:PR_BG + 2], -1.0, None, ALU.mult, None,
             [("prm", PR_BG)], [("prm", PR_BG)])
        h.ts("dve", prm[:, PR_OMKA:PR_OMKA + 4], prm[:, PR_KA:PR_KA + 4], -1.0, 1.0, ALU.mult, ALU.add,
             [("prm", PR_KA)], [("prm", PR_OMKA)])

        NA1 = GATE0
        win = sb("win", [128, 8, NA1], BF16)
        w_in_v = w_in.rearrange("(c p) n -> p c n", p=128)
        WG = [(0, 512), (512, 1152), (1152, 1792), (1792, 2560), (2560, NA1)]
        for gi_, (lo, hi) in enumerate(WG):
            h.dma("pool", win[:, :, lo:hi], w_in_v[:, :, lo:hi], [], [("win", gi_)])

        def wkey(c0, c1):
            ks = [("win", gi_) for gi_, (lo, hi) in enumerate(WG) if lo < c1 and c0 < hi]
            return ks
        w2a2 = sb("w2a2", [128, 512], BF16)
        h.dma("pool", w2a2[0:64, :], rwkv_w2, [], [("w2a2", 0)])
        h.dma("pool", w2a2[64:128, :], rwkv_a2, [], [("w2a2", 1)])
        g2 = sb("g2", [128, 512], BF16)
        h.dma("pool", g2[:], rwkv_g2, [], ["g2"])
        wg2 = sb("wg2", [16, 256], BF16)
        h.dma("pool", wg2[:], gla_wg2, [], ["wg2"])

        xt = sb("xt", [128, D], F32)
        alloc_rms(T, sb, nxn=1)
        prw = sb("prw", [128, 14, 132], F32)
        lastc = sb("lastc", [128, 14, 1], F32)
        xs = sb("xs", [128, 14, 128], F32)
        Fm = [sb("F%d" % i, [128, 4, 128], F32) for i in range(10)]
        gTs = [sb("gT%d" % i, [128, 4, 128], F32) for i in range(2)]
        bons = [sb("bon%d" % i, [128, 4, 128], F32) for i in range(2)]
        ARs = [sb("AR%d" % i, [128, 4, 2, 128], BF16) for i in range(2)]
        Hm = [sb("Hb%d" % i, [128, 4, 128], BF16) for i in range(8)]
        tok = [sb("tok%d" % i, [128, 512], BF16) for i in range(4)]
        SCH = sb("SCH", [128, 8, 4, 128], BF16)
        INV = [sb("INV%d" % i, [128, 8, 128], BF16) for i in range(8)]
        lora_in = sb("lora_in", [128, 128], BF16)
        slg = sb("slg", [128, 128], BF16)
        Zbf = sb("Zbf", [128, 512], BF16)
        Yf = sb("Yf", [128, 512], F32)
        WTbf = sb("WTbf", [128, 4, 128], BF16)
        Ubf = sb("Ubf", [128, 512], BF16)
        Pst = sb("Pst", [128, 4, 64], F32)
        Pbf = sb("Pbf", [128, 4, 64], BF16)
        o1 = sb("o1", [128, 512], F32)
        oo = sb("oo", [128, 512], F32)
        stat = sb("stat", [128, 64], F32)
        ogbf = sb("ogbf", [128, 4, 128], BF16)
        obbf = sb("obbf", [128, 4, 128], BF16)
        Sin = sb("Sin", [64, 8, 64], F32)
        shs = [sb("shs%d" % i, [4, 512], F32) for i in range(2)]
        shf = sb("shf", [128, 14, 4], F32)
        sho = [sb("sho%d" % i, [4, 512], F32) for i in range(2)]
        lga = sb("lga", [16, 128], BF16)
        Sg = sb("Sg", [128, 2, 128], F32)
        Sgbf = sb("Sgbf", [128, 4, 2, 128], BF16)
        SgIn = sb("SgIn", [128, 4, 2, 128], F32)
        xg = sb("xg", [128, 12, 128], F32)
        Gf = [sb("G%d" % i, [128, 2, 128], F32) for i in range(6)]
        silu = sb("Gs", [128, 4, 128], F32)
        qdT, kiT, keT = [sb("Gh%d" % i, [128, 2, 128], BF16) for i in range(3)]
        vgbf = sb("Gv", [128, 4, 128], BF16)
        GS = sb("Ggs", [128, 4, 128], BF16)
        Vgtok = sb("Gt0", [128, 512], BF16)
        Ketok = sb("Gt1", [128, 256], BF16)
        o1g = sb("o1g", [128, 512], F32)
        oog = sb("oog", [128, 512], F32)
        statg = sb("statg", [128, 16], F32)

        T["psT"] = psb("psT", [128, 8, 128], BF16)
        psT = T["psT"]
        PS = [psb("ps%d" % i, [128, 512], F32) for i in range(7)]

        if VERBOSE:
            print("[A1] sbuf remaining after alloc:", nc.sbuf_bytes_remaining, flush=True)
        idf = cst[:, C_ID:C_ID + 128]
        bones = cst[:, C_BO:C_BO + 128]
        rstm = cst[:, C_RST:C_RST + 128]
        m4 = cst[:, C_M4:C_M4 + 512]
        msl = cst[:, C_SL:C_SL + 128]
        miu = cst[:, C_IU:C_IU + 128]

        h.memset("pool", Pst[:], 0.0, ["Pst"])
        h.memset("pool", Pbf[:], 0.0, ["Pbf"])
        h.memset("pool", Sg[:], 0.0, ["Sg"])
        h.memset("pool", lastc[:], 0.0, ["lastc"])

        def v4(ap, nseq, L):
            return ap.rearrange("p m (s l) -> p m s l", s=nseq)

        for b in BLKS:
            samp, nseq, L = geom(b)
            j = b - NPB
            valid = cst[:, (C_VS if samp else C_ONE):(C_VS if samp else C_ONE) + 128]
            load_x_block(h, b, xt, "xt")
            hT, hTk = rms_to_fm(h, T, xt, "xt", PR_G, b % 2)

            W = nseq * (L + 1)
            pv = prw[:, :, 0:W].rearrange("p m (s l) -> p m s l", s=nseq)
            for gi in range(4):
                ms = list(range(4 * gi, min(4 * gi + 4, 14)))
                pt = PS[5 + gi % 2]
                pn = "ps%d" % (5 + gi % 2)
                for mi, m in enumerate(ms):
                    for kc in range(8):
                        h.mm(pt[:, mi * 128:(mi + 1) * 128], win[:, kc, m * 128:(m + 1) * 128], hT[:, kc, :],
                             kc == 0, kc == 7, wkey(m * 128, (m + 1) * 128) + [hTk], [pn])
                nm = len(ms)
                h.cp("act", pv[:, ms[0]:ms[0] + nm, :, 1:L + 1],
                     pt[:, 0:nm * 128].rearrange("p (m s l) -> p m s l", m=nm, s=nseq),
                     [pn], ["prw"])
            gcols = [GLA0 + 128 * i for i in range(8)] + [GLA0 + 1040 + 128 * i for i in range(4)]
            for gi in range(3):
                pt, pn = PS[5 + gi % 2], "ps%d" % (5 + gi % 2)
                for mi in range(4):
                    c0 = gcols[4 * gi + mi]
                    for kc in range(8):
                        h.mm(pt[:, mi * 128:(mi + 1) * 128], win[:, kc, c0:c0 + 128], hT[:, kc, :],
                             kc == 0, kc == 7, wkey(c0, c0 + 128) + [hTk], [pn])
                h.cp("act", xg[:, 4 * gi:4 * gi + 4, :], pt[:].rearrange("p (c l) -> p c l", c=4), [pn], [("xg", gi)])
            for kc in range(8):
                h.mm(PS[6][0:16, 0:128], win[:, kc, GLA0 + 1024:GLA0 + 1040], hT[:, kc, :], kc == 0, kc == 7,
                     wkey(GLA0 + 1024, GLA0 + 1040) + [hTk], ["ps6"])
            h.cp("act", lga[:], PS[6][0:16, 0:128], ["ps6"], ["lga"])
            if not samp:
                h.cp("pool", pv[:, :, 0, 0:1], lastc[:], ["lastc"], ["prw"])
            else:
                for g4 in range(4):
                    ms = list(range(4 * g4, min(4 * g4 + 4, 14)))
                    nm = len(ms)
                    sh_, shk = shs[g4 % 2], "shs%d" % (g4 % 2)
                    h.dma("sp", sh_[:, 0:nm * 128], st_shift[4 * j:4 * j + 4, ms[0] * 128:(ms[0] + nm) * 128], [], [shk])
                    for mi, m in enumerate(ms):
                        h.tr(PS[5][:, mi * 4:(mi + 1) * 4], sh_[0:4, mi * 128:(mi + 1) * 128], idf[0:4, 0:4],
                             [shk, "cst"], ["ps5"])
                    h.cp("act", pv[:, ms[0]:ms[0] + nm, :, 0],
                         PS[5][:, 0:nm * 4].rearrange("p (m s) -> p m s", m=nm), ["ps5"], ["prw"])
            last_prompt = (b == NPB - 1)
            if samp or last_prompt:
                ncol = 4 if samp else 1
                if samp:
                    h.cp("pool", shf[:, :, 0:4], pv[:, :, :, 8], ["prw"], ["shf"])
                else:
                    h.cp("pool", shf[:, :, 0:1], pv[:, :, 0, 128:129], ["prw"], ["shf"])
                for g4 in range(4):
                    ms = list(range(4 * g4, min(4 * g4 + 4, 14)))
                    for mi, m in enumerate(ms):
                        h.tr(PS[6][0:ncol, mi * 128:(mi + 1) * 128], shf[:, m, 0:ncol], idf,
                             ["shf", "cst"], ["ps6"])
                    nm = len(ms)
                    so_, sok = sho[g4 % 2], "sho%d" % (g4 % 2)
                    h.cp("act", so_[0:ncol, 0:nm * 128], PS[6][0:ncol, 0:nm * 128], ["ps6"], [sok])
                    if samp:
                        h.dma("sp", o_shift_s[4 * j:4 * j + 4, ms[0] * 128:(ms[0] + nm) * 128], so_[0:4, 0:nm * 128],
                              [sok], [], isout=True)
                    else:
                        h.dma("sp", o_shift_p[0:1, ms[0] * 128:(ms[0] + nm) * 128], so_[0:1, 0:nm * 128],
                              [sok], [], isout=True)
            if not samp:
                h.cp("pool", lastc[:], pv[:, :, 0, 128:129], ["prw"], ["lastc"])
            xs4 = xs[:].rearrange("p m (s l) -> p m s l", s=nseq)
            cur = pv[:, :, :, 1:L + 1]
            prv = pv[:, :, :, 0:L]
            h.tt("dve", xs4[:, 0:9], prv[:, 0:9], cur[:, 0:9], ALU.subtract, ["prw"], [("xs", 0)])
            h.tt("pool", xs4[:, 9:14], prv[:, 9:14], cur[:, 9:14], ALU.subtract, ["prw"], [("xs", 1)])
            for m in range(14):
                h.stt("dve", xs4[:, m], xs4[:, m], prm[:, PR_MU + m:PR_MU + m + 1], cur[:, m], ALU.mult, ALU.add,
                      [("xs", 0 if m < 9 else 1), "prm", "prw"], [("xs", 2 + m)])
            rT = xs[:, 0:4, :]
            kT = xs[:, 4:8, :]
            vT = xs[:, 8:12, :]

            sw, aa, cum, E, Einv, Eprev, Eend, kk, kh, tmp = Fm
            n = lambda i: "F%d" % i
            N_SW, N_AA, N_CUM, N_E, N_EINV, N_EPREV, N_EEND, N_KK, N_KH, N_TMP = [n(i) for i in range(10)]
            gT, N_GT = gTs[b % 2], "gT%d" % (b % 2)
            bon, N_BON = bons[b % 2], "bon%d" % (b % 2)
            AR, N_AR = ARs[b % 2], "AR%d" % (b % 2)
            bT, kTb, BpT, KpT, vbf = Hm[0], Hm[1], Hm[2], Hm[3], Hm[4]
            h.act(lora_in[0:64, :], xs[0:64, 12, :], AF.Tanh, ["xs"], [("lora_in", 0)])
            h.cp("act", lora_in[64:128, :], xs[64:128, 12, :], ["xs"], [("lora_in", 1)])
            h.act(slg[:], xs[:, 13, :], AF.Sigmoid, ["xs"], ["slg"])
            for hg in range(4):
                h.mm(PS[5][:, hg * 128:(hg + 1) * 128], w2a2[0:64, hg * 128:(hg + 1) * 128], lora_in[0:64, :],
                     True, True, [("w2a2", 0), ("lora_in", 0)], ["ps5"])
            for hg in range(4):
                h.mm(PS[6][:, hg * 128:(hg + 1) * 128], w2a2[64:128, hg * 128:(hg + 1) * 128], lora_in[64:128, :],
                     True, True, [("w2a2", 1), ("lora_in", 1)], ["ps6"])
            for hg in range(4):
                h.act(sw[:, hg, :], PS[5][:, hg * 128:(hg + 1) * 128], AF.Sigmoid, ["ps5", "prm"], [(N_SW, hg)],
                      bias=prm[:, PR_W0 + hg:PR_W0 + hg + 1])
                h.act(aa[:, hg, :], PS[6][:, hg * 128:(hg + 1) * 128], AF.Sigmoid, ["ps6", "prm"], [(N_AA, hg)],
                      bias=prm[:, PR_A0 + hg:PR_A0 + hg + 1])
            for hg in range(4):
                h.mm(PS[5][:, hg * 128:(hg + 1) * 128], g2[:, hg * 128:(hg + 1) * 128], slg[:],
                     True, True, ["g2", "slg"], ["ps5"])
            h.cp("act", gT[:], PS[5][:].rearrange("p (c l) -> p c l", c=4), ["ps5"], [N_GT])
            h.stt("dve", sw[:], sw[:], C0, valid.unsqueeze(1).to_broadcast([128, 4, 128]),
                  ALU.mult, ALU.mult, [N_SW, "cst"], [N_SW])
            for hg in range(4):
                h.scan(cum[:, hg, :], rstm, sw[:, hg, :], [N_SW, "cst"], [(N_CUM, hg)])
            h.act(E[:], cum[:], AF.Exp, [N_CUM], [N_E])
            h.act(Einv[:], cum[:], AF.Exp, [N_CUM], [N_EINV], scale=-1.0)
            h.tt("pool", tmp[:], cum[:], sw[:], ALU.subtract, [N_CUM, N_SW], [N_TMP])
            h.act(Eprev[:], tmp[:], AF.Exp, [N_TMP], [N_EPREV])
            cum4 = cum[:].rearrange("p g (c l) -> p g c l", c=4)
            h.tt("pool", tmp[:].rearrange("p g (c l) -> p g c l", c=4),
                 cum4[:, :, :, 31:32].to_broadcast([128, 4, 4, 32]), cum4, ALU.subtract, [N_CUM], [N_TMP])
            h.act(Eend[:], tmp[:], AF.Exp, [N_TMP], [N_EEND])
            if samp:
                h.tt("pool", Eend[:], Eend[:], valid.unsqueeze(1).to_broadcast([128, 4, 128]), ALU.mult,
                     [N_EEND, "cst"], [N_EEND])
            h.tt("dve", kk[:], kT, bc(prm[:, PR_KK:PR_KK + 4], [128, 4, 128]), ALU.mult, ["xs", "prm"], [N_KK])
            h.act(tmp[:], kk[:], AF.Square, [N_KK], [N_TMP])
            for hg in range(4):
                h.mm(PS[6][:, hg * 128:(hg + 1) * 128], bones, tmp[:, hg, :], True, True, ["cst", N_TMP], ["ps6"])
            h.act(tmp[:], PS[6][:].rearrange("p (c l) -> p c l", c=4), AF.Sqrt, ["ps6"], [N_TMP])
            h.ts("dve", tmp[:], tmp[:], 1e-12, None, ALU.max, None, [N_TMP], [N_TMP])
            h.recip(tmp[:], tmp[:], [N_TMP], [N_TMP])
            h.tt("dve", kk[:], kk[:], tmp[:], ALU.mult, [N_KK, N_TMP], [N_KK])
            h.tt("pool", kh[:], aa[:], bc(prm[:, PR_KA:PR_KA + 4], [128, 4, 128]), ALU.mult, [N_AA, "prm"], [N_KH])
            h.tt("pool", kh[:], kh[:], bc(prm[:, PR_OMKA:PR_OMKA + 4], [128, 4, 128]), ALU.add, [N_KH, "prm"], [N_KH])
            h.tt("pool", kh[:], kh[:], kT, ALU.mult, [N_KH, "xs"], [N_KH])
            h.tt("dve", tmp[:], rT, bc(prm[:, PR_RK:PR_RK + 4], [128, 4, 128]), ALU.mult, ["xs", "prm"], [N_TMP])
            h.tt("dve", tmp[:], tmp[:], kh[:], ALU.mult, [N_TMP, N_KH], [N_TMP])
            for hg in range(4):
                h.mm(PS[5][:, hg * 128:(hg + 1) * 128], bones, tmp[:, hg, :], True, True, ["cst", N_TMP], ["ps5"])
            h.tt("dve", bon[:], PS[5][:].rearrange("p (c l) -> p c l", c=4), vT, ALU.mult, ["ps5", "xs"], [N_BON])
            h.tt("pool", aa[:], aa[:], kk[:], ALU.mult, [N_AA, N_KK], [N_AA])
            h.tt("dve", AR[:, :, 1, :], rT, E[:], ALU.mult, ["xs", N_E], [(N_AR, 1)])
            h.stt("dve", AR[:, :, 0, :], kk[:], -1.0, Eprev[:], ALU.mult, ALU.mult, [N_KK, N_EPREV], [(N_AR, 0)])
            h.tt("dve", bT[:], aa[:], Einv[:], ALU.mult, [N_AA, N_EINV], ["Hb0"])
            h.tt("dve", kTb[:], kh[:], Einv[:], ALU.mult, [N_KH, N_EINV], ["Hb1"])
            h.tt("pool", BpT[:], aa[:], Eend[:], ALU.mult, [N_AA, N_EEND], ["Hb2"])
            h.tt("pool", KpT[:], kh[:], Eend[:], ALU.mult, [N_KH, N_EEND], ["Hb3"])
            h.cp("pool", vbf[:], vT, ["xs"], ["Hb4"])
            h.cp("pool", stat[:, 0:16].rearrange("p (g c) -> p g c", g=4),
                 E[:].rearrange("p g (c l) -> p g c l", c=4)[:, :, :, 31], [N_E], [("stat", 0)])
            gam = stat[:, 0:16].rearrange("p (g c) -> p g c", g=4)

            for hd in range(8):
                hg, pb = hd // 2, 64 * (hd % 2)
                px = PS[hd % 2]
                pxn = "ps%d" % (hd % 2)
                h.mm(px[:, 0:256], bT[pb:pb + 64, hg, :], AR[pb:pb + 64, hg, :, :].rearrange("p a l -> p (a l)"),
                     True, True, ["Hb0", N_AR], [pxn])
                h.mm(px[:, 256:512], kTb[pb:pb + 64, hg, :], AR[pb:pb + 64, hg, :, :].rearrange("p a l -> p (a l)"),
                     True, True, ["Hb1", N_AR], [pxn])
                h.tt("dve", SCH[:, hd, :, :].rearrange("p a l -> p (a l)"), px[:], m4, ALU.mult,
                     [pxn, "cst"], [("SCH", hd)])
            Ac, Nn, An, Xc, Xtc, Xn, Xtn, Nc2 = INV
            NI = ["INV%d" % i for i in range(8)]
            Ac4 = Ac[:].rearrange("p (g a) l -> p g a l", a=2)
            for h2 in range(2):
                pt = PS[2 + h2]
                ptn = "ps%d" % (2 + h2)
                pb = 64 * h2
                for hg in range(4):
                    h.mm(pt[:, hg * 128:(hg + 1) * 128], AR[pb:pb + 64, hg, 0, :], bT[pb:pb + 64, hg, :],
                         True, True, [N_AR, "Hb0"], [ptn])
                h.tt("dve", Ac4[:, :, h2, :], pt[:].rearrange("p (q l) -> p q l", q=4),
                     msl.unsqueeze(1).to_broadcast([128, 4, 128]), ALU.mult, [ptn, "cst"], [NI[0]])
            h.tt("pool", Xc[:], SCH[:, :, 0, :], idb[:].unsqueeze(1).to_broadcast([128, 8, 128]), ALU.add,
                 ["SCH", "idb"], [NI[3]])
            h.tt("pool", Xtc[:], Ac[:], idb[:].unsqueeze(1).to_broadcast([128, 8, 128]), ALU.add,
                 [NI[0], "idb"], [NI[4]])

            Ncur_ap = lambda hd: SCH[:, hd, 0, :]
            Ncur_key = "SCH"
            Acur, Acur_key = Ac, NI[0]
            Xcur, Xcur_key, Xtcur, Xtcur_key = Xc, NI[3], Xtc, NI[4]
            Nnext = [(Nn, NI[1]), (Nc2, NI[7])]
            Anext = [(An, NI[2]), (Ac, NI[0])]
            Xnext = [(Xn, NI[5]), (Xc, NI[3])]
            Xtnext = [(Xtn, NI[6]), (Xtc, NI[4])]
            for lvl in range(4):
                last = (lvl == 3)
                Nx, Nxk = Nnext[lvl % 2]
                Ax, Axk = Anext[lvl % 2]
                Xx, Xxk = Xnext[lvl % 2]
                Xtx, Xtxk = Xtnext[lvl % 2]
                for g2i in range(2):
                    pa, pan = PS[2 * g2i], "ps%d" % (2 * g2i)
                    pbk, pbn = PS[2 * g2i + 1], "ps%d" % (2 * g2i + 1)
                    for q in range(4):
                        hd = 4 * g2i + q
                        h.mm(pa[:, q * 128:(q + 1) * 128], Acur[:, hd, :], Ncur_ap(hd), True, True,
                             [Acur_key, Ncur_key], [pan])
                    h.cp("act", Nx[:, 4 * g2i:4 * g2i + 4, :], pa[:].rearrange("p (q l) -> p q l", q=4),
                         [pan], [(Nxk, g2i)])
                    if not last:
                        for q in range(4):
                            hd = 4 * g2i + q
                            h.mm(pbk[:, q * 128:(q + 1) * 128], Ncur_ap(hd), Acur[:, hd, :], True, True,
                                 [Acur_key, Ncur_key], [pbn])
                        h.cp("act", Ax[:, 4 * g2i:4 * g2i + 4, :], pbk[:].rearrange("p (q l) -> p q l", q=4),
                             [pbn], [(Axk, g2i)])
                for g2i in range(2):
                    pa, pan = PS[4], "ps4"
                    for q in range(4):
                        hd = 4 * g2i + q
                        h.mm(pa[:, q * 128:(q + 1) * 128], Xtcur[:, hd, :], Nx[:, hd, :], True, True,
                             [Xtcur_key, (Nxk, g2i)], [pan])
                    h.tt("dve", Xx[:, 4 * g2i:4 * g2i + 4, :], pa[:].rearrange("p (q l) -> p q l", q=4),
                         Xcur[:, 4 * g2i:4 * g2i + 4, :], ALU.add, [pan, Xcur_key], [(Xxk, g2i)])
                if not last:
                    for g2i in range(2):
                        pa, pan = PS[2 * g2i], "ps%d" % (2 * g2i)
                        for q in range(4):
                            hd = 4 * g2i + q
                            h.mm(pa[:, q * 128:(q + 1) * 128], Nx[:, hd, :], Xtcur[:, hd, :], True, True,
                                 [Xtcur_key, (Nxk, g2i)], [pan])
                        h.tt("dve", Xtx[:, 4 * g2i:4 * g2i + 4, :], pa[:].rearrange("p (q l) -> p q l", q=4),
                             Xtcur[:, 4 * g2i:4 * g2i + 4, :], ALU.add, [pan, Xtcur_key], [(Xtxk, g2i)])
                Ncur_ap = (lambda t: (lambda hd: t[:, hd, :]))(Nx)
                Ncur_key = Nxk
                Acur, Acur_key = Ax, Axk
                Xcur, Xcur_key = Xx, Xxk
                Xtcur, Xtcur_key = Xtx, Xtxk
            X4, X4k = Xcur, Xcur_key

            Atok, Bptok, Kptok, Vtok = tok
            srcs = [(AR[:, :, 0, :], N_AR, Atok, "tok0"), (BpT[:], "Hb2", Bptok, "tok1"),
                    (KpT[:], "Hb3", Kptok, "tok2"), (vbf[:], "Hb4", Vtok, "tok3")]
            for si in range(0, 4, 2):
                for u in range(2):
                    src, srck, dst, dstk = srcs[si + u]
                    for hg in range(4):
                        h.tr(psT[:, u * 4 + hg, :], src[:, hg, :], idb[:], [srck, "idb"], [("psT", u * 4 + hg)])
                for u in range(2):
                    src, srck, dst, dstk = srcs[si + u]
                    h.cp("act", dst[:], psT[:, u * 4:(u + 1) * 4, :].rearrange("p c l -> p (c l)"),
                         ["psT"], [dstk])

            for hd in range(8):
                h.mm(PS[0][:, hd * 64:(hd + 1) * 64], SCH[:, hd, 2, :], Vtok[:, hd * 64:(hd + 1) * 64],
                     True, True, ["SCH", "tok3"], ["ps0"])
            h.cp("act", Zbf[:], PS[0][:], ["ps0"], ["Zbf"])
            for hd in range(8):
                h.mm(PS[1][:, hd * 64:(hd + 1) * 64], X4[:, hd, :], Zbf[:, hd * 64:(hd + 1) * 64],
                     True, True, [X4k, "Zbf"], ["ps1"])
            h.cp("act", Yf[:], PS[1][:], ["ps1"], ["Yf"])
            for hd in range(8):
                hg, pb = hd // 2, 64 * (hd % 2)
                h.mm(PS[2][pb:pb + 64, hg * 128:(hg + 1) * 128], Atok[:, hd * 64:(hd + 1) * 64], X4[:, hd, :],
                     True, True, ["tok0", X4k], ["ps2"])
            h.cp("act", WTbf[:], PS[2][:].rearrange("p (c l) -> p c l", c=4), ["ps2"], ["WTbf"])

            psUs, psUn = (PS[0], PS[1]), ("ps0", "ps1")
            psOs, psOn = (PS[2], PS[3]), ("ps2", "ps3")
            psPn = PS[4]
            Ubf4 = Ubf[:].rearrange("p (g a v) -> p g a v", g=4, a=2)
            Yf4 = Yf[:].rearrange("p (g a v) -> p g a v", g=4, a=2)
            for c in range(4):
                s0 = 32 * c
                if samp:
                    seq = 4 * j + c
                    h.dma("sp", Sin[:], st_wkv[seq].rearrange("h v k -> v h k"), [], ["Sin"])
                    for hg in range(4):
                        h.tr(PS[4][:, hg * 64:(hg + 1) * 64],
                             Sin[:, 2 * hg:2 * hg + 2, :].rearrange("p a k -> p (a k)"), idf[0:64, 0:64],
                             ["Sin", "cst"], ["ps4"])
                    h.cp("act", Pst[:], PS[4][:, 0:256].rearrange("p (c v) -> p c v", c=4), ["ps4"], ["Pst"])
                    h.cp("dve", Pbf[:], Pst[:], ["Pst"], ["Pbf"])
                for hd in range(8):
                    hg, h2 = hd // 2, hd % 2
                    pb = 64 * h2
                    h.mm(psUs[h2][s0:s0 + 32, hg * 64:(hg + 1) * 64], WTbf[pb:pb + 64, hg, s0:s0 + 32],
                         Pbf[pb:pb + 64, hg, :], True, True, ["WTbf", "Pbf"], [psUn[h2]], tp=(pb, s0))
                for hd in range(8):
                    hg, h2 = hd // 2, hd % 2
                    pb = 64 * h2
                    h.mm(psOs[h2][s0:s0 + 32, hg * 64:(hg + 1) * 64], AR[pb:pb + 64, hg, 1, s0:s0 + 32],
                         Pbf[pb:pb + 64, hg, :], True, True, [N_AR, "Pbf"], [psOn[h2]], tp=(pb, s0))
                for h2 in range(2):
                    h.tt("dve", Ubf4[s0:s0 + 32, :, h2, :],
                         psUs[h2][s0:s0 + 32, 0:256].rearrange("p (g v) -> p g v", g=4),
                         Yf4[s0:s0 + 32, :, h2, :], ALU.add, [psUn[h2], "Yf"], [("Ubf", c)])
                for hd in range(8):
                    hg, pb = hd // 2, 64 * (hd % 2)
                    h.mm(psPn[pb:pb + 64, hg * 64:(hg + 1) * 64], Bptok[s0:s0 + 32, hd * 64:(hd + 1) * 64],
                         Ubf[s0:s0 + 32, hd * 64:(hd + 1) * 64], True, False, ["tok1", ("Ubf", c)], ["ps4"],
                         tp=(s0, pb))
                    h.mm(psPn[pb:pb + 64, hg * 64:(hg + 1) * 64], Kptok[s0:s0 + 32, hd * 64:(hd + 1) * 64],
                         Vtok[s0:s0 + 32, hd * 64:(hd + 1) * 64], False, True, ["tok2", "tok3"], ["ps4"],
                         tp=(s0, pb))
                for hg in range(4):
                    h.stt("dve", Pst[:, hg, :], Pst[:, hg, :], gam[:, hg, c:c + 1], psPn[:, hg * 64:(hg + 1) * 64],
                          ALU.mult, ALU.add, ["Pst", ("stat", 0), "ps4"], ["Pst"])
                if not samp and not (b == NPB - 1 and c == 3):
                    h.cp("dve", Pbf[:], Pst[:], ["Pst"], ["Pbf"])
                if samp or (b == NPB - 1 and c == 3):
                    for hg in range(4):
                        h.tr(PS[4][0:64, hg * 128:(hg + 1) * 128], Pst[:, hg, :], idf, ["Pst", "cst"], ["ps4"])
                    h.cp("act", Sin[:].rearrange("p h k -> p (h k)"), PS[4][0:64, :], ["ps4"], ["Sin"])
                    dst = o_wkv_s[4 * j + c] if samp else o_wkv_p[0]
                    h.dma("sp", dst.rearrange("h v k -> v h k"), Sin[:], ["Sin"], [], isout=True)
            for hd in range(8):
                h.mm(PS[0][:, hd * 64:(hd + 1) * 64], SCH[:, hd, 1, :], Ubf[:, hd * 64:(hd + 1) * 64],
                     True, False, ["SCH", "Ubf"], ["ps0"])
                h.mm(PS[0][:, hd * 64:(hd + 1) * 64], SCH[:, hd, 3, :], Vtok[:, hd * 64:(hd + 1) * 64],
                     False, True, ["SCH", "tok3"], ["ps0"])
            o14 = o1[:].rearrange("p (g a v) -> p g a v", g=4, a=2)
            for h2 in range(2):
                h.cp("act", o14[:, :, h2, :], psOs[h2][:, 0:256].rearrange("p (g v) -> p g v", g=4),
                     [psOn[h2]], ["o1"])
            h.tt("dve", oo[:], o1[:], PS[0][:], ALU.add, ["o1", "ps0"], ["oo"])

            oo3 = oo[:].rearrange("p (h v) -> p h v", h=8)
            osq = o1
            h.act(osq[:], oo[:], AF.Square, ["oo"], ["o1"])
            s1, s2, mean, msq, rs, nb = (stat[:, 16:24], stat[:, 24:32], stat[:, 32:40], stat[:, 40:48],
                                         stat[:, 48:56], stat[:, 56:64])
            h.rsum(s1, oo3, ["oo"], [("stat", 1)])
            h.rsum(s2, osq[:].rearrange("p (h v) -> p h v", h=8), ["o1"], [("stat", 2)])
            h.ts("dve", mean, s1, 1.0 / 64, None, ALU.mult, None, [("stat", 1)], [("stat", 3)])
            h.tt("dve", msq, mean, mean, ALU.mult, [("stat", 3)], [("stat", 4)])
            h.stt("dve", rs, s2, 1.0 / 64, msq, ALU.mult, ALU.subtract, [("stat", 2), ("stat", 4)], [("stat", 5)])
            h.ts("dve", rs, rs, 64e-5, None, ALU.add, None, [("stat", 5)], [("stat", 5)])
            h.act(rs, rs, AF.Sqrt, [("stat", 5)], [("stat", 5)])
            h.recip(rs, rs, [("stat", 5)], [("stat", 5)])
            h.tt("dve", oo3, oo3, mean.unsqueeze(2).to_broadcast([128, 8, 64]), ALU.subtract,
                 ["oo", ("stat", 3)], ["oo"])
            h.tt("dve", oo3, oo3, rs.unsqueeze(2).to_broadcast([128, 8, 64]), ALU.mult,
                 ["oo", ("stat", 5)], ["oo"])
            for hg in range(4):
                h.tr(PS[0][:, hg * 128:(hg + 1) * 128], oo[:, hg * 128:(hg + 1) * 128], idf, ["oo", "cst"], ["ps0"])
            ps0v = PS[0][:].rearrange("p (c l) -> p c l", c=4)
            t2 = o1[:].rearrange("p (c l) -> p c l", c=4)
            h.tt("dve", t2, ps0v, bc(prm[:, PR_LW:PR_LW + 4], [128, 4, 128]), ALU.mult, ["ps0", "prm"], ["o1"])
            h.tt("pool", t2, t2, bc(prm[:, PR_LB:PR_LB + 4], [128, 4, 128]), ALU.add, ["o1", "prm"], ["o1"])
            h.tt("pool", t2, t2, bon[:], ALU.add, ["o1", N_BON], ["o1"])
            h.tt("dve", ogbf[:], t2, gT[:], ALU.mult, ["o1", N_GT], ["ogbf"])
            if samp:
                h.dma("sp", og_s[NPB].rearrange("p (c j q t) -> p c j q t", c=4, j=4, q=4)[:, :, j],
                      ogbf[:].rearrange("p c (q l) -> p c q l", q=4)[:, :, :, 0:8], ["ogbf"], [])
            else:
                h.dma("sp", og_s[b], ogbf[:].rearrange("p c l -> p (c l)"), ["ogbf"], [])

            qT, kgT, vgT, ogT = xg[:, 0:2, :], xg[:, 2:4, :], xg[:, 4:8, :], xg[:, 8:12, :]
            for c2 in range(2):
                h.mm(PS[5][:, c2 * 128:(c2 + 1) * 128], wg2[0:16, c2 * 128:(c2 + 1) * 128], lga[:], True, True,
                     ["wg2", "lga"], ["ps5"])
            la_, cg, Eg, Eginv, Egend, gtmp = Gf
            for c2 in range(2):
                h.act(la_[:, c2, :], PS[5][:, c2 * 128:(c2 + 1) * 128], AF.Exp, ["ps5", "prm"], [("G0", c2)],
                      scale=-1.0, bias=prm[:, PR_BG + c2:PR_BG + c2 + 1])
            h.ts("dve", la_[:], la_[:], 1.0, None, ALU.add, None, ["G0"], ["G0"])
            h.act(la_[:], la_[:], AF.Ln, ["G0"], ["G0"])
            h.stt("dve", la_[:], la_[:], -1.0 / 16.0,
                  valid.unsqueeze(1).to_broadcast([128, 2, 128]), ALU.mult, ALU.mult, ["G0", "cst"], ["G0"])
            for c2 in range(2):
                h.scan(cg[:, c2, :], rstm, la_[:, c2, :], ["G0", "cst"], [("G1", c2)])
            h.act(Eg[:], cg[:], AF.Exp, ["G1"], ["G2"])
            h.act(Eginv[:], cg[:], AF.Exp, ["G1"], ["G3"], scale=-1.0)
            cg4 = cg[:].rearrange("p g (c l) -> p g c l", c=4)
            h.tt("pool", gtmp[:].rearrange("p g (c l) -> p g c l", c=4),
                 cg4[:, :, :, 31:32].to_broadcast([128, 2, 4, 32]), cg4, ALU.subtract, ["G1"], ["G5"])
            h.act(Egend[:], gtmp[:], AF.Exp, ["G5"], ["G4"])
            if samp:
                h.tt("pool", Egend[:], Egend[:], valid.unsqueeze(1).to_broadcast([128, 2, 128]),
                     ALU.mult, ["G4", "cst"], ["G4"])
            h.cp("pool", statg[:, 0:8].rearrange("p (g c) -> p g c", g=2),
                 Eg[:].rearrange("p g (c l) -> p g c l", c=4)[:, :, :, 31], ["G2"], [("statg", 0)])
            gamg = statg[:, 0:8].rearrange("p (g c) -> p g c", g=2)
            h.stt("dve", qdT[:], qT, 0.125, Eg[:], ALU.mult, ALU.mult, [("xg", 0), "G2"], ["Gh0"])
            h.tt("pool", kiT[:], kgT, Eginv[:], ALU.mult, [("xg", 0), "G3"], ["Gh1"])
            h.tt("pool", keT[:], kgT, Egend[:], ALU.mult, [("xg", 0), "G4"], ["Gh2"])
            h.cp("pool", vgbf[:], vgT, [("xg", 1)], ["Gv"])
            h.act(silu[:], ogT, AF.Silu, [("xg", 2)], ["Gs"])
            GS4 = GS[:].rearrange("p (g a) l -> p g a l", a=2)
            for h2 in range(2):
                pb = 64 * h2
                pt, ptn = PS[5 + h2], "ps%d" % (5 + h2)
                for c2 in range(2):
                    h.mm(pt[:, c2 * 128:(c2 + 1) * 128], kiT[pb:pb + 64, c2, :], qdT[pb:pb + 64, c2, :], True, True,
                         ["Gh1", "Gh0"], [ptn])
                h.tt("dve", GS4[:, :, h2, :], pt[:, 0:256].rearrange("p (q l) -> p q l", q=2),
                     miu.unsqueeze(1).to_broadcast([128, 2, 128]), ALU.mult, [ptn, "cst"], ["Ggs"])
            for hg in range(4):
                h.tr(psT[:, hg, :], vgbf[:, hg, :], idb[:], ["Gv", "idb"], [("psT", hg)])
            for c2 in range(2):
                h.tr(psT[:, 4 + c2, :], keT[:, c2, :], idb[:], ["Gh2", "idb"], [("psT", 4 + c2)])
            h.cp("act", Vgtok[:], psT[:, 0:4, :].rearrange("p c l -> p (c l)"), ["psT"], ["Gt0"])
            h.cp("act", Ketok[:], psT[:, 4:6, :].rearrange("p c l -> p (c l)"), ["psT"], ["Gt1"])
            if samp:
                h.dma("sp", SgIn[:], st_gla[4 * j:4 * j + 4].rearrange("q (c2 h2) k v -> (h2 k) q c2 v", h2=2),
                      [], ["SgIn"])
                h.cp("act", Sgbf[:], SgIn[:], ["SgIn"], ["Sgbf"])
            else:
                h.cp("act", Sgbf[:, 0, :, :], Sg[:], ["Sg"], [("Sgbf", 0)])
            for c in range(4):
                s0 = 32 * c
                pt, pn = PS[5 + c % 2], "ps%d" % (5 + c % 2)
                for hd in range(4):
                    c2, pb = hd // 2, 64 * (hd % 2)
                    off = c2 * 128
                    h.mm(pt[pb:pb + 64, off:off + 128], Ketok[s0:s0 + 32, hd * 64:(hd + 1) * 64],
                         Vgtok[s0:s0 + 32, hd * 128:(hd + 1) * 128], True, True, ["Gt1", "Gt0"], [pn],
                         tp=(s0, pb))
                dv = pt[:, 0:256].rearrange("p (g v) -> p g v", g=2)
                gb = gamg[:, :, c:c + 1].to_broadcast([128, 2, 128])
                for c2 in range(2):
                    if samp:
                        h.stt("dve", SgIn[:, c, c2, :], SgIn[:, c, c2, :], gamg[:, c2, c:c + 1], dv[:, c2, :],
                              ALU.mult, ALU.add, [("SgIn", c), ("statg", 0), pn], [("SgIn", c)])
                    else:
                        h.stt("dve", Sg[:, c2, :], Sg[:, c2, :], gamg[:, c2, c:c + 1], dv[:, c2, :],
                              ALU.mult, ALU.add, ["Sg", ("statg", 0), pn], ["Sg"])
                if (not samp) and c < 3:
                    h.cp("act", Sgbf[:, c + 1, :, :], Sg[:], ["Sg"], [("Sgbf", c + 1)])
            if samp:
                h.dma("sp", o_gla_s[4 * j:4 * j + 4].rearrange("q (c2 h2) k v -> (h2 k) q c2 v", h2=2),
                      SgIn[:], ["SgIn"], [], isout=True)
            elif b == NPB - 1:
                h.dma("sp", o_gla_p[0].rearrange("(c2 h2) k v -> (h2 k) c2 v", h2=2), Sg[:], ["Sg"], [],
                      isout=True)
            for c in range(4):
                s0 = 32 * c
                for hd in range(4):
                    c2, h2 = hd // 2, hd % 2
                    pb = 64 * h2
                    h.mm(PS[5 + h2][s0:s0 + 32, c2 * 128:(c2 + 1) * 128], qdT[pb:pb + 64, c2, s0:s0 + 32],
                         Sgbf[pb:pb + 64, c, c2, :], True, True, ["Gh0", "Sgbf"], ["ps%d" % (5 + h2)], tp=(pb, s0))
            o1gv = o1g[:].rearrange("p (g a v) -> p g a v", g=2, a=2)
            for h2 in range(2):
                h.cp("act", o1gv[:, :, h2, :], PS[5 + h2][:, 0:256].rearrange("p (g v) -> p g v", g=2),
                     ["ps%d" % (5 + h2)], ["o1g"])
            for hd in range(4):
                h.mm(PS[5][:, hd * 128:(hd + 1) * 128], GS[:, hd, :], Vgtok[:, hd * 128:(hd + 1) * 128], True, True,
                     ["Ggs", "Gt0"], ["ps5"])
            h.tt("dve", oog[:], o1g[:], PS[5][:], ALU.add, ["o1g", "ps5"], ["oog"])
            oo4 = oog[:].rearrange("p (h v) -> p h v", h=4)
            h.act(o1g[:], oog[:], AF.Square, ["oog"], ["o1g"])
            gs2, grs = statg[:, 8:12], statg[:, 12:16]
            h.rsum(gs2, o1g[:].rearrange("p (h v) -> p h v", h=4), ["o1g"], [("statg", 1)])
            h.ts("dve", grs, gs2, 1.0 / 128, 1e-6, ALU.mult, ALU.add, [("statg", 1)], [("statg", 2)])
            h.act(grs, grs, AF.Sqrt, [("statg", 2)], [("statg", 2)])
            h.recip(grs, grs, [("statg", 2)], [("statg", 2)])
            h.tt("dve", oo4, oo4, grs.unsqueeze(2).to_broadcast([128, 4, 128]), ALU.mult, ["oog", ("statg", 2)], ["oog"])
            for hd in range(4):
                h.tr(PS[6][:, hd * 128:(hd + 1) * 128], oog[:, hd * 128:(hd + 1) * 128], idf, ["oog", "cst"], ["ps6"])
            h.stt("dve", obbf[:], PS[6][:].rearrange("p (c l) -> p c l", c=4), prm[:, PR_NW:PR_NW + 1], silu[:],
                  ALU.mult, ALU.mult, ["ps6", "prm", "Gs"], ["obbf"])
            if samp:
                h.dma("sp", ob_s[NPB].rearrange("p (c j q t) -> p c j q t", c=4, j=4, q=4)[:, :, j],
                      obbf[:].rearrange("p c (q l) -> p c q l", q=4)[:, :, :, 0:8], ["obbf"], [])
            else:
                h.dma("sp", ob_s[b], obbf[:].rearrange("p c l -> p (c l)"), ["obbf"], [])

            if dbg is not None and dbg.get("blk") == b and dbg.get("phase") == "a1":
                src, keys = dbg["fn"](dict(locals()))
                h.dma("sp", dbg_out, src, keys, [], isout=True)
        P.emit()

    gx = contextlib.ExitStack()
    wdn = gx.enter_context(nc.sbuf_tensor("g_wdn", [128, 22, D], BF16))
    w_dn_v = ffn_w_down.rearrange("(c p) n -> p c n", p=128)
    wdn_loaded = False
    with contextlib.ExitStack() as ph:
      if "b" in phases:
        P = Prog(nc, gs, "b")
        h = H(P)

        def sb(name, shape, dt):
            return ph.enter_context(nc.sbuf_tensor("b_" + name, list(shape), dt))

        def psb(name, shape, dt):
            return ph.enter_context(nc.psum_tensor("b_" + name, list(shape), dt))

        T = {}
        cst, idb = load_consts(h, sb)
        T["idb"] = idb
        prm = sb("prm", [128, NPRM], F32)
        T["prm"] = prm
        load_param_cols(h, prm, PR_G, norm_mix, 8)
        wing = sb("wing", [128, 8, 2048], BF16)
        w_in_v = w_in.rearrange("(c p) n -> p c n", p=128)
        for q4 in range(4):
            h.dma("pool", wing[:, :, q4 * 512:(q4 + 1) * 512], w_in_v[:, :, GATE0 + q4 * 512:GATE0 + (q4 + 1) * 512],
                  [], [("wing", q4)])
        woa = sb("woa", [128, 4, D], BF16)
        wob = sb("wob", [128, 4, D], BF16)
        wo = sb("wo", [128, 8, D], BF16)
        h.dma("pool", woa[:], w_out_a.rearrange("(c p) n -> p c n", p=128), [], ["woa"])
        h.dma("pool", wob[:], w_out_b.rearrange("(c p) n -> p c n", p=128), [], ["wob"])
        for kc in range(8):
            h.dma("pool", wo[:, kc, :], w_o.rearrange("(c p) n -> p c n", p=128)[:, kc, :], [], [("wo", kc)])
        for kc in range(22):
            h.dma("pool", wdn[:, kc, :], w_dn_v[:, kc, :], [], [("wdn", kc)])
        wdn_loaded = True
        xts = [sb("xt%d" % i, [128, D], F32) for i in range(4)]
        alloc_rms(T, sb)
        ogbs = [sb("ogb%d" % i, [128, 4, 128], BF16) for i in range(2)]
        obbs = [sb("obb%d" % i, [128, 4, 128], BF16) for i in range(2)]
        sgas = [sb("sga%d" % i, [128, 8, 128], F32) for i in range(2)]
        sgbs = [sb("sgb%d" % i, [128, 8, 128], F32) for i in range(2)]
        tas = [sb("ta%d" % i, [128, 8, 128], F32) for i in range(2)]
        mgs = [sb("mg%d" % i, [128, 8, 128], BF16) for i in range(2)]
        x1ts = [sb("x1t%d" % i, [128, D], F32) for i in range(2)]
        T["psT"] = psb("psT", [128, 8, 128], BF16)
        PS = [psb("ps%d" % i, [128, 512], F32) for i in range(7)]
        if FILLERS:
            _pf, _id = PS[6], idb
            P.filler = ((lambda e: e.matmul(_pf[:, 0:128], lhsT=_id[:], rhs=_id[:], start=True, stop=True)), 70.0)
        for b in BLKS2:
            xt, xtn = xts[b % 4], "xt%d" % (b % 4)
            load_x_block(h, b, xt, xtn, packed=True)
            p2 = b % 2
            ogb, obb, sga, sgb, ta, mg, x1t = ogbs[p2], obbs[p2], sgas[p2], sgbs[p2], tas[p2], mgs[p2], x1ts[p2]
            K_ogb, K_obb, K_sga, K_sgb, K_ta, K_mg, K_x1t = ["%s%d" % (nm, p2) for nm in
                                                            ("ogb", "obb", "sga", "sgb", "ta", "mg", "x1t")]
            h.dma("sp", ogb[:].rearrange("p c l -> p (c l)"), og_s[b], [], [K_ogb])
            h.dma("sp", obb[:].rearrange("p c l -> p (c l)"), ob_s[b], [], [K_obb])
            hT, hTk = rms_to_fm(h, T, xt, xtn, PR_G, b % 2)
            for half, dst, dk in ((0, sga, K_sga), (1, sgb, K_sgb)):
                for gi in range(2):
                    pt, pn = PS[gi], "ps%d" % gi
                    for mi in range(4):
                        c0 = half * 1024 + (4 * gi + mi) * 128
                        for kc in range(8):
                            h.mm(pt[:, mi * 128:(mi + 1) * 128], wing[:, kc, c0:c0 + 128], hT[:, kc, :],
                                 kc == 0, kc == 7, [("wing", c0 // 512), hTk], [pn])
                    h.act(dst[:, 4 * gi:4 * gi + 4, :], pt[:].rearrange("p (c l) -> p c l", c=4), AF.Sigmoid,
                          [pn], [(dk, gi)])
            for gi in range(2):
                pt, pn = PS[2 + gi], "ps%d" % (2 + gi)
                for mi in range(4):
                    m = 4 * gi + mi
                    for kc in range(4):
                        h.mm(pt[:, mi * 128:(mi + 1) * 128], woa[:, kc, m * 128:(m + 1) * 128], ogb[:, kc, :],
                             kc == 0, kc == 3, ["woa", K_ogb], [pn])
                h.tt("dve", ta[:, 4 * gi:4 * gi + 4, :], pt[:].rearrange("p (c l) -> p c l", c=4),
                     sga[:, 4 * gi:4 * gi + 4, :], ALU.mult, [pn, (K_sga, gi)], [(K_ta, gi)])
            for gi in range(2):
                pt, pn = PS[4 + gi], "ps%d" % (4 + gi)
                for mi in range(4):
                    m = 4 * gi + mi
                    for kc in range(4):
                        h.mm(pt[:, mi * 128:(mi + 1) * 128], wob[:, kc, m * 128:(m + 1) * 128], obb[:, kc, :],
                             kc == 0, kc == 3, ["wob", K_obb], [pn])
                h.tt("dve", sgb[:, 4 * gi:4 * gi + 4, :], pt[:].rearrange("p (c l) -> p c l", c=4),
                     sgb[:, 4 * gi:4 * gi + 4, :], ALU.mult, [pn, (K_sgb, gi)], [(K_sgb, gi)])
                h.tt("pool", mg[:, 4 * gi:4 * gi + 4, :], ta[:, 4 * gi:4 * gi + 4, :],
                     sgb[:, 4 * gi:4 * gi + 4, :], ALU.add, [(K_ta, gi), (K_sgb, gi)], [(K_mg, gi)])
            for nh in range(2):
                pt, pn = PS[2 + nh], "ps%d" % (2 + nh)
                for kc in range(8):
                    h.mm(pt[:], mg[:, kc, :], wo[:, kc, nh * 512:(nh + 1) * 512], kc == 0, kc == 7,
                         [K_mg, ("wo", kc)], [pn])
                h.tt("dve", x1t[:, nh * 512:(nh + 1) * 512], pt[:], xt[:, nh * 512:(nh + 1) * 512], ALU.add,
                     [pn, xtn], [(K_x1t, nh)])
            h.dma("sp", x1_s[b * 128:(b + 1) * 128, :], x1t[:], [K_x1t], [])
        P.emit()

    with contextlib.ExitStack() as ph:
      if "c" in phases:
        P = Prog(nc, gs, "c")
        h = H(P)

        def sb(name, shape, dt):
            return ph.enter_context(nc.sbuf_tensor("c_" + name, list(shape), dt))

        def psb(name, shape, dt):
            return ph.enter_context(nc.psum_tensor("c_" + name, list(shape), dt))

        T = {}
        cst, idb = load_consts(h, sb)
        T["idb"] = idb
        idf = cst[:, C_ID:C_ID + 128]
        prm = sb("prm", [128, 8], F32)
        T["prm"] = prm
        load_param_cols(h, prm, 0, norm_ffn, 8)
        cw = sb("cw", [128, 4, 44], F32)
        nfb = sb("nfb", [128, D], F32)
        h.dma("sp", nfb[:], norm_final.partition_broadcast(128), [], ["nfb"])
        wup = sb("wup", [128, 8, F2], BF16)
        w_up_v = ffn_w_up.rearrange("(c p) n -> p c n", p=128)
        WQ = [(0, 4), (4, 10), (10, 16), (16, 22)]
        for qi, (c0_, c1_) in enumerate(WQ):
            for half in range(2):
                lo, hi = (half * 22 + c0_) * 128, (half * 22 + c1_) * 128
                h.dma("pool", wup[:, :, lo:hi], w_up_v[:, :, lo:hi], [], [("wup", qi)])
        if not wdn_loaded:
            for kc in range(22):
                h.dma("pool", wdn[:, kc, :], w_dn_v[:, kc, :], [], [("wdn", kc)])
        xts = [sb("xt0", [128, D], F32), sb("xt1", [128, D], F32)]
        alloc_rms(T, sb)
        ub = [sb("ub0", [128, 4, 160], F32), sb("ub1", [128, 4, 160], F32)]
        ucar = sb("ucar", [128, 44, 2], F32)
        cin = sb("cin", [128, 44, 16, 2], F32)
        cout = sb("cout", [128, 44, 32], F32)
        cstg = [sb("cstg%d" % i, [32, 512], F32) for i in range(2)]
        csto = [sb("csto%d" % i, [32, 512], F32) for i in range(2)]
        fss = [sb("fss%d" % i, [128, 1], F32) for i in range(2)]
        frs = [sb("frs%d" % i, [128, 1], F32) for i in range(2)]
        cc = [sb("cc0", [128, 4, 128], F32), sb("cc1", [128, 4, 128], F32)]
        g1 = [sb("g10", [128, 2, 128], F32), sb("g11", [128, 2, 128], F32)]
        g2t = [sb("g20", [128, 2, 128], F32), sb("g21", [128, 2, 128], F32)]
        actTs = [sb("actT%d" % i, [128, 22, 128], BF16) for i in range(2)]
        x2s = [sb("x20", [128, D], F32)] * 2
        T["psT"] = psb("psT", [128, 8, 128], BF16)
        PS = [psb("ps%d" % i, [128, 512], F32) for i in range(7)]
        h.memset("pool", ucar[:], 0.0, ["ucar"])
        cb_v = ffn_conv_b.rearrange("(o f) -> o f", o=1)
        for g4 in range(11):
            cg_, cgk = cstg[g4 % 2], "cstg%d" % (g4 % 2)
            h.dma("sp", cg_[0:3, :], ffn_conv_w[:, g4 * 512:(g4 + 1) * 512], [], [(cgk, 0)])
            h.dma("sp", cg_[3:4, :], cb_v[:, g4 * 512:(g4 + 1) * 512], [], [(cgk, 1)])
            for mi in range(4):
                h.tr(PS[4][:, mi * 4:(mi + 1) * 4], cg_[0:4, mi * 128:(mi + 1) * 128], idf[0:4, 0:4],
                     [cgk, "cst"], ["ps4"])
            h.cp("act", cw[:, :, 4 * g4:4 * g4 + 4], PS[4][:, 0:16].rearrange("p (m j) -> p j m", j=4),
                 ["ps4"], [("cw", g4)])
        if FILLERS:
            _pf2, _id2 = PS[6], idb
            P.filler = ((lambda e: e.matmul(_pf2[:, 0:128], lhsT=_id2[:], rhs=_id2[:], start=True, stop=True)), 70.0)
        for b in BLKS2:
            samp, nseq, L = (True, 16, 8) if b >= NPB else (False, 1, 128)
            j = b - NPB
            xt, xtn = xts[b % 2], "xt%d" % (b % 2)
            actT, actk = actTs[b % 2], "actT%d" % (b % 2)
            x2, x2k = x2s[0], "x20"
            yt, ytk = x2, x2k
            h.dma("sp", xt[:], x1_s[b * 128:(b + 1) * 128, :], [], [xtn])
            hT, hTk = rms_to_fm(h, T, xt, xtn, 0, b % 2)
            W = nseq * (L + 2)
            if samp:
                stc_v = st_conv.rearrange("q t f -> (q t) f")
                for g4 in range(11):
                    cg_, cgk = cstg[g4 % 2], "cstg%d" % (g4 % 2)
                    h.dma("sp", cg_[:], stc_v[:, g4 * 512:(g4 + 1) * 512], [], [cgk])
                    for mi in range(4):
                        m = 4 * g4 + mi
                        h.tr(PS[4][:, mi * 32:(mi + 1) * 32], cg_[0:32, mi * 128:(mi + 1) * 128], idf[0:32, 0:32],
                             [cgk, "cst"], ["ps4"])
                    h.cp("act", cin[:, 4 * g4:4 * g4 + 4, :, :].rearrange("p m q t -> p m (q t)"),
                         PS[4][:, 0:128].rearrange("p (m x) -> p m x", m=4), ["ps4"], [("cin", g4)])
            last_prompt = (b == NPB - 1)
            for gi in range(11):
                u, un = ub[gi % 2], "ub%d" % (gi % 2)
                uv = u[:, :, 0:W].rearrange("p m (s l) -> p m s l", s=nseq)
                pt, pn = PS[gi % 2], "ps%d" % (gi % 2)
                chunks = [2 * gi, 2 * gi + 1, 22 + 2 * gi, 23 + 2 * gi]
                for mi, m in enumerate(chunks):
                    for kc in range(8):
                        h.mm(pt[:, mi * 128:(mi + 1) * 128], wup[:, kc, m * 128:(m + 1) * 128], hT[:, kc, :],
                             kc == 0, kc == 7, [("wup", [q_ for q_, (a_, b_) in enumerate(WQ) if a_ <= (m % 22) < b_][0]), hTk], [pn])
                h.cp("act", uv[:, :, :, 2:L + 2], pt[:].rearrange("p (m s l) -> p m s l", m=4, s=nseq),
                     [pn], [un])
                for half in range(2):
                    m0 = chunks[2 * half]
                    if samp:
                        h.cp("pool", uv[:, 2 * half:2 * half + 2, :, 0:2], cin[:, m0:m0 + 2, :, :],
                             [("cin", m0 // 4)], [un])
                    else:
                        h.cp("pool", uv[:, 2 * half:2 * half + 2, 0, 0:2], ucar[:, m0:m0 + 2, :],
                             [("ucar", gi)], [un])
                for half in range(2):
                    m0 = chunks[2 * half]
                    if samp:
                        h.cp("pool", cout[:, m0:m0 + 2, :].rearrange("p m (q t) -> p m q t", q=16),
                             uv[:, 2 * half:2 * half + 2, :, 8:10], [un], [("cout", gi)])
                    else:
                        h.cp("pool", ucar[:, m0:m0 + 2, :], uv[:, 2 * half:2 * half + 2, 0, 128:130],
                             [un], [("ucar", gi)])
                        if last_prompt:
                            h.cp("pool", cout[:, m0:m0 + 2, 0:2], uv[:, 2 * half:2 * half + 2, 0, 128:130],
                                 [un], [("cout", gi)])
                c_, cn = cc[gi % 2], "cc%d" % (gi % 2)
                for mi, m in enumerate(chunks):
                    eng = "dve"
                    c4 = c_[:, mi, :].rearrange("p (s l) -> p s l", s=nseq)
                    h.act(c4, uv[:, mi, :, 2:L + 2], AF.Identity, [un, "cw"], [(cn, mi)],
                          scale=cw[:, 2, m:m + 1], bias=cw[:, 3, m:m + 1])
                    h.stt(eng, c4, uv[:, mi, :, 1:L + 1], cw[:, 1, m:m + 1], c4, ALU.mult, ALU.add,
                          [un, "cw", (cn, mi)], [(cn, mi)])
                    h.stt(eng, c4, uv[:, mi, :, 0:L], cw[:, 0, m:m + 1], c4, ALU.mult, ALU.add,
                          [un, "cw", (cn, mi)], [(cn, mi)])
                ga, gan = g1[gi % 2], "g1%d" % (gi % 2)
                gb_, gbn = g2t[gi % 2], "g2%d" % (gi % 2)
                gate = c_[:, 2:4, :]
                val = c_[:, 0:2, :]
                h.act(ga[:], gate, AF.Square, [(cn, 2), (cn, 3)], [gan])
                h.ts("dve", ga[:], ga[:], 0.044715, 1.0, ALU.mult, ALU.add, [gan], [gan])
                h.tt("pool", ga[:], ga[:], gate, ALU.mult, [gan, (cn, 2), (cn, 3)], [gan])
                h.act(gb_[:], ga[:], AF.Sigmoid, [gan], [gbn], scale=GELU_S)
                h.tt("pool", gb_[:], gb_[:], gate, ALU.mult, [gbn, (cn, 2), (cn, 3)], [gbn])
                h.tt("dve", actT[:, 2 * gi:2 * gi + 2, :], gb_[:], val, ALU.mult, [gbn, (cn, 0), (cn, 1)],
                     [(actk, gi)])
            if samp or last_prompt:
                ncol = 32 if samp else 2
                for g4 in range(11):
                    for mi in range(4):
                        m = 4 * g4 + mi
                        h.tr(PS[4][0:ncol, mi * 128:(mi + 1) * 128], cout[:, m, 0:ncol], idf, ["cout", "cst"],
                             ["ps4"])
                    co_, cok = csto[g4 % 2], "csto%d" % (g4 % 2)
                    h.cp("act", co_[0:ncol, :], PS[4][0:ncol, :], ["ps4"], [cok])
                    if samp:
                        h.dma("sp", o_conv_s.rearrange("q t f -> (q t) f")[:, g4 * 512:(g4 + 1) * 512],
                              co_[:], [cok], [], isout=True)
                    else:
                        h.dma("sp", o_conv_p[0][:, g4 * 512:(g4 + 1) * 512], co_[0:2, :], [cok], [], isout=True)
            for nh in range(2):
                pt, pn = PS[2 + nh], "ps%d" % (2 + nh)
                for kc in range(22):
                    h.mm(pt[:], actT[:, kc, :], wdn[:, kc, nh * 512:(nh + 1) * 512], kc == 0, kc == 21,
                         [(actk, kc // 2), ("wdn", kc)], [pn])
                h.tt("dve", x2[:, nh * 512:(nh + 1) * 512], pt[:], xt[:, nh * 512:(nh + 1) * 512], ALU.add,
                     [pn, xtn], [(x2k, nh)])
            ss2, rs2 = fss[b % 2], frs[b % 2]
            ssk, rsk = "fss%d" % (b % 2), "frs%d" % (b % 2)
            jk, jkk = T["xn"][b % 2], "xn%d" % (b % 2)
            h.act(jk[:], x2[:], AF.Square, [x2k], [jkk, ssk], accum_out=ss2[:])
            h.ts("dve", rs2[:], ss2[:], 1.0 / D, 1e-6, ALU.mult, ALU.add, [ssk], [rsk])
            h.act(rs2[:], rs2[:], AF.Sqrt, [rsk], [rsk])
            h.recip(rs2[:], rs2[:], [rsk], [rsk])
            h.stt("dve", yt[:], x2[:], rs2[:, 0:1], nfb[:], ALU.mult, ALU.mult, [x2k, rsk, "nfb"], [ytk])
            if samp:
                h.dma("sp", ysm.rearrange("s t d -> (s t) d"), yt[:], [ytk], [], isout=True)
            else:
                h.dma("sp", yp[b * 128:(b + 1) * 128, :], yt[:], [ytk], [], isout=True)
        P.emit()
    gx.close()
    gs.close()
    return nc


_CACHE = {}


def kernel(**inputs):
    f32 = lambda a: np.ascontiguousarray(np.asarray(a), dtype=np.float32)
    if "nc" not in _CACHE:
        nc = bass.Bass("TRN2", target_bir_lowering=False)
        build(nc)
        _CACHE["nc"] = nc
    nc = _CACHE["nc"]
    cst = make_consts()
    shared = {
        "cst": cst,
        "norm_mix": f32(inputs["norm_mix"][0]), "w_in": f32(inputs["w_in"][0]),
        "mu_shift": f32(inputs["mu_shift"][0]), "rwkv_w0": f32(inputs["rwkv_w0"][0]),
        "rwkv_w2": f32(inputs["rwkv_w2"][0]), "rwkv_a0": f32(inputs["rwkv_a0"][0]),
        "rwkv_a2": f32(inputs["rwkv_a2"][0]), "rwkv_g2": f32(inputs["rwkv_g2"][0]),
        "rwkv_k_k": f32(inputs["rwkv_k_k"][0]), "rwkv_k_a": f32(inputs["rwkv_k_a"][0]),
        "rwkv_r_k": f32(inputs["rwkv_r_k"][0]).reshape(512), "rwkv_ln_w": f32(inputs["rwkv_ln_w"][0]),
        "rwkv_ln_b": f32(inputs["rwkv_ln_b"][0]), "gla_wg2": f32(inputs["gla_wg2"][0]),
        "gla_bg": f32(inputs["gla_bg"][0]), "gla_norm_w": f32(inputs["gla_norm_w"][0]),
        "w_out_a": f32(inputs["w_out_a"][0]), "w_out_b": f32(inputs["w_out_b"][0]),
        "w_o": f32(inputs["w_o"][0]), "norm_ffn": f32(inputs["norm_ffn"][0]),
        "ffn_w_up": f32(inputs["ffn_w_up"][0]), "ffn_conv_w": f32(inputs["ffn_conv_w"][0]),
        "ffn_conv_b": f32(inputs["ffn_conv_b"][0]), "ffn_w_down": f32(inputs["ffn_w_down"][0]),
        "norm_final": f32(inputs["norm_final"]),
    }
    in_maps = []
    for c in range(NCORES):
        m = dict(shared)
        sl = slice(16 * c, 16 * c + 16)
        m["xp"] = f32(inputs["x_prompt"][c])
        m["xs"] = f32(inputs["x_sample"][sl])
        m["st_shift"] = f32(inputs["state_rwkv_shift"][0, sl])
        m["st_wkv"] = f32(inputs["state_rwkv_wkv"][0, sl])
        m["st_gla"] = f32(inputs["state_gla"][0, sl])
        m["st_conv"] = f32(inputs["state_ffn_conv"][0, sl])
        in_maps.append(m)
    res = run_bass_kernel_spmd(nc, in_maps, core_ids=list(range(NCORES)))
    R = res.results
    cat = lambda k: np.concatenate([np.asarray(r[k]) for r in R], axis=0)
    y_p = np.stack([np.asarray(r["yp"]) for r in R], axis=0)
    y_s = cat("ys")
    outs = (
        y_p, y_s,
        cat("o_shift_p")[None], cat("o_wkv_p")[None], cat("o_gla_p")[None], cat("o_conv_p")[None],
        cat("o_shift_s")[None], cat("o_wkv_s")[None], cat("o_gla_s")[None], cat("o_conv_s")[None],
    )
    return tuple(np.ascontiguousarray(o, dtype=np.float32) for o in outs)
```
